# Optimizing a Trainium2 kernel written in Bass

```python
import jax, jax.numpy as jnp
from jax import lax
import numpy as np

D_MODEL = 1024
BATCH = 2
SEQ = 8192
DEPTH = 4
DEC_BATCH = 128
DEC_SEQ = 4
PAST_LEN = 2048
PAGE_SIZE = 128

N_A_LAYERS = DEPTH // 2
N_B_LAYERS = DEPTH - N_A_LAYERS
CONV_CH = D_MODEL
CONV_WIDTH = 31
HEAD_DIM = 64
N_KV_HEADS = D_MODEL // 128
WINDOWS = ((128, 1), (512, 4), (2048, 16))
N_GROUPS = len(WINDOWS)
MAX_WINDOW = max(w for w, _ in WINDOWS)
D_FF = 4 * D_MODEL
Q_BLOCK = 128
NORM_EPS = 1e-6

kernel_name = "yoco_conformer_conv_dilated_swa_decoder_step"


def rms_norm(x, g):
    xf = x.astype(jnp.float32)
    y = xf * lax.rsqrt(jnp.mean(xf * xf, axis=-1, keepdims=True) + NORM_EPS)
    return (y * g.astype(jnp.float32)).astype(x.dtype)


def layer_norm(x, g, b):
    xf = x.astype(jnp.float32)
    mu = jnp.mean(xf, axis=-1, keepdims=True)
    xc = xf - mu
    y = xc * lax.rsqrt(jnp.mean(xc * xc, axis=-1, keepdims=True) + NORM_EPS)
    return (y * g.astype(jnp.float32) + b.astype(jnp.float32)).astype(x.dtype)


def alibi_slopes():
    n = N_GROUPS * N_KV_HEADS
    s = 2.0 ** (-8.0 * jnp.arange(1, n + 1, dtype=jnp.float32) / n)
    return s.reshape(N_GROUPS, N_KV_HEADS)


def conv_module(h, conv_state, w_pw1, b_pw1, w_dw, b_dw, ln_g, ln_b, w_pw2, b_pw2):
    u = h @ w_pw1 + b_pw1
    a, gate = jnp.split(u, 2, axis=-1)
    u = a * jax.nn.sigmoid(gate)
    full = jnp.concatenate([conv_state.astype(u.dtype), u], axis=1)
    c = lax.conv_general_dilated(full, w_dw[:, None, :], window_strides=(1,), padding='VALID',
                                 dimension_numbers=('NWC', 'WIO', 'NWC'),
                                 feature_group_count=CONV_CH) + b_dw
    c = layer_norm(c, ln_g, ln_b)
    c = jax.nn.silu(c)
    out = c @ w_pw2 + b_pw2
    return out, full[:, -(CONV_WIDTH - 1):]


def sq_relu_mlp(h, w_up, w_down):
    z = jax.nn.relu(h @ w_up)
    return (z * z) @ w_down


def dilated_attention(q, k, v, q_idx):
    N, T = q.shape[0], q.shape[1]
    slopes = alibi_slopes()
    scale = HEAD_DIM ** -0.5

    def block(args):
        qb, ib = args
        ms, dens, outs = [], [], []
        for g, (w, d) in enumerate(WINDOWS):
            offs = jnp.arange(w // d + 1, dtype=jnp.int32) * d
            kidx = ib[:, None] - offs[None, :]
            valid = kidx >= 0
            kidx = jnp.maximum(kidx, 0)
            kg = jnp.take(k, kidx, axis=1)
            vg = jnp.take(v, kidx, axis=1)
            s = jnp.einsum('nthd,ntkhd->nthk', qb[:, :, g], kg,
                           preferred_element_type=jnp.float32) * scale
            s = s - slopes[g][:, None] * offs.astype(jnp.float32)[None, :]
            s = jnp.where(valid[None, :, None, :], s, -jnp.inf)
            m = jnp.max(s, axis=-1)
            p = jnp.exp(s - m[..., None])
            den = jnp.sum(p, axis=-1)
            o = jnp.einsum('nthk,ntkhd->nthd', p, vg,
                           preferred_element_type=jnp.float32) / den[..., None]
            ms.append(m); dens.append(den); outs.append(o)
        m = jnp.stack(ms)
        den = jnp.stack(dens)
        o = jnp.stack(outs)
        wgt = den * jnp.exp(m - jnp.max(m, axis=0, keepdims=True))
        out = jnp.sum(wgt[..., None] * o, axis=0) / jnp.sum(wgt, axis=0)[..., None]
        return out.astype(q.dtype)

    if T > Q_BLOCK and T % Q_BLOCK == 0:
        nb = T // Q_BLOCK
        qb = jnp.swapaxes(q.reshape(N, nb, Q_BLOCK, N_GROUPS, N_KV_HEADS, HEAD_DIM), 0, 1)
        ib = q_idx.reshape(nb, Q_BLOCK)
        out = lax.map(block, (qb, ib))
        return jnp.swapaxes(out, 0, 1).reshape(N, T, N_KV_HEADS, HEAD_DIM)
    return block((q, q_idx))


def trunk(x, conv_state, kv_past, norm_mix_g, norm_mlp_g, conv_w_pw1, conv_b_pw1, conv_w_dw, conv_b_dw,
          conv_ln_g, conv_ln_b, conv_w_pw2, conv_b_pw2, kv_norm_g, w_kv, k_norm_g, attn_w_q, q_norm_g,
          attn_w_o, mlp_w_up, mlp_w_down):
    N, T, _ = x.shape
    P = kv_past.shape[1]
    new_conv = []
    kv_new = None
    k_all = v_all = q_idx = None
    for layer in range(DEPTH):
        h = rms_norm(x, norm_mix_g[layer])
        if layer < N_A_LAYERS:
            o, st = conv_module(h, conv_state[layer], conv_w_pw1[layer], conv_b_pw1[layer],
                                conv_w_dw[layer], conv_b_dw[layer], conv_ln_g[layer], conv_ln_b[layer],
                                conv_w_pw2[layer], conv_b_pw2[layer])
            new_conv.append(st)
            x = x + o
        else:
            if layer == N_A_LAYERS:
                hk = rms_norm(x, kv_norm_g)
                kv = (hk @ w_kv).reshape(N, T, 2, N_KV_HEADS, HEAD_DIM)
                kv_new = jnp.stack([rms_norm(kv[:, :, 0], k_norm_g), kv[:, :, 1]], axis=2)
                kv_all = jnp.concatenate([kv_past.astype(kv_new.dtype), kv_new], axis=1)
                k_all, v_all = kv_all[:, :, 0], kv_all[:, :, 1]
                q_idx = P + jnp.arange(T, dtype=jnp.int32)
            j = layer - N_A_LAYERS
            q = (h @ attn_w_q[j]).reshape(N, T, N_GROUPS, N_KV_HEADS, HEAD_DIM)
            q = rms_norm(q, q_norm_g[j])
            o = dilated_attention(q, k_all, v_all, q_idx)
            x = x + o.reshape(N, T, N_KV_HEADS * HEAD_DIM) @ attn_w_o[j]
        x = x + sq_relu_mlp(rms_norm(x, norm_mlp_g[layer]), mlp_w_up[layer], mlp_w_down[layer])
    return x, jnp.stack(new_conv, axis=0), kv_new


def setup_inputs(seed: int = 0) -> dict:
    key = jax.random.key(seed)
    ks = jax.random.split(key, 24)
    f32 = jnp.float32
    win_buf = min(MAX_WINDOW, PAST_LEN)
    qw = N_GROUPS * N_KV_HEADS * HEAD_DIM
    ow = N_KV_HEADS * HEAD_DIM
    nrm = lambda k, s: jax.random.normal(k, s, f32)
    return {
        "x_prompt": nrm(ks[0], (BATCH, SEQ, D_MODEL)),
        "x_sample": nrm(ks[1], (DEC_BATCH, DEC_SEQ, D_MODEL)),
        "state_conv": 0.5 * nrm(ks[2], (N_A_LAYERS, DEC_BATCH, CONV_WIDTH - 1, CONV_CH)),
        "cache_kv": nrm(ks[3], (DEC_BATCH, win_buf, 2, N_KV_HEADS, HEAD_DIM)),
        "norm_mix_g": 1.0 + 0.02 * nrm(ks[4], (DEPTH, D_MODEL)),
        "norm_mlp_g": 1.0 + 0.02 * nrm(ks[5], (DEPTH, D_MODEL)),
        "conv_w_pw1": nrm(ks[6], (N_A_LAYERS, D_MODEL, 2 * CONV_CH)) * D_MODEL ** -0.5,
        "conv_b_pw1": 0.02 * nrm(ks[7], (N_A_LAYERS, 2 * CONV_CH)),
        "conv_w_dw": nrm(ks[8], (N_A_LAYERS, CONV_WIDTH, CONV_CH)) * CONV_WIDTH ** -0.5,
        "conv_b_dw": 0.02 * nrm(ks[9], (N_A_LAYERS, CONV_CH)),
        "conv_ln_g": 1.0 + 0.02 * nrm(ks[10], (N_A_LAYERS, CONV_CH)),
        "conv_ln_b": 0.02 * nrm(ks[11], (N_A_LAYERS, CONV_CH)),
        "conv_w_pw2": nrm(ks[12], (N_A_LAYERS, CONV_CH, D_MODEL)) * CONV_CH ** -0.5,
        "conv_b_pw2": 0.02 * nrm(ks[13], (N_A_LAYERS, D_MODEL)),
        "kv_norm_g": 1.0 + 0.02 * nrm(ks[14], (D_MODEL,)),
        "w_kv": nrm(ks[15], (D_MODEL, 2 * N_KV_HEADS * HEAD_DIM)) * D_MODEL ** -0.5,
        "k_norm_g": 1.0 + 0.02 * nrm(ks[16], (HEAD_DIM,)),
        "attn_w_q": nrm(ks[17], (N_B_LAYERS, D_MODEL, qw)) * D_MODEL ** -0.5,
        "q_norm_g": 1.0 + 0.02 * nrm(ks[18], (N_B_LAYERS, HEAD_DIM)),
        "attn_w_o": nrm(ks[19], (N_B_LAYERS, ow, D_MODEL)) * ow ** -0.5,
        "mlp_w_up": nrm(ks[20], (DEPTH, D_MODEL, D_FF)) * D_MODEL ** -0.5,
        "mlp_w_down": nrm(ks[21], (DEPTH, D_FF, D_MODEL)) * D_FF ** -0.5,
    }


def reference(x_prompt, x_sample, state_conv, cache_kv, norm_mix_g, norm_mlp_g, conv_w_pw1, conv_b_pw1,
              conv_w_dw, conv_b_dw, conv_ln_g, conv_ln_b, conv_w_pw2, conv_b_pw2, kv_norm_g, w_kv, k_norm_g,
              attn_w_q, q_norm_g, attn_w_o, mlp_w_up, mlp_w_down):
    weights = (norm_mix_g, norm_mlp_g, conv_w_pw1, conv_b_pw1, conv_w_dw, conv_b_dw, conv_ln_g, conv_ln_b,
               conv_w_pw2, conv_b_pw2, kv_norm_g, w_kv, k_norm_g, attn_w_q, q_norm_g, attn_w_o,
               mlp_w_up, mlp_w_down)
    nb, t_prompt = x_prompt.shape[0], x_prompt.shape[1]
    conv0 = jnp.zeros((N_A_LAYERS, nb, CONV_WIDTH - 1, CONV_CH), x_prompt.dtype)
    kv0 = jnp.zeros((nb, 0, 2, N_KV_HEADS, HEAD_DIM), x_prompt.dtype)
    y_prompt, conv_prompt, kv_rows_prompt = trunk(x_prompt, conv0, kv0, *weights)
    kv_prompt = kv_rows_prompt[:, -min(MAX_WINDOW, t_prompt):]
    y_sample, conv_sample, kv_sample = trunk(x_sample, state_conv, cache_kv, *weights)
    return (y_prompt, y_sample, conv_prompt, conv_sample, kv_prompt, kv_sample)
```

```python
import os
import time
import numpy as np
import concourse.bass as bass
import concourse.mybir as mybir
from concourse.bass_utils import run_bass_kernel_spmd
from contextlib import ExitStack

F32, BF16 = mybir.dt.float32, mybir.dt.bfloat16
AF = mybir.ActivationFunctionType
ALU = mybir.AluOpType
AX = mybir.AxisListType

NCORES = 8
TA = 4224
NST = 1408
TB = 2112
OWN0 = 2176
EPS = 1e-6
ARENA = 51800
WINDOWS = ((128, 1), (512, 4), (2048, 16))
K_STOP = int(os.environ.get('K_STOP', '99'))
K_NST = int(os.environ.get('K_NST', '3'))
K_CORE = os.environ.get('K_CORE')
K_DBG = int(os.environ.get('K_DBG', '0'))

PRM = {}
_o = 0
for _n, _w in (("nmg", 32), ("nml", 32), ("bpw1", 32), ("wdw", 2 * 8 * 31), ("bdw", 16), ("lng", 16),
               ("lnb", 16), ("bpw2", 16), ("kvg", 8), ("kng", 1), ("qng", 2), ("valid", 1), ("hbias", 1),
               ("eps", 1), ("zero", 1)):
    PRM[_n] = _o
    _o += _w
NPRM = _o


class Plan:
    ENG = ("pe", "act", "dve", "pool", "sp")
    NDS = 8

    def __init__(self):
        self.items = {e: [] for e in self.ENG}
        self.cnt = {e: 0 for e in self.ENG}
        self.known = {e: {} for e in self.ENG}
        self.bw = {}
        self.br = {}
        self.dma_n = {}
        self.dma_rr = {e: 0 for e in self.ENG}
        self.sems = set("s_" + e for e in self.ENG)

    def _need(self, eng, tok):
        if tok is None:
            return
        sem, val = tok
        if sem == "s_pe" and eng == "pe":
            return
        if self.known[eng].get(sem, 0) >= val:
            return
        self.known[eng][sem] = val
        self.items[eng].append(("w", sem, val))

    def _deps(self, eng, reads, writes):
        for b in reads:
            self._need(eng, self.bw.get(b))
            if isinstance(b, tuple) and b[0] == "ps":
                for s, v in self.br.get(b, {}).items():
                    if s != "s_" + eng:
                        self._need(eng, (s, v))
        for b in writes:
            self._need(eng, self.bw.get(b))
            for s, v in self.br.get(b, {}).items():
                self._need(eng, (s, v))

    def _mark(self, tok, reads, writes):
        for b in reads:
            d = self.br.setdefault(b, {})
            if d.get(tok[0], 0) < tok[1]:
                d[tok[0]] = tok[1]
        for b in writes:
            self.bw[b] = tok
            self.br[b] = {}

    def op(self, eng, fn, reads=(), writes=()):
        self._deps(eng, reads, writes)
        self.cnt[eng] += 1
        tok = ("s_" + eng, self.cnt[eng])
        self.items[eng].append(("x", fn))
        self._mark(tok, reads, writes)
        return tok

    def dma(self, q, out, in_, reads=(), writes=()):
        i = self.dma_rr[q] % self.NDS
        self.dma_rr[q] += 1
        sem = "d_%s%d" % (q, i)
        self.sems.add(sem)
        n = self.dma_n.get(sem, 0)
        if n:
            self._need(q, (sem, 16 * n))
        self._deps(q, reads, writes)
        self.dma_n[sem] = n + 1
        tok = (sem, 16 * (n + 1))
        self.items[q].append(("d", out, in_, sem))
        self._mark(tok, reads, writes)
        return tok

    def barrier(self):
        for e in self.ENG:
            for o in self.ENG:
                if self.cnt[o]:
                    self._need(e, ("s_" + o, self.cnt[o]))
            for sem, n in self.dma_n.items():
                self._need(e, (sem, 16 * n))


def replay(nc, plan):
    with ExitStack() as es:
        es.enter_context(nc.allow_low_precision("bf16 matmul operands, fp32 accumulation"))
        sems = {s: es.enter_context(nc.semaphore(s)) for s in sorted(plan.sems)}
        block = es.enter_context(nc.Block())

        def run(e, name):
            own = sems["s_" + name]
            for it in plan.items[name]:
                if it[0] == "w":
                    e.wait_ge(sems[it[1]], it[2])
                elif it[0] == "x":
                    it[1](e).then_inc(own, 1)
                else:
                    e.dma_start(out=it[1], in_=it[2]).then_inc(sems[it[3]], 16)

        @block.tensor
        def _(e):
            run(e, "pe")

        @block.scalar
        def _(e):
            run(e, "act")

        @block.vector
        def _(e):
            run(e, "dve")

        @block.gpsimd
        def _(e):
            run(e, "pool")

        @block.sync
        def _(e):
            run(e, "sp")


def build_program():
    nc = bass.Bass("TRN2", target_bir_lowering=False)
    plan = Plan()
    D = {}

    def din(name, shape, dt=F32):
        D[name] = nc.dram_tensor(name, list(shape), dt, kind="ExternalInput").ap()

    def dout(name, shape, dt=F32):
        D[name] = nc.dram_tensor(name, list(shape), dt, kind="ExternalOutput").ap()

    def dscr(name, shape, dt=F32):
        D[name] = nc.dram_tensor(name, list(shape), dt, kind="Internal").ap()

    din("xin", [128, 8, TA])
    din("prm", [128, NPRM])
    din("ident", [128, 128])
    din("onesb", [128, 128])
    din("w_pw1", [2, 1024, 2048])
    din("w_pw2", [2, 1024, 1024])
    din("w_kv", [1024, 1024])
    din("w_q", [2, 1024, 1536])
    din("w_o", [2, 512, 1024])
    din("w_up", [4, 1024, 4096])
    din("w_dn", [4, 4096, 1024])
    din("scT", [128, 2, 8, 16, 30])
    din("scN", [2, 16, 30, 1024])
    din("cache", [16, 2048, 1024])
    din("emask", [128, 12, 512])
    din("smc", [128, 708])
    dout("yT", [128, 8, TB])
    dout("kT", [128, 4, TA])
    dout("vN", [TA, 512])
    dout("cs", [2, 16, 30, 1024])
    dout("cp", [2, 30, 1024])
    if K_DBG:
        dout("dbg_h", [128, 8, TB], BF16)
        dout("dbg_q", [4, 128, 3, TB], BF16)
        dout("dbg_o", [128, 4, TB], BF16)
        dout("dbg_n", [4, 128, 2048])
        dout("dbg_d", [4, 128, 2048])
        dout("dbg_sraw", [128, 96])
        dout("dbg_pm", [128, 192], BF16)
        dout("dbg_osel", [128, 512], BF16)
        dout("dbg_qtm", [128, 1536], BF16)
    dscr("xs", [128, 8, TB])
    dscr("KTs", [4, 128, TA], BF16)
    dscr("Vs", [TA, 512], BF16)

    arena = nc.alloc_sbuf_tensor("arena", [128, ARENA], F32)
    psum = nc.alloc_psum_tensor("psum", [128, 8, 512], F32)

    class Bump:
        def __init__(self, start=0):
            self.off = start

        def __call__(self, shape, dt):
            n = int(np.prod(shape))
            nb = n * (4 if dt == F32 else 2)
            nb = (nb + 31) // 32 * 32
            assert self.off + nb <= ARENA * 4, ("SBUF overflow", self.off + nb)
            a = arena[:, self.off // 4:(self.off + nb) // 4]
            if dt != F32:
                a = a.bitcast(dt)
            a = a[:, 0:n]
            if len(shape) == 2:
                a = a.rearrange("p (a b) -> p a b", a=shape[0])
            elif len(shape) == 3:
                a = a.rearrange("p (a b c) -> p a b c", a=shape[0], b=shape[1])
            elif len(shape) == 4:
                a = a.rearrange("p (a b c d) -> p a b c d", a=shape[0], b=shape[1], c=shape[2])
            self.off += nb
            return a

    pb = Bump(0)
    prm = pb([NPRM], F32)
    identf = pb([128], F32)
    identb = pb([128], BF16)
    onesf = pb([128], F32)
    onesb16 = pb([128], BF16)
    oneshf = pb([128], F32)
    smc = pb([708], F32)
    tselb = pb([4], BF16)
    qg8 = pb([2], F32)
    ones64 = pb([64], BF16)
    PERSIST_END = pb.off

    def P(name, i=0, w=1):
        o = PRM[name] + i
        return prm[:, o:o + w]

    plan.dma("sp", prm, D["prm"], writes=["prm"])
    plan.dma("sp", identf, D["ident"], writes=["identf"])
    plan.dma("sp", oneshf, D["onesb"], writes=["oneshf"])
    plan.op("dve", lambda e: e.tensor_copy(out=identb, in_=identf), reads=["identf"], writes=["identb"])
    plan.op("dve", lambda e: e.memset(onesf, 1.0 / 1024), writes=["onesf"])
    plan.op("dve", lambda e: e.memset(onesb16, 1.0 / 1024), writes=["onesb16"])
    plan.op("dve", lambda e: e.memset(ones64, 1.0), writes=["ones64"])
    plan.dma("sp", smc, D["smc"], writes=["smc"])
    plan.op("dve", lambda e: e.tensor_copy(out=tselb, in_=smc[:, 704:708]), reads=["smc"], writes=["tselb"])
    plan.op("dve", lambda e: e.tensor_scalar(out=qg8, in0=P("qng", 0, 2), scalar1=0.125, scalar2=None, op0=ALU.mult),
            reads=["prm"], writes=["qg8"])

    class RR:
        def __init__(self, items):
            self.items = items
            self.i = 0

        def __call__(self):
            v = self.items[self.i % len(self.items)]
            self.i += 1
            return v

    def mm_group(e, bank_tiles, kc, lhs_fn, rhs_fn):
        ins = None
        for k in range(kc):
            for (b, off, n) in bank_tiles:
                ins = e.matmul(psum[:, b, 0:n], lhsT=lhs_fn(k), rhs=rhs_fn(k, off, n),
                               start=(k == 0), stop=(k == kc - 1))
        return ins

    wslot = RR([0, 1, 2])

    def proj(Wap, kc, runs_per_group, nchunks_per_group, rhs_fn, in_keys_fn, tilesets, banksets, evac_fn, wbuf):
        Wr = Wap.rearrange("(k p) o -> p k o", p=128)
        oi = 0
        for g, runs in enumerate(runs_per_group):
            s = wslot()
            for (dc, sc, w) in runs:
                plan.dma("pool", wbuf[s][:, 0:kc, dc:dc + w], Wr[:, :, sc:sc + w], writes=[("wb", s)])
            for lo in range(nchunks_per_group[g]):
                for ts in tilesets:
                    banks = banksets()
                    bt = [(banks[i], off, n) for i, (ti, off, n) in enumerate(ts)]
                    rk = [("wb", s)]
                    for (ti, off, n) in ts:
                        rk += in_keys_fn(ti)
                    plan.op("pe", (lambda e, bt=bt, s=s, lo=lo: mm_group(
                        e, bt, kc, lambda k: wbuf[s][:, k, lo * 128:(lo + 1) * 128], rhs_fn)),
                        reads=rk, writes=[("ps", b) for (b, _, _) in bt])
                    for i, (ti, off, n) in enumerate(ts):
                        evac_fn(oi, ti, off, n, banks[i])
                oi += 1

    def allk(name, ti, nk=8):
        return [(name, k, ti) for k in range(nk)]

    R2B = [0, 512, 1024, 1408, 1440]

    def r2k(k, lo, hi):
        return [("R2", k, i) for i in range(4) if lo < R2B[i + 1] and hi > R2B[i]]

    def r2all(lo, hi):
        return [x for k in range(8) for x in r2k(k, lo, hi)]

    a = Bump(PERSIST_END)
    xT = a([8, NST], F32)
    R1 = a([8, NST], BF16)
    R2 = a([8, NST + 32], BF16)
    sq = a([8, 512], F32)
    sig = [a([512], F32) for _ in range(3)]
    stA = [a([512], F32) for _ in range(2)]
    stA.append(stA[0])
    stB = [a([512], F32) for _ in range(2)]
    stB.append(stB[0])
    stC = [a([512], F32) for _ in range(2)]
    stC.append(stC[0])
    stD = [a([512], F32) for _ in range(2)]
    stD.append(stD[0])
    wbuf = [a([8, 512], BF16) for _ in range(3)]
    dg = [a([31, 128], BF16) for _ in range(2)]
    usamp = a([8, 16, 34], BF16)
    usampf = [a([8, 64], F32) for _ in range(2)]
    utailf = [a([8, 30], F32) for _ in range(2)]
    tailb = [a([8, 30], BF16) for _ in range(2)]
    kf = [a([512], F32) for _ in range(2)]
    kf.append(kf[0])
    kb = [a([512], BF16) for _ in range(2)]
    kb.append(kb[0])
    vf, vb = kf, kb
    tstage = sq.rearrange("p a b -> p (a b)")[:, 0:1024]
    hT = R1
    cT = R1
    uT = R2
    caT = R2[:, :, 0:NST]
    zT = R2[:, :, 0:NST]

    TL = [(0, 0, 512), (1, 512, 512), (2, 1024, 384)]
    set3 = RR([(0, 1, 2), (3, 4, 5)])
    bank6 = RR([0, 1, 2, 3, 4, 5])
    bankS = RR([6, 7])

    def rsqrt_chain(t, n, bank, tag):
        plan.op("act", lambda e: e.activation(out=stB[t][:, 0:n], in_=psum[:, bank, 0:n], func=AF.Sqrt,
                                               bias=P("eps"), scale=1.0),
                reads=[("ps", bank), "prm"], writes=[("stB", t % 2)])
        plan.op("dve", lambda e: e.reciprocal(out=stA[t][:, 0:n], in_=stB[t][:, 0:n]),
                reads=[("stB", t % 2)], writes=[("stA", t % 2)])

    def norm_stage(gname, gidx, dst, dst_name):
        for (t, off, n) in TL:
            plan.op("act", lambda e, off=off, n=n: e.activation(out=sq[:, :, 0:n], in_=xT[:, :, off:off + n],
                                                                 func=AF.Square),
                    reads=allk("xT", t), writes=["sq"])
            b = bankS()
            plan.op("pe", lambda e, b=b, n=n: mm_group(e, [(b, 0, n)], 8, lambda k: onesf,
                                                         lambda k, o_, n_: sq[:, k, 0:n_]),
                    reads=["sq", "onesf"], writes=[("ps", b)])
            rsqrt_chain(t, n, b, "n")
            for k in range(8):
                plan.op("dve", lambda e, k=k, t=t, off=off, n=n: e.scalar_tensor_tensor(
                    out=dst[:, k, off:off + n], in0=xT[:, k, off:off + n], scalar=P(gname, gidx * 8 + k),
                    in1=stA[t][:, 0:n], op0=ALU.mult, op1=ALU.mult),
                    reads=[("xT", k, t), ("stA", t % 2), "prm"], writes=[(dst_name, k, t)])

    for st in range(K_NST):
        c0 = st * NST
        for (t, off, n) in TL:
            plan.dma("sp", xT[:, :, off:off + n], D["xin"][:, :, c0 + off:c0 + off + n], writes=allk("xT", t))
        for l in range(2):
            norm_stage("nmg", l, hT, "R1")
            if K_STOP <= 1:
                break
            if st == 0:
                plan.op("dve", lambda e: e.memset(uT[:, :, 0:30], 0.0), writes=r2all(0, 30))
                for hh in range(2):
                    plan.dma("pool", usamp[:, 4 * hh:4 * hh + 4, :, 0:30], D["scT"][:, l, 4 * hh:4 * hh + 4],
                             writes=["usamp"])
            else:
                plan.op("dve", lambda e, l=l: e.tensor_copy(out=uT[:, :, 0:30], in_=tailb[l]),
                        reads=[("tailb", l)], writes=r2all(0, 30))
            def ev_pw1(oi, ti, off, n, bank, l=l, st=st):
                c, isg = oi // 2, oi % 2
                if not isg:
                    ev_pw1.abank[ti] = bank
                    return
                ab = ev_pw1.abank[ti]
                plan.op("act", lambda e: e.activation(out=sig[ti][:, 0:n], in_=psum[:, bank, 0:n], func=AF.Sigmoid,
                                                       bias=P("bpw1", l * 16 + 8 + c), scale=1.0),
                        reads=[("ps", bank), "prm"], writes=[("sig", ti)])
                plan.op("dve", lambda e: e.scalar_tensor_tensor(
                    out=uT[:, c, 30 + off:30 + off + n], in0=psum[:, ab, 0:n], scalar=P("bpw1", l * 16 + c),
                    in1=sig[ti][:, 0:n], op0=ALU.add, op1=ALU.mult),
                    reads=[("ps", ab), ("sig", ti), "prm"], writes=r2k(c, 30 + off, 30 + off + n))
                if st == 0 and ti == 0:
                    plan.op("dve", lambda e: e.scalar_tensor_tensor(
                        out=usampf[l][:, c, :], in0=psum[:, ab, 0:64], scalar=P("bpw1", l * 16 + c),
                        in1=sig[ti][:, 0:64], op0=ALU.add, op1=ALU.mult),
                        reads=[("ps", ab), ("sig", ti), "prm"], writes=[("usampf", l, c)])
                    plan.op("dve", lambda e: e.tensor_copy(
                        out=usamp[:, c, :, 30:34], in_=usampf[l][:, c, :].rearrange("p (s t) -> p s t", t=4)),
                        reads=[("usampf", l, c)], writes=["usamp"])
                if st == 2 and ti == 2:
                    plan.op("dve", lambda e: e.scalar_tensor_tensor(
                        out=utailf[l][:, c, :], in0=psum[:, ab, n - 30:n], scalar=P("bpw1", l * 16 + c),
                        in1=sig[ti][:, n - 30:n], op0=ALU.add, op1=ALU.mult),
                        reads=[("ps", ab), ("sig", ti), "prm"], writes=[("utailf", l, c)])
            ev_pw1.abank = {}
            runs = []
            for c in range(0, 8, 2):
                runs.append([(0, c * 128, 128), (128, 1024 + c * 128, 128),
                             (256, (c + 1) * 128, 128), (384, 1024 + (c + 1) * 128, 128)])
            proj(D["w_pw1"][l], 8, runs, [4] * 4, lambda k, off, n: hT[:, k, off:off + n],
                 lambda ti: allk("R1", ti), [TL], set3, ev_pw1, wbuf)
            if st == 1:
                plan.op("dve", lambda e: e.tensor_scalar(out=uT[:, :, 30 + 738:30 + 768], in0=uT[:, :, 30 + 738:30 + 768],
                                                          scalar1=P("valid"), scalar2=None, op0=ALU.mult),
                        reads=r2all(768, 798) + ["prm"], writes=r2all(768, 798))
            if K_STOP <= 2:
                break
            for c in range(8):
                ds = c % 2
                plan.op("dve", lambda e, ds=ds, c=c, l=l: e.tensor_tensor(
                    out=dg[ds], in0=identb.unsqueeze(1).broadcast_to([128, 31, 128]),
                    in1=P("wdw", (l * 8 + c) * 31, 31).unsqueeze(2).broadcast_to([128, 31, 128]), op=ALU.mult),
                    reads=["identb", "prm"], writes=[("dg", ds)])
                for (t, off, n) in TL:
                    b = bank6()

                    def conv_mm(e, ds=ds, c=c, off=off, n=n, b=b, st=st, t=t):
                        ins = None
                        for j in range(31):
                            ins = e.matmul(psum[:, b, 0:n], lhsT=dg[ds][:, j, :], rhs=uT[:, c, off + j:off + j + n],
                                           start=(j == 0), stop=(j == 30))
                        if st == 0 and t == 0:
                            for j in range(31):
                                ins = e.matmul(psum[:, b, 0:64].rearrange("p (s t) -> p s t", t=4), lhsT=dg[ds][:, j, :],
                                               rhs=usamp[:, c, :, j:j + 4], start=(j == 0), stop=(j == 30))
                        return ins
                    rk = [("dg", ds)] + r2k(c, off, off + n + 30)
                    if st == 0 and t == 0:
                        rk.append("usamp")
                    plan.op("pe", conv_mm, reads=rk, writes=[("ps", b)])
                    plan.op("act", lambda e, c=c, off=off, n=n, b=b, l=l: e.activation(
                        out=cT[:, c, off:off + n], in_=psum[:, b, 0:n], func=AF.Identity, bias=P("bdw", l * 8 + c),
                        scale=1.0), reads=[("ps", b), "prm"], writes=[("R1", c, t)])
            plan.op("dve", lambda e, l=l: e.tensor_copy(out=tailb[l], in_=uT[:, :, NST:NST + 30]),
                    reads=r2all(NST, NST + 30), writes=[("tailb", l)])
            if K_STOP <= 3:
                break
            for (t, off, n) in TL:
                plan.op("act", lambda e, off=off, n=n: e.activation(out=sq[:, :, 0:n], in_=cT[:, :, off:off + n],
                                                                     func=AF.Square),
                        reads=allk("R1", t), writes=["sq"])
                b1, b2 = bankS(), bankS()
                plan.op("pe", lambda e, b1=b1, off=off, n=n: mm_group(
                    e, [(b1, off, n)], 8, lambda k: onesb16, lambda k, o_, n_: cT[:, k, o_:o_ + n_]),
                    reads=allk("R1", t) + ["onesb16"], writes=[("ps", b1)])
                plan.op("pe", lambda e, b2=b2, n=n: mm_group(
                    e, [(b2, 0, n)], 8, lambda k: onesf, lambda k, o_, n_: sq[:, k, 0:n_]),
                    reads=["sq", "onesf"], writes=[("ps", b2)])
                plan.op("act", lambda e, t=t, n=n, b1=b1: e.activation(out=stC[t][:, 0:n], in_=psum[:, b1, 0:n],
                                                                        func=AF.Copy),
                        reads=[("ps", b1)], writes=[("stC", t % 2)])
                plan.op("dve", lambda e, t=t, n=n: e.tensor_tensor(out=stD[t][:, 0:n], in0=stC[t][:, 0:n],
                                                                    in1=stC[t][:, 0:n], op=ALU.mult),
                        reads=[("stC", t % 2)], writes=[("stD", t % 2)])
                plan.op("dve", lambda e, t=t, n=n, b2=b2: e.tensor_tensor(out=stD[t][:, 0:n], in0=psum[:, b2, 0:n],
                                                                           in1=stD[t][:, 0:n], op=ALU.subtract),
                        reads=[("ps", b2), ("stD", t % 2)], writes=[("stD", t % 2)])
                plan.op("act", lambda e, t=t, n=n: e.activation(out=stB[t][:, 0:n], in_=stD[t][:, 0:n], func=AF.Sqrt,
                                                                 bias=P("eps"), scale=1.0),
                        reads=[("stD", t % 2), "prm"], writes=[("stB", t % 2)])
                plan.op("dve", lambda e, t=t, n=n: e.reciprocal(out=stA[t][:, 0:n], in_=stB[t][:, 0:n]),
                        reads=[("stB", t % 2)], writes=[("stA", t % 2)])
                plan.op("dve", lambda e, t=t, n=n: e.scalar_tensor_tensor(
                    out=stC[t][:, 0:n], in0=stC[t][:, 0:n], scalar=-1.0, in1=stA[t][:, 0:n], op0=ALU.mult,
                    op1=ALU.mult), reads=[("stC", t % 2), ("stA", t % 2)], writes=[("stC", t % 2)])
                plan.op("dve", lambda e, t=t, off=off, n=n: e.tensor_tensor(
                    out=sq[:, :, 0:n], in0=cT[:, :, off:off + n],
                    in1=stA[t][:, 0:n].unsqueeze(1).broadcast_to([128, 8, n]), op=ALU.mult),
                    reads=allk("R1", t) + [("stA", t % 2)], writes=["sq"])
                plan.op("dve", lambda e, t=t, n=n: e.tensor_tensor(
                    out=sq[:, :, 0:n], in0=sq[:, :, 0:n],
                    in1=stC[t][:, 0:n].unsqueeze(1).broadcast_to([128, 8, n]), op=ALU.add),
                    reads=["sq", ("stC", t % 2)], writes=["sq"])
                for k in range(8):
                    plan.op("act", lambda e, k=k, off=off, n=n, l=l: e.activation(
                        out=caT[:, k, off:off + n], in_=sq[:, k, 0:n], func=AF.Silu, bias=P("lnb", l * 8 + k),
                        scale=P("lng", l * 8 + k)), reads=["sq", "prm"], writes=[("R2", k, t)])
            if K_STOP <= 4:
                break
            def ev_pw2(oi, ti, off, n, bank, l=l):
                plan.op("dve", lambda e: e.scalar_tensor_tensor(
                    out=xT[:, oi, off:off + n], in0=psum[:, bank, 0:n], scalar=P("bpw2", l * 8 + oi),
                    in1=xT[:, oi, off:off + n], op0=ALU.add, op1=ALU.add),
                    reads=[("ps", bank), ("xT", oi, ti), "prm"], writes=[("xT", oi, ti)])
            proj(D["w_pw2"][l], 8, [[(0, 0, 512)], [(0, 512, 512)]], [4, 4],
                 lambda k, off, n: caT[:, k, off:off + n], lambda ti: allk("R2", ti), [TL], set3, ev_pw2, wbuf)
            if K_STOP <= 5:
                break
            norm_stage("nml", l, hT, "R1")
            for hg in range(4):
                def ev_up(oi, ti, off, n, bank):
                    plan.op("act", lambda e: e.activation(out=zT[:, oi, off:off + n], in_=psum[:, bank, 0:n],
                                                           func=AF.Relu),
                            reads=[("ps", bank)], writes=[("R2", oi, ti)])
                    plan.op("dve", lambda e: e.tensor_tensor(out=zT[:, oi, off:off + n], in0=zT[:, oi, off:off + n],
                                                              in1=zT[:, oi, off:off + n], op=ALU.mult),
                            reads=[("R2", oi, ti)], writes=[("R2", oi, ti)])
                proj(D["w_up"][l][:, hg * 1024:(hg + 1) * 1024], 8, [[(0, 0, 512)], [(0, 512, 512)]], [4, 4],
                     lambda k, off, n: hT[:, k, off:off + n], lambda ti: allk("R1", ti), [TL], set3, ev_up, wbuf)

                def ev_dn(oi, ti, off, n, bank):
                    plan.op("dve", lambda e: e.tensor_tensor(out=xT[:, oi, off:off + n], in0=psum[:, bank, 0:n],
                                                              in1=xT[:, oi, off:off + n], op=ALU.add),
                            reads=[("ps", bank), ("xT", oi, ti)], writes=[("xT", oi, ti)])
                proj(D["w_dn"][l][hg * 1024:(hg + 1) * 1024, :], 8, [[(0, 0, 512)], [(0, 512, 512)]], [4, 4],
                     lambda k, off, n: zT[:, k, off:off + n], lambda ti: allk("R2", ti), [TL], set3, ev_dn, wbuf)
        if K_STOP <= 6:
            continue
        norm_stage("kvg", 0, hT, "R1")

        def ev_k(oi, ti, off, n, bank, c0=c0):
            plan.op("act", lambda e: e.activation(out=sig[ti][:, 0:n], in_=psum[:, bank, 0:n], func=AF.Square),
                    reads=[("ps", bank)], writes=[("sig", ti)])
            b2 = bankS()
            plan.op("pe", lambda e: e.matmul(psum[:, b2, 0:n], lhsT=oneshf, rhs=sig[ti][:, 0:n], start=True, stop=True),
                    reads=[("sig", ti), "oneshf"], writes=[("ps", b2)])
            rsqrt_chain(ti, n, b2, "k")
            plan.op("dve", lambda e: e.scalar_tensor_tensor(
                out=kf[ti][:, 0:n], in0=psum[:, bank, 0:n], scalar=P("kng"), in1=stA[ti][:, 0:n], op0=ALU.mult,
                op1=ALU.mult), reads=[("ps", bank), ("stA", ti % 2), "prm"], writes=[("kf", ti % 2)])
            plan.op("act", lambda e: e.activation(out=kb[ti][:, 0:n], in_=kf[ti][:, 0:n], func=AF.Copy),
                    reads=[("kf", ti % 2)], writes=[("kb", ti % 2)])
            plan.dma("sp", D["kT"][:, oi, c0 + off:c0 + off + n], kf[ti][:, 0:n], reads=[("kf", ti % 2)])
            plan.dma("sp", D["KTs"][oi][:, c0 + off:c0 + off + n], kb[ti][:, 0:n], reads=[("kb", ti % 2)], writes=["KTs"])
        proj(D["w_kv"][:, 0:512], 8, [[(0, 0, 512)]], [4], lambda k, off, n: hT[:, k, off:off + n],
             lambda ti: allk("R1", ti), [TL], set3, ev_k, wbuf)
        if K_STOP <= 7:
            continue
        s = wslot()
        plan.dma("pool", wbuf[s], D["w_kv"].rearrange("(k p) o -> p k o", p=128)[:, :, 512:1024], writes=[("wb", s)])
        for tb in range(NST // 128):
            b = bank6()
            ti = tb // 4
            plan.op("pe", lambda e, b=b, tb=tb, s=s: mm_group(
                e, [(b, 0, 512)], 8, lambda k: hT[:, k, tb * 128:(tb + 1) * 128], lambda k, o_, n_: wbuf[s][:, k, :]),
                reads=allk("R1", ti) + [("wb", s)], writes=[("ps", b)])
            i = tb % 2
            plan.op("act", lambda e, b=b, i=i: e.activation(out=vf[i], in_=psum[:, b, :], func=AF.Copy),
                    reads=[("ps", b)], writes=[("kf", i)])
            plan.op("dve", lambda e, i=i: e.tensor_copy(out=vb[i], in_=vf[i]),
                    reads=[("kf", i)], writes=[("kb", i)])
            r0 = c0 + tb * 128
            plan.dma("sp", D["vN"][r0:r0 + 128, :], vf[i], reads=[("kf", i)])
            plan.dma("sp", D["Vs"][r0:r0 + 128, :], vb[i], reads=[("kb", i)], writes=["Vs"])
        if K_STOP <= 8:
            continue
        if st == 0:
            plan.dma("sp", D["xs"][:, :, 0:64], xT[:, :, 0:64], reads=allk("xT", 0), writes=["xs"])
        elif st == 1:
            plan.dma("sp", D["xs"][:, :, 64:704], xT[:, :, 768:1408], reads=allk("xT", 1) + allk("xT", 2), writes=["xs"])
        else:
            plan.dma("sp", D["xs"][:, :, 704:2112], xT[:, :, 0:1408],
                     reads=allk("xT", 0) + allk("xT", 1) + allk("xT", 2), writes=["xs"])
        if K_STOP <= 9:
            continue
        if st in (0, 2):
            for l in range(2):
                src = usampf[l] if st == 0 else utailf[l]
                w = 64 if st == 0 else 30
                bA, bB = bankS(), bankS()

                def tr(e, src=src, w=w, bA=bA, bB=bB):
                    ins = None
                    for k in range(8):
                        bk = bA if k < 4 else bB
                        ins = e.transpose(psum[0:w, bk, (k % 4) * 128:(k % 4 + 1) * 128], src[:, k, :], identf)
                    return ins
                rk = [("usampf", l, c) for c in range(8)] if st == 0 else [("utailf", l, c) for c in range(8)]
                plan.op("pe", tr, reads=rk + ["identf"], writes=[("ps", bA), ("ps", bB)])
                plan.op("act", lambda e, w=w, bA=bA: e.activation(out=tstage[0:w, 0:512], in_=psum[0:w, bA, :],
                                                                   func=AF.Copy),
                        reads=[("ps", bA)], writes=["sq"])
                plan.op("act", lambda e, w=w, bB=bB: e.activation(out=tstage[0:w, 512:1024], in_=psum[0:w, bB, :],
                                                                   func=AF.Copy),
                        reads=[("ps", bB)], writes=["sq"])
                if st == 0:
                    for s_ in range(16):
                        plan.dma("sp", D["cs"][l, s_, 26:30, :], tstage[4 * s_:4 * s_ + 4, :],
                                 reads=["sq"])
                    plan.dma("sp", D["cs"][l, :, 0:26, :], D["scN"][l, :, 4:30, :])
                else:
                    plan.dma("sp", D["cp"][l], tstage[0:30, :], reads=["sq"])

    plan.barrier()
    if K_STOP > 10:
        bb = Bump(PERSIST_END)
        wbufB = [bb([8, 512], BF16) for _ in range(3)]
        sA = [bb([512], F32) for _ in range(2)]
        sB = [bb([512], F32) for _ in range(2)]
        sC = [bb([512], F32) for _ in range(4)]
        hB = bb([8, TB], BF16)
        X0 = bb.off
        xB = bb([8, TB], F32)
        Z0 = bb.off
        zB = bb([8, TB], BF16)
        S0 = bb.off
        sqB = bb([8, 512], F32)
        ab = Bump(X0)
        QT = ab([3, TB], BF16)
        KT = [ab([TA], BF16) for _ in range(2)]
        Vg = [ab([69, 128], BF16) for _ in range(2)]
        assert ab.off <= Z0
        ab = Bump(Z0)
        oT = ab([4, TB], BF16)
        Em = ab([3, 512], BF16)
        ex = [ab([512], BF16) for _ in range(2)]
        PT = [ab([512], BF16) for _ in range(2)]
        QTs = ab([3, 4, 64], BF16)
        KTsamp = ab([4, 64], BF16)
        assert ab.off <= S0
        ab = Bump(X0)
        CB = [ab([9, 1024], BF16) for _ in range(2)]
        Qtm = ab([3, 512], BF16)
        Vnew = [ab([512], BF16) for _ in range(2)]
        sraw = ab([3, 4, 8], F32)
        pexp = ab([3, 4, 8], F32)
        pn1 = ab([2, 48], F32)
        pm = ab([6, 4, 8], BF16)
        pz = ab([2, 4, 4, 8], BF16)
        pD = ab([32], BF16)
        pns = ab([32], BF16)
        prod = [ab([512], F32) for _ in range(2)]
        rDs = ab([1], F32)
        osel = ab([512], BF16)
        assert ab.off <= Z0
        ab = Bump(S0)
        accN = ab([2048], F32)
        accD = ab([2048], F32)

        TLB = [(0, 0, 64), (1, 64, 512), (2, 576, 512), (3, 1088, 512), (4, 1600, 512)]
        TS = [TLB[0:3], TLB[3:5]]
        setB = RR([(0, 1, 2), (3, 4, 5)])
        setAtt = RR([(0, 1, 2), (3, 0, 1)])
        bank4 = RR([0, 1, 2, 3])
        bankN = RR([4, 5])
        bankD = RR([6, 7])
        statB = RR([6, 7])

        def xk(ti):
            return allk("xB", ti)

        def normB(gname, gidx):
            for (t, off, n) in TLB:
                plan.op("act", lambda e, off=off, n=n: e.activation(out=sqB[:, :, 0:n], in_=xB[:, :, off:off + n],
                                                                     func=AF.Square), reads=xk(t), writes=["sqB"])
                b = statB()
                plan.op("pe", lambda e, b=b, n=n: mm_group(e, [(b, 0, n)], 8, lambda k: onesf,
                                                             lambda k, o_, n_: sqB[:, k, 0:n_]),
                        reads=["sqB", "onesf"], writes=[("ps", b)])
                i = t % 2
                plan.op("act", lambda e, i=i, n=n, b=b: e.activation(out=sB[i][:, 0:n], in_=psum[:, b, 0:n],
                                                                      func=AF.Sqrt, bias=P("eps"), scale=1.0),
                        reads=[("ps", b), "prm"], writes=[("sB", i)])
                plan.op("dve", lambda e, i=i, n=n: e.reciprocal(out=sA[i][:, 0:n], in_=sB[i][:, 0:n]),
                        reads=[("sB", i)], writes=[("sA", i)])
                for k in range(8):
                    plan.op("dve", lambda e, k=k, i=i, off=off, n=n: e.scalar_tensor_tensor(
                        out=hB[:, k, off:off + n], in0=xB[:, k, off:off + n], scalar=P(gname, gidx * 8 + k),
                        in1=sA[i][:, 0:n], op0=ALU.mult, op1=ALU.mult),
                        reads=[("xB", k, t), ("sA", i), "prm"], writes=[("hB", k, t)])

        EsM = smc[:, 0:192].rearrange("p (j t h) -> p j t h", j=6, t=4)
        sel32 = smc[0:32, 192:704]
        for (t, off, n) in TLB:
            plan.dma("sp", xB[:, :, off:off + n], D["xs"][:, :, off:off + n], reads=["xs"], writes=xk(t))
        for lb in range(2):
            L = 2 + lb
            normB("nmg", L)
            if lb == 1:
                for (t, off, n) in TLB:
                    plan.dma("sp", D["xs"][:, :, off:off + n], xB[:, :, off:off + n], reads=xk(t), writes=["xs"])
            if K_DBG and lb == 0:
                plan.dma("sp", D["dbg_h"], hB, reads=[x for t in range(5) for x in allk("hB", t)])
            plan.barrier()
            for hp_ in range(4):
                plan.dma("sp", KTsamp[:, hp_, :], D["KTs"][hp_][:, 0:64], reads=["KTs"], writes=["KTsamp"])
            for hp in range(4):
                kv = hp % 2
                plan.dma("sp", KT[kv], D["KTs"][hp], reads=["KTs"], writes=[("KT", kv)])
                Vs = D["Vs"]
                c_lo = hp * 128
                plan.dma("sp", Vg[kv][:, 0:17, :],
                         Vs[OWN0 - 128:OWN0 + 2048, c_lo:c_lo + 128].rearrange("(b i) c -> i b c", i=128),
                         reads=["Vs"], writes=[("Vg", kv)])
                for r in range(4):
                    plan.dma("sp", Vg[kv][:, 17 + 5 * r:22 + 5 * r, :],
                             Vs[OWN0 - 512 + r:OWN0 + 2048:4, c_lo:c_lo + 128].rearrange("(b i) c -> i b c", i=128),
                             reads=["Vs"], writes=[("Vg", kv)])
                for r in range(16):
                    plan.dma("sp", Vg[kv][:, 37 + 2 * r:39 + 2 * r, :],
                             Vs[OWN0 - 2048 + r:OWN0 + 2048:16, c_lo:c_lo + 128].rearrange("(b i) c -> i b c", i=128),
                             reads=["Vs"], writes=[("Vg", kv)])
                plan.dma("pool", Em, D["emask"][:, hp:12:4, :], writes=["Em"])

                def ev_q(oi, ti, off, n, bank, lb=lb):
                    i = ti % 4
                    plan.op("act", lambda e: e.activation(out=sC[i][:, 0:n], in_=psum[:, bank, 0:n], func=AF.Square),
                            reads=[("ps", bank)], writes=[("sC", i)])
                    b2 = statB()
                    plan.op("pe", lambda e: e.matmul(psum[:, b2, 0:n], lhsT=oneshf, rhs=sC[i][:, 0:n], start=True,
                                                      stop=True), reads=[("sC", i), "oneshf"], writes=[("ps", b2)])
                    j = ti % 2
                    plan.op("act", lambda e: e.activation(out=sB[j][:, 0:n], in_=psum[:, b2, 0:n], func=AF.Sqrt,
                                                           bias=P("eps"), scale=1.0),
                            reads=[("ps", b2), "prm"], writes=[("sB", j)])
                    plan.op("dve", lambda e: e.reciprocal(out=sA[j][:, 0:n], in_=sB[j][:, 0:n]),
                            reads=[("sB", j)], writes=[("sA", j)])
                    plan.op("dve", lambda e: e.scalar_tensor_tensor(
                        out=QT[:, oi, off:off + n], in0=psum[:, bank, 0:n], scalar=qg8[:, lb:lb + 1],
                        in1=sA[j][:, 0:n], op0=ALU.mult, op1=ALU.mult),
                        reads=[("ps", bank), ("sA", j), "qg8"], writes=["QT"])
                proj(D["w_q"][lb], 8, [[(g * 128, (g * 4 + hp) * 128, 128) for g in range(3)]], [3],
                     lambda k, off, n: hB[:, k, off:off + n], lambda ti: allk("hB", ti), TS, setAtt, ev_q, wbufB)
                if K_DBG and lb == 0:
                    plan.dma("sp", D["dbg_q"][hp], QT, reads=["QT"])
                plan.op("act", lambda e, hp=hp: e.activation(out=QTs[:, :, hp, :], in_=QT[:, :, 0:64], func=AF.Copy),
                        reads=["QT"], writes=["QTs"])
                for g, (w_, d) in enumerate(WINDOWS):
                    qbs = [(r, c) for r in range(d) for c in range(16 // d)]
                    vbase = [0, 17, 37][g]
                    nper = 16 // d + 1

                    def vblk(r, kbi, vbase=vbase, nper=nper):
                        return vbase + r * nper + (kbi + 1)
                    for q4 in range(4):
                        bN, bD = bankN(), bankD()
                        for pr in range(2):
                            pair = qbs[q4 * 4 + pr * 2:q4 * 4 + pr * 2 + 2]
                            bX, bY = bank4(), bank4()

                            def st_mm(e, pair=pair, bX=bX, bY=bY, g=g, d=d, kv=kv):
                                ins = None
                                for qi, (r, c) in enumerate(pair):
                                    q0 = 64 + r + d * 128 * c
                                    for pc in range(2):
                                        k0 = OWN0 + r + d * 128 * (c - 1 + pc)
                                        for h, bk in ((0, bX), (1, bY)):
                                            ins = e.matmul(psum[:, bk, (qi * 2 + pc) * 128:(qi * 2 + pc + 1) * 128],
                                                           lhsT=KT[kv][h * 64:(h + 1) * 64, k0:k0 + 127 * d + 1:d],
                                                           rhs=QT[h * 64:(h + 1) * 64, g, q0:q0 + 127 * d + 1:d],
                                                           start=True, stop=True, tile_position=(h * 64, 0))
                                return ins
                            plan.op("pe", st_mm, reads=[("KT", kv), "QT"], writes=[("ps", bX), ("ps", bY)])
                            halo = any(c == 0 for (r, c) in pair)
                            for h, bk in ((0, bX), (1, bY)):
                                plan.op("act", lambda e, h=h, bk=bk: e.activation(out=ex[h], in_=psum[:, bk, :],
                                                                                   func=AF.Exp),
                                        reads=[("ps", bk)], writes=[("ex", h)])
                                if halo:
                                    for qi, (r, c) in enumerate(pair):
                                        if c == 0:
                                            plan.op("act", lambda e, h=h, bk=bk, qi=qi: e.activation(
                                                out=ex[h][:, qi * 256:qi * 256 + 128],
                                                in_=psum[:, bk, qi * 256:qi * 256 + 128], func=AF.Exp,
                                                bias=P("hbias"), scale=1.0),
                                                reads=[("ps", bk), "prm"], writes=[("ex", h)])
                                plan.op("dve", lambda e, h=h, g=g: e.tensor_tensor(
                                    out=PT[h].rearrange("p (a b) -> p a b", a=2),
                                    in0=ex[h].rearrange("p (a b) -> p a b", a=2),
                                    in1=Em[:, g, h * 256:(h + 1) * 256].unsqueeze(1).broadcast_to([128, 2, 256]),
                                    op=ALU.mult), reads=[("ex", h), "Em"], writes=[("PT", h)])

                            def pv_mm(e, pair=pair, pr=pr, bN=bN, bD=bD, kv=kv, vblk=vblk):
                                ins = None
                                for qi, (r, c) in enumerate(pair):
                                    sl = (pr * 2 + qi) * 128
                                    for h in range(2):
                                        for pc in range(2):
                                            rhs = PT[h][:, (qi * 2 + pc) * 128:(qi * 2 + pc + 1) * 128]
                                            e.matmul(psum[h * 64:(h + 1) * 64, bN, sl:sl + 128],
                                                     lhsT=Vg[kv][:, vblk(r, c - 1 + pc), h * 64:(h + 1) * 64],
                                                     rhs=rhs, start=(pc == 0), stop=(pc == 1))
                                            ins = e.matmul(psum[h * 64:(h + 1) * 64, bD, sl:sl + 128],
                                                           lhsT=ones64, rhs=rhs, start=(pc == 0), stop=(pc == 1))
                                return ins
                            plan.op("pe", pv_mm, reads=[("PT", 0), ("PT", 1), ("Vg", kv), "ones64"],
                                    writes=[("ps", bN), ("ps", bD)])
                        if g == 0:
                            cs_ = slice(q4 * 512, q4 * 512 + 512)
                            dstN, dstD = accN[:, cs_], accD[:, cs_]
                            srcN, srcD = psum[:, bN, :], psum[:, bD, :]
                        elif g == 1:
                            dstN, dstD = accN[:, q4:2048:4], accD[:, q4:2048:4]
                            srcN, srcD = psum[:, bN, :], psum[:, bD, :]
                        else:
                            dstN = accN.rearrange("p (i r) -> p r i", r=16)[:, q4 * 4:q4 * 4 + 4, :]
                            dstD = accD.rearrange("p (i r) -> p r i", r=16)[:, q4 * 4:q4 * 4 + 4, :]
                            srcN = psum[:, bN, :].rearrange("p (a b) -> p a b", a=4)
                            srcD = psum[:, bD, :].rearrange("p (a b) -> p a b", a=4)
                        if g == 0:
                            plan.op("act", lambda e, dstN=dstN, srcN=srcN: e.activation(out=dstN, in_=srcN, func=AF.Copy),
                                    reads=[("ps", bN)], writes=["accN"])
                            plan.op("dve", lambda e, dstD=dstD, srcD=srcD: e.tensor_copy(out=dstD, in_=srcD),
                                    reads=[("ps", bD)], writes=["accD"])
                        else:
                            plan.op("dve", lambda e, dstN=dstN, srcN=srcN: e.tensor_tensor(out=dstN, in0=srcN, in1=dstN,
                                                                                            op=ALU.add),
                                    reads=[("ps", bN), "accN"], writes=["accN"])
                            plan.op("dve", lambda e, dstD=dstD, srcD=srcD: e.tensor_tensor(out=dstD, in0=srcD, in1=dstD,
                                                                                            op=ALU.add),
                                    reads=[("ps", bD), "accD"], writes=["accD"])
                if K_DBG and lb == 0:
                    plan.dma("sp", D["dbg_n"][hp], accN, reads=["accN"])
                    plan.dma("sp", D["dbg_d"][hp], accD, reads=["accD"])
                plan.op("dve", lambda e: e.reciprocal(out=accD, in_=accD), reads=["accD"], writes=["accD"])
                plan.op("dve", lambda e, hp=hp: e.tensor_tensor(out=oT[:, hp, 64:TB], in0=accN, in1=accD, op=ALU.mult),
                        reads=["accN", "accD"], writes=["oTp"])
            plan.barrier()
            plan.op("dve", lambda e: e.memset(pm, 0.0), writes=["pm"])
            plan.op("dve", lambda e: e.memset(pz, 0.0), writes=["pz"])
            for i_ in range(2):
                plan.op("dve", lambda e, i_=i_: e.memset(Vnew[i_], 0.0), writes=[("Vnew", i_)])
            for g in range(3):
                bq = bank4()
                plan.op("pe", lambda e, g=g, bq=bq: [e.transpose(
                    psum[0:64, bq, hp_ * 64:(hp_ + 1) * 64].bitcast(BF16), QTs[:, g, hp_, :], identb)
                    for hp_ in range(4)][-1], reads=["QTs", "identb"], writes=[("ps", bq)])
                plan.op("act", lambda e, g=g, bq=bq: e.activation(out=Qtm[0:64, g, :],
                                                                   in_=psum[0:64, bq, 0:256].bitcast(BF16), func=AF.Copy),
                        reads=[("ps", bq)], writes=["Qtm"])
            qsets = RR([(0, 1, 2), (3, 4, 5)])
            for sq_ in range(16):
                cb = sq_ % 2
                C = D["cache"][sq_]
                plan.dma("pool", CB[cb][:, 0, :], C[1920:2048, :], writes=[("CB", cb)])
                plan.dma("pool", CB[cb][:, 1:5, :], C[1536:2048, :].rearrange("(m t) f -> m t f", t=4),
                         writes=[("CB", cb)])
                plan.dma("pool", CB[cb][:, 5:9, :], C.rearrange("(m t) f -> m t f", t=16)[:, 0:4, :],
                         writes=[("CB", cb)])
                plan.dma("sp", Vnew[cb][0:4, :], D["Vs"][4 * sq_:4 * sq_ + 4, :], reads=["Vs"], writes=[("Vnew", cb)])
                c4 = slice(4 * sq_, 4 * sq_ + 4)
                def snew(e, c4=c4):
                    ins = None
                    for g in range(3):
                        for hp_ in range(4):
                            for par in range(2):
                                ins = e.matmul(psum[0:4, 6 + par, (g * 4 + hp_) * 4:(g * 4 + hp_) * 4 + 4],
                                               lhsT=KTsamp[par * 64:(par + 1) * 64, hp_, c4],
                                               rhs=QTs[par * 64:(par + 1) * 64, g, hp_, c4],
                                               start=True, stop=True, tile_position=(par * 64, 0))
                    return ins
                plan.op("pe", snew, reads=["KTsamp", "QTs"], writes=[("ps", 6), ("ps", 7)])
                for par in range(2):
                    plan.op("act", lambda e, par=par: e.activation(out=pn1[0:4, par, :], in_=psum[0:4, 6 + par, 0:48],
                                                                    func=AF.Exp), reads=[("ps", 6 + par)], writes=[("pn1", par)])
                    plan.op("dve", lambda e, par=par: e.tensor_tensor(
                        out=pm[0:4, 3:6, :, par:8:2],
                        in0=pn1[0:4, par, :].rearrange("p (g h t) -> p g t h", g=3, h=4),
                        in1=EsM[0:4, 3:6, :, par:8:2], op=ALU.mult),
                        reads=[("pn1", par), "smc"], writes=["pm"])
                for t in range(4):
                    qb_ = qsets()
                    plan.op("pe", lambda e, t=t, qb_=qb_, sq_=sq_: [e.matmul(
                        psum[:, qb_[g], :], lhsT=identb[0:64, 4 * sq_ + t:4 * sq_ + t + 1].broadcast_to([64, 128]),
                        rhs=Qtm[0:64, g, :], start=True, stop=True) for g in range(3)][-1],
                        reads=["Qtm", "identb"], writes=[("ps", b_) for b_ in qb_])
                    for g in range(3):
                        blk = 0 if g == 0 else (1 + t if g == 1 else 5 + t)
                        pi = (t * 3 + g) % 2
                        plan.op("dve", lambda e, g=g, blk=blk, pi=pi, qb_=qb_, cb=cb: e.tensor_tensor(
                            out=prod[pi], in0=CB[cb][:, blk, 0:512], in1=psum[:, qb_[g], :], op=ALU.mult),
                            reads=[("CB", cb), ("ps", qb_[g])], writes=[("prod", pi)])
                        plan.op("dve", lambda e, g=g, t=t, pi=pi: e.tensor_reduce(
                            out=sraw[:, g, t, :], in_=prod[pi].rearrange("p (h d) -> p h d", d=64), axis=AX.X,
                            op=ALU.add), reads=[("prod", pi)], writes=["sraw"])
                plan.op("act", lambda e: e.activation(out=pexp, in_=sraw, func=AF.Exp), reads=["sraw"], writes=["pexp"])
                plan.op("dve", lambda e: e.tensor_tensor(out=pm[:, 0:3, :, :], in0=pexp, in1=EsM[:, 0:3, :, :],
                                                          op=ALU.mult), reads=["pexp", "smc"], writes=["pm"])
                plan.op("dve", lambda e: e.tensor_copy(
                    out=pz.rearrange("p g a b h -> p g (a b) h")[:, :, 0:16:5, :], in_=pm[:, 1:3, :, :]),
                    reads=["pm"], writes=["pz"])
                plan.op("dve", lambda e: e.tensor_reduce(
                    out=pD, in_=pm.rearrange("p j t h -> p (t h) j"), axis=AX.X, op=ALU.add),
                    reads=["pm"], writes=["pD"])
                plan.op("dve", lambda e: e.tensor_reduce(
                    out=pns, in_=pm[:, 3:6, :, :].rearrange("p j t h -> p (t h) j"), axis=AX.X, op=ALU.add),
                    reads=["pm"], writes=["pns"])

                def pvs(e, cb=cb):
                    e.matmul(psum[0:32, 6, :], lhsT=pm[:, 0, :, :].rearrange("p t h -> p (t h)"),
                             rhs=CB[cb][:, 0, 512:1024], start=True, stop=False)
                    for g in (1, 2):
                        for t in range(4):
                            blk = 1 + t if g == 1 else 5 + t
                            e.matmul(psum[0:32, 6, :], lhsT=pz[:, g - 1, t, :, :].rearrange("p a h -> p (a h)"),
                                     rhs=CB[cb][:, blk, 512:1024], start=False, stop=False)
                    e.matmul(psum[0:32, 6, :], lhsT=pns, rhs=Vnew[cb], start=False, stop=True)
                    return e.matmul(psum[0:32, 7, 0:1], lhsT=pD, rhs=ones64[:, 0:1], start=True, stop=True)
                plan.op("pe", pvs, reads=["pm", "pz", "pns", "pD", ("CB", cb), ("Vnew", cb), "ones64"],
                        writes=[("ps", 6), ("ps", 7)])
                plan.op("dve", lambda e: e.reciprocal(out=rDs[0:32, :], in_=psum[0:32, 7, 0:1]),
                        reads=[("ps", 7)], writes=["rDs"])
                plan.op("dve", lambda e: e.scalar_tensor_tensor(out=osel[0:32, :], in0=psum[0:32, 6, :],
                                                                 scalar=rDs[0:32, :], in1=sel32, op0=ALU.mult,
                                                                 op1=ALU.mult),
                        reads=[("ps", 6), "rDs", "smc"], writes=["osel"])
                plan.op("pe", lambda e: [e.matmul(psum[:, 7, 8 + 4 * hp_:12 + 4 * hp_],
                                                  lhsT=osel[0:32, hp_ * 128:(hp_ + 1) * 128], rhs=tselb[0:32, :],
                                                  start=True, stop=True) for hp_ in range(4)][-1],
                        reads=["osel", "tselb"], writes=[("ps", 7)])
                plan.op("act", lambda e, c4=c4: e.activation(
                    out=oT[:, :, c4], in_=psum[:, 7, 8:24].rearrange("p (a b) -> p a b", a=4), func=AF.Copy),
                    reads=[("ps", 7)], writes=["oTs"])
            if K_DBG and lb == 0:
                plan.dma("sp", D["dbg_o"], oT, reads=["oTs", "oTp"])
                plan.dma("sp", D["dbg_sraw"], sraw.rearrange("p a b c -> p (a b c)"), reads=["sraw"])
                plan.dma("sp", D["dbg_pm"], pm.rearrange("p a b c -> p (a b c)"), reads=["pm"])
                plan.dma("sp", D["dbg_osel"], osel, reads=["osel"])
                plan.dma("sp", D["dbg_qtm"], Qtm.rearrange("p a b -> p (a b)"), reads=["Qtm"])
            plan.barrier()
            if K_STOP <= 11:
                break
            for (t, off, n) in TLB:
                plan.dma("sp", xB[:, :, off:off + n], D["xs"][:, :, off:off + n], reads=["xs"], writes=xk(t))

            def ev_add(oi, ti, off, n, bank):
                plan.op("dve", lambda e: e.tensor_tensor(out=xB[:, oi, off:off + n], in0=psum[:, bank, 0:n],
                                                          in1=xB[:, oi, off:off + n], op=ALU.add),
                        reads=[("ps", bank), ("xB", oi, ti)], writes=[("xB", oi, ti)])
            proj(D["w_o"][lb], 4, [[(0, 0, 512)], [(0, 512, 512)]], [4, 4],
                 lambda k, off, n: oT[:, k, off:off + n], lambda ti: ["oTs", "oTp"], TS, setB, ev_add, wbufB)
            normB("nml", L)
            for hg in range(4):
                def ev_upB(oi, ti, off, n, bank):
                    plan.op("act", lambda e: e.activation(out=zB[:, oi, off:off + n], in_=psum[:, bank, 0:n],
                                                           func=AF.Relu), reads=[("ps", bank)], writes=[("zB", oi, ti)])
                    plan.op("dve", lambda e: e.tensor_tensor(out=zB[:, oi, off:off + n], in0=zB[:, oi, off:off + n],
                                                              in1=zB[:, oi, off:off + n], op=ALU.mult),
                            reads=[("zB", oi, ti)], writes=[("zB", oi, ti)])
                proj(D["w_up"][L][:, hg * 1024:(hg + 1) * 1024], 8, [[(0, 0, 512)], [(0, 512, 512)]], [4, 4],
                     lambda k, off, n: hB[:, k, off:off + n], lambda ti: allk("hB", ti), TS, setB, ev_upB, wbufB)
                proj(D["w_dn"][L][hg * 1024:(hg + 1) * 1024, :], 8, [[(0, 0, 512)], [(0, 512, 512)]], [4, 4],
                     lambda k, off, n: zB[:, k, off:off + n], lambda ti: allk("zB", ti), TS, setB, ev_add, wbufB)
        for (t, off, n) in TLB:
            plan.dma("sp", D["yT"][:, :, off:off + n], xB[:, :, off:off + n], reads=xk(t))

    plan.barrier()
    replay(nc, plan)
    return nc


_NC_CACHE = {}


def _feat_major(v):
    v = np.asarray(v, np.float32)
    lead = v.shape[:-1]
    return np.ascontiguousarray(np.moveaxis(v.reshape(lead + (8, 128)), -1, 0))


def _emask():
    n = 24
    slopes = (2.0 ** (-8.0 * np.arange(1, n + 1, dtype=np.float64) / n)).reshape(3, 8)
    k = np.arange(128)[:, None]
    q = np.arange(128)[None, :]
    out = np.zeros((128, 3, 4, 2, 2, 128), np.float64)
    for g, (w, d) in enumerate(WINDOWS):
        for h in range(8):
            sl = slopes[g, h] * d
            prev = np.where(q <= k, np.exp(-sl * (q - k + 128)), 0.0)
            cur = np.where(q >= k, np.exp(-sl * (q - k)), 0.0)
            out[:, g, h // 2, h % 2, 0, :] = prev
            out[:, g, h // 2, h % 2, 1, :] = cur
    return out.reshape(128, 12, 512).astype(np.float32)


def _smc():
    slopes = (2.0 ** (-8.0 * np.arange(1, 25, dtype=np.float64) / 24)).reshape(3, 8)
    m = np.arange(128, dtype=np.float64)[:, None, None]
    t = np.arange(4, dtype=np.float64)[None, :, None]
    E = np.zeros((128, 6, 4, 8))
    d0 = 128 + t - m
    E[:, 0] = np.where(m >= t, np.exp(-slopes[0][None, None, :] * d0), 0.0)
    E[:, 1] = np.exp(-slopes[1][None, None, :] * 4 * (128 - m)) * np.ones((1, 4, 1))
    E[:, 2] = np.exp(-slopes[2][None, None, :] * 16 * (128 - m)) * np.ones((1, 4, 1))
    E[:, 3] = np.where((m <= t) & (m < 4), np.exp(-slopes[0][None, None, :] * (t - m)), 0.0)
    E[:, 4] = np.where(m == t, 1.0, 0.0) * np.ones((1, 1, 8))
    E[:, 5] = E[:, 4]
    out = np.zeros((128, 708), np.float32)
    out[:, 0:192] = E.reshape(128, 192)
    r = np.arange(32)
    f = np.arange(512)
    out[0:32, 192:704] = ((f[None, :] // 64) == (r[:, None] % 8)).astype(np.float32)
    out[0:32, 704:708] = ((r[:, None] // 8) == np.arange(4)[None, :]).astype(np.float32)
    return out


def kernel(x_prompt, x_sample, state_conv, cache_kv, norm_mix_g, norm_mlp_g, conv_w_pw1, conv_b_pw1,
           conv_w_dw, conv_b_dw, conv_ln_g, conv_ln_b, conv_w_pw2, conv_b_pw2, kv_norm_g, w_kv, k_norm_g,
           attn_w_q, q_norm_g, attn_w_o, mlp_w_up, mlp_w_down):
    f = lambda v: np.ascontiguousarray(np.asarray(v, np.float32))
    if "nc" not in _NC_CACHE:
        t0 = time.time()
        _NC_CACHE["nc"] = build_program()
        print("build time", time.time() - t0)
    nc = _NC_CACHE["nc"]
    x_prompt, x_sample, state_conv, cache_kv = f(x_prompt), f(x_sample), f(state_conv), f(cache_kv)
    prm = np.zeros((128, NPRM), np.float32)

    def put(name, arr):
        arr = np.asarray(arr, np.float32).reshape(128, -1)
        prm[:, PRM[name]:PRM[name] + arr.shape[1]] = arr
    put("nmg", _feat_major(norm_mix_g))
    put("nml", _feat_major(norm_mlp_g))
    put("bpw1", np.moveaxis(np.asarray(conv_b_pw1, np.float32).reshape(2, 16, 128), -1, 0))
    put("wdw", np.moveaxis(np.asarray(conv_w_dw, np.float32).reshape(2, 31, 8, 128), (3, 0, 2, 1), (0, 1, 2, 3)))
    put("bdw", _feat_major(conv_b_dw))
    put("lng", _feat_major(conv_ln_g))
    put("lnb", _feat_major(conv_ln_b))
    put("bpw2", _feat_major(conv_b_pw2))
    put("kvg", _feat_major(kv_norm_g))
    put("kng", np.tile(np.asarray(k_norm_g, np.float32), 2)[:, None])
    put("qng", np.tile(np.asarray(q_norm_g, np.float32), (1, 2)).T)
    prm[:, PRM["eps"]] = EPS
    ident = np.eye(128, dtype=np.float32)
    onesb = np.kron(np.eye(2, dtype=np.float32), np.full((64, 64), 1.0 / 64, np.float32))
    emask = _emask()
    smc = _smc()
    xpT = [np.ascontiguousarray(x_prompt[b].T).reshape(8, 128, 8192).transpose(1, 0, 2) for b in range(2)]
    shared = dict(ident=ident, onesb=onesb, emask=emask, smc=smc, w_pw1=f(conv_w_pw1), w_pw2=f(conv_w_pw2), w_kv=f(w_kv),
                  w_q=f(attn_w_q), w_o=f(attn_w_o), w_up=f(mlp_w_up), w_dn=f(mlp_w_down))
    in_maps = []
    for c in range(NCORES):
        b, j = c // 4, c % 4
        S = 2048 * j
        xin = np.zeros((128, 8, TA), np.float32)
        xs_ = x_sample[16 * c:16 * c + 16].reshape(64, 1024)
        xin[:, :, 0:64] = xs_.T.reshape(8, 128, 64).transpose(1, 0, 2)
        lo = S - 2112
        src_lo = max(lo, 0)
        xin[:, :, 64 + (src_lo - lo):TA] = xpT[b][:, :, src_lo:S + 2048]
        p = prm.copy()
        p[:, PRM["valid"]] = 1.0 if j > 0 else 0.0
        p[:, PRM["hbias"]] = 0.0 if j > 0 else -30000.0
        sc = state_conv[:, 16 * c:16 * c + 16]
        scT = np.ascontiguousarray(sc.reshape(2, 16, 30, 8, 128).transpose(4, 0, 3, 1, 2))
        m = dict(shared)
        m.update(xin=xin, prm=p, scT=scT, scN=np.ascontiguousarray(sc),
                 cache=np.ascontiguousarray(cache_kv[16 * c:16 * c + 16].reshape(16, 2048, 1024)))
        in_maps.append(m)
    if K_CORE is not None:
        t0 = time.time()
        res = run_bass_kernel_spmd(nc, [in_maps[int(K_CORE)]], core_ids=[0])
        print("run time", time.time() - t0)
        R = [res.results[0]] * NCORES
    else:
        res = run_bass_kernel_spmd(nc, in_maps, core_ids=list(range(NCORES)))
        R = res.results
    y_prompt = np.zeros((2, 8192, 1024), np.float32)
    y_sample = np.zeros((128, 4, 1024), np.float32)
    conv_prompt = np.zeros((2, 2, 30, 1024), np.float32)
    conv_sample = np.zeros((2, 128, 30, 1024), np.float32)
    kv_prompt = np.zeros((2, 2048, 2, 8, 64), np.float32)
    kv_sample = np.zeros((128, 4, 2, 8, 64), np.float32)
    for c in range(NCORES):
        b, j = c // 4, c % 4
        r = R[c]
        yT = np.asarray(r["yT"]).transpose(1, 0, 2).reshape(1024, TB)
        y_sample[16 * c:16 * c + 16] = yT[:, 0:64].T.reshape(16, 4, 1024)
        y_prompt[b, 2048 * j:2048 * j + 2048] = yT[:, 64:].T
        conv_sample[:, 16 * c:16 * c + 16] = np.asarray(r["cs"])
        kT = np.asarray(r["kT"]).transpose(1, 0, 2).reshape(512, TA)
        vN = np.asarray(r["vN"])
        kv_sample[16 * c:16 * c + 16, :, 0] = kT[:, 0:64].T.reshape(16, 4, 8, 64)
        kv_sample[16 * c:16 * c + 16, :, 1] = vN[0:64].reshape(16, 4, 8, 64)
        if j == 3:
            conv_prompt[:, b] = np.asarray(r["cp"])
            kv_prompt[b, :, 0] = kT[:, OWN0:].T.reshape(2048, 8, 64)
            kv_prompt[b, :, 1] = vN[OWN0:].reshape(2048, 8, 64)
    return (y_prompt, y_sample, conv_prompt, conv_sample, kv_prompt, kv_sample)
```

```python
import os
import time
import numpy as np
import concourse.bass as bass
import concourse.mybir as mybir
from concourse.bass_utils import run_bass_kernel_spmd
from contextlib import ExitStack

F32, BF16 = mybir.dt.float32, mybir.dt.bfloat16
AF = mybir.ActivationFunctionType
ALU = mybir.AluOpType
AX = mybir.AxisListType

NCORES = 8
TA = 4224
NST = 1408
TB = 2112
OWN0 = 2176
EPS = 1e-6
ARENA = 51800
WINDOWS = ((128, 1), (512, 4), (2048, 16))
K_STOP = int(os.environ.get('K_STOP', '99'))
K_NST = int(os.environ.get('K_NST', '3'))
K_CORE = os.environ.get('K_CORE')
K_DBG = int(os.environ.get('K_DBG', '0'))

PRM = {}
_o = 0
for _n, _w in (("nmg", 32), ("nml", 32), ("bpw1", 32), ("wdw", 2 * 8 * 31), ("bdw", 16), ("lng", 16),
               ("lnb", 16), ("bpw2", 16), ("kvg", 8), ("kng", 1), ("qng", 2), ("valid", 1), ("hbias", 1),
               ("eps", 1), ("zero", 1)):
    PRM[_n] = _o
    _o += _w
NPRM = _o


class Plan:
    ENG = ("pe", "act", "dve", "pool", "sp")
    NDS = 8

    def __init__(self):
        self.items = {e: [] for e in self.ENG}
        self.cnt = {e: 0 for e in self.ENG}
        self.known = {e: {} for e in self.ENG}
        self.bw = {}
        self.br = {}
        self.dma_n = {}
        self.dma_rr = {e: 0 for e in self.ENG}
        self.sems = set("s_" + e for e in self.ENG)

    def _need(self, eng, tok):
        if tok is None:
            return
        sem, val = tok
        if sem == "s_pe" and eng == "pe":
            return
        if self.known[eng].get(sem, 0) >= val:
            return
        self.known[eng][sem] = val
        self.items[eng].append(("w", sem, val))

    def _deps(self, eng, reads, writes):
        for b in reads:
            self._need(eng, self.bw.get(b))
            if isinstance(b, tuple) and b[0] == "ps":
                for s, v in self.br.get(b, {}).items():
                    if s != "s_" + eng:
                        self._need(eng, (s, v))
        for b in writes:
            self._need(eng, self.bw.get(b))
            for s, v in self.br.get(b, {}).items():
                self._need(eng, (s, v))

    def _mark(self, tok, reads, writes):
        for b in reads:
            d = self.br.setdefault(b, {})
            if d.get(tok[0], 0) < tok[1]:
                d[tok[0]] = tok[1]
        for b in writes:
            self.bw[b] = tok
            self.br[b] = {}

    def op(self, eng, fn, reads=(), writes=()):
        self._deps(eng, reads, writes)
        self.cnt[eng] += 1
        tok = ("s_" + eng, self.cnt[eng])
        self.items[eng].append(("x", fn))
        self._mark(tok, reads, writes)
        return tok

    def dma(self, q, out, in_, reads=(), writes=()):
        i = self.dma_rr[q] % self.NDS
        self.dma_rr[q] += 1
        sem = "d_%s%d" % (q, i)
        self.sems.add(sem)
        n = self.dma_n.get(sem, 0)
        if n:
            self._need(q, (sem, 16 * n))
        self._deps(q, reads, writes)
        self.dma_n[sem] = n + 1
        tok = (sem, 16 * (n + 1))
        self.items[q].append(("d", out, in_, sem))
        self._mark(tok, reads, writes)
        return tok

    def barrier(self):
        for e in self.ENG:
            for o in self.ENG:
                if self.cnt[o]:
                    self._need(e, ("s_" + o, self.cnt[o]))
            for sem, n in self.dma_n.items():
                self._need(e, (sem, 16 * n))


def replay(nc, plan):
    with ExitStack() as es:
        es.enter_context(nc.allow_low_precision("bf16 matmul operands, fp32 accumulation"))
        sems = {s: es.enter_context(nc.semaphore(s)) for s in sorted(plan.sems)}
        block = es.enter_context(nc.Block())

        def run(e, name):
            own = sems["s_" + name]
            for it in plan.items[name]:
                if it[0] == "w":
                    e.wait_ge(sems[it[1]], it[2])
                elif it[0] == "x":
                    it[1](e).then_inc(own, 1)
                else:
                    e.dma_start(out=it[1], in_=it[2]).then_inc(sems[it[3]], 16)

        @block.tensor
        def _(e):
            run(e, "pe")

        @block.scalar
        def _(e):
            run(e, "act")

        @block.vector
        def _(e):
            run(e, "dve")

        @block.gpsimd
        def _(e):
            run(e, "pool")

        @block.sync
        def _(e):
            run(e, "sp")


def build_program():
    nc = bass.Bass("TRN2", target_bir_lowering=False)
    plan = Plan()
    D = {}

    def din(name, shape, dt=F32):
        D[name] = nc.dram_tensor(name, list(shape), dt, kind="ExternalInput").ap()

    def dout(name, shape, dt=F32):
        D[name] = nc.dram_tensor(name, list(shape), dt, kind="ExternalOutput").ap()

    def dscr(name, shape, dt=F32):
        D[name] = nc.dram_tensor(name, list(shape), dt, kind="Internal").ap()

    din("xin", [128, 8, TA])
    din("prm", [128, NPRM])
    din("ident", [128, 128])
    din("onesb", [128, 128])
    din("w_pw1", [2, 1024, 2048])
    din("w_pw2", [2, 1024, 1024])
    din("w_kv", [1024, 1024])
    din("w_q", [2, 1024, 1536])
    din("w_o", [2, 512, 1024])
    din("w_up", [4, 1024, 4096])
    din("w_dn", [4, 4096, 1024])
    din("scT", [128, 2, 8, 16, 30])
    din("scN", [2, 16, 30, 1024])
    din("cache", [16, 2048, 1024])
    din("emask", [128, 12, 512])
    din("smc", [128, 708])
    dout("yT", [128, 8, TB])
    dout("kT", [128, 4, TA])
    dout("vN", [TA, 512])
    dout("cs", [2, 16, 30, 1024])
    dout("cp", [2, 30, 1024])
    if K_DBG:
        dout("dbg_h", [128, 8, TB], BF16)
        dout("dbg_q", [4, 128, 3, TB], BF16)
        dout("dbg_o", [128, 4, TB], BF16)
        dout("dbg_n", [4, 128, 2048])
        dout("dbg_d", [4, 128, 2048])
        dout("dbg_sraw", [128, 96])
        dout("dbg_pm", [128, 192], BF16)
        dout("dbg_osel", [128, 512], BF16)
        dout("dbg_qtm", [128, 1536], BF16)
    dscr("xs", [128, 8, TB])
    dscr("KTs", [4, 128, TA], BF16)
    dscr("Vs", [TA, 512], BF16)

    arena = nc.alloc_sbuf_tensor("arena", [128, ARENA], F32)
    psum = nc.alloc_psum_tensor("psum", [128, 8, 512], F32)

    class Bump:
        def __init__(self, start=0):
            self.off = start

        def __call__(self, shape, dt):
            n = int(np.prod(shape))
            nb = n * (4 if dt == F32 else 2)
            nb = (nb + 31) // 32 * 32
            assert self.off + nb <= ARENA * 4, ("SBUF overflow", self.off + nb)
            a = arena[:, self.off // 4:(self.off + nb) // 4]
            if dt != F32:
                a = a.bitcast(dt)
            a = a[:, 0:n]
            if len(shape) == 2:
                a = a.rearrange("p (a b) -> p a b", a=shape[0])
            elif len(shape) == 3:
                a = a.rearrange("p (a b c) -> p a b c", a=shape[0], b=shape[1])
            elif len(shape) == 4:
                a = a.rearrange("p (a b c d) -> p a b c d", a=shape[0], b=shape[1], c=shape[2])
            self.off += nb
            return a

    pb = Bump(0)
    prm = pb([NPRM], F32)
    identf = pb([128], F32)
    identb = pb([128], BF16)
    onesf = pb([128], F32)
    onesb16 = pb([128], BF16)
    oneshf = pb([128], F32)
    oneshb = pb([128], BF16)
    smc = pb([708], F32)
    tselb = pb([4], BF16)
    qg8 = pb([2], F32)
    ones64 = pb([64], BF16)
    PERSIST_END = pb.off

    def P(name, i=0, w=1):
        o = PRM[name] + i
        return prm[:, o:o + w]

    plan.dma("sp", prm, D["prm"], writes=["prm"])
    plan.dma("sp", identf, D["ident"], writes=["identf"])
    plan.dma("sp", oneshf, D["onesb"], writes=["oneshf"])
    plan.op("dve", lambda e: e.tensor_copy(out=identb, in_=identf), reads=["identf"], writes=["identb"])
    plan.op("dve", lambda e: e.tensor_copy(out=oneshb, in_=oneshf), reads=["oneshf"], writes=["oneshb"])
    plan.op("dve", lambda e: e.memset(onesf, 1.0 / 1024), writes=["onesf"])
    plan.op("dve", lambda e: e.memset(onesb16, 1.0 / 1024), writes=["onesb16"])
    plan.op("dve", lambda e: e.memset(ones64, 1.0), writes=["ones64"])
    plan.dma("sp", smc, D["smc"], writes=["smc"])
    plan.op("dve", lambda e: e.tensor_copy(out=tselb, in_=smc[:, 704:708]), reads=["smc"], writes=["tselb"])
    plan.op("dve", lambda e: e.tensor_scalar(out=qg8, in0=P("qng", 0, 2), scalar1=0.125, scalar2=None, op0=ALU.mult),
            reads=["prm"], writes=["qg8"])

    class RR:
        def __init__(self, items):
            self.items = items
            self.i = 0

        def __call__(self):
            v = self.items[self.i % len(self.items)]
            self.i += 1
            return v

    def mm_group(e, bank_tiles, kc, lhs_fn, rhs_fn):
        ins = None
        for k in range(kc):
            for (b, off, n) in bank_tiles:
                ins = e.matmul(psum[:, b, 0:n], lhsT=lhs_fn(k), rhs=rhs_fn(k, off, n),
                               start=(k == 0), stop=(k == kc - 1))
        return ins

    wslot = RR([0, 1, 2])

    def proj(Wap, kc, runs_per_group, nchunks_per_group, rhs_fn, in_keys_fn, tilesets, banksets, evac_fn, wbuf):
        Wr = Wap.rearrange("(k p) o -> p k o", p=128)
        oi = 0
        for g, runs in enumerate(runs_per_group):
            s = wslot()
            for (dc, sc, w) in runs:
                plan.dma("pool", wbuf[s][:, 0:kc, dc:dc + w], Wr[:, :, sc:sc + w], writes=[("wb", s)])
            for lo in range(nchunks_per_group[g]):
                for ts in tilesets:
                    banks = banksets()
                    bt = [(banks[i], off, n) for i, (ti, off, n) in enumerate(ts)]
                    rk = [("wb", s)]
                    for (ti, off, n) in ts:
                        rk += in_keys_fn(ti)
                    plan.op("pe", (lambda e, bt=bt, s=s, lo=lo: mm_group(
                        e, bt, kc, lambda k: wbuf[s][:, k, lo * 128:(lo + 1) * 128], rhs_fn)),
                        reads=rk, writes=[("ps", b) for (b, _, _) in bt])
                    for i, (ti, off, n) in enumerate(ts):
                        evac_fn(oi, ti, off, n, banks[i])
                oi += 1

    def allk(name, ti, nk=8):
        return [(name, k, ti) for k in range(nk)]

    R2B = [0, 512, 1024, 1408, 1440]

    def r2k(k, lo, hi):
        return [("R2", k, i) for i in range(4) if lo < R2B[i + 1] and hi > R2B[i]]

    def r2all(lo, hi):
        return [x for k in range(8) for x in r2k(k, lo, hi)]

    a = Bump(PERSIST_END)
    xT = a([8, NST], F32)
    R1 = a([8, NST], BF16)
    R2 = a([8, NST + 32], BF16)
    sq = a([8, 512], F32)
    sqb = sq.rearrange("p a b -> p (a b)").bitcast(BF16)[:, 0:4096].rearrange("p (a b) -> p a b", a=8)
    sig = [a([512], F32) for _ in range(3)]
    stA = [a([512], F32) for _ in range(2)]
    stA.append(stA[0])
    stB = [a([512], F32) for _ in range(2)]
    stB.append(stB[0])
    stC = [a([512], F32) for _ in range(2)]
    stC.append(stC[0])
    stD = [a([512], F32) for _ in range(2)]
    stD.append(stD[0])
    wbuf = [a([8, 512], BF16) for _ in range(3)]
    dg = [a([31, 128], BF16) for _ in range(2)]
    usamp = a([8, 16, 34], BF16)
    usampf = [a([8, 64], F32) for _ in range(2)]
    utailf = [a([8, 30], F32) for _ in range(2)]
    tailb = [a([8, 30], BF16) for _ in range(2)]
    kf = [a([512], F32) for _ in range(2)]
    kf.append(kf[0])
    kb = [a([512], BF16) for _ in range(2)]
    kb.append(kb[0])
    vf, vb = kf, kb
    tstage = sq.rearrange("p a b -> p (a b)")[:, 0:1024]
    hT = R1
    cT = R1
    uT = R2
    caT = R2[:, :, 0:NST]
    zT = R2[:, :, 0:NST]

    TL = [(0, 0, 512), (1, 512, 512), (2, 1024, 384)]
    set3 = RR([(0, 1, 2), (3, 4, 5)])
    bank6 = RR([0, 1, 2, 3, 4, 5])
    bankS = RR([6, 7])

    def rsqrt_chain(t, n, bank, tag):
        plan.op("act", lambda e: e.activation(out=stB[t][:, 0:n], in_=psum[:, bank, 0:n], func=AF.Ln,
                                               bias=P("eps"), scale=1.0),
                reads=[("ps", bank), "prm"], writes=[("stB", t % 2)])
        plan.op("act", lambda e: e.activation(out=stA[t][:, 0:n], in_=stB[t][:, 0:n], func=AF.Exp, scale=-0.5),
                reads=[("stB", t % 2)], writes=[("stA", t % 2)])

    def norm_stage(gname, gidx, dst, dst_name):
        for (t, off, n) in TL:
            plan.op("act", lambda e, off=off, n=n: e.activation(out=sqb[:, :, 0:n], in_=xT[:, :, off:off + n],
                                                                 func=AF.Square),
                    reads=allk("xT", t), writes=["sq"])
            b = bankS()
            plan.op("pe", lambda e, b=b, n=n: mm_group(e, [(b, 0, n)], 8, lambda k: onesb16,
                                                         lambda k, o_, n_: sqb[:, k, 0:n_]),
                    reads=["sq", "onesb16"], writes=[("ps", b)])
            rsqrt_chain(t, n, b, "n")
            for k in range(8):
                plan.op("dve", lambda e, k=k, t=t, off=off, n=n: e.scalar_tensor_tensor(
                    out=dst[:, k, off:off + n], in0=xT[:, k, off:off + n], scalar=P(gname, gidx * 8 + k),
                    in1=stA[t][:, 0:n], op0=ALU.mult, op1=ALU.mult),
                    reads=[("xT", k, t), ("stA", t % 2), "prm"], writes=[(dst_name, k, t)])

    for st in range(K_NST):
        c0 = st * NST
        for (t, off, n) in TL:
            plan.dma("sp", xT[:, :, off:off + n], D["xin"][:, :, c0 + off:c0 + off + n], writes=allk("xT", t))
        for l in range(2):
            norm_stage("nmg", l, hT, "R1")
            if K_STOP <= 1:
                break
            if st == 0:
                plan.op("dve", lambda e: e.memset(uT[:, :, 0:30], 0.0), writes=r2all(0, 30))
                for hh in range(2):
                    plan.dma("pool", usamp[:, 4 * hh:4 * hh + 4, :, 0:30], D["scT"][:, l, 4 * hh:4 * hh + 4],
                             writes=["usamp"])
            else:
                plan.op("dve", lambda e, l=l: e.tensor_copy(out=uT[:, :, 0:30], in_=tailb[l]),
                        reads=[("tailb", l)], writes=r2all(0, 30))
            def ev_pw1(oi, ti, off, n, bank, l=l, st=st):
                c, isg = oi // 2, oi % 2
                if not isg:
                    ev_pw1.abank[ti] = bank
                    return
                ab = ev_pw1.abank[ti]
                plan.op("act", lambda e: e.activation(out=sig[ti][:, 0:n], in_=psum[:, bank, 0:n], func=AF.Sigmoid,
                                                       bias=P("bpw1", l * 16 + 8 + c), scale=1.0),
                        reads=[("ps", bank), "prm"], writes=[("sig", ti)])
                plan.op("dve", lambda e: e.scalar_tensor_tensor(
                    out=uT[:, c, 30 + off:30 + off + n], in0=psum[:, ab, 0:n], scalar=P("bpw1", l * 16 + c),
                    in1=sig[ti][:, 0:n], op0=ALU.add, op1=ALU.mult),
                    reads=[("ps", ab), ("sig", ti), "prm"], writes=r2k(c, 30 + off, 30 + off + n))
                if st == 0 and ti == 0:
                    plan.op("dve", lambda e: e.scalar_tensor_tensor(
                        out=usampf[l][:, c, :], in0=psum[:, ab, 0:64], scalar=P("bpw1", l * 16 + c),
                        in1=sig[ti][:, 0:64], op0=ALU.add, op1=ALU.mult),
                        reads=[("ps", ab), ("sig", ti), "prm"], writes=[("usampf", l, c)])
                    plan.op("dve", lambda e: e.tensor_copy(
                        out=usamp[:, c, :, 30:34], in_=usampf[l][:, c, :].rearrange("p (s t) -> p s t", t=4)),
                        reads=[("usampf", l, c)], writes=["usamp"])
                if st == 2 and ti == 2:
                    plan.op("dve", lambda e: e.scalar_tensor_tensor(
                        out=utailf[l][:, c, :], in0=psum[:, ab, n - 30:n], scalar=P("bpw1", l * 16 + c),
                        in1=sig[ti][:, n - 30:n], op0=ALU.add, op1=ALU.mult),
                        reads=[("ps", ab), ("sig", ti), "prm"], writes=[("utailf", l, c)])
            ev_pw1.abank = {}
            runs = []
            for c in range(0, 8, 2):
                runs.append([(0, c * 128, 128), (128, 1024 + c * 128, 128),
                             (256, (c + 1) * 128, 128), (384, 1024 + (c + 1) * 128, 128)])
            proj(D["w_pw1"][l], 8, runs, [4] * 4, lambda k, off, n: hT[:, k, off:off + n],
                 lambda ti: allk("R1", ti), [TL], set3, ev_pw1, wbuf)
            if st == 1:
                plan.op("dve", lambda e: e.tensor_scalar(out=uT[:, :, 30 + 738:30 + 768], in0=uT[:, :, 30 + 738:30 + 768],
                                                          scalar1=P("valid"), scalar2=None, op0=ALU.mult),
                        reads=r2all(768, 798) + ["prm"], writes=r2all(768, 798))
            if K_STOP <= 2:
                break
            for c in range(8):
                ds = c % 2
                plan.op("dve", lambda e, ds=ds, c=c, l=l: e.tensor_tensor(
                    out=dg[ds], in0=identb.unsqueeze(1).broadcast_to([128, 31, 128]),
                    in1=P("wdw", (l * 8 + c) * 31, 31).unsqueeze(2).broadcast_to([128, 31, 128]), op=ALU.mult),
                    reads=["identb", "prm"], writes=[("dg", ds)])
                for (t, off, n) in TL:
                    b = bank6()

                    def conv_mm(e, ds=ds, c=c, off=off, n=n, b=b, st=st, t=t):
                        ins = None
                        for j in range(31):
                            ins = e.matmul(psum[:, b, 0:n], lhsT=dg[ds][:, j, :], rhs=uT[:, c, off + j:off + j + n],
                                           start=(j == 0), stop=(j == 30))
                        if st == 0 and t == 0:
                            for j in range(31):
                                ins = e.matmul(psum[:, b, 0:64].rearrange("p (s t) -> p s t", t=4), lhsT=dg[ds][:, j, :],
                                               rhs=usamp[:, c, :, j:j + 4], start=(j == 0), stop=(j == 30))
                        return ins
                    rk = [("dg", ds)] + r2k(c, off, off + n + 30)
                    if st == 0 and t == 0:
                        rk.append("usamp")
                    plan.op("pe", conv_mm, reads=rk, writes=[("ps", b)])
                    plan.op("act", lambda e, c=c, off=off, n=n, b=b, l=l: e.activation(
                        out=cT[:, c, off:off + n], in_=psum[:, b, 0:n], func=AF.Identity, bias=P("bdw", l * 8 + c),
                        scale=1.0), reads=[("ps", b), "prm"], writes=[("R1", c, t)])
            plan.op("dve", lambda e, l=l: e.tensor_copy(out=tailb[l], in_=uT[:, :, NST:NST + 30]),
                    reads=r2all(NST, NST + 30), writes=[("tailb", l)])
            if K_STOP <= 3:
                break
            for (t, off, n) in TL:
                plan.op("act", lambda e, off=off, n=n: e.activation(out=sqb[:, :, 0:n], in_=cT[:, :, off:off + n],
                                                                     func=AF.Square),
                        reads=allk("R1", t), writes=["sq"])
                b1, b2 = bankS(), bankS()
                plan.op("pe", lambda e, b1=b1, off=off, n=n: mm_group(
                    e, [(b1, off, n)], 8, lambda k: onesb16, lambda k, o_, n_: cT[:, k, o_:o_ + n_]),
                    reads=allk("R1", t) + ["onesb16"], writes=[("ps", b1)])
                plan.op("pe", lambda e, b2=b2, n=n: mm_group(
                    e, [(b2, 0, n)], 8, lambda k: onesb16, lambda k, o_, n_: sqb[:, k, 0:n_]),
                    reads=["sq", "onesb16"], writes=[("ps", b2)])
                plan.op("act", lambda e, t=t, n=n, b1=b1: e.activation(out=stC[t][:, 0:n], in_=psum[:, b1, 0:n],
                                                                        func=AF.Copy),
                        reads=[("ps", b1)], writes=[("stC", t % 2)])
                plan.op("dve", lambda e, t=t, n=n: e.tensor_tensor(out=stD[t][:, 0:n], in0=stC[t][:, 0:n],
                                                                    in1=stC[t][:, 0:n], op=ALU.mult),
                        reads=[("stC", t % 2)], writes=[("stD", t % 2)])
                plan.op("dve", lambda e, t=t, n=n, b2=b2: e.tensor_tensor(out=stD[t][:, 0:n], in0=psum[:, b2, 0:n],
                                                                           in1=stD[t][:, 0:n], op=ALU.subtract),
                        reads=[("ps", b2), ("stD", t % 2)], writes=[("stD", t % 2)])
                plan.op("act", lambda e, t=t, n=n: e.activation(out=stB[t][:, 0:n], in_=stD[t][:, 0:n], func=AF.Ln,
                                                                 bias=P("eps"), scale=1.0),
                        reads=[("stD", t % 2), "prm"], writes=[("stB", t % 2)])
                plan.op("act", lambda e, t=t, n=n: e.activation(out=stA[t][:, 0:n], in_=stB[t][:, 0:n], func=AF.Exp,
                                                                 scale=-0.5),
                        reads=[("stB", t % 2)], writes=[("stA", t % 2)])
                plan.op("dve", lambda e, t=t, n=n: e.scalar_tensor_tensor(
                    out=stC[t][:, 0:n], in0=stC[t][:, 0:n], scalar=-1.0, in1=stA[t][:, 0:n], op0=ALU.mult,
                    op1=ALU.mult), reads=[("stC", t % 2), ("stA", t % 2)], writes=[("stC", t % 2)])
                plan.op("dve", lambda e, t=t, off=off, n=n: e.tensor_tensor(
                    out=sq[:, :, 0:n], in0=cT[:, :, off:off + n],
                    in1=stA[t][:, 0:n].unsqueeze(1).broadcast_to([128, 8, n]), op=ALU.mult),
                    reads=allk("R1", t) + [("stA", t % 2)], writes=["sq"])
                plan.op("dve", lambda e, t=t, n=n: e.tensor_tensor(
                    out=sq[:, :, 0:n], in0=sq[:, :, 0:n],
                    in1=stC[t][:, 0:n].unsqueeze(1).broadcast_to([128, 8, n]), op=ALU.add),
                    reads=["sq", ("stC", t % 2)], writes=["sq"])
                for k in range(8):
                    plan.op("act", lambda e, k=k, off=off, n=n, l=l: e.activation(
                        out=caT[:, k, off:off + n], in_=sq[:, k, 0:n], func=AF.Silu, bias=P("lnb", l * 8 + k),
                        scale=P("lng", l * 8 + k)), reads=["sq", "prm"], writes=[("R2", k, t)])
            if K_STOP <= 4:
                break
            def ev_pw2(oi, ti, off, n, bank, l=l):
                plan.op("dve", lambda e: e.scalar_tensor_tensor(
                    out=xT[:, oi, off:off + n], in0=psum[:, bank, 0:n], scalar=P("bpw2", l * 8 + oi),
                    in1=xT[:, oi, off:off + n], op0=ALU.add, op1=ALU.add),
                    reads=[("ps", bank), ("xT", oi, ti), "prm"], writes=[("xT", oi, ti)])
            proj(D["w_pw2"][l], 8, [[(0, 0, 512)], [(0, 512, 512)]], [4, 4],
                 lambda k, off, n: caT[:, k, off:off + n], lambda ti: allk("R2", ti), [TL], set3, ev_pw2, wbuf)
            if K_STOP <= 5:
                break
            norm_stage("nml", l, hT, "R1")
            for hg in range(4):
                def ev_up(oi, ti, off, n, bank):
                    plan.op("act", lambda e: e.activation(out=zT[:, oi, off:off + n], in_=psum[:, bank, 0:n],
                                                           func=AF.Relu),
                            reads=[("ps", bank)], writes=[("R2", oi, ti)])
                    plan.op("dve", lambda e: e.tensor_tensor(out=zT[:, oi, off:off + n], in0=zT[:, oi, off:off + n],
                                                              in1=zT[:, oi, off:off + n], op=ALU.mult),
                            reads=[("R2", oi, ti)], writes=[("R2", oi, ti)])
                proj(D["w_up"][l][:, hg * 1024:(hg + 1) * 1024], 8, [[(0, 0, 512)], [(0, 512, 512)]], [4, 4],
                     lambda k, off, n: hT[:, k, off:off + n], lambda ti: allk("R1", ti), [TL], set3, ev_up, wbuf)

                def ev_dn(oi, ti, off, n, bank):
                    plan.op("dve", lambda e: e.tensor_tensor(out=xT[:, oi, off:off + n], in0=psum[:, bank, 0:n],
                                                              in1=xT[:, oi, off:off + n], op=ALU.add),
                            reads=[("ps", bank), ("xT", oi, ti)], writes=[("xT", oi, ti)])
                proj(D["w_dn"][l][hg * 1024:(hg + 1) * 1024, :], 8, [[(0, 0, 512)], [(0, 512, 512)]], [4, 4],
                     lambda k, off, n: zT[:, k, off:off + n], lambda ti: allk("R2", ti), [TL], set3, ev_dn, wbuf)
        if K_STOP <= 6:
            continue
        norm_stage("kvg", 0, hT, "R1")

        def ev_k(oi, ti, off, n, bank, c0=c0):
            sgb = sig[ti].bitcast(BF16)
            plan.op("act", lambda e: e.activation(out=sgb[:, 0:n], in_=psum[:, bank, 0:n], func=AF.Square),
                    reads=[("ps", bank)], writes=[("sig", ti)])
            b2 = bankS()
            plan.op("pe", lambda e: e.matmul(psum[:, b2, 0:n], lhsT=oneshb, rhs=sgb[:, 0:n], start=True, stop=True),
                    reads=[("sig", ti), "oneshb"], writes=[("ps", b2)])
            rsqrt_chain(ti, n, b2, "k")
            plan.op("dve", lambda e: e.scalar_tensor_tensor(
                out=kf[ti][:, 0:n], in0=psum[:, bank, 0:n], scalar=P("kng"), in1=stA[ti][:, 0:n], op0=ALU.mult,
                op1=ALU.mult), reads=[("ps", bank), ("stA", ti % 2), "prm"], writes=[("kf", ti % 2)])
            plan.op("act", lambda e: e.activation(out=kb[ti][:, 0:n], in_=kf[ti][:, 0:n], func=AF.Copy),
                    reads=[("kf", ti % 2)], writes=[("kb", ti % 2)])
            plan.dma("sp", D["kT"][:, oi, c0 + off:c0 + off + n], kf[ti][:, 0:n], reads=[("kf", ti % 2)])
            plan.dma("sp", D["KTs"][oi][:, c0 + off:c0 + off + n], kb[ti][:, 0:n], reads=[("kb", ti % 2)], writes=["KTs"])
        proj(D["w_kv"][:, 0:512], 8, [[(0, 0, 512)]], [4], lambda k, off, n: hT[:, k, off:off + n],
             lambda ti: allk("R1", ti), [TL], set3, ev_k, wbuf)
        if K_STOP <= 7:
            continue
        s = wslot()
        plan.dma("pool", wbuf[s], D["w_kv"].rearrange("(k p) o -> p k o", p=128)[:, :, 512:1024], writes=[("wb", s)])
        for tb in range(NST // 128):
            b = bank6()
            ti = tb // 4
            plan.op("pe", lambda e, b=b, tb=tb, s=s: mm_group(
                e, [(b, 0, 512)], 8, lambda k: hT[:, k, tb * 128:(tb + 1) * 128], lambda k, o_, n_: wbuf[s][:, k, :]),
                reads=allk("R1", ti) + [("wb", s)], writes=[("ps", b)])
            i = tb % 2
            plan.op("act", lambda e, b=b, i=i: e.activation(out=vf[i], in_=psum[:, b, :], func=AF.Copy),
                    reads=[("ps", b)], writes=[("kf", i)])
            plan.op("dve", lambda e, i=i: e.tensor_copy(out=vb[i], in_=vf[i]),
                    reads=[("kf", i)], writes=[("kb", i)])
            r0 = c0 + tb * 128
            plan.dma("sp", D["vN"][r0:r0 + 128, :], vf[i], reads=[("kf", i)])
            plan.dma("sp", D["Vs"][r0:r0 + 128, :], vb[i], reads=[("kb", i)], writes=["Vs"])
        if K_STOP <= 8:
            continue
        if st == 0:
            plan.dma("sp", D["xs"][:, :, 0:64], xT[:, :, 0:64], reads=allk("xT", 0), writes=["xs"])
        elif st == 1:
            plan.dma("sp", D["xs"][:, :, 64:704], xT[:, :, 768:1408], reads=allk("xT", 1) + allk("xT", 2), writes=["xs"])
        else:
            plan.dma("sp", D["xs"][:, :, 704:2112], xT[:, :, 0:1408],
                     reads=allk("xT", 0) + allk("xT", 1) + allk("xT", 2), writes=["xs"])
        if K_STOP <= 9:
            continue
        if st in (0, 2):
            for l in range(2):
                src = usampf[l] if st == 0 else utailf[l]
                w = 64 if st == 0 else 30
                bA, bB = bankS(), bankS()

                def tr(e, src=src, w=w, bA=bA, bB=bB):
                    ins = None
                    for k in range(8):
                        bk = bA if k < 4 else bB
                        ins = e.transpose(psum[0:w, bk, (k % 4) * 128:(k % 4 + 1) * 128], src[:, k, :], identf)
                    return ins
                rk = [("usampf", l, c) for c in range(8)] if st == 0 else [("utailf", l, c) for c in range(8)]
                plan.op("pe", tr, reads=rk + ["identf"], writes=[("ps", bA), ("ps", bB)])
                plan.op("act", lambda e, w=w, bA=bA: e.activation(out=tstage[0:w, 0:512], in_=psum[0:w, bA, :],
                                                                   func=AF.Copy),
                        reads=[("ps", bA)], writes=["sq"])
                plan.op("act", lambda e, w=w, bB=bB: e.activation(out=tstage[0:w, 512:1024], in_=psum[0:w, bB, :],
                                                                   func=AF.Copy),
                        reads=[("ps", bB)], writes=["sq"])
                if st == 0:
                    for s_ in range(16):
                        plan.dma("sp", D["cs"][l, s_, 26:30, :], tstage[4 * s_:4 * s_ + 4, :],
                                 reads=["sq"])
                    plan.dma("sp", D["cs"][l, :, 0:26, :], D["scN"][l, :, 4:30, :])
                else:
                    plan.dma("sp", D["cp"][l], tstage[0:30, :], reads=["sq"])

    plan.barrier()
    if K_STOP > 10:
        bb = Bump(PERSIST_END)
        wbufB = [bb([8, 512], BF16) for _ in range(3)]
        sA = [bb([512], F32) for _ in range(2)]
        sB = [bb([512], F32) for _ in range(2)]
        sC = [bb([512], F32) for _ in range(4)]
        hB = bb([8, TB], BF16)
        X0 = bb.off
        xB = bb([8, TB], F32)
        Z0 = bb.off
        zB = bb([8, TB], BF16)
        S0 = bb.off
        sqB = bb([8, 512], F32)
        sqBb = sqB.rearrange("p a b -> p (a b)").bitcast(BF16)[:, 0:4096].rearrange("p (a b) -> p a b", a=8)
        ab = Bump(X0)
        QT = ab([3, TB], BF16)
        KT = [ab([TA], BF16) for _ in range(2)]
        Vg = [ab([69, 128], BF16) for _ in range(2)]
        ex2 = [ab([512], BF16) for _ in range(2)]
        assert ab.off <= Z0
        ab = Bump(Z0)
        oT = ab([4, TB], BF16)
        Em = ab([3, 512], BF16)
        ex = [ab([512], BF16) for _ in range(2)]
        ex += ex2
        PT = [ab([512], BF16) for _ in range(2)]
        QTs = ab([3, 4, 64], BF16)
        KTsamp = ab([4, 64], BF16)
        PT += [ab([512], BF16) for _ in range(2)]
        assert ab.off <= S0
        ab = Bump(X0)
        CB = [ab([9, 1024], BF16) for _ in range(2)]
        Qtm = ab([3, 512], BF16)
        Vnew = [ab([512], BF16) for _ in range(2)]
        sraw = ab([3, 4, 8], F32)
        pexp = ab([3, 4, 8], F32)
        pn1 = ab([2, 48], F32)
        pm = ab([6, 4, 8], BF16)
        pz = ab([2, 4, 4, 8], BF16)
        pD = ab([32], BF16)
        pns = ab([32], BF16)
        prod = [ab([512], F32) for _ in range(2)]
        rDs = ab([1], F32)
        osel = ab([512], BF16)
        assert ab.off <= Z0
        ab = Bump(S0)
        accN = ab([2048], F32)
        accD = ab([2048], F32)

        TLB = [(0, 0, 64), (1, 64, 512), (2, 576, 512), (3, 1088, 512), (4, 1600, 512)]
        TS = [TLB[0:3], TLB[3:5]]
        setB = RR([(0, 1, 2), (3, 4, 5)])
        setAtt = RR([(0, 1, 2), (3, 0, 1)])
        bank4 = RR([0, 1, 2, 3])
        bankN = RR([4, 5])
        bankD = RR([6, 7])
        statB = RR([6, 7])

        def xk(ti):
            return allk("xB", ti)

        def normB(gname, gidx):
            for (t, off, n) in TLB:
                plan.op("act", lambda e, off=off, n=n: e.activation(out=sqBb[:, :, 0:n], in_=xB[:, :, off:off + n],
                                                                     func=AF.Square), reads=xk(t), writes=["sqB"])
                b = statB()
                plan.op("pe", lambda e, b=b, n=n: mm_group(e, [(b, 0, n)], 8, lambda k: onesb16,
                                                             lambda k, o_, n_: sqBb[:, k, 0:n_]),
                        reads=["sqB", "onesb16"], writes=[("ps", b)])
                i = t % 2
                plan.op("act", lambda e, i=i, n=n, b=b: e.activation(out=sB[i][:, 0:n], in_=psum[:, b, 0:n],
                                                                      func=AF.Ln, bias=P("eps"), scale=1.0),
                        reads=[("ps", b), "prm"], writes=[("sB", i)])
                plan.op("act", lambda e, i=i, n=n: e.activation(out=sA[i][:, 0:n], in_=sB[i][:, 0:n], func=AF.Exp,
                                                                 scale=-0.5),
                        reads=[("sB", i)], writes=[("sA", i)])
                for k in range(8):
                    plan.op("dve", lambda e, k=k, i=i, off=off, n=n: e.scalar_tensor_tensor(
                        out=hB[:, k, off:off + n], in0=xB[:, k, off:off + n], scalar=P(gname, gidx * 8 + k),
                        in1=sA[i][:, 0:n], op0=ALU.mult, op1=ALU.mult),
                        reads=[("xB", k, t), ("sA", i), "prm"], writes=[("hB", k, t)])

        EsM = smc[:, 0:192].rearrange("p (j t h) -> p j t h", j=6, t=4)
        sel32 = smc[0:32, 192:704]
        for (t, off, n) in TLB:
            plan.dma("sp", xB[:, :, off:off + n], D["xs"][:, :, off:off + n], reads=["xs"], writes=xk(t))
        for lb in range(2):
            L = 2 + lb
            normB("nmg", L)
            if lb == 1:
                for (t, off, n) in TLB:
                    plan.dma("sp", D["xs"][:, :, off:off + n], xB[:, :, off:off + n], reads=xk(t), writes=["xs"])
            if K_DBG and lb == 0:
                plan.dma("sp", D["dbg_h"], hB, reads=[x for t in range(5) for x in allk("hB", t)])
            plan.barrier()
            for hp_ in range(4):
                plan.dma("sp", KTsamp[:, hp_, :], D["KTs"][hp_][:, 0:64], reads=["KTs"], writes=["KTsamp"])
            for hp in range(4):
                kv = hp % 2
                plan.dma("sp", KT[kv], D["KTs"][hp], reads=["KTs"], writes=[("KT", kv)])
                Vs = D["Vs"]
                c_lo = hp * 128
                plan.dma("sp", Vg[kv][:, 0:17, :],
                         Vs[OWN0 - 128:OWN0 + 2048, c_lo:c_lo + 128].rearrange("(b i) c -> i b c", i=128),
                         reads=["Vs"], writes=[("Vg", kv)])
                for r in range(4):
                    plan.dma("sp", Vg[kv][:, 17 + 5 * r:22 + 5 * r, :],
                             Vs[OWN0 - 512 + r:OWN0 + 2048:4, c_lo:c_lo + 128].rearrange("(b i) c -> i b c", i=128),
                             reads=["Vs"], writes=[("Vg", kv)])
                for r in range(16):
                    plan.dma("sp", Vg[kv][:, 37 + 2 * r:39 + 2 * r, :],
                             Vs[OWN0 - 2048 + r:OWN0 + 2048:16, c_lo:c_lo + 128].rearrange("(b i) c -> i b c", i=128),
                             reads=["Vs"], writes=[("Vg", kv)])
                plan.dma("pool", Em, D["emask"][:, hp:12:4, :], writes=["Em"])

                def ev_q(oi, ti, off, n, bank, lb=lb):
                    i = ti % 4
                    scb = sC[i].bitcast(BF16)
                    plan.op("act", lambda e: e.activation(out=scb[:, 0:n], in_=psum[:, bank, 0:n], func=AF.Square),
                            reads=[("ps", bank)], writes=[("sC", i)])
                    b2 = statB()
                    plan.op("pe", lambda e: e.matmul(psum[:, b2, 0:n], lhsT=oneshb, rhs=scb[:, 0:n], start=True,
                                                      stop=True), reads=[("sC", i), "oneshb"], writes=[("ps", b2)])
                    j = ti % 2
                    plan.op("act", lambda e: e.activation(out=sB[j][:, 0:n], in_=psum[:, b2, 0:n], func=AF.Ln,
                                                           bias=P("eps"), scale=1.0),
                            reads=[("ps", b2), "prm"], writes=[("sB", j)])
                    plan.op("act", lambda e: e.activation(out=sA[j][:, 0:n], in_=sB[j][:, 0:n], func=AF.Exp,
                                                           scale=-0.5),
                            reads=[("sB", j)], writes=[("sA", j)])
                    plan.op("dve", lambda e: e.scalar_tensor_tensor(
                        out=QT[:, oi, off:off + n], in0=psum[:, bank, 0:n], scalar=qg8[:, lb:lb + 1],
                        in1=sA[j][:, 0:n], op0=ALU.mult, op1=ALU.mult),
                        reads=[("ps", bank), ("sA", j), "qg8"], writes=["QT"])
                proj(D["w_q"][lb], 8, [[(g * 128, (g * 4 + hp) * 128, 128) for g in range(3)]], [3],
                     lambda k, off, n: hB[:, k, off:off + n], lambda ti: allk("hB", ti), TS, setAtt, ev_q, wbufB)
                if K_DBG and lb == 0:
                    plan.dma("sp", D["dbg_q"][hp], QT, reads=["QT"])
                plan.op("act", lambda e, hp=hp: e.activation(out=QTs[:, :, hp, :], in_=QT[:, :, 0:64], func=AF.Copy),
                        reads=["QT"], writes=["QTs"])
                for g, (w_, d) in enumerate(WINDOWS):
                    qbs = [(r, c) for r in range(d) for c in range(16 // d)]
                    vbase = [0, 17, 37][g]
                    nper = 16 // d + 1

                    def vblk(r, kbi, vbase=vbase, nper=nper):
                        return vbase + r * nper + (kbi + 1)
                    for q4 in range(4):
                        bN, bD = bankN(), bankD()
                        for pr in range(2):
                            pair = qbs[q4 * 4 + pr * 2:q4 * 4 + pr * 2 + 2]
                            bX, bY = bank4(), bank4()

                            def st_mm(e, pair=pair, bX=bX, bY=bY, g=g, d=d, kv=kv):
                                ins = None
                                for qi, (r, c) in enumerate(pair):
                                    q0 = 64 + r + d * 128 * c
                                    for pc in range(2):
                                        k0 = OWN0 + r + d * 128 * (c - 1 + pc)
                                        for h, bk in ((0, bX), (1, bY)):
                                            ins = e.matmul(psum[:, bk, (qi * 2 + pc) * 128:(qi * 2 + pc + 1) * 128],
                                                           lhsT=KT[kv][h * 64:(h + 1) * 64, k0:k0 + 127 * d + 1:d],
                                                           rhs=QT[h * 64:(h + 1) * 64, g, q0:q0 + 127 * d + 1:d],
                                                           start=True, stop=True, tile_position=(h * 64, 0))
                                return ins
                            plan.op("pe", st_mm, reads=[("KT", kv), "QT"], writes=[("ps", bX), ("ps", bY)])
                            halo = any(c == 0 for (r, c) in pair)
                            for h, bk in ((0, bX), (1, bY)):
                                hx = pr * 2 + h
                                plan.op("act", lambda e, hx=hx, bk=bk: e.activation(out=ex[hx], in_=psum[:, bk, :],
                                                                                     func=AF.Exp),
                                        reads=[("ps", bk)], writes=[("ex", hx)])
                                if halo:
                                    for qi, (r, c) in enumerate(pair):
                                        if c == 0:
                                            plan.op("act", lambda e, hx=hx, bk=bk, qi=qi: e.activation(
                                                out=ex[hx][:, qi * 256:qi * 256 + 128],
                                                in_=psum[:, bk, qi * 256:qi * 256 + 128], func=AF.Exp,
                                                bias=P("hbias"), scale=1.0),
                                                reads=[("ps", bk), "prm"], writes=[("ex", hx)])
                                plan.op("dve", lambda e, h=h, g=g, hx=hx: e.tensor_tensor(
                                    out=PT[hx].rearrange("p (a b) -> p a b", a=2),
                                    in0=ex[hx].rearrange("p (a b) -> p a b", a=2),
                                    in1=Em[:, g, h * 256:(h + 1) * 256].unsqueeze(1).broadcast_to([128, 2, 256]),
                                    op=ALU.mult), reads=[("ex", hx), "Em"], writes=[("PT", hx)])

                            def pv_mm(e, pair=pair, pr=pr, bN=bN, bD=bD, kv=kv, vblk=vblk):
                                ins = None
                                for qi, (r, c) in enumerate(pair):
                                    sl = (pr * 2 + qi) * 128
                                    for h in range(2):
                                        for pc in range(2):
                                            rhs = PT[pr * 2 + h][:, (qi * 2 + pc) * 128:(qi * 2 + pc + 1) * 128]
                                            e.matmul(psum[h * 64:(h + 1) * 64, bN, sl:sl + 128],
                                                     lhsT=Vg[kv][:, vblk(r, c - 1 + pc), h * 64:(h + 1) * 64],
                                                     rhs=rhs, start=(pc == 0), stop=(pc == 1))
                                            ins = e.matmul(psum[h * 64:(h + 1) * 64, bD, sl:sl + 128],
                                                           lhsT=ones64, rhs=rhs, start=(pc == 0), stop=(pc == 1))
                                return ins
                            plan.op("pe", pv_mm, reads=[("PT", pr * 2), ("PT", pr * 2 + 1), ("Vg", kv), "ones64"],
                                    writes=[("ps", bN), ("ps", bD)])
                        if g == 0:
                            cs_ = slice(q4 * 512, q4 * 512 + 512)
                            dstN, dstD = accN[:, cs_], accD[:, cs_]
                            srcN, srcD = psum[:, bN, :], psum[:, bD, :]
                        elif g == 1:
                            dstN, dstD = accN[:, q4:2048:4], accD[:, q4:2048:4]
                            srcN, srcD = psum[:, bN, :], psum[:, bD, :]
                        else:
                            dstN = accN.rearrange("p (i r) -> p r i", r=16)[:, q4 * 4:q4 * 4 + 4, :]
                            dstD = accD.rearrange("p (i r) -> p r i", r=16)[:, q4 * 4:q4 * 4 + 4, :]
                            srcN = psum[:, bN, :].rearrange("p (a b) -> p a b", a=4)
                            srcD = psum[:, bD, :].rearrange("p (a b) -> p a b", a=4)
                        if g == 0:
                            plan.op("act", lambda e, dstN=dstN, srcN=srcN: e.activation(out=dstN, in_=srcN, func=AF.Copy),
                                    reads=[("ps", bN)], writes=["accN"])
                            plan.op("dve", lambda e, dstD=dstD, srcD=srcD: e.tensor_copy(out=dstD, in_=srcD),
                                    reads=[("ps", bD)], writes=["accD"])
                        else:
                            plan.op("dve", lambda e, dstN=dstN, srcN=srcN: e.tensor_tensor(out=dstN, in0=srcN, in1=dstN,
                                                                                            op=ALU.add),
                                    reads=[("ps", bN), "accN"], writes=["accN"])
                            plan.op("dve", lambda e, dstD=dstD, srcD=srcD: e.tensor_tensor(out=dstD, in0=srcD, in1=dstD,
                                                                                            op=ALU.add),
                                    reads=[("ps", bD), "accD"], writes=["accD"])
                if K_DBG and lb == 0:
                    plan.dma("sp", D["dbg_n"][hp], accN, reads=["accN"])
                    plan.dma("sp", D["dbg_d"][hp], accD, reads=["accD"])
                plan.op("act", lambda e: e.activation(out=accD, in_=accD, func=AF.Ln), reads=["accD"], writes=["accD"])
                plan.op("act", lambda e: e.activation(out=accD, in_=accD, func=AF.Exp, scale=-1.0),
                        reads=["accD"], writes=["accD"])
                plan.op("dve", lambda e, hp=hp: e.tensor_tensor(out=oT[:, hp, 64:TB], in0=accN, in1=accD, op=ALU.mult),
                        reads=["accN", "accD"], writes=["oTp"])
            plan.barrier()
            plan.op("dve", lambda e: e.memset(pm, 0.0), writes=["pm"])
            plan.op("dve", lambda e: e.memset(pz, 0.0), writes=["pz"])
            for i_ in range(2):
                plan.op("dve", lambda e, i_=i_: e.memset(Vnew[i_], 0.0), writes=[("Vnew", i_)])
            for g in range(3):
                bq = bank4()
                plan.op("pe", lambda e, g=g, bq=bq: [e.transpose(
                    psum[0:64, bq, hp_ * 64:(hp_ + 1) * 64].bitcast(BF16), QTs[:, g, hp_, :], identb)
                    for hp_ in range(4)][-1], reads=["QTs", "identb"], writes=[("ps", bq)])
                plan.op("act", lambda e, g=g, bq=bq: e.activation(out=Qtm[0:64, g, :],
                                                                   in_=psum[0:64, bq, 0:256].bitcast(BF16), func=AF.Copy),
                        reads=[("ps", bq)], writes=["Qtm"])
            qsets = RR([(0, 1, 2), (3, 4, 5)])
            for sq_ in range(16):
                cb = sq_ % 2
                C = D["cache"][sq_]
                plan.dma("pool", CB[cb][:, 0, :], C[1920:2048, :], writes=[("CB", cb)])
                plan.dma("pool", CB[cb][:, 1:5, :], C[1536:2048, :].rearrange("(m t) f -> m t f", t=4),
                         writes=[("CB", cb)])
                plan.dma("pool", CB[cb][:, 5:9, :], C.rearrange("(m t) f -> m t f", t=16)[:, 0:4, :],
                         writes=[("CB", cb)])
                plan.dma("sp", Vnew[cb][0:4, :], D["Vs"][4 * sq_:4 * sq_ + 4, :], reads=["Vs"], writes=[("Vnew", cb)])
                c4 = slice(4 * sq_, 4 * sq_ + 4)
                def snew(e, c4=c4):
                    ins = None
                    for g in range(3):
                        for hp_ in range(4):
                            for par in range(2):
                                ins = e.matmul(psum[0:4, 6 + par, (g * 4 + hp_) * 4:(g * 4 + hp_) * 4 + 4],
                                               lhsT=KTsamp[par * 64:(par + 1) * 64, hp_, c4],
                                               rhs=QTs[par * 64:(par + 1) * 64, g, hp_, c4],
                                               start=True, stop=True, tile_position=(par * 64, 0))
                    return ins
                plan.op("pe", snew, reads=["KTsamp", "QTs"], writes=[("ps", 6), ("ps", 7)])
                for par in range(2):
                    plan.op("act", lambda e, par=par: e.activation(out=pn1[0:4, par, :], in_=psum[0:4, 6 + par, 0:48],
                                                                    func=AF.Exp), reads=[("ps", 6 + par)], writes=[("pn1", par)])
                    plan.op("dve", lambda e, par=par: e.tensor_tensor(
                        out=pm[0:4, 3:6, :, par:8:2],
                        in0=pn1[0:4, par, :].rearrange("p (g h t) -> p g t h", g=3, h=4),
                        in1=EsM[0:4, 3:6, :, par:8:2], op=ALU.mult),
                        reads=[("pn1", par), "smc"], writes=["pm"])
                for t in range(4):
                    qb_ = qsets()
                    plan.op("pe", lambda e, t=t, qb_=qb_, sq_=sq_: [e.matmul(
                        psum[:, qb_[g], :], lhsT=identb[0:64, 4 * sq_ + t:4 * sq_ + t + 1].broadcast_to([64, 128]),
                        rhs=Qtm[0:64, g, :], start=True, stop=True) for g in range(3)][-1],
                        reads=["Qtm", "identb"], writes=[("ps", b_) for b_ in qb_])
                    for g in range(3):
                        blk = 0 if g == 0 else (1 + t if g == 1 else 5 + t)
                        pi = (t * 3 + g) % 2
                        plan.op("dve", lambda e, g=g, blk=blk, pi=pi, qb_=qb_, cb=cb: e.tensor_tensor(
                            out=prod[pi], in0=CB[cb][:, blk, 0:512], in1=psum[:, qb_[g], :], op=ALU.mult),
                            reads=[("CB", cb), ("ps", qb_[g])], writes=[("prod", pi)])
                        plan.op("dve", lambda e, g=g, t=t, pi=pi: e.tensor_reduce(
                            out=sraw[:, g, t, :], in_=prod[pi].rearrange("p (h d) -> p h d", d=64), axis=AX.X,
                            op=ALU.add), reads=[("prod", pi)], writes=["sraw"])
                plan.op("act", lambda e: e.activation(out=pexp, in_=sraw, func=AF.Exp), reads=["sraw"], writes=["pexp"])
                plan.op("dve", lambda e: e.tensor_tensor(out=pm[:, 0:3, :, :], in0=pexp, in1=EsM[:, 0:3, :, :],
                                                          op=ALU.mult), reads=["pexp", "smc"], writes=["pm"])
                plan.op("dve", lambda e: e.tensor_copy(
                    out=pz.rearrange("p g a b h -> p g (a b) h")[:, :, 0:16:5, :], in_=pm[:, 1:3, :, :]),
                    reads=["pm"], writes=["pz"])
                plan.op("dve", lambda e: e.tensor_reduce(
                    out=pD, in_=pm.rearrange("p j t h -> p (t h) j"), axis=AX.X, op=ALU.add),
                    reads=["pm"], writes=["pD"])
                plan.op("dve", lambda e: e.tensor_reduce(
                    out=pns, in_=pm[:, 3:6, :, :].rearrange("p j t h -> p (t h) j"), axis=AX.X, op=ALU.add),
                    reads=["pm"], writes=["pns"])

                def pvs(e, cb=cb):
                    e.matmul(psum[0:32, 6, :], lhsT=pm[:, 0, :, :].rearrange("p t h -> p (t h)"),
                             rhs=CB[cb][:, 0, 512:1024], start=True, stop=False)
                    for g in (1, 2):
                        for t in range(4):
                            blk = 1 + t if g == 1 else 5 + t
                            e.matmul(psum[0:32, 6, :], lhsT=pz[:, g - 1, t, :, :].rearrange("p a h -> p (a h)"),
                                     rhs=CB[cb][:, blk, 512:1024], start=False, stop=False)
                    e.matmul(psum[0:32, 6, :], lhsT=pns, rhs=Vnew[cb], start=False, stop=True)
                    return e.matmul(psum[0:32, 7, 0:1], lhsT=pD, rhs=ones64[:, 0:1], start=True, stop=True)
                plan.op("pe", pvs, reads=["pm", "pz", "pns", "pD", ("CB", cb), ("Vnew", cb), "ones64"],
                        writes=[("ps", 6), ("ps", 7)])
                plan.op("dve", lambda e: e.reciprocal(out=rDs[0:32, :], in_=psum[0:32, 7, 0:1]),
                        reads=[("ps", 7)], writes=["rDs"])
                plan.op("dve", lambda e: e.scalar_tensor_tensor(out=osel[0:32, :], in0=psum[0:32, 6, :],
                                                                 scalar=rDs[0:32, :], in1=sel32, op0=ALU.mult,
                                                                 op1=ALU.mult),
                        reads=[("ps", 6), "rDs", "smc"], writes=["osel"])
                plan.op("pe", lambda e: [e.matmul(psum[:, 7, 8 + 4 * hp_:12 + 4 * hp_],
                                                  lhsT=osel[0:32, hp_ * 128:(hp_ + 1) * 128], rhs=tselb[0:32, :],
                                                  start=True, stop=True) for hp_ in range(4)][-1],
                        reads=["osel", "tselb"], writes=[("ps", 7)])
                plan.op("act", lambda e, c4=c4: e.activation(
                    out=oT[:, :, c4], in_=psum[:, 7, 8:24].rearrange("p (a b) -> p a b", a=4), func=AF.Copy),
                    reads=[("ps", 7)], writes=["oTs"])
            if K_DBG and lb == 0:
                plan.dma("sp", D["dbg_o"], oT, reads=["oTs", "oTp"])
                plan.dma("sp", D["dbg_sraw"], sraw.rearrange("p a b c -> p (a b c)"), reads=["sraw"])
                plan.dma("sp", D["dbg_pm"], pm.rearrange("p a b c -> p (a b c)"), reads=["pm"])
                plan.dma("sp", D["dbg_osel"], osel, reads=["osel"])
                plan.dma("sp", D["dbg_qtm"], Qtm.rearrange("p a b -> p (a b)"), reads=["Qtm"])
            plan.barrier()
            if K_STOP <= 11:
                break
            for (t, off, n) in TLB:
                plan.dma("sp", xB[:, :, off:off + n], D["xs"][:, :, off:off + n], reads=["xs"], writes=xk(t))

            def ev_add(oi, ti, off, n, bank):
                plan.op("dve", lambda e: e.tensor_tensor(out=xB[:, oi, off:off + n], in0=psum[:, bank, 0:n],
                                                          in1=xB[:, oi, off:off + n], op=ALU.add),
                        reads=[("ps", bank), ("xB", oi, ti)], writes=[("xB", oi, ti)])
            proj(D["w_o"][lb], 4, [[(0, 0, 512)], [(0, 512, 512)]], [4, 4],
                 lambda k, off, n: oT[:, k, off:off + n], lambda ti: ["oTs", "oTp"], TS, setB, ev_add, wbufB)
            normB("nml", L)
            for hg in range(4):
                def ev_upB(oi, ti, off, n, bank):
                    plan.op("act", lambda e: e.activation(out=zB[:, oi, off:off + n], in_=psum[:, bank, 0:n],
                                                           func=AF.Relu), reads=[("ps", bank)], writes=[("zB", oi, ti)])
                    plan.op("dve", lambda e: e.tensor_tensor(out=zB[:, oi, off:off + n], in0=zB[:, oi, off:off + n],
                                                              in1=zB[:, oi, off:off + n], op=ALU.mult),
                            reads=[("zB", oi, ti)], writes=[("zB", oi, ti)])
                proj(D["w_up"][L][:, hg * 1024:(hg + 1) * 1024], 8, [[(0, 0, 512)], [(0, 512, 512)]], [4, 4],
                     lambda k, off, n: hB[:, k, off:off + n], lambda ti: allk("hB", ti), TS, setB, ev_upB, wbufB)
                proj(D["w_dn"][L][hg * 1024:(hg + 1) * 1024, :], 8, [[(0, 0, 512)], [(0, 512, 512)]], [4, 4],
                     lambda k, off, n: zB[:, k, off:off + n], lambda ti: allk("zB", ti), TS, setB, ev_add, wbufB)
        for (t, off, n) in TLB:
            plan.dma("sp", D["yT"][:, :, off:off + n], xB[:, :, off:off + n], reads=xk(t))

    plan.barrier()
    replay(nc, plan)
    return nc


_NC_CACHE = {}


def _feat_major(v):
    v = np.asarray(v, np.float32)
    lead = v.shape[:-1]
    return np.ascontiguousarray(np.moveaxis(v.reshape(lead + (8, 128)), -1, 0))


def _emask():
    n = 24
    slopes = (2.0 ** (-8.0 * np.arange(1, n + 1, dtype=np.float64) / n)).reshape(3, 8)
    k = np.arange(128)[:, None]
    q = np.arange(128)[None, :]
    out = np.zeros((128, 3, 4, 2, 2, 128), np.float64)
    for g, (w, d) in enumerate(WINDOWS):
        for h in range(8):
            sl = slopes[g, h] * d
            prev = np.where(q <= k, np.exp(-sl * (q - k + 128)), 0.0)
            cur = np.where(q >= k, np.exp(-sl * (q - k)), 0.0)
            out[:, g, h // 2, h % 2, 0, :] = prev
            out[:, g, h // 2, h % 2, 1, :] = cur
    return out.reshape(128, 12, 512).astype(np.float32)


def _smc():
    slopes = (2.0 ** (-8.0 * np.arange(1, 25, dtype=np.float64) / 24)).reshape(3, 8)
    m = np.arange(128, dtype=np.float64)[:, None, None]
    t = np.arange(4, dtype=np.float64)[None, :, None]
    E = np.zeros((128, 6, 4, 8))
    d0 = 128 + t - m
    E[:, 0] = np.where(m >= t, np.exp(-slopes[0][None, None, :] * d0), 0.0)
    E[:, 1] = np.exp(-slopes[1][None, None, :] * 4 * (128 - m)) * np.ones((1, 4, 1))
    E[:, 2] = np.exp(-slopes[2][None, None, :] * 16 * (128 - m)) * np.ones((1, 4, 1))
    E[:, 3] = np.where((m <= t) & (m < 4), np.exp(-slopes[0][None, None, :] * (t - m)), 0.0)
    E[:, 4] = np.where(m == t, 1.0, 0.0) * np.ones((1, 1, 8))
    E[:, 5] = E[:, 4]
    out = np.zeros((128, 708), np.float32)
    out[:, 0:192] = E.reshape(128, 192)
    r = np.arange(32)
    f = np.arange(512)
    out[0:32, 192:704] = ((f[None, :] // 64) == (r[:, None] % 8)).astype(np.float32)
    out[0:32, 704:708] = ((r[:, None] // 8) == np.arange(4)[None, :]).astype(np.float32)
    return out


def kernel(x_prompt, x_sample, state_conv, cache_kv, norm_mix_g, norm_mlp_g, conv_w_pw1, conv_b_pw1,
           conv_w_dw, conv_b_dw, conv_ln_g, conv_ln_b, conv_w_pw2, conv_b_pw2, kv_norm_g, w_kv, k_norm_g,
           attn_w_q, q_norm_g, attn_w_o, mlp_w_up, mlp_w_down):
    f = lambda v: np.ascontiguousarray(np.asarray(v, np.float32))
    if "nc" not in _NC_CACHE:
        t0 = time.time()
        _NC_CACHE["nc"] = build_program()
        print("build time", time.time() - t0)
    nc = _NC_CACHE["nc"]
    x_prompt, x_sample, state_conv, cache_kv = f(x_prompt), f(x_sample), f(state_conv), f(cache_kv)
    prm = np.zeros((128, NPRM), np.float32)

    def put(name, arr):
        arr = np.asarray(arr, np.float32).reshape(128, -1)
        prm[:, PRM[name]:PRM[name] + arr.shape[1]] = arr
    put("nmg", _feat_major(norm_mix_g))
    put("nml", _feat_major(norm_mlp_g))
    put("bpw1", np.moveaxis(np.asarray(conv_b_pw1, np.float32).reshape(2, 16, 128), -1, 0))
    put("wdw", np.moveaxis(np.asarray(conv_w_dw, np.float32).reshape(2, 31, 8, 128), (3, 0, 2, 1), (0, 1, 2, 3)))
    put("bdw", _feat_major(conv_b_dw))
    put("lng", _feat_major(conv_ln_g))
    put("lnb", _feat_major(conv_ln_b))
    put("bpw2", _feat_major(conv_b_pw2))
    put("kvg", _feat_major(kv_norm_g))
    put("kng", np.tile(np.asarray(k_norm_g, np.float32), 2)[:, None])
    put("qng", np.tile(np.asarray(q_norm_g, np.float32), (1, 2)).T)
    prm[:, PRM["eps"]] = EPS
    ident = np.eye(128, dtype=np.float32)
    onesb = np.kron(np.eye(2, dtype=np.float32), np.full((64, 64), 1.0 / 64, np.float32))
    emask = _emask()
    smc = _smc()
    xpT = [np.ascontiguousarray(x_prompt[b].T).reshape(8, 128, 8192).transpose(1, 0, 2) for b in range(2)]
    shared = dict(ident=ident, onesb=onesb, emask=emask, smc=smc, w_pw1=f(conv_w_pw1), w_pw2=f(conv_w_pw2), w_kv=f(w_kv),
                  w_q=f(attn_w_q), w_o=f(attn_w_o), w_up=f(mlp_w_up), w_dn=f(mlp_w_down))
    in_maps = []
    for c in range(NCORES):
        b, j = c // 4, c % 4
        S = 2048 * j
        xin = np.zeros((128, 8, TA), np.float32)
        xs_ = x_sample[16 * c:16 * c + 16].reshape(64, 1024)
        xin[:, :, 0:64] = xs_.T.reshape(8, 128, 64).transpose(1, 0, 2)
        lo = S - 2112
        src_lo = max(lo, 0)
        xin[:, :, 64 + (src_lo - lo):TA] = xpT[b][:, :, src_lo:S + 2048]
        p = prm.copy()
        p[:, PRM["valid"]] = 1.0 if j > 0 else 0.0
        p[:, PRM["hbias"]] = 0.0 if j > 0 else -30000.0
        sc = state_conv[:, 16 * c:16 * c + 16]
        scT = np.ascontiguousarray(sc.reshape(2, 16, 30, 8, 128).transpose(4, 0, 3, 1, 2))
        m = dict(shared)
        m.update(xin=xin, prm=p, scT=scT, scN=np.ascontiguousarray(sc),
                 cache=np.ascontiguousarray(cache_kv[16 * c:16 * c + 16].reshape(16, 2048, 1024)))
        in_maps.append(m)
    if K_CORE is not None:
        t0 = time.time()
        res = run_bass_kernel_spmd(nc, [in_maps[int(K_CORE)]], core_ids=[0])
        print("run time", time.time() - t0)
        R = [res.results[0]] * NCORES
    else:
        res = run_bass_kernel_spmd(nc, in_maps, core_ids=list(range(NCORES)))
        R = res.results
    y_prompt = np.zeros((2, 8192, 1024), np.float32)
    y_sample = np.zeros((128, 4, 1024), np.float32)
    conv_prompt = np.zeros((2, 2, 30, 1024), np.float32)
    conv_sample = np.zeros((2, 128, 30, 1024), np.float32)
    kv_prompt = np.zeros((2, 2048, 2, 8, 64), np.float32)
    kv_sample = np.zeros((128, 4, 2, 8, 64), np.float32)
    for c in range(NCORES):
        b, j = c // 4, c % 4
        r = R[c]
        yT = np.asarray(r["yT"]).transpose(1, 0, 2).reshape(1024, TB)
        y_sample[16 * c:16 * c + 16] = yT[:, 0:64].T.reshape(16, 4, 1024)
        y_prompt[b, 2048 * j:2048 * j + 2048] = yT[:, 64:].T
        conv_sample[:, 16 * c:16 * c + 16] = np.asarray(r["cs"])
        kT = np.asarray(r["kT"]).transpose(1, 0, 2).reshape(512, TA)
        vN = np.asarray(r["vN"])
        kv_sample[16 * c:16 * c + 16, :, 0] = kT[:, 0:64].T.reshape(16, 4, 8, 64)
        kv_sample[16 * c:16 * c + 16, :, 1] = vN[0:64].reshape(16, 4, 8, 64)
        if j == 3:
            conv_prompt[:, b] = np.asarray(r["cp"])
            kv_prompt[b, :, 0] = kT[:, OWN0:].T.reshape(2048, 8, 64)
            kv_prompt[b, :, 1] = vN[OWN0:].reshape(2048, 8, 64)
    return (y_prompt, y_sample, conv_prompt, conv_sample, kv_prompt, kv_sample)
```

```python
import os
import time
import numpy as np
import concourse.bass as bass
import concourse.mybir as mybir
from concourse.bass_utils import run_bass_kernel_spmd
from contextlib import ExitStack

F32, BF16 = mybir.dt.float32, mybir.dt.bfloat16
AF = mybir.ActivationFunctionType
ALU = mybir.AluOpType
AX = mybir.AxisListType

NCORES = 8
TA = 4224
NST = 1408
TB = 2112
OWN0 = 2176
EPS = 1e-6
ARENA = 51800
WINDOWS = ((128, 1), (512, 4), (2048, 16))
K_STOP = int(os.environ.get('K_STOP', '99'))
K_NST = int(os.environ.get('K_NST', '3'))
K_CORE = os.environ.get('K_CORE')
K_DBG = int(os.environ.get('K_DBG', '0'))

PRM = {}
_o = 0
for _n, _w in (("nmg", 32), ("nml", 32), ("bpw1", 32), ("wdw", 2 * 8 * 31), ("bdw", 16), ("lng", 16),
               ("lnb", 16), ("bpw2", 16), ("kvg", 8), ("kng", 1), ("qng", 2), ("valid", 1), ("hbias", 1),
               ("eps", 1), ("zero", 1)):
    PRM[_n] = _o
    _o += _w
NPRM = _o


class Plan:
    ENG = ("pe", "act", "dve", "pool", "sp")
    NDS = 8

    def __init__(self):
        self.items = {e: [] for e in self.ENG}
        self.cnt = {e: 0 for e in self.ENG}
        self.known = {e: {} for e in self.ENG}
        self.bw = {}
        self.br = {}
        self.dma_n = {}
        self.dma_rr = {e: 0 for e in self.ENG}
        self.sems = set("s_" + e for e in self.ENG)

    def _need(self, eng, tok):
        if tok is None:
            return
        sem, val = tok
        if sem == "s_pe" and eng == "pe":
            return
        if self.known[eng].get(sem, 0) >= val:
            return
        self.known[eng][sem] = val
        self.items[eng].append(("w", sem, val))

    def _deps(self, eng, reads, writes):
        for b in reads:
            self._need(eng, self.bw.get(b))
            if isinstance(b, tuple) and b[0] == "ps":
                for s, v in self.br.get(b, {}).items():
                    if s != "s_" + eng:
                        self._need(eng, (s, v))
        for b in writes:
            self._need(eng, self.bw.get(b))
            for s, v in self.br.get(b, {}).items():
                self._need(eng, (s, v))

    def _mark(self, tok, reads, writes):
        for b in reads:
            d = self.br.setdefault(b, {})
            if d.get(tok[0], 0) < tok[1]:
                d[tok[0]] = tok[1]
        for b in writes:
            self.bw[b] = tok
            self.br[b] = {}

    def op(self, eng, fn, reads=(), writes=()):
        self._deps(eng, reads, writes)
        self.cnt[eng] += 1
        tok = ("s_" + eng, self.cnt[eng])
        self.items[eng].append(("x", fn))
        self._mark(tok, reads, writes)
        return tok

    def dma(self, q, out, in_, reads=(), writes=()):
        i = self.dma_rr[q] % self.NDS
        self.dma_rr[q] += 1
        sem = "d_%s%d" % (q, i)
        self.sems.add(sem)
        n = self.dma_n.get(sem, 0)
        if n:
            self._need(q, (sem, 16 * n))
        self._deps(q, reads, writes)
        self.dma_n[sem] = n + 1
        tok = (sem, 16 * (n + 1))
        self.items[q].append(("d", out, in_, sem))
        self._mark(tok, reads, writes)
        return tok

    def barrier(self):
        for e in self.ENG:
            for o in self.ENG:
                if self.cnt[o]:
                    self._need(e, ("s_" + o, self.cnt[o]))
            for sem, n in self.dma_n.items():
                self._need(e, (sem, 16 * n))


def replay(nc, plan):
    with ExitStack() as es:
        es.enter_context(nc.allow_low_precision("bf16 matmul operands, fp32 accumulation"))
        sems = {s: es.enter_context(nc.semaphore(s)) for s in sorted(plan.sems)}
        block = es.enter_context(nc.Block())

        def run(e, name):
            own = sems["s_" + name]
            for it in plan.items[name]:
                if it[0] == "w":
                    e.wait_ge(sems[it[1]], it[2])
                elif it[0] == "x":
                    it[1](e).then_inc(own, 1)
                else:
                    e.dma_start(out=it[1], in_=it[2]).then_inc(sems[it[3]], 16)

        @block.tensor
        def _(e):
            run(e, "pe")

        @block.scalar
        def _(e):
            run(e, "act")

        @block.vector
        def _(e):
            run(e, "dve")

        @block.gpsimd
        def _(e):
            run(e, "pool")

        @block.sync
        def _(e):
            run(e, "sp")


def build_program():
    nc = bass.Bass("TRN2", target_bir_lowering=False)
    plan = Plan()
    D = {}

    def din(name, shape, dt=F32):
        D[name] = nc.dram_tensor(name, list(shape), dt, kind="ExternalInput").ap()

    def dout(name, shape, dt=F32):
        D[name] = nc.dram_tensor(name, list(shape), dt, kind="ExternalOutput").ap()

    def dscr(name, shape, dt=F32):
        D[name] = nc.dram_tensor(name, list(shape), dt, kind="Internal").ap()

    din("xin", [128, 8, TA])
    din("prm", [128, NPRM])
    din("ident", [128, 128])
    din("onesb", [128, 128])
    din("w_pw1", [2, 1024, 2048])
    din("w_pw2", [2, 1024, 1024])
    din("w_kv", [1024, 1024])
    din("w_q", [2, 1024, 1536])
    din("w_o", [2, 512, 1024])
    din("w_up", [4, 1024, 4096])
    din("w_dn", [4, 4096, 1024])
    din("scT", [128, 2, 8, 16, 30])
    din("scN", [2, 16, 30, 1024])
    din("cache", [16, 2048, 1024])
    din("emask", [128, 12, 512])
    din("smc", [128, 708])
    dout("yT", [128, 8, TB])
    dout("kT", [128, 4, TA])
    dout("vN", [TA, 512])
    dout("cs", [2, 16, 30, 1024])
    dout("cp", [2, 30, 1024])
    if K_DBG:
        dout("dbg_h", [128, 8, TB], BF16)
        dout("dbg_q", [4, 128, 3, TB], BF16)
        dout("dbg_o", [128, 4, TB], BF16)
        dout("dbg_n", [4, 128, 2048])
        dout("dbg_d", [4, 128, 2048])
        dout("dbg_sraw", [128, 96])
        dout("dbg_pm", [128, 192], BF16)
        dout("dbg_osel", [128, 512], BF16)
        dout("dbg_qtm", [128, 1536], BF16)
    dscr("xs", [128, 8, TB])
    dscr("KTs", [4, 128, TA], BF16)
    dscr("Vs", [TA, 512], BF16)

    arena = nc.alloc_sbuf_tensor("arena", [128, ARENA], F32)
    psum = nc.alloc_psum_tensor("psum", [128, 8, 512], F32)

    class Bump:
        def __init__(self, start=0):
            self.off = start

        def __call__(self, shape, dt):
            n = int(np.prod(shape))
            nb = n * (4 if dt == F32 else 2)
            nb = (nb + 31) // 32 * 32
            assert self.off + nb <= ARENA * 4, ("SBUF overflow", self.off + nb)
            a = arena[:, self.off // 4:(self.off + nb) // 4]
            if dt != F32:
                a = a.bitcast(dt)
            a = a[:, 0:n]
            if len(shape) == 2:
                a = a.rearrange("p (a b) -> p a b", a=shape[0])
            elif len(shape) == 3:
                a = a.rearrange("p (a b c) -> p a b c", a=shape[0], b=shape[1])
            elif len(shape) == 4:
                a = a.rearrange("p (a b c d) -> p a b c d", a=shape[0], b=shape[1], c=shape[2])
            self.off += nb
            return a

    pb = Bump(0)
    prm = pb([NPRM], F32)
    identf = pb([128], F32)
    identb = pb([128], BF16)
    onesf = pb([128], F32)
    onesb16 = pb([128], BF16)
    oneshf = pb([128], F32)
    oneshb = pb([128], BF16)
    smc = pb([708], F32)
    tselb = pb([4], BF16)
    qg8 = pb([2], F32)
    ones64 = pb([64], BF16)
    PERSIST_END = pb.off

    def P(name, i=0, w=1):
        o = PRM[name] + i
        return prm[:, o:o + w]

    plan.dma("sp", prm, D["prm"], writes=["prm"])
    plan.dma("sp", identf, D["ident"], writes=["identf"])
    plan.dma("sp", oneshf, D["onesb"], writes=["oneshf"])
    plan.op("dve", lambda e: e.tensor_copy(out=identb, in_=identf), reads=["identf"], writes=["identb"])
    plan.op("dve", lambda e: e.tensor_copy(out=oneshb, in_=oneshf), reads=["oneshf"], writes=["oneshb"])
    plan.op("dve", lambda e: e.memset(onesf, 1.0 / 1024), writes=["onesf"])
    plan.op("dve", lambda e: e.memset(onesb16, 1.0 / 1024), writes=["onesb16"])
    plan.op("dve", lambda e: e.memset(ones64, 1.0), writes=["ones64"])
    plan.dma("sp", smc, D["smc"], writes=["smc"])
    plan.op("dve", lambda e: e.tensor_copy(out=tselb, in_=smc[:, 704:708]), reads=["smc"], writes=["tselb"])
    plan.op("dve", lambda e: e.tensor_scalar(out=qg8, in0=P("qng", 0, 2), scalar1=0.125, scalar2=None, op0=ALU.mult),
            reads=["prm"], writes=["qg8"])

    class RR:
        def __init__(self, items):
            self.items = items
            self.i = 0

        def __call__(self):
            v = self.items[self.i % len(self.items)]
            self.i += 1
            return v

    def mm_group(e, bank_tiles, kc, lhs_fn, rhs_fn):
        ins = None
        for k in range(kc):
            for (b, off, n) in bank_tiles:
                ins = e.matmul(psum[:, b, 0:n], lhsT=lhs_fn(k), rhs=rhs_fn(k, off, n),
                               start=(k == 0), stop=(k == kc - 1))
        return ins

    wslot = RR([0, 1, 2])

    def proj(Wap, kc, runs_per_group, nchunks_per_group, rhs_fn, in_keys_fn, tilesets, banksets, evac_fn, wbuf):
        Wr = Wap.rearrange("(k p) o -> p k o", p=128)
        oi = 0
        pending = []
        for g, runs in enumerate(runs_per_group):
            s = wslot()
            for (dc, sc, w) in runs:
                plan.dma("pool", wbuf[s][:, 0:kc, dc:dc + w], Wr[:, :, sc:sc + w], writes=[("wb", s)])
            for lo in range(nchunks_per_group[g]):
                for ts in tilesets:
                    banks = banksets()
                    bt = [(banks[i], off, n) for i, (ti, off, n) in enumerate(ts)]
                    rk = [("wb", s)]
                    for (ti, off, n) in ts:
                        rk += in_keys_fn(ti)
                    plan.op("pe", (lambda e, bt=bt, s=s, lo=lo: mm_group(
                        e, bt, kc, lambda k: wbuf[s][:, k, lo * 128:(lo + 1) * 128], rhs_fn)),
                        reads=rk, writes=[("ps", b) for (b, _, _) in bt])
                    for fn in pending:
                        fn()
                    pending = []
                    for i, (ti, off, n) in enumerate(ts):
                        r_ = evac_fn(oi, ti, off, n, banks[i])
                        if r_ is not None:
                            pending.append(r_)
                oi += 1
        for fn in pending:
            fn()

    def allk(name, ti, nk=8):
        return [(name, k, ti) for k in range(nk)]

    R2B = [0, 512, 1024, 1408, 1440]

    def r2k(k, lo, hi):
        return [("R2", k, i) for i in range(4) if lo < R2B[i + 1] and hi > R2B[i]]

    def r2all(lo, hi):
        return [x for k in range(8) for x in r2k(k, lo, hi)]

    a = Bump(PERSIST_END)
    xT = a([8, NST], F32)
    R1 = a([8, NST], BF16)
    R2 = a([8, NST + 32], BF16)
    sq = a([8, 512], F32)
    sqb = sq.rearrange("p a b -> p (a b)").bitcast(BF16)[:, 0:4096].rearrange("p (a b) -> p a b", a=8)
    sig = [a([512], F32) for _ in range(3)]
    stA = [a([512], F32) for _ in range(2)]
    stA.append(stA[0])
    stB = [a([512], F32) for _ in range(2)]
    stB.append(stB[0])
    stC = [a([512], F32) for _ in range(2)]
    stC.append(stC[0])
    stD = [a([512], F32) for _ in range(2)]
    stD.append(stD[0])
    wbuf = [a([8, 512], BF16) for _ in range(3)]
    dg = [a([31, 128], BF16) for _ in range(2)]
    usamp = a([8, 16, 34], BF16)
    usampf = [a([8, 64], F32) for _ in range(2)]
    utailf = [a([8, 30], F32) for _ in range(2)]
    tailb = [a([8, 30], BF16) for _ in range(2)]
    kf = [a([512], F32) for _ in range(2)]
    kf.append(kf[0])
    kb = [a([512], BF16) for _ in range(2)]
    kb.append(kb[0])
    vf, vb = kf, kb
    tstage = sq.rearrange("p a b -> p (a b)")[:, 0:1024]
    hT = R1
    cT = R1
    uT = R2
    caT = R2[:, :, 0:NST]
    zT = R2[:, :, 0:NST]

    TL = [(0, 0, 512), (1, 512, 512), (2, 1024, 384)]
    set3 = RR([(0, 1, 2), (3, 4, 5)])
    bank6 = RR([0, 1, 2, 3, 4, 5])
    bankS = RR([6, 7])

    def rsqrt_chain(t, n, bank, tag):
        plan.op("act", lambda e: e.activation(out=stB[t][:, 0:n], in_=psum[:, bank, 0:n], func=AF.Ln,
                                               bias=P("eps"), scale=1.0),
                reads=[("ps", bank), "prm"], writes=[("stB", t % 2)])
        plan.op("act", lambda e: e.activation(out=stA[t][:, 0:n], in_=stB[t][:, 0:n], func=AF.Exp, scale=-0.5),
                reads=[("stB", t % 2)], writes=[("stA", t % 2)])

    def norm_stage(gname, gidx, dst, dst_name):
        for (t, off, n) in TL:
            plan.op("act", lambda e, off=off, n=n: e.activation(out=sqb[:, :, 0:n], in_=xT[:, :, off:off + n],
                                                                 func=AF.Square),
                    reads=allk("xT", t), writes=["sq"])
            b = bankS()
            plan.op("pe", lambda e, b=b, n=n: mm_group(e, [(b, 0, n)], 8, lambda k: onesb16,
                                                         lambda k, o_, n_: sqb[:, k, 0:n_]),
                    reads=["sq", "onesb16"], writes=[("ps", b)])
            rsqrt_chain(t, n, b, "n")
            for k in range(8):
                plan.op("dve", lambda e, k=k, t=t, off=off, n=n: e.scalar_tensor_tensor(
                    out=dst[:, k, off:off + n], in0=xT[:, k, off:off + n], scalar=P(gname, gidx * 8 + k),
                    in1=stA[t][:, 0:n], op0=ALU.mult, op1=ALU.mult),
                    reads=[("xT", k, t), ("stA", t % 2), "prm"], writes=[(dst_name, k, t)])

    for st in range(K_NST):
        c0 = st * NST
        for (t, off, n) in TL:
            plan.dma("sp", xT[:, :, off:off + n], D["xin"][:, :, c0 + off:c0 + off + n], writes=allk("xT", t))
        for l in range(2):
            norm_stage("nmg", l, hT, "R1")
            if K_STOP <= 1:
                break
            if st == 0:
                plan.op("dve", lambda e: e.memset(uT[:, :, 0:30], 0.0), writes=r2all(0, 30))
                for hh in range(2):
                    plan.dma("pool", usamp[:, 4 * hh:4 * hh + 4, :, 0:30], D["scT"][:, l, 4 * hh:4 * hh + 4],
                             writes=["usamp"])
            else:
                plan.op("dve", lambda e, l=l: e.tensor_copy(out=uT[:, :, 0:30], in_=tailb[l]),
                        reads=[("tailb", l)], writes=r2all(0, 30))
            def ev_pw1(oi, ti, off, n, bank, l=l, st=st):
                c, isg = oi // 2, oi % 2
                if not isg:
                    ev_pw1.abank[ti] = bank
                    return
                ab = ev_pw1.abank[ti]
                plan.op("act", lambda e: e.activation(out=sig[ti][:, 0:n], in_=psum[:, bank, 0:n], func=AF.Sigmoid,
                                                       bias=P("bpw1", l * 16 + 8 + c), scale=1.0),
                        reads=[("ps", bank), "prm"], writes=[("sig", ti)])
                plan.op("dve", lambda e: e.scalar_tensor_tensor(
                    out=uT[:, c, 30 + off:30 + off + n], in0=psum[:, ab, 0:n], scalar=P("bpw1", l * 16 + c),
                    in1=sig[ti][:, 0:n], op0=ALU.add, op1=ALU.mult),
                    reads=[("ps", ab), ("sig", ti), "prm"], writes=r2k(c, 30 + off, 30 + off + n))
                if st == 0 and ti == 0:
                    plan.op("dve", lambda e: e.scalar_tensor_tensor(
                        out=usampf[l][:, c, :], in0=psum[:, ab, 0:64], scalar=P("bpw1", l * 16 + c),
                        in1=sig[ti][:, 0:64], op0=ALU.add, op1=ALU.mult),
                        reads=[("ps", ab), ("sig", ti), "prm"], writes=[("usampf", l, c)])
                    plan.op("dve", lambda e: e.tensor_copy(
                        out=usamp[:, c, :, 30:34], in_=usampf[l][:, c, :].rearrange("p (s t) -> p s t", t=4)),
                        reads=[("usampf", l, c)], writes=["usamp"])
                if st == 2 and ti == 2:
                    plan.op("dve", lambda e: e.scalar_tensor_tensor(
                        out=utailf[l][:, c, :], in0=psum[:, ab, n - 30:n], scalar=P("bpw1", l * 16 + c),
                        in1=sig[ti][:, n - 30:n], op0=ALU.add, op1=ALU.mult),
                        reads=[("ps", ab), ("sig", ti), "prm"], writes=[("utailf", l, c)])
            ev_pw1.abank = {}
            runs = []
            for c in range(0, 8, 2):
                runs.append([(0, c * 128, 128), (128, 1024 + c * 128, 128),
                             (256, (c + 1) * 128, 128), (384, 1024 + (c + 1) * 128, 128)])
            proj(D["w_pw1"][l], 8, runs, [4] * 4, lambda k, off, n: hT[:, k, off:off + n],
                 lambda ti: allk("R1", ti), [TL], set3, ev_pw1, wbuf)
            if st == 1:
                plan.op("dve", lambda e: e.tensor_scalar(out=uT[:, :, 30 + 738:30 + 768], in0=uT[:, :, 30 + 738:30 + 768],
                                                          scalar1=P("valid"), scalar2=None, op0=ALU.mult),
                        reads=r2all(768, 798) + ["prm"], writes=r2all(768, 798))
            if K_STOP <= 2:
                break
            for c in range(8):
                ds = c % 2
                plan.op("dve", lambda e, ds=ds, c=c, l=l: e.tensor_tensor(
                    out=dg[ds], in0=identb.unsqueeze(1).broadcast_to([128, 31, 128]),
                    in1=P("wdw", (l * 8 + c) * 31, 31).unsqueeze(2).broadcast_to([128, 31, 128]), op=ALU.mult),
                    reads=["identb", "prm"], writes=[("dg", ds)])
                for (t, off, n) in TL:
                    b = bank6()

                    def conv_mm(e, ds=ds, c=c, off=off, n=n, b=b, st=st, t=t):
                        ins = None
                        for j in range(31):
                            ins = e.matmul(psum[:, b, 0:n], lhsT=dg[ds][:, j, :], rhs=uT[:, c, off + j:off + j + n],
                                           start=(j == 0), stop=(j == 30))
                        if st == 0 and t == 0:
                            for j in range(31):
                                ins = e.matmul(psum[:, b, 0:64].rearrange("p (s t) -> p s t", t=4), lhsT=dg[ds][:, j, :],
                                               rhs=usamp[:, c, :, j:j + 4], start=(j == 0), stop=(j == 30))
                        return ins
                    rk = [("dg", ds)] + r2k(c, off, off + n + 30)
                    if st == 0 and t == 0:
                        rk.append("usamp")
                    plan.op("pe", conv_mm, reads=rk, writes=[("ps", b)])
                    plan.op("act", lambda e, c=c, off=off, n=n, b=b, l=l: e.activation(
                        out=cT[:, c, off:off + n], in_=psum[:, b, 0:n], func=AF.Identity, bias=P("bdw", l * 8 + c),
                        scale=1.0), reads=[("ps", b), "prm"], writes=[("R1", c, t)])
            plan.op("dve", lambda e, l=l: e.tensor_copy(out=tailb[l], in_=uT[:, :, NST:NST + 30]),
                    reads=r2all(NST, NST + 30), writes=[("tailb", l)])
            if K_STOP <= 3:
                break
            for (t, off, n) in TL:
                plan.op("act", lambda e, off=off, n=n: e.activation(out=sqb[:, :, 0:n], in_=cT[:, :, off:off + n],
                                                                     func=AF.Square),
                        reads=allk("R1", t), writes=["sq"])
                b1, b2 = bankS(), bankS()
                plan.op("pe", lambda e, b1=b1, off=off, n=n: mm_group(
                    e, [(b1, off, n)], 8, lambda k: onesb16, lambda k, o_, n_: cT[:, k, o_:o_ + n_]),
                    reads=allk("R1", t) + ["onesb16"], writes=[("ps", b1)])
                plan.op("pe", lambda e, b2=b2, n=n: mm_group(
                    e, [(b2, 0, n)], 8, lambda k: onesb16, lambda k, o_, n_: sqb[:, k, 0:n_]),
                    reads=["sq", "onesb16"], writes=[("ps", b2)])
                plan.op("act", lambda e, t=t, n=n, b1=b1: e.activation(out=stC[t][:, 0:n], in_=psum[:, b1, 0:n],
                                                                        func=AF.Copy),
                        reads=[("ps", b1)], writes=[("stC", t % 2)])
                plan.op("dve", lambda e, t=t, n=n: e.tensor_tensor(out=stD[t][:, 0:n], in0=stC[t][:, 0:n],
                                                                    in1=stC[t][:, 0:n], op=ALU.mult),
                        reads=[("stC", t % 2)], writes=[("stD", t % 2)])
                plan.op("dve", lambda e, t=t, n=n, b2=b2: e.tensor_tensor(out=stD[t][:, 0:n], in0=psum[:, b2, 0:n],
                                                                           in1=stD[t][:, 0:n], op=ALU.subtract),
                        reads=[("ps", b2), ("stD", t % 2)], writes=[("stD", t % 2)])
                plan.op("act", lambda e, t=t, n=n: e.activation(out=stB[t][:, 0:n], in_=stD[t][:, 0:n], func=AF.Ln,
                                                                 bias=P("eps"), scale=1.0),
                        reads=[("stD", t % 2), "prm"], writes=[("stB", t % 2)])
                plan.op("act", lambda e, t=t, n=n: e.activation(out=stA[t][:, 0:n], in_=stB[t][:, 0:n], func=AF.Exp,
                                                                 scale=-0.5),
                        reads=[("stB", t % 2)], writes=[("stA", t % 2)])
                plan.op("dve", lambda e, t=t, n=n: e.scalar_tensor_tensor(
                    out=stC[t][:, 0:n], in0=stC[t][:, 0:n], scalar=-1.0, in1=stA[t][:, 0:n], op0=ALU.mult,
                    op1=ALU.mult), reads=[("stC", t % 2), ("stA", t % 2)], writes=[("stC", t % 2)])
                plan.op("dve", lambda e, t=t, off=off, n=n: e.tensor_tensor(
                    out=sq[:, :, 0:n], in0=cT[:, :, off:off + n],
                    in1=stA[t][:, 0:n].unsqueeze(1).broadcast_to([128, 8, n]), op=ALU.mult),
                    reads=allk("R1", t) + [("stA", t % 2)], writes=["sq"])
                plan.op("dve", lambda e, t=t, n=n: e.tensor_tensor(
                    out=sq[:, :, 0:n], in0=sq[:, :, 0:n],
                    in1=stC[t][:, 0:n].unsqueeze(1).broadcast_to([128, 8, n]), op=ALU.add),
                    reads=["sq", ("stC", t % 2)], writes=["sq"])
                for k in range(8):
                    plan.op("act", lambda e, k=k, off=off, n=n, l=l: e.activation(
                        out=caT[:, k, off:off + n], in_=sq[:, k, 0:n], func=AF.Silu, bias=P("lnb", l * 8 + k),
                        scale=P("lng", l * 8 + k)), reads=["sq", "prm"], writes=[("R2", k, t)])
            if K_STOP <= 4:
                break
            def ev_pw2(oi, ti, off, n, bank, l=l):
                plan.op("dve", lambda e: e.scalar_tensor_tensor(
                    out=xT[:, oi, off:off + n], in0=psum[:, bank, 0:n], scalar=P("bpw2", l * 8 + oi),
                    in1=xT[:, oi, off:off + n], op0=ALU.add, op1=ALU.add),
                    reads=[("ps", bank), ("xT", oi, ti), "prm"], writes=[("xT", oi, ti)])
            proj(D["w_pw2"][l], 8, [[(0, 0, 512)], [(0, 512, 512)]], [4, 4],
                 lambda k, off, n: caT[:, k, off:off + n], lambda ti: allk("R2", ti), [TL], set3, ev_pw2, wbuf)
            if K_STOP <= 5:
                break
            norm_stage("nml", l, hT, "R1")
            for hg in range(4):
                def ev_up(oi, ti, off, n, bank):
                    plan.op("act", lambda e: e.activation(out=zT[:, oi, off:off + n], in_=psum[:, bank, 0:n],
                                                           func=AF.Relu),
                            reads=[("ps", bank)], writes=[("R2", oi, ti)])
                    plan.op("dve", lambda e: e.tensor_tensor(out=zT[:, oi, off:off + n], in0=zT[:, oi, off:off + n],
                                                              in1=zT[:, oi, off:off + n], op=ALU.mult),
                            reads=[("R2", oi, ti)], writes=[("R2", oi, ti)])
                proj(D["w_up"][l][:, hg * 1024:(hg + 1) * 1024], 8, [[(0, 0, 512)], [(0, 512, 512)]], [4, 4],
                     lambda k, off, n: hT[:, k, off:off + n], lambda ti: allk("R1", ti), [TL], set3, ev_up, wbuf)

                def ev_dn(oi, ti, off, n, bank):
                    plan.op("dve", lambda e: e.tensor_tensor(out=xT[:, oi, off:off + n], in0=psum[:, bank, 0:n],
                                                              in1=xT[:, oi, off:off + n], op=ALU.add),
                            reads=[("ps", bank), ("xT", oi, ti)], writes=[("xT", oi, ti)])
                proj(D["w_dn"][l][hg * 1024:(hg + 1) * 1024, :], 8, [[(0, 0, 512)], [(0, 512, 512)]], [4, 4],
                     lambda k, off, n: zT[:, k, off:off + n], lambda ti: allk("R2", ti), [TL], set3, ev_dn, wbuf)
        if K_STOP <= 6:
            continue
        norm_stage("kvg", 0, hT, "R1")

        def ev_k(oi, ti, off, n, bank, c0=c0):
            sgb = sig[ti].bitcast(BF16)
            plan.op("act", lambda e: e.activation(out=sgb[:, 0:n], in_=psum[:, bank, 0:n], func=AF.Square),
                    reads=[("ps", bank)], writes=[("sig", ti)])
            b2 = bankS()
            plan.op("pe", lambda e: e.matmul(psum[:, b2, 0:n], lhsT=oneshb, rhs=sgb[:, 0:n], start=True, stop=True),
                    reads=[("sig", ti), "oneshb"], writes=[("ps", b2)])
            rsqrt_chain(ti, n, b2, "k")
            plan.op("dve", lambda e: e.scalar_tensor_tensor(
                out=kf[ti][:, 0:n], in0=psum[:, bank, 0:n], scalar=P("kng"), in1=stA[ti][:, 0:n], op0=ALU.mult,
                op1=ALU.mult), reads=[("ps", bank), ("stA", ti % 2), "prm"], writes=[("kf", ti % 2)])
            plan.op("act", lambda e: e.activation(out=kb[ti][:, 0:n], in_=kf[ti][:, 0:n], func=AF.Copy),
                    reads=[("kf", ti % 2)], writes=[("kb", ti % 2)])
            plan.dma("sp", D["kT"][:, oi, c0 + off:c0 + off + n], kf[ti][:, 0:n], reads=[("kf", ti % 2)])
            plan.dma("sp", D["KTs"][oi][:, c0 + off:c0 + off + n], kb[ti][:, 0:n], reads=[("kb", ti % 2)], writes=["KTs"])
        proj(D["w_kv"][:, 0:512], 8, [[(0, 0, 512)]], [4], lambda k, off, n: hT[:, k, off:off + n],
             lambda ti: allk("R1", ti), [TL], set3, ev_k, wbuf)
        if K_STOP <= 7:
            continue
        s = wslot()
        plan.dma("pool", wbuf[s], D["w_kv"].rearrange("(k p) o -> p k o", p=128)[:, :, 512:1024], writes=[("wb", s)])
        for tb in range(NST // 128):
            b = bank6()
            ti = tb // 4
            plan.op("pe", lambda e, b=b, tb=tb, s=s: mm_group(
                e, [(b, 0, 512)], 8, lambda k: hT[:, k, tb * 128:(tb + 1) * 128], lambda k, o_, n_: wbuf[s][:, k, :]),
                reads=allk("R1", ti) + [("wb", s)], writes=[("ps", b)])
            i = tb % 2
            plan.op("act", lambda e, b=b, i=i: e.activation(out=vf[i], in_=psum[:, b, :], func=AF.Copy),
                    reads=[("ps", b)], writes=[("kf", i)])
            plan.op("dve", lambda e, i=i: e.tensor_copy(out=vb[i], in_=vf[i]),
                    reads=[("kf", i)], writes=[("kb", i)])
            r0 = c0 + tb * 128
            plan.dma("sp", D["vN"][r0:r0 + 128, :], vf[i], reads=[("kf", i)])
            plan.dma("sp", D["Vs"][r0:r0 + 128, :], vb[i], reads=[("kb", i)], writes=["Vs"])
        if K_STOP <= 8:
            continue
        if st == 0:
            plan.dma("sp", D["xs"][:, :, 0:64], xT[:, :, 0:64], reads=allk("xT", 0), writes=["xs"])
        elif st == 1:
            plan.dma("sp", D["xs"][:, :, 64:704], xT[:, :, 768:1408], reads=allk("xT", 1) + allk("xT", 2), writes=["xs"])
        else:
            plan.dma("sp", D["xs"][:, :, 704:2112], xT[:, :, 0:1408],
                     reads=allk("xT", 0) + allk("xT", 1) + allk("xT", 2), writes=["xs"])
        if K_STOP <= 9:
            continue
        if st in (0, 2):
            for l in range(2):
                src = usampf[l] if st == 0 else utailf[l]
                w = 64 if st == 0 else 30
                bA, bB = bankS(), bankS()

                def tr(e, src=src, w=w, bA=bA, bB=bB):
                    ins = None
                    for k in range(8):
                        bk = bA if k < 4 else bB
                        ins = e.transpose(psum[0:w, bk, (k % 4) * 128:(k % 4 + 1) * 128], src[:, k, :], identf)
                    return ins
                rk = [("usampf", l, c) for c in range(8)] if st == 0 else [("utailf", l, c) for c in range(8)]
                plan.op("pe", tr, reads=rk + ["identf"], writes=[("ps", bA), ("ps", bB)])
                plan.op("act", lambda e, w=w, bA=bA: e.activation(out=tstage[0:w, 0:512], in_=psum[0:w, bA, :],
                                                                   func=AF.Copy),
                        reads=[("ps", bA)], writes=["sq"])
                plan.op("act", lambda e, w=w, bB=bB: e.activation(out=tstage[0:w, 512:1024], in_=psum[0:w, bB, :],
                                                                   func=AF.Copy),
                        reads=[("ps", bB)], writes=["sq"])
                if st == 0:
                    for s_ in range(16):
                        plan.dma("sp", D["cs"][l, s_, 26:30, :], tstage[4 * s_:4 * s_ + 4, :],
                                 reads=["sq"])
                    plan.dma("sp", D["cs"][l, :, 0:26, :], D["scN"][l, :, 4:30, :])
                else:
                    plan.dma("sp", D["cp"][l], tstage[0:30, :], reads=["sq"])

    plan.barrier()
    if K_STOP > 10:
        bb = Bump(PERSIST_END)
        wbufB = [bb([8, 512], BF16) for _ in range(3)]
        sA = [bb([512], F32) for _ in range(2)]
        sB = [bb([512], F32) for _ in range(2)]
        sC = [bb([512], F32) for _ in range(4)]
        hB = bb([8, TB], BF16)
        X0 = bb.off
        xB = bb([8, TB], F32)
        Z0 = bb.off
        zB = bb([8, TB], BF16)
        S0 = bb.off
        sqB = bb([8, 512], F32)
        sqBb = sqB.rearrange("p a b -> p (a b)").bitcast(BF16)[:, 0:4096].rearrange("p (a b) -> p a b", a=8)
        ab = Bump(X0)
        QT = ab([3, TB], BF16)
        KT = [ab([TA], BF16) for _ in range(2)]
        Vg = [ab([69, 128], BF16) for _ in range(2)]
        ex2 = [ab([512], BF16) for _ in range(2)]
        assert ab.off <= Z0
        ab = Bump(Z0)
        oT = ab([4, TB], BF16)
        Em = ab([3, 512], BF16)
        ex = [ab([512], BF16) for _ in range(2)]
        ex += ex2
        PT = [ab([512], BF16) for _ in range(2)]
        QTs = ab([3, 4, 64], BF16)
        KTsamp = ab([4, 64], BF16)
        PT += [ab([512], BF16) for _ in range(2)]
        assert ab.off <= S0
        ab = Bump(X0)
        CB = [ab([9, 1024], BF16) for _ in range(2)]
        Qtm = ab([3, 512], BF16)
        Vnew = [ab([512], BF16) for _ in range(2)]
        sraw = ab([3, 4, 8], F32)
        pexp = ab([3, 4, 8], F32)
        pn1 = ab([2, 48], F32)
        pm = ab([6, 4, 8], BF16)
        pz = ab([2, 4, 4, 8], BF16)
        pD = ab([32], BF16)
        pns = ab([32], BF16)
        prod = [ab([3, 512], F32) for _ in range(2)]
        rDs = ab([1], F32)
        osel = ab([512], BF16)
        assert ab.off <= Z0
        ab = Bump(S0)
        accN = ab([2048], F32)
        accD = ab([2048], F32)

        TLB = [(0, 0, 64), (1, 64, 512), (2, 576, 512), (3, 1088, 512), (4, 1600, 512)]
        TS = [TLB[0:3], TLB[3:5]]
        setB = RR([(0, 1, 2), (3, 4, 5)])
        setAtt = RR([(0, 1, 2), (3, 4)])
        statQ = RR([5, 6, 7])
        sBq = sA + sB
        bank4 = RR([0, 1, 2, 3])
        bankN = RR([4, 5])
        bankD = RR([6, 7])
        statB = RR([6, 7])

        def xk(ti):
            return allk("xB", ti)

        def normB(gname, gidx):
            for (t, off, n) in TLB:
                plan.op("act", lambda e, off=off, n=n: e.activation(out=sqBb[:, :, 0:n], in_=xB[:, :, off:off + n],
                                                                     func=AF.Square), reads=xk(t), writes=["sqB"])
                b = statB()
                plan.op("pe", lambda e, b=b, n=n: mm_group(e, [(b, 0, n)], 8, lambda k: onesb16,
                                                             lambda k, o_, n_: sqBb[:, k, 0:n_]),
                        reads=["sqB", "onesb16"], writes=[("ps", b)])
                i = t % 2
                plan.op("act", lambda e, i=i, n=n, b=b: e.activation(out=sB[i][:, 0:n], in_=psum[:, b, 0:n],
                                                                      func=AF.Ln, bias=P("eps"), scale=1.0),
                        reads=[("ps", b), "prm"], writes=[("sB", i)])
                plan.op("act", lambda e, i=i, n=n: e.activation(out=sA[i][:, 0:n], in_=sB[i][:, 0:n], func=AF.Exp,
                                                                 scale=-0.5),
                        reads=[("sB", i)], writes=[("sA", i)])
                for k in range(8):
                    plan.op("dve", lambda e, k=k, i=i, off=off, n=n: e.scalar_tensor_tensor(
                        out=hB[:, k, off:off + n], in0=xB[:, k, off:off + n], scalar=P(gname, gidx * 8 + k),
                        in1=sA[i][:, 0:n], op0=ALU.mult, op1=ALU.mult),
                        reads=[("xB", k, t), ("sA", i), "prm"], writes=[("hB", k, t)])

        EsM = smc[:, 0:192].rearrange("p (j t h) -> p j t h", j=6, t=4)
        sel32 = smc[0:32, 192:704]
        for (t, off, n) in TLB:
            plan.dma("sp", xB[:, :, off:off + n], D["xs"][:, :, off:off + n], reads=["xs"], writes=xk(t))
        for lb in range(2):
            L = 2 + lb
            normB("nmg", L)
            if lb == 1:
                for (t, off, n) in TLB:
                    plan.dma("sp", D["xs"][:, :, off:off + n], xB[:, :, off:off + n], reads=xk(t), writes=["xs"])
            if K_DBG and lb == 0:
                plan.dma("sp", D["dbg_h"], hB, reads=[x for t in range(5) for x in allk("hB", t)])
            plan.barrier()
            for hp_ in range(4):
                plan.dma("sp", KTsamp[:, hp_, :], D["KTs"][hp_][:, 0:64], reads=["KTs"], writes=["KTsamp"])
            for hp in range(4):
                kv = hp % 2
                plan.dma("sp", KT[kv], D["KTs"][hp], reads=["KTs"], writes=[("KT", kv)])
                Vs = D["Vs"]
                c_lo = hp * 128
                plan.dma("sp", Vg[kv][:, 0:17, :],
                         Vs[OWN0 - 128:OWN0 + 2048, c_lo:c_lo + 128].rearrange("(b i) c -> i b c", i=128),
                         reads=["Vs"], writes=[("Vg", kv)])
                for r in range(4):
                    plan.dma("sp", Vg[kv][:, 17 + 5 * r:22 + 5 * r, :],
                             Vs[OWN0 - 512 + r:OWN0 + 2048:4, c_lo:c_lo + 128].rearrange("(b i) c -> i b c", i=128),
                             reads=["Vs"], writes=[("Vg", kv)])
                for r in range(16):
                    plan.dma("sp", Vg[kv][:, 37 + 2 * r:39 + 2 * r, :],
                             Vs[OWN0 - 2048 + r:OWN0 + 2048:16, c_lo:c_lo + 128].rearrange("(b i) c -> i b c", i=128),
                             reads=["Vs"], writes=[("Vg", kv)])
                plan.dma("pool", Em, D["emask"][:, hp:12:4, :], writes=["Em"])

                def ev_q(oi, ti, off, n, bank, lb=lb):
                    i = ti % 4
                    scb = sC[i].bitcast(BF16)
                    plan.op("act", lambda e: e.activation(out=scb[:, 0:n], in_=psum[:, bank, 0:n], func=AF.Square),
                            reads=[("ps", bank)], writes=[("sC", i)])
                    def late():
                        b2 = statQ()
                        plan.op("pe", lambda e: e.matmul(psum[:, b2, 0:n], lhsT=oneshb, rhs=scb[:, 0:n], start=True,
                                                          stop=True), reads=[("sC", i), "oneshb"], writes=[("ps", b2)])
                        j = ti % 4
                        plan.op("act", lambda e: e.activation(out=sBq[j][:, 0:n], in_=psum[:, b2, 0:n], func=AF.Ln,
                                                               bias=P("eps"), scale=1.0),
                                reads=[("ps", b2), "prm"], writes=[("sBq", j)])
                        plan.op("act", lambda e: e.activation(out=sBq[j][:, 0:n], in_=sBq[j][:, 0:n], func=AF.Exp,
                                                               scale=-0.5),
                                reads=[("sBq", j)], writes=[("sBq", j)])
                        plan.op("dve", lambda e: e.scalar_tensor_tensor(
                            out=QT[:, oi, off:off + n], in0=psum[:, bank, 0:n], scalar=qg8[:, lb:lb + 1],
                            in1=sBq[j][:, 0:n], op0=ALU.mult, op1=ALU.mult),
                            reads=[("ps", bank), ("sBq", j), "qg8"], writes=["QT"])
                    return late
                proj(D["w_q"][lb], 8, [[(g * 128, (g * 4 + hp) * 128, 128) for g in range(3)]], [3],
                     lambda k, off, n: hB[:, k, off:off + n], lambda ti: allk("hB", ti), TS, setAtt, ev_q, wbufB)
                if K_DBG and lb == 0:
                    plan.dma("sp", D["dbg_q"][hp], QT, reads=["QT"])
                plan.op("act", lambda e, hp=hp: e.activation(out=QTs[:, :, hp, :], in_=QT[:, :, 0:64], func=AF.Copy),
                        reads=["QT"], writes=["QTs"])
                units = []
                for g, (w_, d) in enumerate(WINDOWS):
                    qbs = [(r, c) for r in range(d) for c in range(16 // d)]
                    for q4 in range(4):
                        for pr in range(2):
                            units.append((g, d, q4, pr, qbs[q4 * 4 + pr * 2:q4 * 4 + pr * 2 + 2]))
                nd_banks = {}

                def front(ui, kv=kv, units=units, nd_banks=nd_banks):
                    g, d, q4, pr, pair = units[ui]
                    if pr == 0:
                        nd_banks[(g, q4)] = (bankN(), bankD())
                    bX, bY = bank4(), bank4()
                    par = ui % 2

                    def st_mm(e):
                        ins = None
                        for qi, (r, c) in enumerate(pair):
                            q0 = 64 + r + d * 128 * c
                            for pc in range(2):
                                k0 = OWN0 + r + d * 128 * (c - 1 + pc)
                                for h, bk in ((0, bX), (1, bY)):
                                    ins = e.matmul(psum[:, bk, (qi * 2 + pc) * 128:(qi * 2 + pc + 1) * 128],
                                                   lhsT=KT[kv][h * 64:(h + 1) * 64, k0:k0 + 127 * d + 1:d],
                                                   rhs=QT[h * 64:(h + 1) * 64, g, q0:q0 + 127 * d + 1:d],
                                                   start=True, stop=True, tile_position=(h * 64, 0))
                        return ins
                    plan.op("pe", st_mm, reads=[("KT", kv), "QT"], writes=[("ps", bX), ("ps", bY)])
                    for h, bk in ((0, bX), (1, bY)):
                        hx = par * 2 + h
                        plan.op("act", lambda e, hx=hx, bk=bk: e.activation(out=ex[hx], in_=psum[:, bk, :], func=AF.Exp),
                                reads=[("ps", bk)], writes=[("ex", hx)])
                        for qi, (r, c) in enumerate(pair):
                            if c == 0:
                                plan.op("act", lambda e, hx=hx, bk=bk, qi=qi: e.activation(
                                    out=ex[hx][:, qi * 256:qi * 256 + 128], in_=psum[:, bk, qi * 256:qi * 256 + 128],
                                    func=AF.Exp, bias=P("hbias"), scale=1.0),
                                    reads=[("ps", bk), "prm"], writes=[("ex", hx)])
                        plan.op("dve", lambda e, h=h, g=g, hx=hx: e.tensor_tensor(
                            out=PT[hx].rearrange("p (a b) -> p a b", a=2),
                            in0=ex[hx].rearrange("p (a b) -> p a b", a=2),
                            in1=Em[:, g, h * 256:(h + 1) * 256].unsqueeze(1).broadcast_to([128, 2, 256]),
                            op=ALU.mult), reads=[("ex", hx), "Em"], writes=[("PT", hx)])

                def back(ui, kv=kv, units=units, nd_banks=nd_banks):
                    g, d, q4, pr, pair = units[ui]
                    bN, bD = nd_banks[(g, q4)]
                    par = ui % 2
                    vbase = [0, 17, 37][g]
                    nper = 16 // d + 1

                    def pv_mm(e):
                        ins = None
                        for qi, (r, c) in enumerate(pair):
                            sl = (pr * 2 + qi) * 128
                            for h in range(2):
                                for pc in range(2):
                                    rhs = PT[par * 2 + h][:, (qi * 2 + pc) * 128:(qi * 2 + pc + 1) * 128]
                                    e.matmul(psum[h * 64:(h + 1) * 64, bN, sl:sl + 128],
                                             lhsT=Vg[kv][:, vbase + r * nper + c + pc, h * 64:(h + 1) * 64],
                                             rhs=rhs, start=(pc == 0), stop=(pc == 1))
                                    ins = e.matmul(psum[h * 64:(h + 1) * 64, bD, sl:sl + 128],
                                                   lhsT=ones64, rhs=rhs, start=(pc == 0), stop=(pc == 1))
                        return ins
                    plan.op("pe", pv_mm, reads=[("PT", par * 2), ("PT", par * 2 + 1), ("Vg", kv), "ones64"],
                            writes=[("ps", bN), ("ps", bD)])
                    if pr == 0:
                        return
                    if g == 0:
                        cs_ = slice(q4 * 512, q4 * 512 + 512)
                        dstN, dstD = accN[:, cs_], accD[:, cs_]
                        srcN, srcD = psum[:, bN, :], psum[:, bD, :]
                    elif g == 1:
                        dstN, dstD = accN[:, q4:2048:4], accD[:, q4:2048:4]
                        srcN, srcD = psum[:, bN, :], psum[:, bD, :]
                    else:
                        dstN = accN.rearrange("p (i r) -> p r i", r=16)[:, q4 * 4:q4 * 4 + 4, :]
                        dstD = accD.rearrange("p (i r) -> p r i", r=16)[:, q4 * 4:q4 * 4 + 4, :]
                        srcN = psum[:, bN, :].rearrange("p (a b) -> p a b", a=4)
                        srcD = psum[:, bD, :].rearrange("p (a b) -> p a b", a=4)
                    if g == 0:
                        plan.op("act", lambda e: e.activation(out=dstN, in_=srcN, func=AF.Copy),
                                reads=[("ps", bN)], writes=["accN"])
                        plan.op("dve", lambda e: e.tensor_copy(out=dstD, in_=srcD), reads=[("ps", bD)], writes=["accD"])
                    else:
                        plan.op("dve", lambda e: e.tensor_tensor(out=dstN, in0=srcN, in1=dstN, op=ALU.add),
                                reads=[("ps", bN), "accN"], writes=["accN"])
                        plan.op("dve", lambda e: e.tensor_tensor(out=dstD, in0=srcD, in1=dstD, op=ALU.add),
                                reads=[("ps", bD), "accD"], writes=["accD"])
                front(0)
                for ui in range(len(units)):
                    if ui + 1 < len(units):
                        front(ui + 1)
                    back(ui)
                if K_DBG and lb == 0:
                    plan.dma("sp", D["dbg_n"][hp], accN, reads=["accN"])
                    plan.dma("sp", D["dbg_d"][hp], accD, reads=["accD"])
                plan.op("act", lambda e: e.activation(out=accD, in_=accD, func=AF.Ln), reads=["accD"], writes=["accD"])
                plan.op("act", lambda e: e.activation(out=accD, in_=accD, func=AF.Exp, scale=-1.0),
                        reads=["accD"], writes=["accD"])
                plan.op("dve", lambda e, hp=hp: e.tensor_tensor(out=oT[:, hp, 64:TB], in0=accN, in1=accD, op=ALU.mult),
                        reads=["accN", "accD"], writes=["oTp"])
            plan.barrier()
            plan.op("dve", lambda e: e.memset(pm, 0.0), writes=["pm"])
            plan.op("dve", lambda e: e.memset(pz, 0.0), writes=["pz"])
            for i_ in range(2):
                plan.op("dve", lambda e, i_=i_: e.memset(Vnew[i_], 0.0), writes=[("Vnew", i_)])
            for g in range(3):
                bq = bank4()
                plan.op("pe", lambda e, g=g, bq=bq: [e.transpose(
                    psum[0:64, bq, hp_ * 64:(hp_ + 1) * 64].bitcast(BF16), QTs[:, g, hp_, :], identb)
                    for hp_ in range(4)][-1], reads=["QTs", "identb"], writes=[("ps", bq)])
                plan.op("act", lambda e, g=g, bq=bq: e.activation(out=Qtm[0:64, g, :],
                                                                   in_=psum[0:64, bq, 0:256].bitcast(BF16), func=AF.Copy),
                        reads=[("ps", bq)], writes=["Qtm"])
            qsets = RR([(0, 1, 2), (3, 4, 5)])
            for sq_ in range(16):
                cb = sq_ % 2
                C = D["cache"][sq_]
                plan.dma("pool", CB[cb][:, 0, :], C[1920:2048, :], writes=[("CB", cb)])
                plan.dma("pool", CB[cb][:, 1:5, :], C[1536:2048, :].rearrange("(m t) f -> m t f", t=4),
                         writes=[("CB", cb)])
                plan.dma("pool", CB[cb][:, 5:9, :], C.rearrange("(m t) f -> m t f", t=16)[:, 0:4, :],
                         writes=[("CB", cb)])
                plan.dma("sp", Vnew[cb][0:4, :], D["Vs"][4 * sq_:4 * sq_ + 4, :], reads=["Vs"], writes=[("Vnew", cb)])
                c4 = slice(4 * sq_, 4 * sq_ + 4)
                def snew(e, c4=c4):
                    ins = None
                    for g in range(3):
                        for hp_ in range(4):
                            for par in range(2):
                                ins = e.matmul(psum[0:4, 6 + par, (g * 4 + hp_) * 4:(g * 4 + hp_) * 4 + 4],
                                               lhsT=KTsamp[par * 64:(par + 1) * 64, hp_, c4],
                                               rhs=QTs[par * 64:(par + 1) * 64, g, hp_, c4],
                                               start=True, stop=True, tile_position=(par * 64, 0))
                    return ins
                plan.op("pe", snew, reads=["KTsamp", "QTs"], writes=[("ps", 6), ("ps", 7)])
                for par in range(2):
                    plan.op("act", lambda e, par=par: e.activation(out=pn1[0:4, par, :], in_=psum[0:4, 6 + par, 0:48],
                                                                    func=AF.Exp), reads=[("ps", 6 + par)], writes=[("pn1", par)])
                    plan.op("dve", lambda e, par=par: e.tensor_tensor(
                        out=pm[0:4, 3:6, :, par:8:2],
                        in0=pn1[0:4, par, :].rearrange("p (g h t) -> p g t h", g=3, h=4),
                        in1=EsM[0:4, 3:6, :, par:8:2], op=ALU.mult),
                        reads=[("pn1", par), "smc"], writes=["pm"])
                for t in range(4):
                    qb_ = qsets()
                    plan.op("pe", lambda e, t=t, qb_=qb_, sq_=sq_: [e.matmul(
                        psum[:, qb_[g], :], lhsT=identb[0:64, 4 * sq_ + t:4 * sq_ + t + 1].broadcast_to([64, 128]),
                        rhs=Qtm[0:64, g, :], start=True, stop=True) for g in range(3)][-1],
                        reads=["Qtm", "identb"], writes=[("ps", b_) for b_ in qb_])
                    pi = t % 2
                    for g in range(3):
                        blk = 0 if g == 0 else (1 + t if g == 1 else 5 + t)
                        plan.op("dve", lambda e, g=g, blk=blk, pi=pi, qb_=qb_, cb=cb: e.tensor_tensor(
                            out=prod[pi][:, g, :], in0=CB[cb][:, blk, 0:512], in1=psum[:, qb_[g], :], op=ALU.mult),
                            reads=[("CB", cb), ("ps", qb_[g])], writes=[("prod", pi, g)])
                    plan.op("dve", lambda e, t=t, pi=pi: e.tensor_reduce(
                        out=sraw[:, :, t, :], in_=prod[pi].rearrange("p g (h d) -> p g h d", d=64), axis=AX.X,
                        op=ALU.add), reads=[("prod", pi, g_) for g_ in range(3)], writes=["sraw"])
                plan.op("act", lambda e: e.activation(out=pexp, in_=sraw, func=AF.Exp), reads=["sraw"], writes=["pexp"])
                plan.op("dve", lambda e: e.tensor_tensor(out=pm[:, 0:3, :, :], in0=pexp, in1=EsM[:, 0:3, :, :],
                                                          op=ALU.mult), reads=["pexp", "smc"], writes=["pm"])
                plan.op("dve", lambda e: e.tensor_copy(
                    out=pz.rearrange("p g a b h -> p g (a b) h")[:, :, 0:16:5, :], in_=pm[:, 1:3, :, :]),
                    reads=["pm"], writes=["pz"])
                plan.op("dve", lambda e: e.tensor_reduce(
                    out=pD, in_=pm.rearrange("p j t h -> p (t h) j"), axis=AX.X, op=ALU.add),
                    reads=["pm"], writes=["pD"])
                plan.op("dve", lambda e: e.tensor_reduce(
                    out=pns, in_=pm[:, 3:6, :, :].rearrange("p j t h -> p (t h) j"), axis=AX.X, op=ALU.add),
                    reads=["pm"], writes=["pns"])

                def pvs(e, cb=cb):
                    e.matmul(psum[0:32, 6, :], lhsT=pm[:, 0, :, :].rearrange("p t h -> p (t h)"),
                             rhs=CB[cb][:, 0, 512:1024], start=True, stop=False)
                    for g in (1, 2):
                        for t in range(4):
                            blk = 1 + t if g == 1 else 5 + t
                            e.matmul(psum[0:32, 6, :], lhsT=pz[:, g - 1, t, :, :].rearrange("p a h -> p (a h)"),
                                     rhs=CB[cb][:, blk, 512:1024], start=False, stop=False)
                    e.matmul(psum[0:32, 6, :], lhsT=pns, rhs=Vnew[cb], start=False, stop=True)
                    return e.matmul(psum[0:32, 7, 0:1], lhsT=pD, rhs=ones64[:, 0:1], start=True, stop=True)
                plan.op("pe", pvs, reads=["pm", "pz", "pns", "pD", ("CB", cb), ("Vnew", cb), "ones64"],
                        writes=[("ps", 6), ("ps", 7)])
                plan.op("dve", lambda e: e.reciprocal(out=rDs[0:32, :], in_=psum[0:32, 7, 0:1]),
                        reads=[("ps", 7)], writes=["rDs"])
                plan.op("dve", lambda e: e.scalar_tensor_tensor(out=osel[0:32, :], in0=psum[0:32, 6, :],
                                                                 scalar=rDs[0:32, :], in1=sel32, op0=ALU.mult,
                                                                 op1=ALU.mult),
                        reads=[("ps", 6), "rDs", "smc"], writes=["osel"])
                plan.op("pe", lambda e: [e.matmul(psum[:, 7, 8 + 4 * hp_:12 + 4 * hp_],
                                                  lhsT=osel[0:32, hp_ * 128:(hp_ + 1) * 128], rhs=tselb[0:32, :],
                                                  start=True, stop=True) for hp_ in range(4)][-1],
                        reads=["osel", "tselb"], writes=[("ps", 7)])
                plan.op("act", lambda e, c4=c4: e.activation(
                    out=oT[:, :, c4], in_=psum[:, 7, 8:24].rearrange("p (a b) -> p a b", a=4), func=AF.Copy),
                    reads=[("ps", 7)], writes=["oTs"])
            if K_DBG and lb == 0:
                plan.dma("sp", D["dbg_o"], oT, reads=["oTs", "oTp"])
                plan.dma("sp", D["dbg_sraw"], sraw.rearrange("p a b c -> p (a b c)"), reads=["sraw"])
                plan.dma("sp", D["dbg_pm"], pm.rearrange("p a b c -> p (a b c)"), reads=["pm"])
                plan.dma("sp", D["dbg_osel"], osel, reads=["osel"])
                plan.dma("sp", D["dbg_qtm"], Qtm.rearrange("p a b -> p (a b)"), reads=["Qtm"])
            plan.barrier()
            if K_STOP <= 11:
                break
            for (t, off, n) in TLB:
                plan.dma("sp", xB[:, :, off:off + n], D["xs"][:, :, off:off + n], reads=["xs"], writes=xk(t))

            def ev_add(oi, ti, off, n, bank):
                plan.op("dve", lambda e: e.tensor_tensor(out=xB[:, oi, off:off + n], in0=psum[:, bank, 0:n],
                                                          in1=xB[:, oi, off:off + n], op=ALU.add),
                        reads=[("ps", bank), ("xB", oi, ti)], writes=[("xB", oi, ti)])
            proj(D["w_o"][lb], 4, [[(0, 0, 512)], [(0, 512, 512)]], [4, 4],
                 lambda k, off, n: oT[:, k, off:off + n], lambda ti: ["oTs", "oTp"], TS, setB, ev_add, wbufB)
            normB("nml", L)
            for hg in range(4):
                def ev_upB(oi, ti, off, n, bank):
                    plan.op("act", lambda e: e.activation(out=zB[:, oi, off:off + n], in_=psum[:, bank, 0:n],
                                                           func=AF.Relu), reads=[("ps", bank)], writes=[("zB", oi, ti)])
                    plan.op("dve", lambda e: e.tensor_tensor(out=zB[:, oi, off:off + n], in0=zB[:, oi, off:off + n],
                                                              in1=zB[:, oi, off:off + n], op=ALU.mult),
                            reads=[("zB", oi, ti)], writes=[("zB", oi, ti)])
                proj(D["w_up"][L][:, hg * 1024:(hg + 1) * 1024], 8, [[(0, 0, 512)], [(0, 512, 512)]], [4, 4],
                     lambda k, off, n: hB[:, k, off:off + n], lambda ti: allk("hB", ti), TS, setB, ev_upB, wbufB)
                proj(D["w_dn"][L][hg * 1024:(hg + 1) * 1024, :], 8, [[(0, 0, 512)], [(0, 512, 512)]], [4, 4],
                     lambda k, off, n: zB[:, k, off:off + n], lambda ti: allk("zB", ti), TS, setB, ev_add, wbufB)
        for (t, off, n) in TLB:
            plan.dma("sp", D["yT"][:, :, off:off + n], xB[:, :, off:off + n], reads=xk(t))

    plan.barrier()
    replay(nc, plan)
    return nc


_NC_CACHE = {}


def _feat_major(v):
    v = np.asarray(v, np.float32)
    lead = v.shape[:-1]
    return np.ascontiguousarray(np.moveaxis(v.reshape(lead + (8, 128)), -1, 0))


def _emask():
    n = 24
    slopes = (2.0 ** (-8.0 * np.arange(1, n + 1, dtype=np.float64) / n)).reshape(3, 8)
    k = np.arange(128)[:, None]
    q = np.arange(128)[None, :]
    out = np.zeros((128, 3, 4, 2, 2, 128), np.float64)
    for g, (w, d) in enumerate(WINDOWS):
        for h in range(8):
            sl = slopes[g, h] * d
            prev = np.where(q <= k, np.exp(-sl * (q - k + 128)), 0.0)
            cur = np.where(q >= k, np.exp(-sl * (q - k)), 0.0)
            out[:, g, h // 2, h % 2, 0, :] = prev
            out[:, g, h // 2, h % 2, 1, :] = cur
    return out.reshape(128, 12, 512).astype(np.float32)


def _smc():
    slopes = (2.0 ** (-8.0 * np.arange(1, 25, dtype=np.float64) / 24)).reshape(3, 8)
    m = np.arange(128, dtype=np.float64)[:, None, None]
    t = np.arange(4, dtype=np.float64)[None, :, None]
    E = np.zeros((128, 6, 4, 8))
    d0 = 128 + t - m
    E[:, 0] = np.where(m >= t, np.exp(-slopes[0][None, None, :] * d0), 0.0)
    E[:, 1] = np.exp(-slopes[1][None, None, :] * 4 * (128 - m)) * np.ones((1, 4, 1))
    E[:, 2] = np.exp(-slopes[2][None, None, :] * 16 * (128 - m)) * np.ones((1, 4, 1))
    E[:, 3] = np.where((m <= t) & (m < 4), np.exp(-slopes[0][None, None, :] * (t - m)), 0.0)
    E[:, 4] = np.where(m == t, 1.0, 0.0) * np.ones((1, 1, 8))
    E[:, 5] = E[:, 4]
    out = np.zeros((128, 708), np.float32)
    out[:, 0:192] = E.reshape(128, 192)
    r = np.arange(32)
    f = np.arange(512)
    out[0:32, 192:704] = ((f[None, :] // 64) == (r[:, None] % 8)).astype(np.float32)
    out[0:32, 704:708] = ((r[:, None] // 8) == np.arange(4)[None, :]).astype(np.float32)
    return out


def kernel(x_prompt, x_sample, state_conv, cache_kv, norm_mix_g, norm_mlp_g, conv_w_pw1, conv_b_pw1,
           conv_w_dw, conv_b_dw, conv_ln_g, conv_ln_b, conv_w_pw2, conv_b_pw2, kv_norm_g, w_kv, k_norm_g,
           attn_w_q, q_norm_g, attn_w_o, mlp_w_up, mlp_w_down):
    f = lambda v: np.ascontiguousarray(np.asarray(v, np.float32))
    if "nc" not in _NC_CACHE:
        t0 = time.time()
        _NC_CACHE["nc"] = build_program()
        print("build time", time.time() - t0)
    nc = _NC_CACHE["nc"]
    x_prompt, x_sample, state_conv, cache_kv = f(x_prompt), f(x_sample), f(state_conv), f(cache_kv)
    prm = np.zeros((128, NPRM), np.float32)

    def put(name, arr):
        arr = np.asarray(arr, np.float32).reshape(128, -1)
        prm[:, PRM[name]:PRM[name] + arr.shape[1]] = arr
    put("nmg", _feat_major(norm_mix_g))
    put("nml", _feat_major(norm_mlp_g))
    put("bpw1", np.moveaxis(np.asarray(conv_b_pw1, np.float32).reshape(2, 16, 128), -1, 0))
    put("wdw", np.moveaxis(np.asarray(conv_w_dw, np.float32).reshape(2, 31, 8, 128), (3, 0, 2, 1), (0, 1, 2, 3)))
    put("bdw", _feat_major(conv_b_dw))
    put("lng", _feat_major(conv_ln_g))
    put("lnb", _feat_major(conv_ln_b))
    put("bpw2", _feat_major(conv_b_pw2))
    put("kvg", _feat_major(kv_norm_g))
    put("kng", np.tile(np.asarray(k_norm_g, np.float32), 2)[:, None])
    put("qng", np.tile(np.asarray(q_norm_g, np.float32), (1, 2)).T)
    prm[:, PRM["eps"]] = EPS
    ident = np.eye(128, dtype=np.float32)
    onesb = np.kron(np.eye(2, dtype=np.float32), np.full((64, 64), 1.0 / 64, np.float32))
    emask = _emask()
    smc = _smc()
    xpT = [np.ascontiguousarray(x_prompt[b].T).reshape(8, 128, 8192).transpose(1, 0, 2) for b in range(2)]
    shared = dict(ident=ident, onesb=onesb, emask=emask, smc=smc, w_pw1=f(conv_w_pw1), w_pw2=f(conv_w_pw2), w_kv=f(w_kv),
                  w_q=f(attn_w_q), w_o=f(attn_w_o), w_up=f(mlp_w_up), w_dn=f(mlp_w_down))
    in_maps = []
    for c in range(NCORES):
        b, j = c // 4, c % 4
        S = 2048 * j
        xin = np.zeros((128, 8, TA), np.float32)
        xs_ = x_sample[16 * c:16 * c + 16].reshape(64, 1024)
        xin[:, :, 0:64] = xs_.T.reshape(8, 128, 64).transpose(1, 0, 2)
        lo = S - 2112
        src_lo = max(lo, 0)
        xin[:, :, 64 + (src_lo - lo):TA] = xpT[b][:, :, src_lo:S + 2048]
        p = prm.copy()
        p[:, PRM["valid"]] = 1.0 if j > 0 else 0.0
        p[:, PRM["hbias"]] = 0.0 if j > 0 else -30000.0
        sc = state_conv[:, 16 * c:16 * c + 16]
        scT = np.ascontiguousarray(sc.reshape(2, 16, 30, 8, 128).transpose(4, 0, 3, 1, 2))
        m = dict(shared)
        m.update(xin=xin, prm=p, scT=scT, scN=np.ascontiguousarray(sc),
                 cache=np.ascontiguousarray(cache_kv[16 * c:16 * c + 16].reshape(16, 2048, 1024)))
        in_maps.append(m)
    if K_CORE is not None:
        t0 = time.time()
        res = run_bass_kernel_spmd(nc, [in_maps[int(K_CORE)]], core_ids=[0])
        print("run time", time.time() - t0)
        R = [res.results[0]] * NCORES
    else:
        res = run_bass_kernel_spmd(nc, in_maps, core_ids=list(range(NCORES)))
        R = res.results
    y_prompt = np.zeros((2, 8192, 1024), np.float32)
    y_sample = np.zeros((128, 4, 1024), np.float32)
    conv_prompt = np.zeros((2, 2, 30, 1024), np.float32)
    conv_sample = np.zeros((2, 128, 30, 1024), np.float32)
    kv_prompt = np.zeros((2, 2048, 2, 8, 64), np.float32)
    kv_sample = np.zeros((128, 4, 2, 8, 64), np.float32)
    for c in range(NCORES):
        b, j = c // 4, c % 4
        r = R[c]
        yT = np.asarray(r["yT"]).transpose(1, 0, 2).reshape(1024, TB)
        y_sample[16 * c:16 * c + 16] = yT[:, 0:64].T.reshape(16, 4, 1024)
        y_prompt[b, 2048 * j:2048 * j + 2048] = yT[:, 64:].T
        conv_sample[:, 16 * c:16 * c + 16] = np.asarray(r["cs"])
        kT = np.asarray(r["kT"]).transpose(1, 0, 2).reshape(512, TA)
        vN = np.asarray(r["vN"])
        kv_sample[16 * c:16 * c + 16, :, 0] = kT[:, 0:64].T.reshape(16, 4, 8, 64)
        kv_sample[16 * c:16 * c + 16, :, 1] = vN[0:64].reshape(16, 4, 8, 64)
        if j == 3:
            conv_prompt[:, b] = np.asarray(r["cp"])
            kv_prompt[b, :, 0] = kT[:, OWN0:].T.reshape(2048, 8, 64)
            kv_prompt[b, :, 1] = vN[OWN0:].reshape(2048, 8, 64)
    return (y_prompt, y_sample, conv_prompt, conv_sample, kv_prompt, kv_sample)
```

```python
import os
import time
import numpy as np
import concourse.bass as bass
import concourse.mybir as mybir
from concourse.bass_utils import run_bass_kernel_spmd
from contextlib import ExitStack

F32, BF16 = mybir.dt.float32, mybir.dt.bfloat16
AF = mybir.ActivationFunctionType
ALU = mybir.AluOpType
AX = mybir.AxisListType

NCORES = 8
TA = 4224
NST = 1408
TB = 2112
OWN0 = 2176
EPS = 1e-6
ARENA = 51800
WINDOWS = ((128, 1), (512, 4), (2048, 16))
K_STOP = int(os.environ.get('K_STOP', '99'))
K_NST = int(os.environ.get('K_NST', '3'))
K_CORE = os.environ.get('K_CORE')
K_DBG = int(os.environ.get('K_DBG', '0'))

PRM = {}
_o = 0
for _n, _w in (("nmg", 32), ("nml", 32), ("bpw1", 32), ("wdw", 2 * 8 * 31), ("bdw", 16), ("lng", 16),
               ("lnb", 16), ("bpw2", 16), ("kvg", 8), ("kng", 1), ("qng", 2), ("valid", 1), ("hbias", 1),
               ("eps", 1), ("zero", 1)):
    PRM[_n] = _o
    _o += _w
NPRM = _o


class Plan:
    ENG = ("pe", "act", "dve", "pool", "sp")
    NDS = 8

    def __init__(self):
        self.items = {e: [] for e in self.ENG}
        self.cnt = {e: 0 for e in self.ENG}
        self.known = {e: {} for e in self.ENG}
        self.bw = {}
        self.br = {}
        self.dma_n = {}
        self.dma_rr = {e: 0 for e in self.ENG}
        self.sems = set("s_" + e for e in self.ENG)

    def _need(self, eng, tok):
        if tok is None:
            return
        sem, val = tok
        if sem == "s_pe" and eng == "pe":
            return
        if self.known[eng].get(sem, 0) >= val:
            return
        self.known[eng][sem] = val
        self.items[eng].append(("w", sem, val))

    def _deps(self, eng, reads, writes):
        for b in reads:
            self._need(eng, self.bw.get(b))
            if isinstance(b, tuple) and b[0] == "ps":
                for s, v in self.br.get(b, {}).items():
                    if s != "s_" + eng:
                        self._need(eng, (s, v))
        for b in writes:
            self._need(eng, self.bw.get(b))
            for s, v in self.br.get(b, {}).items():
                self._need(eng, (s, v))

    def _mark(self, tok, reads, writes):
        for b in reads:
            d = self.br.setdefault(b, {})
            if d.get(tok[0], 0) < tok[1]:
                d[tok[0]] = tok[1]
        for b in writes:
            self.bw[b] = tok
            self.br[b] = {}

    def op(self, eng, fn, reads=(), writes=()):
        self._deps(eng, reads, writes)
        self.cnt[eng] += 1
        tok = ("s_" + eng, self.cnt[eng])
        self.items[eng].append(("x", fn))
        self._mark(tok, reads, writes)
        return tok

    def dma(self, q, out, in_, reads=(), writes=()):
        i = self.dma_rr[q] % self.NDS
        self.dma_rr[q] += 1
        sem = "d_%s%d" % (q, i)
        self.sems.add(sem)
        n = self.dma_n.get(sem, 0)
        if n:
            self._need(q, (sem, 16 * n))
        self._deps(q, reads, writes)
        self.dma_n[sem] = n + 1
        tok = (sem, 16 * (n + 1))
        self.items[q].append(("d", out, in_, sem))
        self._mark(tok, reads, writes)
        return tok

    def barrier(self):
        for e in self.ENG:
            for o in self.ENG:
                if self.cnt[o]:
                    self._need(e, ("s_" + o, self.cnt[o]))
            for sem, n in self.dma_n.items():
                self._need(e, (sem, 16 * n))


def replay(nc, plan):
    with ExitStack() as es:
        es.enter_context(nc.allow_low_precision("bf16 matmul operands, fp32 accumulation"))
        sems = {s: es.enter_context(nc.semaphore(s)) for s in sorted(plan.sems)}
        block = es.enter_context(nc.Block())

        def run(e, name):
            own = sems["s_" + name]
            for it in plan.items[name]:
                if it[0] == "w":
                    e.wait_ge(sems[it[1]], it[2])
                elif it[0] == "x":
                    it[1](e).then_inc(own, 1)
                else:
                    e.dma_start(out=it[1], in_=it[2]).then_inc(sems[it[3]], 16)

        @block.tensor
        def _(e):
            run(e, "pe")

        @block.scalar
        def _(e):
            run(e, "act")

        @block.vector
        def _(e):
            run(e, "dve")

        @block.gpsimd
        def _(e):
            run(e, "pool")

        @block.sync
        def _(e):
            run(e, "sp")


def build_program():
    nc = bass.Bass("TRN2", target_bir_lowering=False)
    plan = Plan()
    D = {}

    def din(name, shape, dt=F32):
        D[name] = nc.dram_tensor(name, list(shape), dt, kind="ExternalInput").ap()

    def dout(name, shape, dt=F32):
        D[name] = nc.dram_tensor(name, list(shape), dt, kind="ExternalOutput").ap()

    def dscr(name, shape, dt=F32):
        D[name] = nc.dram_tensor(name, list(shape), dt, kind="Internal").ap()

    din("xin", [128, 8, TA])
    din("prm", [128, NPRM])
    din("ident", [128, 128])
    din("onesb", [128, 128])
    din("w_pw1", [2, 1024, 2048])
    din("w_pw2", [2, 1024, 1024])
    din("w_kv", [1024, 1024])
    din("w_q", [2, 1024, 1536])
    din("w_o", [2, 512, 1024])
    din("w_up", [4, 1024, 4096])
    din("w_dn", [4, 4096, 1024])
    din("scT", [128, 2, 8, 16, 30])
    din("scN", [2, 16, 30, 1024])
    din("cache", [16, 2048, 1024])
    din("emask", [128, 12, 512])
    din("smc", [128, 708])
    dout("yT", [128, 8, TB])
    dout("kT", [128, 4, TA])
    dout("vN", [TA, 512])
    dout("cs", [2, 16, 30, 1024])
    dout("cp", [2, 30, 1024])
    if K_DBG:
        dout("dbg_h", [128, 8, TB], BF16)
        dout("dbg_q", [4, 128, 3, TB], BF16)
        dout("dbg_o", [128, 4, TB], BF16)
        dout("dbg_n", [4, 128, 2048])
        dout("dbg_d", [4, 128, 2048])
        dout("dbg_sraw", [128, 96])
        dout("dbg_pm", [128, 192], BF16)
        dout("dbg_osel", [128, 512], BF16)
        dout("dbg_qtm", [128, 1536], BF16)
    dscr("xs", [128, 8, TB])
    dscr("KTs", [4, 128, TA], BF16)
    dscr("Vs", [TA, 512], BF16)

    arena = nc.alloc_sbuf_tensor("arena", [128, ARENA], F32)
    psum = nc.alloc_psum_tensor("psum", [128, 8, 512], F32)

    class Bump:
        def __init__(self, start=0):
            self.off = start

        def __call__(self, shape, dt):
            n = int(np.prod(shape))
            nb = n * (4 if dt == F32 else 2)
            nb = (nb + 31) // 32 * 32
            assert self.off + nb <= ARENA * 4, ("SBUF overflow", self.off + nb)
            a = arena[:, self.off // 4:(self.off + nb) // 4]
            if dt != F32:
                a = a.bitcast(dt)
            a = a[:, 0:n]
            if len(shape) == 2:
                a = a.rearrange("p (a b) -> p a b", a=shape[0])
            elif len(shape) == 3:
                a = a.rearrange("p (a b c) -> p a b c", a=shape[0], b=shape[1])
            elif len(shape) == 4:
                a = a.rearrange("p (a b c d) -> p a b c d", a=shape[0], b=shape[1], c=shape[2])
            self.off += nb
            return a

    pb = Bump(0)
    prm = pb([NPRM], F32)
    identf = pb([128], F32)
    identb = pb([128], BF16)
    onesf = pb([128], F32)
    onesb16 = pb([128], BF16)
    oneshf = pb([128], F32)
    oneshb = pb([128], BF16)
    smc = pb([708], F32)
    tselb = pb([4], BF16)
    qg8 = pb([2], F32)
    ones64 = pb([64], BF16)
    PERSIST_END = pb.off

    def P(name, i=0, w=1):
        o = PRM[name] + i
        return prm[:, o:o + w]

    plan.dma("sp", prm, D["prm"], writes=["prm"])
    plan.dma("sp", identf, D["ident"], writes=["identf"])
    plan.dma("sp", oneshf, D["onesb"], writes=["oneshf"])
    plan.op("dve", lambda e: e.tensor_copy(out=identb, in_=identf), reads=["identf"], writes=["identb"])
    plan.op("dve", lambda e: e.tensor_copy(out=oneshb, in_=oneshf), reads=["oneshf"], writes=["oneshb"])
    plan.op("dve", lambda e: e.memset(onesf, 1.0 / 1024), writes=["onesf"])
    plan.op("dve", lambda e: e.memset(onesb16, 1.0 / 1024), writes=["onesb16"])
    plan.op("dve", lambda e: e.memset(ones64, 1.0), writes=["ones64"])
    plan.dma("sp", smc, D["smc"], writes=["smc"])
    plan.op("dve", lambda e: e.tensor_copy(out=tselb, in_=smc[:, 704:708]), reads=["smc"], writes=["tselb"])
    plan.op("dve", lambda e: e.tensor_scalar(out=qg8, in0=P("qng", 0, 2), scalar1=0.125, scalar2=None, op0=ALU.mult),
            reads=["prm"], writes=["qg8"])

    class RR:
        def __init__(self, items):
            self.items = items
            self.i = 0

        def __call__(self):
            v = self.items[self.i % len(self.items)]
            self.i += 1
            return v

    def mm_group(e, bank_tiles, kc, lhs_fn, rhs_fn):
        ins = None
        for k in range(kc):
            for (b, off, n) in bank_tiles:
                ins = e.matmul(psum[:, b, 0:n], lhsT=lhs_fn(k), rhs=rhs_fn(k, off, n),
                               start=(k == 0), stop=(k == kc - 1))
        return ins

    wslot = RR([0, 1, 2])

    def proj(Wap, kc, runs_per_group, nchunks_per_group, rhs_fn, in_keys_fn, tilesets, bankrr, evac_fn, wbuf):
        Wr = Wap.rearrange("(k p) o -> p k o", p=128)
        tiles = [t_ for ts in tilesets for t_ in ts]
        oi0 = 0
        pending = []
        for g, runs in enumerate(runs_per_group):
            s = wslot()
            for (dc, sc, w) in runs:
                plan.dma("pool", wbuf[s][:, 0:kc, dc:dc + w], Wr[:, :, sc:sc + w], writes=[("wb", s)])
            for (ti, off, n) in tiles:
                for lo in range(nchunks_per_group[g]):
                    b = bankrr()
                    if isinstance(b, tuple):
                        b = b[0]
                    plan.op("pe", (lambda e, b=b, off=off, n=n, s=s, lo=lo: mm_group(
                        e, [(b, off, n)], kc, lambda k: wbuf[s][:, k, lo * 128:(lo + 1) * 128], rhs_fn)),
                        reads=[("wb", s)] + in_keys_fn(ti), writes=[("ps", b)])
                    for fn in pending:
                        fn()
                    pending = []
                    r_ = evac_fn(oi0 + lo, ti, off, n, b)
                    if r_ is not None:
                        pending.append(r_)
            oi0 += nchunks_per_group[g]
        for fn in pending:
            fn()

    def allk(name, ti, nk=8):
        return [(name, k, ti) for k in range(nk)]

    R2B = [0, 512, 1024, 1408, 1440]

    def r2k(k, lo, hi):
        return [("R2", k, i) for i in range(4) if lo < R2B[i + 1] and hi > R2B[i]]

    def r2all(lo, hi):
        return [x for k in range(8) for x in r2k(k, lo, hi)]

    a = Bump(PERSIST_END)
    xT = a([8, NST], F32)
    R1 = a([8, NST], BF16)
    R2 = a([8, NST + 32], BF16)
    sq = a([8, 512], F32)
    sqb = sq.rearrange("p a b -> p (a b)").bitcast(BF16)[:, 0:4096].rearrange("p (a b) -> p a b", a=8)
    sig = [a([512], F32) for _ in range(3)]
    stA = [a([512], F32) for _ in range(2)]
    stA.append(stA[0])
    stB = [a([512], F32) for _ in range(2)]
    stB.append(stB[0])
    stC = [a([512], F32) for _ in range(2)]
    stC.append(stC[0])
    stD = [a([512], F32) for _ in range(2)]
    stD.append(stD[0])
    wbuf = [a([8, 512], BF16) for _ in range(3)]
    dg = [a([31, 128], BF16) for _ in range(2)]
    usamp = a([8, 16, 34], BF16)
    usampf = [a([8, 64], F32) for _ in range(2)]
    utailf = [a([8, 30], F32) for _ in range(2)]
    tailb = [a([8, 30], BF16) for _ in range(2)]
    kf = [a([512], F32) for _ in range(2)]
    kf.append(kf[0])
    kb = [a([512], BF16) for _ in range(2)]
    kb.append(kb[0])
    vf, vb = kf, kb
    tstage = sq.rearrange("p a b -> p (a b)")[:, 0:1024]
    hT = R1
    cT = R1
    uT = R2
    caT = R2[:, :, 0:NST]
    zT = R2[:, :, 0:NST]

    TL = [(0, 0, 512), (1, 512, 512), (2, 1024, 384)]
    set3 = RR([0, 1, 2, 3, 4, 5])
    bank6 = RR([0, 1, 2, 3, 4, 5])
    bankS = RR([6, 7])

    def rsqrt_chain(t, n, bank, tag):
        plan.op("act", lambda e: e.activation(out=stB[t][:, 0:n], in_=psum[:, bank, 0:n], func=AF.Ln,
                                               bias=P("eps"), scale=1.0),
                reads=[("ps", bank), "prm"], writes=[("stB", t % 2)])
        plan.op("act", lambda e: e.activation(out=stA[t][:, 0:n], in_=stB[t][:, 0:n], func=AF.Exp, scale=-0.5),
                reads=[("stB", t % 2)], writes=[("stA", t % 2)])

    def norm_stage(gname, gidx, dst, dst_name):
        for (t, off, n) in TL:
            plan.op("act", lambda e, off=off, n=n: e.activation(out=sqb[:, :, 0:n], in_=xT[:, :, off:off + n],
                                                                 func=AF.Square),
                    reads=allk("xT", t), writes=["sq"])
            b = bankS()
            plan.op("pe", lambda e, b=b, n=n: mm_group(e, [(b, 0, n)], 8, lambda k: onesb16,
                                                         lambda k, o_, n_: sqb[:, k, 0:n_]),
                    reads=["sq", "onesb16"], writes=[("ps", b)])
            rsqrt_chain(t, n, b, "n")
            for k in range(8):
                plan.op("dve", lambda e, k=k, t=t, off=off, n=n: e.scalar_tensor_tensor(
                    out=dst[:, k, off:off + n], in0=xT[:, k, off:off + n], scalar=P(gname, gidx * 8 + k),
                    in1=stA[t][:, 0:n], op0=ALU.mult, op1=ALU.mult),
                    reads=[("xT", k, t), ("stA", t % 2), "prm"], writes=[(dst_name, k, t)])

    for st in range(K_NST):
        c0 = st * NST
        for (t, off, n) in TL:
            plan.dma("sp", xT[:, :, off:off + n], D["xin"][:, :, c0 + off:c0 + off + n], writes=allk("xT", t))
        for l in range(2):
            norm_stage("nmg", l, hT, "R1")
            if K_STOP <= 1:
                break
            if st == 0:
                plan.op("dve", lambda e: e.memset(uT[:, :, 0:30], 0.0), writes=r2all(0, 30))
                for hh in range(2):
                    plan.dma("pool", usamp[:, 4 * hh:4 * hh + 4, :, 0:30], D["scT"][:, l, 4 * hh:4 * hh + 4],
                             writes=["usamp"])
            else:
                plan.op("dve", lambda e, l=l: e.tensor_copy(out=uT[:, :, 0:30], in_=tailb[l]),
                        reads=[("tailb", l)], writes=r2all(0, 30))
            def ev_pw1(oi, ti, off, n, bank, l=l, st=st):
                c, isg = oi // 2, oi % 2
                if not isg:
                    ev_pw1.abank[ti] = bank
                    return
                ab = ev_pw1.abank[ti]
                plan.op("act", lambda e: e.activation(out=sig[ti][:, 0:n], in_=psum[:, bank, 0:n], func=AF.Sigmoid,
                                                       bias=P("bpw1", l * 16 + 8 + c), scale=1.0),
                        reads=[("ps", bank), "prm"], writes=[("sig", ti)])
                plan.op("dve", lambda e: e.scalar_tensor_tensor(
                    out=uT[:, c, 30 + off:30 + off + n], in0=psum[:, ab, 0:n], scalar=P("bpw1", l * 16 + c),
                    in1=sig[ti][:, 0:n], op0=ALU.add, op1=ALU.mult),
                    reads=[("ps", ab), ("sig", ti), "prm"], writes=r2k(c, 30 + off, 30 + off + n))
                if st == 0 and ti == 0:
                    plan.op("dve", lambda e: e.scalar_tensor_tensor(
                        out=usampf[l][:, c, :], in0=psum[:, ab, 0:64], scalar=P("bpw1", l * 16 + c),
                        in1=sig[ti][:, 0:64], op0=ALU.add, op1=ALU.mult),
                        reads=[("ps", ab), ("sig", ti), "prm"], writes=[("usampf", l, c)])
                    plan.op("dve", lambda e: e.tensor_copy(
                        out=usamp[:, c, :, 30:34], in_=usampf[l][:, c, :].rearrange("p (s t) -> p s t", t=4)),
                        reads=[("usampf", l, c)], writes=["usamp"])
                if st == 2 and ti == 2:
                    plan.op("dve", lambda e: e.scalar_tensor_tensor(
                        out=utailf[l][:, c, :], in0=psum[:, ab, n - 30:n], scalar=P("bpw1", l * 16 + c),
                        in1=sig[ti][:, n - 30:n], op0=ALU.add, op1=ALU.mult),
                        reads=[("ps", ab), ("sig", ti), "prm"], writes=[("utailf", l, c)])
            ev_pw1.abank = {}
            runs = []
            for c in range(0, 8, 2):
                runs.append([(0, c * 128, 128), (128, 1024 + c * 128, 128),
                             (256, (c + 1) * 128, 128), (384, 1024 + (c + 1) * 128, 128)])
            proj(D["w_pw1"][l], 8, runs, [4] * 4, lambda k, off, n: hT[:, k, off:off + n],
                 lambda ti: allk("R1", ti), [TL], set3, ev_pw1, wbuf)
            if st == 1:
                plan.op("dve", lambda e: e.tensor_scalar(out=uT[:, :, 30 + 738:30 + 768], in0=uT[:, :, 30 + 738:30 + 768],
                                                          scalar1=P("valid"), scalar2=None, op0=ALU.mult),
                        reads=r2all(768, 798) + ["prm"], writes=r2all(768, 798))
            if K_STOP <= 2:
                break
            for c in range(8):
                ds = c % 2
                plan.op("dve", lambda e, ds=ds, c=c, l=l: e.tensor_tensor(
                    out=dg[ds], in0=identb.unsqueeze(1).broadcast_to([128, 31, 128]),
                    in1=P("wdw", (l * 8 + c) * 31, 31).unsqueeze(2).broadcast_to([128, 31, 128]), op=ALU.mult),
                    reads=["identb", "prm"], writes=[("dg", ds)])
                for (t, off, n) in TL:
                    b = bank6()

                    def conv_mm(e, ds=ds, c=c, off=off, n=n, b=b, st=st, t=t):
                        ins = None
                        for j in range(31):
                            ins = e.matmul(psum[:, b, 0:n], lhsT=dg[ds][:, j, :], rhs=uT[:, c, off + j:off + j + n],
                                           start=(j == 0), stop=(j == 30))
                        if st == 0 and t == 0:
                            for j in range(31):
                                ins = e.matmul(psum[:, b, 0:64].rearrange("p (s t) -> p s t", t=4), lhsT=dg[ds][:, j, :],
                                               rhs=usamp[:, c, :, j:j + 4], start=(j == 0), stop=(j == 30))
                        return ins
                    rk = [("dg", ds)] + r2k(c, off, off + n + 30)
                    if st == 0 and t == 0:
                        rk.append("usamp")
                    plan.op("pe", conv_mm, reads=rk, writes=[("ps", b)])
                    plan.op("act", lambda e, c=c, off=off, n=n, b=b, l=l: e.activation(
                        out=cT[:, c, off:off + n], in_=psum[:, b, 0:n], func=AF.Identity, bias=P("bdw", l * 8 + c),
                        scale=1.0), reads=[("ps", b), "prm"], writes=[("R1", c, t)])
            plan.op("dve", lambda e, l=l: e.tensor_copy(out=tailb[l], in_=uT[:, :, NST:NST + 30]),
                    reads=r2all(NST, NST + 30), writes=[("tailb", l)])
            if K_STOP <= 3:
                break
            for (t, off, n) in TL:
                plan.op("act", lambda e, off=off, n=n: e.activation(out=sqb[:, :, 0:n], in_=cT[:, :, off:off + n],
                                                                     func=AF.Square),
                        reads=allk("R1", t), writes=["sq"])
                b1, b2 = bankS(), bankS()
                plan.op("pe", lambda e, b1=b1, off=off, n=n: mm_group(
                    e, [(b1, off, n)], 8, lambda k: onesb16, lambda k, o_, n_: cT[:, k, o_:o_ + n_]),
                    reads=allk("R1", t) + ["onesb16"], writes=[("ps", b1)])
                plan.op("pe", lambda e, b2=b2, n=n: mm_group(
                    e, [(b2, 0, n)], 8, lambda k: onesb16, lambda k, o_, n_: sqb[:, k, 0:n_]),
                    reads=["sq", "onesb16"], writes=[("ps", b2)])
                plan.op("act", lambda e, t=t, n=n, b1=b1: e.activation(out=stC[t][:, 0:n], in_=psum[:, b1, 0:n],
                                                                        func=AF.Copy),
                        reads=[("ps", b1)], writes=[("stC", t % 2)])
                plan.op("dve", lambda e, t=t, n=n: e.tensor_tensor(out=stD[t][:, 0:n], in0=stC[t][:, 0:n],
                                                                    in1=stC[t][:, 0:n], op=ALU.mult),
                        reads=[("stC", t % 2)], writes=[("stD", t % 2)])
                plan.op("dve", lambda e, t=t, n=n, b2=b2: e.tensor_tensor(out=stD[t][:, 0:n], in0=psum[:, b2, 0:n],
                                                                           in1=stD[t][:, 0:n], op=ALU.subtract),
                        reads=[("ps", b2), ("stD", t % 2)], writes=[("stD", t % 2)])
                plan.op("act", lambda e, t=t, n=n: e.activation(out=stB[t][:, 0:n], in_=stD[t][:, 0:n], func=AF.Ln,
                                                                 bias=P("eps"), scale=1.0),
                        reads=[("stD", t % 2), "prm"], writes=[("stB", t % 2)])
                plan.op("act", lambda e, t=t, n=n: e.activation(out=stA[t][:, 0:n], in_=stB[t][:, 0:n], func=AF.Exp,
                                                                 scale=-0.5),
                        reads=[("stB", t % 2)], writes=[("stA", t % 2)])
                plan.op("dve", lambda e, t=t, n=n: e.scalar_tensor_tensor(
                    out=stC[t][:, 0:n], in0=stC[t][:, 0:n], scalar=-1.0, in1=stA[t][:, 0:n], op0=ALU.mult,
                    op1=ALU.mult), reads=[("stC", t % 2), ("stA", t % 2)], writes=[("stC", t % 2)])
                plan.op("dve", lambda e, t=t, off=off, n=n: e.tensor_tensor(
                    out=sq[:, :, 0:n], in0=cT[:, :, off:off + n],
                    in1=stA[t][:, 0:n].unsqueeze(1).broadcast_to([128, 8, n]), op=ALU.mult),
                    reads=allk("R1", t) + [("stA", t % 2)], writes=["sq"])
                plan.op("dve", lambda e, t=t, n=n: e.tensor_tensor(
                    out=sq[:, :, 0:n], in0=sq[:, :, 0:n],
                    in1=stC[t][:, 0:n].unsqueeze(1).broadcast_to([128, 8, n]), op=ALU.add),
                    reads=["sq", ("stC", t % 2)], writes=["sq"])
                for k in range(8):
                    plan.op("act", lambda e, k=k, off=off, n=n, l=l: e.activation(
                        out=caT[:, k, off:off + n], in_=sq[:, k, 0:n], func=AF.Silu, bias=P("lnb", l * 8 + k),
                        scale=P("lng", l * 8 + k)), reads=["sq", "prm"], writes=[("R2", k, t)])
            if K_STOP <= 4:
                break
            def ev_pw2(oi, ti, off, n, bank, l=l):
                plan.op("dve", lambda e: e.scalar_tensor_tensor(
                    out=xT[:, oi, off:off + n], in0=psum[:, bank, 0:n], scalar=P("bpw2", l * 8 + oi),
                    in1=xT[:, oi, off:off + n], op0=ALU.add, op1=ALU.add),
                    reads=[("ps", bank), ("xT", oi, ti), "prm"], writes=[("xT", oi, ti)])
            proj(D["w_pw2"][l], 8, [[(0, 0, 512)], [(0, 512, 512)]], [4, 4],
                 lambda k, off, n: caT[:, k, off:off + n], lambda ti: allk("R2", ti), [TL], set3, ev_pw2, wbuf)
            if K_STOP <= 5:
                break
            norm_stage("nml", l, hT, "R1")
            for hg in range(4):
                def ev_up(oi, ti, off, n, bank):
                    plan.op("act", lambda e: e.activation(out=zT[:, oi, off:off + n], in_=psum[:, bank, 0:n],
                                                           func=AF.Relu),
                            reads=[("ps", bank)], writes=[("R2", oi, ti)])
                    plan.op("dve", lambda e: e.tensor_tensor(out=zT[:, oi, off:off + n], in0=zT[:, oi, off:off + n],
                                                              in1=zT[:, oi, off:off + n], op=ALU.mult),
                            reads=[("R2", oi, ti)], writes=[("R2", oi, ti)])
                proj(D["w_up"][l][:, hg * 1024:(hg + 1) * 1024], 8, [[(0, 0, 512)], [(0, 512, 512)]], [4, 4],
                     lambda k, off, n: hT[:, k, off:off + n], lambda ti: allk("R1", ti), [TL], set3, ev_up, wbuf)

                def ev_dn(oi, ti, off, n, bank):
                    plan.op("dve", lambda e: e.tensor_tensor(out=xT[:, oi, off:off + n], in0=psum[:, bank, 0:n],
                                                              in1=xT[:, oi, off:off + n], op=ALU.add),
                            reads=[("ps", bank), ("xT", oi, ti)], writes=[("xT", oi, ti)])
                proj(D["w_dn"][l][hg * 1024:(hg + 1) * 1024, :], 8, [[(0, 0, 512)], [(0, 512, 512)]], [4, 4],
                     lambda k, off, n: zT[:, k, off:off + n], lambda ti: allk("R2", ti), [TL], set3, ev_dn, wbuf)
        if K_STOP <= 6:
            continue
        norm_stage("kvg", 0, hT, "R1")

        def ev_k(oi, ti, off, n, bank, c0=c0):
            sgb = sig[ti].bitcast(BF16)
            plan.op("act", lambda e: e.activation(out=sgb[:, 0:n], in_=psum[:, bank, 0:n], func=AF.Square),
                    reads=[("ps", bank)], writes=[("sig", ti)])
            b2 = bankS()
            plan.op("pe", lambda e: e.matmul(psum[:, b2, 0:n], lhsT=oneshb, rhs=sgb[:, 0:n], start=True, stop=True),
                    reads=[("sig", ti), "oneshb"], writes=[("ps", b2)])
            rsqrt_chain(ti, n, b2, "k")
            plan.op("dve", lambda e: e.scalar_tensor_tensor(
                out=kf[ti][:, 0:n], in0=psum[:, bank, 0:n], scalar=P("kng"), in1=stA[ti][:, 0:n], op0=ALU.mult,
                op1=ALU.mult), reads=[("ps", bank), ("stA", ti % 2), "prm"], writes=[("kf", ti % 2)])
            plan.op("act", lambda e: e.activation(out=kb[ti][:, 0:n], in_=kf[ti][:, 0:n], func=AF.Copy),
                    reads=[("kf", ti % 2)], writes=[("kb", ti % 2)])
            plan.dma("sp", D["kT"][:, oi, c0 + off:c0 + off + n], kf[ti][:, 0:n], reads=[("kf", ti % 2)])
            plan.dma("sp", D["KTs"][oi][:, c0 + off:c0 + off + n], kb[ti][:, 0:n], reads=[("kb", ti % 2)], writes=["KTs"])
        proj(D["w_kv"][:, 0:512], 8, [[(0, 0, 512)]], [4], lambda k, off, n: hT[:, k, off:off + n],
             lambda ti: allk("R1", ti), [TL], set3, ev_k, wbuf)
        if K_STOP <= 7:
            continue
        s = wslot()
        plan.dma("pool", wbuf[s], D["w_kv"].rearrange("(k p) o -> p k o", p=128)[:, :, 512:1024], writes=[("wb", s)])
        for tb in range(NST // 128):
            b = bank6()
            ti = tb // 4
            plan.op("pe", lambda e, b=b, tb=tb, s=s: mm_group(
                e, [(b, 0, 512)], 8, lambda k: hT[:, k, tb * 128:(tb + 1) * 128], lambda k, o_, n_: wbuf[s][:, k, :]),
                reads=allk("R1", ti) + [("wb", s)], writes=[("ps", b)])
            i = tb % 2
            plan.op("act", lambda e, b=b, i=i: e.activation(out=vf[i], in_=psum[:, b, :], func=AF.Copy),
                    reads=[("ps", b)], writes=[("kf", i)])
            plan.op("dve", lambda e, i=i: e.tensor_copy(out=vb[i], in_=vf[i]),
                    reads=[("kf", i)], writes=[("kb", i)])
            r0 = c0 + tb * 128
            plan.dma("sp", D["vN"][r0:r0 + 128, :], vf[i], reads=[("kf", i)])
            plan.dma("sp", D["Vs"][r0:r0 + 128, :], vb[i], reads=[("kb", i)], writes=["Vs"])
        if K_STOP <= 8:
            continue
        if st == 0:
            plan.dma("sp", D["xs"][:, :, 0:64], xT[:, :, 0:64], reads=allk("xT", 0), writes=["xs"])
        elif st == 1:
            plan.dma("sp", D["xs"][:, :, 64:704], xT[:, :, 768:1408], reads=allk("xT", 1) + allk("xT", 2), writes=["xs"])
        else:
            plan.dma("sp", D["xs"][:, :, 704:2112], xT[:, :, 0:1408],
                     reads=allk("xT", 0) + allk("xT", 1) + allk("xT", 2), writes=["xs"])
        if K_STOP <= 9:
            continue
        if st in (0, 2):
            for l in range(2):
                src = usampf[l] if st == 0 else utailf[l]
                w = 64 if st == 0 else 30
                bA, bB = bankS(), bankS()

                def tr(e, src=src, w=w, bA=bA, bB=bB):
                    ins = None
                    for k in range(8):
                        bk = bA if k < 4 else bB
                        ins = e.transpose(psum[0:w, bk, (k % 4) * 128:(k % 4 + 1) * 128], src[:, k, :], identf)
                    return ins
                rk = [("usampf", l, c) for c in range(8)] if st == 0 else [("utailf", l, c) for c in range(8)]
                plan.op("pe", tr, reads=rk + ["identf"], writes=[("ps", bA), ("ps", bB)])
                plan.op("act", lambda e, w=w, bA=bA: e.activation(out=tstage[0:w, 0:512], in_=psum[0:w, bA, :],
                                                                   func=AF.Copy),
                        reads=[("ps", bA)], writes=["sq"])
                plan.op("act", lambda e, w=w, bB=bB: e.activation(out=tstage[0:w, 512:1024], in_=psum[0:w, bB, :],
                                                                   func=AF.Copy),
                        reads=[("ps", bB)], writes=["sq"])
                if st == 0:
                    for s_ in range(16):
                        plan.dma("sp", D["cs"][l, s_, 26:30, :], tstage[4 * s_:4 * s_ + 4, :],
                                 reads=["sq"])
                    plan.dma("sp", D["cs"][l, :, 0:26, :], D["scN"][l, :, 4:30, :])
                else:
                    plan.dma("sp", D["cp"][l], tstage[0:30, :], reads=["sq"])

    plan.barrier()
    if K_STOP > 10:
        bb = Bump(PERSIST_END)
        wbufB = [bb([8, 512], BF16) for _ in range(3)]
        sA = [bb([512], F32) for _ in range(2)]
        sB = [bb([512], F32) for _ in range(2)]
        sC = [bb([512], F32) for _ in range(4)]
        hB = bb([8, TB], BF16)
        X0 = bb.off
        xB = bb([8, TB], F32)
        Z0 = bb.off
        zB = bb([8, TB], BF16)
        S0 = bb.off
        sqB = bb([8, 512], F32)
        sqBb = sqB.rearrange("p a b -> p (a b)").bitcast(BF16)[:, 0:4096].rearrange("p (a b) -> p a b", a=8)
        ab = Bump(X0)
        QT = ab([3, TB], BF16)
        KT = [ab([TA], BF16) for _ in range(2)]
        Vg = [ab([69, 128], BF16) for _ in range(2)]
        ex2 = [ab([512], BF16) for _ in range(2)]
        assert ab.off <= Z0
        ab = Bump(Z0)
        oT = ab([4, TB], BF16)
        Em = ab([3, 512], BF16)
        ex = [ab([512], BF16) for _ in range(2)]
        ex += ex2
        PT = [ab([512], BF16) for _ in range(2)]
        QTs = ab([3, 4, 64], BF16)
        KTsamp = ab([4, 64], BF16)
        PT += [ab([512], BF16) for _ in range(2)]
        assert ab.off <= S0
        ab = Bump(X0)
        CB = [ab([9, 1024], BF16) for _ in range(2)]
        Qtm = ab([3, 512], BF16)
        Vnew = [ab([512], BF16) for _ in range(2)]
        sraw = ab([3, 4, 8], F32)
        pexp = ab([3, 4, 8], F32)
        pn1 = ab([2, 48], F32)
        pm = ab([6, 4, 8], BF16)
        pz = ab([2, 4, 4, 8], BF16)
        pD = ab([32], BF16)
        pns = ab([32], BF16)
        prod = [ab([3, 512], F32) for _ in range(2)]
        rDs = ab([1], F32)
        osel = ab([512], BF16)
        assert ab.off <= Z0
        ab = Bump(S0)
        accN = ab([2048], F32)
        accD = ab([2048], F32)

        TLB = [(0, 0, 64), (1, 64, 512), (2, 576, 512), (3, 1088, 512), (4, 1600, 512)]
        TS = [TLB[0:3], TLB[3:5]]
        setB = RR([0, 1, 2, 3, 4, 5])
        setAtt = RR([0, 1, 2, 3, 4])
        statQ = RR([5, 6, 7])
        sBq = sA + sB
        bank4 = RR([0, 1, 2, 3])
        bankN = RR([4, 5])
        bankD = RR([6, 7])
        statB = RR([6, 7])

        def xk(ti):
            return allk("xB", ti)

        def normB(gname, gidx):
            for (t, off, n) in TLB:
                plan.op("act", lambda e, off=off, n=n: e.activation(out=sqBb[:, :, 0:n], in_=xB[:, :, off:off + n],
                                                                     func=AF.Square), reads=xk(t), writes=["sqB"])
                b = statB()
                plan.op("pe", lambda e, b=b, n=n: mm_group(e, [(b, 0, n)], 8, lambda k: onesb16,
                                                             lambda k, o_, n_: sqBb[:, k, 0:n_]),
                        reads=["sqB", "onesb16"], writes=[("ps", b)])
                i = t % 2
                plan.op("act", lambda e, i=i, n=n, b=b: e.activation(out=sB[i][:, 0:n], in_=psum[:, b, 0:n],
                                                                      func=AF.Ln, bias=P("eps"), scale=1.0),
                        reads=[("ps", b), "prm"], writes=[("sB", i)])
                plan.op("act", lambda e, i=i, n=n: e.activation(out=sA[i][:, 0:n], in_=sB[i][:, 0:n], func=AF.Exp,
                                                                 scale=-0.5),
                        reads=[("sB", i)], writes=[("sA", i)])
                for k in range(8):
                    plan.op("dve", lambda e, k=k, i=i, off=off, n=n: e.scalar_tensor_tensor(
                        out=hB[:, k, off:off + n], in0=xB[:, k, off:off + n], scalar=P(gname, gidx * 8 + k),
                        in1=sA[i][:, 0:n], op0=ALU.mult, op1=ALU.mult),
                        reads=[("xB", k, t), ("sA", i), "prm"], writes=[("hB", k, t)])

        EsM = smc[:, 0:192].rearrange("p (j t h) -> p j t h", j=6, t=4)
        sel32 = smc[0:32, 192:704]
        for (t, off, n) in TLB:
            plan.dma("sp", xB[:, :, off:off + n], D["xs"][:, :, off:off + n], reads=["xs"], writes=xk(t))
        for lb in range(2):
            L = 2 + lb
            normB("nmg", L)
            if lb == 1:
                for (t, off, n) in TLB:
                    plan.dma("sp", D["xs"][:, :, off:off + n], xB[:, :, off:off + n], reads=xk(t), writes=["xs"])
            if K_DBG and lb == 0:
                plan.dma("sp", D["dbg_h"], hB, reads=[x for t in range(5) for x in allk("hB", t)])
            plan.barrier()
            for hp_ in range(4):
                plan.dma("sp", KTsamp[:, hp_, :], D["KTs"][hp_][:, 0:64], reads=["KTs"], writes=["KTsamp"])
            for hp in range(4):
                kv = hp % 2
                plan.dma("sp", KT[kv], D["KTs"][hp], reads=["KTs"], writes=[("KT", kv)])
                Vs = D["Vs"]
                c_lo = hp * 128
                plan.dma("sp", Vg[kv][:, 0:17, :],
                         Vs[OWN0 - 128:OWN0 + 2048, c_lo:c_lo + 128].rearrange("(b i) c -> i b c", i=128),
                         reads=["Vs"], writes=[("Vg", kv)])
                for r in range(4):
                    plan.dma("sp", Vg[kv][:, 17 + 5 * r:22 + 5 * r, :],
                             Vs[OWN0 - 512 + r:OWN0 + 2048:4, c_lo:c_lo + 128].rearrange("(b i) c -> i b c", i=128),
                             reads=["Vs"], writes=[("Vg", kv)])
                for r in range(16):
                    plan.dma("sp", Vg[kv][:, 37 + 2 * r:39 + 2 * r, :],
                             Vs[OWN0 - 2048 + r:OWN0 + 2048:16, c_lo:c_lo + 128].rearrange("(b i) c -> i b c", i=128),
                             reads=["Vs"], writes=[("Vg", kv)])
                plan.dma("pool", Em, D["emask"][:, hp:12:4, :], writes=["Em"])

                def ev_q(oi, ti, off, n, bank, lb=lb):
                    i = ti % 4
                    scb = sC[i].bitcast(BF16)
                    plan.op("act", lambda e: e.activation(out=scb[:, 0:n], in_=psum[:, bank, 0:n], func=AF.Square),
                            reads=[("ps", bank)], writes=[("sC", i)])
                    def late():
                        b2 = statQ()
                        plan.op("pe", lambda e: e.matmul(psum[:, b2, 0:n], lhsT=oneshb, rhs=scb[:, 0:n], start=True,
                                                          stop=True), reads=[("sC", i), "oneshb"], writes=[("ps", b2)])
                        j = ti % 4
                        plan.op("act", lambda e: e.activation(out=sBq[j][:, 0:n], in_=psum[:, b2, 0:n], func=AF.Ln,
                                                               bias=P("eps"), scale=1.0),
                                reads=[("ps", b2), "prm"], writes=[("sBq", j)])
                        plan.op("act", lambda e: e.activation(out=sBq[j][:, 0:n], in_=sBq[j][:, 0:n], func=AF.Exp,
                                                               scale=-0.5),
                                reads=[("sBq", j)], writes=[("sBq", j)])
                        plan.op("dve", lambda e: e.scalar_tensor_tensor(
                            out=QT[:, oi, off:off + n], in0=psum[:, bank, 0:n], scalar=qg8[:, lb:lb + 1],
                            in1=sBq[j][:, 0:n], op0=ALU.mult, op1=ALU.mult),
                            reads=[("ps", bank), ("sBq", j), "qg8"], writes=["QT"])
                    return late
                proj(D["w_q"][lb], 8, [[(g * 128, (g * 4 + hp) * 128, 128) for g in range(3)]], [3],
                     lambda k, off, n: hB[:, k, off:off + n], lambda ti: allk("hB", ti), TS, setAtt, ev_q, wbufB)
                if K_DBG and lb == 0:
                    plan.dma("sp", D["dbg_q"][hp], QT, reads=["QT"])
                plan.op("act", lambda e, hp=hp: e.activation(out=QTs[:, :, hp, :], in_=QT[:, :, 0:64], func=AF.Copy),
                        reads=["QT"], writes=["QTs"])
                units = []
                for g, (w_, d) in enumerate(WINDOWS):
                    qbs = [(r, c) for r in range(d) for c in range(16 // d)]
                    for q4 in range(4):
                        for pr in range(2):
                            units.append((g, d, q4, pr, qbs[q4 * 4 + pr * 2:q4 * 4 + pr * 2 + 2]))
                nd_banks = {}

                def front(ui, kv=kv, units=units, nd_banks=nd_banks):
                    g, d, q4, pr, pair = units[ui]
                    if pr == 0:
                        nd_banks[(g, q4)] = (bankN(), bankD())
                    bX, bY = bank4(), bank4()
                    par = ui % 2

                    def st_mm(e):
                        ins = None
                        for qi, (r, c) in enumerate(pair):
                            q0 = 64 + r + d * 128 * c
                            for pc in range(2):
                                k0 = OWN0 + r + d * 128 * (c - 1 + pc)
                                for h, bk in ((0, bX), (1, bY)):
                                    ins = e.matmul(psum[:, bk, (qi * 2 + pc) * 128:(qi * 2 + pc + 1) * 128],
                                                   lhsT=KT[kv][h * 64:(h + 1) * 64, k0:k0 + 127 * d + 1:d],
                                                   rhs=QT[h * 64:(h + 1) * 64, g, q0:q0 + 127 * d + 1:d],
                                                   start=True, stop=True, tile_position=(h * 64, 0))
                        return ins
                    plan.op("pe", st_mm, reads=[("KT", kv), "QT"], writes=[("ps", bX), ("ps", bY)])
                    for h, bk in ((0, bX), (1, bY)):
                        hx = par * 2 + h
                        plan.op("act", lambda e, hx=hx, bk=bk: e.activation(out=ex[hx], in_=psum[:, bk, :], func=AF.Exp),
                                reads=[("ps", bk)], writes=[("ex", hx)])
                        for qi, (r, c) in enumerate(pair):
                            if c == 0:
                                plan.op("act", lambda e, hx=hx, bk=bk, qi=qi: e.activation(
                                    out=ex[hx][:, qi * 256:qi * 256 + 128], in_=psum[:, bk, qi * 256:qi * 256 + 128],
                                    func=AF.Exp, bias=P("hbias"), scale=1.0),
                                    reads=[("ps", bk), "prm"], writes=[("ex", hx)])
                        plan.op("dve", lambda e, h=h, g=g, hx=hx: e.tensor_tensor(
                            out=PT[hx].rearrange("p (a b) -> p a b", a=2),
                            in0=ex[hx].rearrange("p (a b) -> p a b", a=2),
                            in1=Em[:, g, h * 256:(h + 1) * 256].unsqueeze(1).broadcast_to([128, 2, 256]),
                            op=ALU.mult), reads=[("ex", hx), "Em"], writes=[("PT", hx)])

                def back(ui, kv=kv, units=units, nd_banks=nd_banks):
                    g, d, q4, pr, pair = units[ui]
                    bN, bD = nd_banks[(g, q4)]
                    par = ui % 2
                    vbase = [0, 17, 37][g]
                    nper = 16 // d + 1

                    def pv_mm(e):
                        ins = None
                        for qi, (r, c) in enumerate(pair):
                            sl = (pr * 2 + qi) * 128
                            for h in range(2):
                                for pc in range(2):
                                    rhs = PT[par * 2 + h][:, (qi * 2 + pc) * 128:(qi * 2 + pc + 1) * 128]
                                    e.matmul(psum[h * 64:(h + 1) * 64, bN, sl:sl + 128],
                                             lhsT=Vg[kv][:, vbase + r * nper + c + pc, h * 64:(h + 1) * 64],
                                             rhs=rhs, start=(pc == 0), stop=(pc == 1))
                                    ins = e.matmul(psum[h * 64:(h + 1) * 64, bD, sl:sl + 128],
                                                   lhsT=ones64, rhs=rhs, start=(pc == 0), stop=(pc == 1))
                        return ins
                    plan.op("pe", pv_mm, reads=[("PT", par * 2), ("PT", par * 2 + 1), ("Vg", kv), "ones64"],
                            writes=[("ps", bN), ("ps", bD)])
                    if pr == 0:
                        return
                    if g == 0:
                        cs_ = slice(q4 * 512, q4 * 512 + 512)
                        dstN, dstD = accN[:, cs_], accD[:, cs_]
                        srcN, srcD = psum[:, bN, :], psum[:, bD, :]
                    elif g == 1:
                        dstN, dstD = accN[:, q4:2048:4], accD[:, q4:2048:4]
                        srcN, srcD = psum[:, bN, :], psum[:, bD, :]
                    else:
                        dstN = accN.rearrange("p (i r) -> p r i", r=16)[:, q4 * 4:q4 * 4 + 4, :]
                        dstD = accD.rearrange("p (i r) -> p r i", r=16)[:, q4 * 4:q4 * 4 + 4, :]
                        srcN = psum[:, bN, :].rearrange("p (a b) -> p a b", a=4)
                        srcD = psum[:, bD, :].rearrange("p (a b) -> p a b", a=4)
                    if g == 0:
                        plan.op("act", lambda e: e.activation(out=dstN, in_=srcN, func=AF.Copy),
                                reads=[("ps", bN)], writes=["accN"])
                        plan.op("dve", lambda e: e.tensor_copy(out=dstD, in_=srcD), reads=[("ps", bD)], writes=["accD"])
                    else:
                        plan.op("dve", lambda e: e.tensor_tensor(out=dstN, in0=srcN, in1=dstN, op=ALU.add),
                                reads=[("ps", bN), "accN"], writes=["accN"])
                        plan.op("dve", lambda e: e.tensor_tensor(out=dstD, in0=srcD, in1=dstD, op=ALU.add),
                                reads=[("ps", bD), "accD"], writes=["accD"])
                front(0)
                for ui in range(len(units)):
                    if ui + 1 < len(units):
                        front(ui + 1)
                    back(ui)
                if K_DBG and lb == 0:
                    plan.dma("sp", D["dbg_n"][hp], accN, reads=["accN"])
                    plan.dma("sp", D["dbg_d"][hp], accD, reads=["accD"])
                plan.op("act", lambda e: e.activation(out=accD, in_=accD, func=AF.Ln), reads=["accD"], writes=["accD"])
                plan.op("act", lambda e: e.activation(out=accD, in_=accD, func=AF.Exp, scale=-1.0),
                        reads=["accD"], writes=["accD"])
                plan.op("dve", lambda e, hp=hp: e.tensor_tensor(out=oT[:, hp, 64:TB], in0=accN, in1=accD, op=ALU.mult),
                        reads=["accN", "accD"], writes=["oTp"])
            plan.barrier()
            plan.op("dve", lambda e: e.memset(pm, 0.0), writes=["pm"])
            plan.op("dve", lambda e: e.memset(pz, 0.0), writes=["pz"])
            for i_ in range(2):
                plan.op("dve", lambda e, i_=i_: e.memset(Vnew[i_], 0.0), writes=[("Vnew", i_)])
            for g in range(3):
                bq = bank4()
                plan.op("pe", lambda e, g=g, bq=bq: [e.transpose(
                    psum[0:64, bq, hp_ * 64:(hp_ + 1) * 64].bitcast(BF16), QTs[:, g, hp_, :], identb)
                    for hp_ in range(4)][-1], reads=["QTs", "identb"], writes=[("ps", bq)])
                plan.op("act", lambda e, g=g, bq=bq: e.activation(out=Qtm[0:64, g, :],
                                                                   in_=psum[0:64, bq, 0:256].bitcast(BF16), func=AF.Copy),
                        reads=[("ps", bq)], writes=["Qtm"])
            qsets = RR([(0, 1, 2), (3, 4, 5)])
            for sq_ in range(16):
                cb = sq_ % 2
                C = D["cache"][sq_]
                plan.dma("pool", CB[cb][:, 0, :], C[1920:2048, :], writes=[("CB", cb)])
                plan.dma("pool", CB[cb][:, 1:5, :], C[1536:2048, :].rearrange("(m t) f -> m t f", t=4),
                         writes=[("CB", cb)])
                plan.dma("pool", CB[cb][:, 5:9, :], C.rearrange("(m t) f -> m t f", t=16)[:, 0:4, :],
                         writes=[("CB", cb)])
                plan.dma("sp", Vnew[cb][0:4, :], D["Vs"][4 * sq_:4 * sq_ + 4, :], reads=["Vs"], writes=[("Vnew", cb)])
                c4 = slice(4 * sq_, 4 * sq_ + 4)
                def snew(e, c4=c4):
                    ins = None
                    for g in range(3):
                        for hp_ in range(4):
                            for par in range(2):
                                ins = e.matmul(psum[0:4, 6 + par, (g * 4 + hp_) * 4:(g * 4 + hp_) * 4 + 4],
                                               lhsT=KTsamp[par * 64:(par + 1) * 64, hp_, c4],
                                               rhs=QTs[par * 64:(par + 1) * 64, g, hp_, c4],
                                               start=True, stop=True, tile_position=(par * 64, 0))
                    return ins
                plan.op("pe", snew, reads=["KTsamp", "QTs"], writes=[("ps", 6), ("ps", 7)])
                for par in range(2):
                    plan.op("act", lambda e, par=par: e.activation(out=pn1[0:4, par, :], in_=psum[0:4, 6 + par, 0:48],
                                                                    func=AF.Exp), reads=[("ps", 6 + par)], writes=[("pn1", par)])
                    plan.op("dve", lambda e, par=par: e.tensor_tensor(
                        out=pm[0:4, 3:6, :, par:8:2],
                        in0=pn1[0:4, par, :].rearrange("p (g h t) -> p g t h", g=3, h=4),
                        in1=EsM[0:4, 3:6, :, par:8:2], op=ALU.mult),
                        reads=[("pn1", par), "smc"], writes=["pm"])
                for t in range(4):
                    qb_ = qsets()
                    plan.op("pe", lambda e, t=t, qb_=qb_, sq_=sq_: [e.matmul(
                        psum[:, qb_[g], :], lhsT=identb[0:64, 4 * sq_ + t:4 * sq_ + t + 1].broadcast_to([64, 128]),
                        rhs=Qtm[0:64, g, :], start=True, stop=True) for g in range(3)][-1],
                        reads=["Qtm", "identb"], writes=[("ps", b_) for b_ in qb_])
                    pi = t % 2
                    for g in range(3):
                        blk = 0 if g == 0 else (1 + t if g == 1 else 5 + t)
                        plan.op("dve", lambda e, g=g, blk=blk, pi=pi, qb_=qb_, cb=cb: e.tensor_tensor(
                            out=prod[pi][:, g, :], in0=CB[cb][:, blk, 0:512], in1=psum[:, qb_[g], :], op=ALU.mult),
                            reads=[("CB", cb), ("ps", qb_[g])], writes=[("prod", pi, g)])
                    plan.op("dve", lambda e, t=t, pi=pi: e.tensor_reduce(
                        out=sraw[:, :, t, :], in_=prod[pi].rearrange("p g (h d) -> p g h d", d=64), axis=AX.X,
                        op=ALU.add), reads=[("prod", pi, g_) for g_ in range(3)], writes=["sraw"])
                plan.op("act", lambda e: e.activation(out=pexp, in_=sraw, func=AF.Exp), reads=["sraw"], writes=["pexp"])
                plan.op("dve", lambda e: e.tensor_tensor(out=pm[:, 0:3, :, :], in0=pexp, in1=EsM[:, 0:3, :, :],
                                                          op=ALU.mult), reads=["pexp", "smc"], writes=["pm"])
                plan.op("dve", lambda e: e.tensor_copy(
                    out=pz.rearrange("p g a b h -> p g (a b) h")[:, :, 0:16:5, :], in_=pm[:, 1:3, :, :]),
                    reads=["pm"], writes=["pz"])
                plan.op("dve", lambda e: e.tensor_reduce(
                    out=pD, in_=pm.rearrange("p j t h -> p (t h) j"), axis=AX.X, op=ALU.add),
                    reads=["pm"], writes=["pD"])
                plan.op("dve", lambda e: e.tensor_reduce(
                    out=pns, in_=pm[:, 3:6, :, :].rearrange("p j t h -> p (t h) j"), axis=AX.X, op=ALU.add),
                    reads=["pm"], writes=["pns"])

                def pvs(e, cb=cb):
                    e.matmul(psum[0:32, 6, :], lhsT=pm[:, 0, :, :].rearrange("p t h -> p (t h)"),
                             rhs=CB[cb][:, 0, 512:1024], start=True, stop=False)
                    for g in (1, 2):
                        for t in range(4):
                            blk = 1 + t if g == 1 else 5 + t
                            e.matmul(psum[0:32, 6, :], lhsT=pz[:, g - 1, t, :, :].rearrange("p a h -> p (a h)"),
                                     rhs=CB[cb][:, blk, 512:1024], start=False, stop=False)
                    e.matmul(psum[0:32, 6, :], lhsT=pns, rhs=Vnew[cb], start=False, stop=True)
                    return e.matmul(psum[0:32, 7, 0:1], lhsT=pD, rhs=ones64[:, 0:1], start=True, stop=True)
                plan.op("pe", pvs, reads=["pm", "pz", "pns", "pD", ("CB", cb), ("Vnew", cb), "ones64"],
                        writes=[("ps", 6), ("ps", 7)])
                plan.op("dve", lambda e: e.reciprocal(out=rDs[0:32, :], in_=psum[0:32, 7, 0:1]),
                        reads=[("ps", 7)], writes=["rDs"])
                plan.op("dve", lambda e: e.scalar_tensor_tensor(out=osel[0:32, :], in0=psum[0:32, 6, :],
                                                                 scalar=rDs[0:32, :], in1=sel32, op0=ALU.mult,
                                                                 op1=ALU.mult),
                        reads=[("ps", 6), "rDs", "smc"], writes=["osel"])
                plan.op("pe", lambda e: [e.matmul(psum[:, 7, 8 + 4 * hp_:12 + 4 * hp_],
                                                  lhsT=osel[0:32, hp_ * 128:(hp_ + 1) * 128], rhs=tselb[0:32, :],
                                                  start=True, stop=True) for hp_ in range(4)][-1],
                        reads=["osel", "tselb"], writes=[("ps", 7)])
                plan.op("act", lambda e, c4=c4: e.activation(
                    out=oT[:, :, c4], in_=psum[:, 7, 8:24].rearrange("p (a b) -> p a b", a=4), func=AF.Copy),
                    reads=[("ps", 7)], writes=["oTs"])
            if K_DBG and lb == 0:
                plan.dma("sp", D["dbg_o"], oT, reads=["oTs", "oTp"])
                plan.dma("sp", D["dbg_sraw"], sraw.rearrange("p a b c -> p (a b c)"), reads=["sraw"])
                plan.dma("sp", D["dbg_pm"], pm.rearrange("p a b c -> p (a b c)"), reads=["pm"])
                plan.dma("sp", D["dbg_osel"], osel, reads=["osel"])
                plan.dma("sp", D["dbg_qtm"], Qtm.rearrange("p a b -> p (a b)"), reads=["Qtm"])
            plan.barrier()
            if K_STOP <= 11:
                break
            for (t, off, n) in TLB:
                plan.dma("sp", xB[:, :, off:off + n], D["xs"][:, :, off:off + n], reads=["xs"], writes=xk(t))

            def ev_add(oi, ti, off, n, bank):
                plan.op("dve", lambda e: e.tensor_tensor(out=xB[:, oi, off:off + n], in0=psum[:, bank, 0:n],
                                                          in1=xB[:, oi, off:off + n], op=ALU.add),
                        reads=[("ps", bank), ("xB", oi, ti)], writes=[("xB", oi, ti)])
            proj(D["w_o"][lb], 4, [[(0, 0, 512)], [(0, 512, 512)]], [4, 4],
                 lambda k, off, n: oT[:, k, off:off + n], lambda ti: ["oTs", "oTp"], TS, setB, ev_add, wbufB)
            normB("nml", L)
            for hg in range(4):
                def ev_upB(oi, ti, off, n, bank):
                    plan.op("act", lambda e: e.activation(out=zB[:, oi, off:off + n], in_=psum[:, bank, 0:n],
                                                           func=AF.Relu), reads=[("ps", bank)], writes=[("zB", oi, ti)])
                    plan.op("dve", lambda e: e.tensor_tensor(out=zB[:, oi, off:off + n], in0=zB[:, oi, off:off + n],
                                                              in1=zB[:, oi, off:off + n], op=ALU.mult),
                            reads=[("zB", oi, ti)], writes=[("zB", oi, ti)])
                proj(D["w_up"][L][:, hg * 1024:(hg + 1) * 1024], 8, [[(0, 0, 512)], [(0, 512, 512)]], [4, 4],
                     lambda k, off, n: hB[:, k, off:off + n], lambda ti: allk("hB", ti), TS, setB, ev_upB, wbufB)
                proj(D["w_dn"][L][hg * 1024:(hg + 1) * 1024, :], 8, [[(0, 0, 512)], [(0, 512, 512)]], [4, 4],
                     lambda k, off, n: zB[:, k, off:off + n], lambda ti: allk("zB", ti), TS, setB, ev_add, wbufB)
        for (t, off, n) in TLB:
            plan.dma("sp", D["yT"][:, :, off:off + n], xB[:, :, off:off + n], reads=xk(t))

    plan.barrier()
    replay(nc, plan)
    return nc


_NC_CACHE = {}


def _feat_major(v):
    v = np.asarray(v, np.float32)
    lead = v.shape[:-1]
    return np.ascontiguousarray(np.moveaxis(v.reshape(lead + (8, 128)), -1, 0))


def _emask():
    n = 24
    slopes = (2.0 ** (-8.0 * np.arange(1, n + 1, dtype=np.float64) / n)).reshape(3, 8)
    k = np.arange(128)[:, None]
    q = np.arange(128)[None, :]
    out = np.zeros((128, 3, 4, 2, 2, 128), np.float64)
    for g, (w, d) in enumerate(WINDOWS):
        for h in range(8):
            sl = slopes[g, h] * d
            prev = np.where(q <= k, np.exp(-sl * (q - k + 128)), 0.0)
            cur = np.where(q >= k, np.exp(-sl * (q - k)), 0.0)
            out[:, g, h // 2, h % 2, 0, :] = prev
            out[:, g, h // 2, h % 2, 1, :] = cur
    return out.reshape(128, 12, 512).astype(np.float32)


def _smc():
    slopes = (2.0 ** (-8.0 * np.arange(1, 25, dtype=np.float64) / 24)).reshape(3, 8)
    m = np.arange(128, dtype=np.float64)[:, None, None]
    t = np.arange(4, dtype=np.float64)[None, :, None]
    E = np.zeros((128, 6, 4, 8))
    d0 = 128 + t - m
    E[:, 0] = np.where(m >= t, np.exp(-slopes[0][None, None, :] * d0), 0.0)
    E[:, 1] = np.exp(-slopes[1][None, None, :] * 4 * (128 - m)) * np.ones((1, 4, 1))
    E[:, 2] = np.exp(-slopes[2][None, None, :] * 16 * (128 - m)) * np.ones((1, 4, 1))
    E[:, 3] = np.where((m <= t) & (m < 4), np.exp(-slopes[0][None, None, :] * (t - m)), 0.0)
    E[:, 4] = np.where(m == t, 1.0, 0.0) * np.ones((1, 1, 8))
    E[:, 5] = E[:, 4]
    out = np.zeros((128, 708), np.float32)
    out[:, 0:192] = E.reshape(128, 192)
    r = np.arange(32)
    f = np.arange(512)
    out[0:32, 192:704] = ((f[None, :] // 64) == (r[:, None] % 8)).astype(np.float32)
    out[0:32, 704:708] = ((r[:, None] // 8) == np.arange(4)[None, :]).astype(np.float32)
    return out


def kernel(x_prompt, x_sample, state_conv, cache_kv, norm_mix_g, norm_mlp_g, conv_w_pw1, conv_b_pw1,
           conv_w_dw, conv_b_dw, conv_ln_g, conv_ln_b, conv_w_pw2, conv_b_pw2, kv_norm_g, w_kv, k_norm_g,
           attn_w_q, q_norm_g, attn_w_o, mlp_w_up, mlp_w_down):
    f = lambda v: np.ascontiguousarray(np.asarray(v, np.float32))
    if "nc" not in _NC_CACHE:
        t0 = time.time()
        _NC_CACHE["nc"] = build_program()
        print("build time", time.time() - t0)
    nc = _NC_CACHE["nc"]
    x_prompt, x_sample, state_conv, cache_kv = f(x_prompt), f(x_sample), f(state_conv), f(cache_kv)
    prm = np.zeros((128, NPRM), np.float32)

    def put(name, arr):
        arr = np.asarray(arr, np.float32).reshape(128, -1)
        prm[:, PRM[name]:PRM[name] + arr.shape[1]] = arr
    put("nmg", _feat_major(norm_mix_g))
    put("nml", _feat_major(norm_mlp_g))
    put("bpw1", np.moveaxis(np.asarray(conv_b_pw1, np.float32).reshape(2, 16, 128), -1, 0))
    put("wdw", np.moveaxis(np.asarray(conv_w_dw, np.float32).reshape(2, 31, 8, 128), (3, 0, 2, 1), (0, 1, 2, 3)))
    put("bdw", _feat_major(conv_b_dw))
    put("lng", _feat_major(conv_ln_g))
    put("lnb", _feat_major(conv_ln_b))
    put("bpw2", _feat_major(conv_b_pw2))
    put("kvg", _feat_major(kv_norm_g))
    put("kng", np.tile(np.asarray(k_norm_g, np.float32), 2)[:, None])
    put("qng", np.tile(np.asarray(q_norm_g, np.float32), (1, 2)).T)
    prm[:, PRM["eps"]] = EPS
    ident = np.eye(128, dtype=np.float32)
    onesb = np.kron(np.eye(2, dtype=np.float32), np.full((64, 64), 1.0 / 64, np.float32))
    emask = _emask()
    smc = _smc()
    xpT = [np.ascontiguousarray(x_prompt[b].T).reshape(8, 128, 8192).transpose(1, 0, 2) for b in range(2)]
    shared = dict(ident=ident, onesb=onesb, emask=emask, smc=smc, w_pw1=f(conv_w_pw1), w_pw2=f(conv_w_pw2), w_kv=f(w_kv),
                  w_q=f(attn_w_q), w_o=f(attn_w_o), w_up=f(mlp_w_up), w_dn=f(mlp_w_down))
    in_maps = []
    for c in range(NCORES):
        b, j = c // 4, c % 4
        S = 2048 * j
        xin = np.zeros((128, 8, TA), np.float32)
        xs_ = x_sample[16 * c:16 * c + 16].reshape(64, 1024)
        xin[:, :, 0:64] = xs_.T.reshape(8, 128, 64).transpose(1, 0, 2)
        lo = S - 2112
        src_lo = max(lo, 0)
        xin[:, :, 64 + (src_lo - lo):TA] = xpT[b][:, :, src_lo:S + 2048]
        p = prm.copy()
        p[:, PRM["valid"]] = 1.0 if j > 0 else 0.0
        p[:, PRM["hbias"]] = 0.0 if j > 0 else -30000.0
        sc = state_conv[:, 16 * c:16 * c + 16]
        scT = np.ascontiguousarray(sc.reshape(2, 16, 30, 8, 128).transpose(4, 0, 3, 1, 2))
        m = dict(shared)
        m.update(xin=xin, prm=p, scT=scT, scN=np.ascontiguousarray(sc),
                 cache=np.ascontiguousarray(cache_kv[16 * c:16 * c + 16].reshape(16, 2048, 1024)))
        in_maps.append(m)
    if K_CORE is not None:
        t0 = time.time()
        res = run_bass_kernel_spmd(nc, [in_maps[int(K_CORE)]], core_ids=[0])
        print("run time", time.time() - t0)
        R = [res.results[0]] * NCORES
    else:
        res = run_bass_kernel_spmd(nc, in_maps, core_ids=list(range(NCORES)))
        R = res.results
    y_prompt = np.zeros((2, 8192, 1024), np.float32)
    y_sample = np.zeros((128, 4, 1024), np.float32)
    conv_prompt = np.zeros((2, 2, 30, 1024), np.float32)
    conv_sample = np.zeros((2, 128, 30, 1024), np.float32)
    kv_prompt = np.zeros((2, 2048, 2, 8, 64), np.float32)
    kv_sample = np.zeros((128, 4, 2, 8, 64), np.float32)
    for c in range(NCORES):
        b, j = c // 4, c % 4
        r = R[c]
        yT = np.asarray(r["yT"]).transpose(1, 0, 2).reshape(1024, TB)
        y_sample[16 * c:16 * c + 16] = yT[:, 0:64].T.reshape(16, 4, 1024)
        y_prompt[b, 2048 * j:2048 * j + 2048] = yT[:, 64:].T
        conv_sample[:, 16 * c:16 * c + 16] = np.asarray(r["cs"])
        kT = np.asarray(r["kT"]).transpose(1, 0, 2).reshape(512, TA)
        vN = np.asarray(r["vN"])
        kv_sample[16 * c:16 * c + 16, :, 0] = kT[:, 0:64].T.reshape(16, 4, 8, 64)
        kv_sample[16 * c:16 * c + 16, :, 1] = vN[0:64].reshape(16, 4, 8, 64)
        if j == 3:
            conv_prompt[:, b] = np.asarray(r["cp"])
            kv_prompt[b, :, 0] = kT[:, OWN0:].T.reshape(2048, 8, 64)
            kv_prompt[b, :, 1] = vN[OWN0:].reshape(2048, 8, 64)
    return (y_prompt, y_sample, conv_prompt, conv_sample, kv_prompt, kv_sample)
```

```python
import os
import time
import numpy as np
import concourse.bass as bass
import concourse.mybir as mybir
from concourse.bass_utils import run_bass_kernel_spmd
from contextlib import ExitStack

F32, BF16 = mybir.dt.float32, mybir.dt.bfloat16
AF = mybir.ActivationFunctionType
ALU = mybir.AluOpType
AX = mybir.AxisListType

NCORES = 8
TA = 4224
NST = 1408
TB = 2112
OWN0 = 2176
EPS = 1e-6
ARENA = 51800
WINDOWS = ((128, 1), (512, 4), (2048, 16))
K_STOP = int(os.environ.get('K_STOP', '99'))
K_NST = int(os.environ.get('K_NST', '3'))
K_CORE = os.environ.get('K_CORE')
K_DBG = int(os.environ.get('K_DBG', '0'))

PRM = {}
_o = 0
for _n, _w in (("nmg", 32), ("nml", 32), ("bpw1", 32), ("wdw", 2 * 8 * 31), ("bdw", 16), ("lng", 16),
               ("lnb", 16), ("bpw2", 16), ("kvg", 8), ("kng", 1), ("qng", 2), ("valid", 1), ("hbias", 1),
               ("eps", 1), ("zero", 1)):
    PRM[_n] = _o
    _o += _w
NPRM = _o


class Plan:
    ENG = ("pe", "act", "dve", "pool", "sp")
    NDS = 8

    def __init__(self):
        self.items = {e: [] for e in self.ENG}
        self.cnt = {e: 0 for e in self.ENG}
        self.known = {e: {} for e in self.ENG}
        self.bw = {}
        self.br = {}
        self.dma_n = {}
        self.dma_rr = {e: 0 for e in self.ENG}
        self.sems = set("s_" + e for e in self.ENG)

    def _need(self, eng, tok):
        if tok is None:
            return
        sem, val = tok
        if sem == "s_pe" and eng == "pe":
            return
        if self.known[eng].get(sem, 0) >= val:
            return
        self.known[eng][sem] = val
        self.items[eng].append(("w", sem, val))

    def _deps(self, eng, reads, writes):
        for b in reads:
            self._need(eng, self.bw.get(b))
            if isinstance(b, tuple) and b[0] == "ps":
                for s, v in self.br.get(b, {}).items():
                    if s != "s_" + eng:
                        self._need(eng, (s, v))
        for b in writes:
            self._need(eng, self.bw.get(b))
            for s, v in self.br.get(b, {}).items():
                self._need(eng, (s, v))

    def _mark(self, tok, reads, writes):
        for b in reads:
            d = self.br.setdefault(b, {})
            if d.get(tok[0], 0) < tok[1]:
                d[tok[0]] = tok[1]
        for b in writes:
            self.bw[b] = tok
            self.br[b] = {}

    def op(self, eng, fn, reads=(), writes=()):
        self._deps(eng, reads, writes)
        self.cnt[eng] += 1
        tok = ("s_" + eng, self.cnt[eng])
        self.items[eng].append(("x", fn))
        self._mark(tok, reads, writes)
        return tok

    def dma(self, q, out, in_, reads=(), writes=()):
        i = self.dma_rr[q] % self.NDS
        self.dma_rr[q] += 1
        sem = "d_%s%d" % (q, i)
        self.sems.add(sem)
        n = self.dma_n.get(sem, 0)
        if n:
            self._need(q, (sem, 16 * n))
        self._deps(q, reads, writes)
        self.dma_n[sem] = n + 1
        tok = (sem, 16 * (n + 1))
        self.items[q].append(("d", out, in_, sem))
        self._mark(tok, reads, writes)
        return tok

    def barrier(self):
        for e in self.ENG:
            for o in self.ENG:
                if self.cnt[o]:
                    self._need(e, ("s_" + o, self.cnt[o]))
            for sem, n in self.dma_n.items():
                self._need(e, (sem, 16 * n))


def replay(nc, plan):
    with ExitStack() as es:
        es.enter_context(nc.allow_low_precision("bf16 matmul operands, fp32 accumulation"))
        sems = {s: es.enter_context(nc.semaphore(s)) for s in sorted(plan.sems)}
        block = es.enter_context(nc.Block())

        def run(e, name):
            own = sems["s_" + name]
            for it in plan.items[name]:
                if it[0] == "w":
                    e.wait_ge(sems[it[1]], it[2])
                elif it[0] == "x":
                    it[1](e).then_inc(own, 1)
                else:
                    e.dma_start(out=it[1], in_=it[2]).then_inc(sems[it[3]], 16)

        @block.tensor
        def _(e):
            run(e, "pe")

        @block.scalar
        def _(e):
            run(e, "act")

        @block.vector
        def _(e):
            run(e, "dve")

        @block.gpsimd
        def _(e):
            run(e, "pool")

        @block.sync
        def _(e):
            run(e, "sp")


def build_program():
    nc = bass.Bass("TRN2", target_bir_lowering=False)
    plan = Plan()
    D = {}

    def din(name, shape, dt=F32):
        D[name] = nc.dram_tensor(name, list(shape), dt, kind="ExternalInput").ap()

    def dout(name, shape, dt=F32):
        D[name] = nc.dram_tensor(name, list(shape), dt, kind="ExternalOutput").ap()

    def dscr(name, shape, dt=F32):
        D[name] = nc.dram_tensor(name, list(shape), dt, kind="Internal").ap()

    din("xin", [128, 8, TA])
    din("prm", [128, NPRM])
    din("ident", [128, 128])
    din("onesb", [128, 128])
    din("w_pw1", [2, 1024, 2048])
    din("w_pw2", [2, 1024, 1024])
    din("w_kv", [1024, 1024])
    din("w_q", [2, 1024, 1536])
    din("w_o", [2, 512, 1024])
    din("w_up", [4, 1024, 4096])
    din("w_dn", [4, 4096, 1024])
    din("scT", [128, 2, 8, 16, 30])
    din("scN", [2, 16, 30, 1024])
    din("cache", [16, 2048, 1024])
    din("emask", [128, 12, 512])
    din("smc", [128, 708])
    dout("yT", [128, 8, TB])
    dout("kT", [128, 4, TA])
    dout("vN", [TA, 512])
    dout("cs", [2, 16, 30, 1024])
    dout("cp", [2, 30, 1024])
    if K_DBG:
        dout("dbg_h", [128, 8, TB], BF16)
        dout("dbg_q", [4, 128, 3, TB], BF16)
        dout("dbg_o", [128, 4, TB], BF16)
        dout("dbg_n", [4, 128, 2048])
        dout("dbg_d", [4, 128, 2048])
        dout("dbg_sraw", [128, 96])
        dout("dbg_pm", [128, 192], BF16)
        dout("dbg_osel", [128, 512], BF16)
        dout("dbg_qtm", [128, 1536], BF16)
    dscr("xs", [128, 8, TB])
    dscr("KTs", [4, 128, TA], BF16)
    dscr("Vs", [TA, 512], BF16)

    arena = nc.alloc_sbuf_tensor("arena", [128, ARENA], F32)
    psum = nc.alloc_psum_tensor("psum", [128, 8, 512], F32)

    class Bump:
        def __init__(self, start=0):
            self.off = start

        def __call__(self, shape, dt):
            n = int(np.prod(shape))
            nb = n * (4 if dt == F32 else 2)
            nb = (nb + 31) // 32 * 32
            assert self.off + nb <= ARENA * 4, ("SBUF overflow", self.off + nb)
            a = arena[:, self.off // 4:(self.off + nb) // 4]
            if dt != F32:
                a = a.bitcast(dt)
            a = a[:, 0:n]
            if len(shape) == 2:
                a = a.rearrange("p (a b) -> p a b", a=shape[0])
            elif len(shape) == 3:
                a = a.rearrange("p (a b c) -> p a b c", a=shape[0], b=shape[1])
            elif len(shape) == 4:
                a = a.rearrange("p (a b c d) -> p a b c d", a=shape[0], b=shape[1], c=shape[2])
            self.off += nb
            return a

    pb = Bump(0)
    prm = pb([NPRM], F32)
    identf = pb([128], F32)
    identb = pb([128], BF16)
    onesf = pb([128], F32)
    onesb16 = pb([128], BF16)
    oneshf = pb([128], F32)
    oneshb = pb([128], BF16)
    smc = pb([708], F32)
    tselb = pb([4], BF16)
    qg8 = pb([2], F32)
    ones64 = pb([64], BF16)
    PERSIST_END = pb.off

    def P(name, i=0, w=1):
        o = PRM[name] + i
        return prm[:, o:o + w]

    plan.dma("sp", prm, D["prm"], writes=["prm"])
    plan.dma("sp", identf, D["ident"], writes=["identf"])
    plan.dma("sp", oneshf, D["onesb"], writes=["oneshf"])
    plan.op("dve", lambda e: e.tensor_copy(out=identb, in_=identf), reads=["identf"], writes=["identb"])
    plan.op("dve", lambda e: e.tensor_copy(out=oneshb, in_=oneshf), reads=["oneshf"], writes=["oneshb"])
    plan.op("dve", lambda e: e.memset(onesf, 1.0 / 1024), writes=["onesf"])
    plan.op("dve", lambda e: e.memset(onesb16, 1.0 / 1024), writes=["onesb16"])
    plan.op("dve", lambda e: e.memset(ones64, 1.0), writes=["ones64"])
    plan.dma("sp", smc, D["smc"], writes=["smc"])
    plan.op("dve", lambda e: e.tensor_copy(out=tselb, in_=smc[:, 704:708]), reads=["smc"], writes=["tselb"])
    plan.op("dve", lambda e: e.tensor_scalar(out=qg8, in0=P("qng", 0, 2), scalar1=0.125, scalar2=None, op0=ALU.mult),
            reads=["prm"], writes=["qg8"])

    class RR:
        def __init__(self, items):
            self.items = items
            self.i = 0

        def __call__(self):
            v = self.items[self.i % len(self.items)]
            self.i += 1
            return v

    def mm_group(e, bank_tiles, kc, lhs_fn, rhs_fn):
        ins = None
        for k in range(kc):
            for (b, off, n) in bank_tiles:
                ins = e.matmul(psum[:, b, 0:n], lhsT=lhs_fn(k), rhs=rhs_fn(k, off, n),
                               start=(k == 0), stop=(k == kc - 1))
        return ins

    wslot = RR([0, 1, 2])

    def proj(Wap, kc, runs_per_group, nchunks_per_group, rhs_fn, in_keys_fn, tilesets, bankrr, evac_fn, wbuf):
        Wr = Wap.rearrange("(k p) o -> p k o", p=128)
        tiles = [t_ for ts in tilesets for t_ in ts]
        oi0 = 0
        pending = []
        for g, runs in enumerate(runs_per_group):
            s = wslot()
            for (dc, sc, w) in runs:
                plan.dma("pool", wbuf[s][:, 0:kc, dc:dc + w], Wr[:, :, sc:sc + w], writes=[("wb", s)])
            for (ti, off, n) in tiles:
                for lo in range(nchunks_per_group[g]):
                    b = bankrr()
                    if isinstance(b, tuple):
                        b = b[0]
                    plan.op("pe", (lambda e, b=b, off=off, n=n, s=s, lo=lo: mm_group(
                        e, [(b, off, n)], kc, lambda k: wbuf[s][:, k, lo * 128:(lo + 1) * 128], rhs_fn)),
                        reads=[("wb", s)] + in_keys_fn(ti), writes=[("ps", b)])
                    for fn in pending:
                        fn()
                    pending = []
                    r_ = evac_fn(oi0 + lo, ti, off, n, b)
                    if r_ is not None:
                        pending.append(r_)
            oi0 += nchunks_per_group[g]
        for fn in pending:
            fn()

    def allk(name, ti, nk=8):
        return [(name, k, ti) for k in range(nk)]

    R2B = [0, 512, 1024, 1408, 1440]

    def r2k(k, lo, hi):
        return [("R2", k, i) for i in range(4) if lo < R2B[i + 1] and hi > R2B[i]]

    def r2all(lo, hi):
        return [x for k in range(8) for x in r2k(k, lo, hi)]

    a = Bump(PERSIST_END)
    xT = a([8, NST], F32)
    R1 = a([8, NST], BF16)
    R2 = a([8, NST + 32], BF16)
    sq = a([8, 512], F32)
    sqb = sq.rearrange("p a b -> p (a b)").bitcast(BF16)[:, 0:4096].rearrange("p (a b) -> p a b", a=8)
    sig = [a([512], F32) for _ in range(3)]
    stA = [a([512], F32) for _ in range(2)]
    stA.append(stA[0])
    stB = [a([512], F32) for _ in range(2)]
    stB.append(stB[0])
    stC = [a([512], F32) for _ in range(2)]
    stC.append(stC[0])
    stD = [a([512], F32) for _ in range(2)]
    stD.append(stD[0])
    wbuf = [a([8, 512], BF16) for _ in range(3)]
    dg = [a([31, 128], BF16) for _ in range(2)]
    usamp = a([8, 16, 34], BF16)
    usampf = [a([8, 64], F32) for _ in range(2)]
    utailf = [a([8, 30], F32) for _ in range(2)]
    tailb = [a([8, 30], BF16) for _ in range(2)]
    kf = [a([512], F32) for _ in range(2)]
    kf.append(kf[0])
    kb = [a([512], BF16) for _ in range(2)]
    kb.append(kb[0])
    vf, vb = kf, kb
    tstage = sq.rearrange("p a b -> p (a b)")[:, 0:1024]
    hT = R1
    cT = R1
    uT = R2
    caT = R2[:, :, 0:NST]
    zT = R2[:, :, 0:NST]

    TL = [(0, 0, 512), (1, 512, 512), (2, 1024, 384)]
    set3 = RR([0, 1, 2, 3, 4, 5])
    bank6 = RR([0, 1, 2, 3, 4, 5])
    bankS = RR([6, 7])

    def rsqrt_chain(t, n, bank, tag):
        plan.op("act", lambda e: e.activation(out=stB[t][:, 0:n], in_=psum[:, bank, 0:n], func=AF.Ln,
                                               bias=P("eps"), scale=1.0),
                reads=[("ps", bank), "prm"], writes=[("stB", t % 2)])
        plan.op("act", lambda e: e.activation(out=stA[t][:, 0:n], in_=stB[t][:, 0:n], func=AF.Exp, scale=-0.5),
                reads=[("stB", t % 2)], writes=[("stA", t % 2)])

    def norm_stage(gname, gidx, dst, dst_name):
        for (t, off, n) in TL:
            plan.op("act", lambda e, off=off, n=n: e.activation(out=sqb[:, :, 0:n], in_=xT[:, :, off:off + n],
                                                                 func=AF.Square),
                    reads=allk("xT", t), writes=["sq"])
            b = bankS()
            plan.op("pe", lambda e, b=b, n=n: mm_group(e, [(b, 0, n)], 8, lambda k: onesb16,
                                                         lambda k, o_, n_: sqb[:, k, 0:n_]),
                    reads=["sq", "onesb16"], writes=[("ps", b)])
            rsqrt_chain(t, n, b, "n")
            for k in range(8):
                plan.op("dve", lambda e, k=k, t=t, off=off, n=n: e.scalar_tensor_tensor(
                    out=dst[:, k, off:off + n], in0=xT[:, k, off:off + n], scalar=P(gname, gidx * 8 + k),
                    in1=stA[t][:, 0:n], op0=ALU.mult, op1=ALU.mult),
                    reads=[("xT", k, t), ("stA", t % 2), "prm"], writes=[(dst_name, k, t)])

    for st in range(K_NST):
        c0 = st * NST
        for (t, off, n) in TL:
            plan.dma("sp", xT[:, :, off:off + n], D["xin"][:, :, c0 + off:c0 + off + n], writes=allk("xT", t))
        for l in range(2):
            norm_stage("nmg", l, hT, "R1")
            if K_STOP <= 1:
                break
            if st == 0:
                plan.op("dve", lambda e: e.memset(uT[:, :, 0:30], 0.0), writes=r2all(0, 30))
                for hh in range(2):
                    plan.dma("pool", usamp[:, 4 * hh:4 * hh + 4, :, 0:30], D["scT"][:, l, 4 * hh:4 * hh + 4],
                             writes=["usamp"])
            else:
                plan.op("dve", lambda e, l=l: e.tensor_copy(out=uT[:, :, 0:30], in_=tailb[l]),
                        reads=[("tailb", l)], writes=r2all(0, 30))
            def ev_pw1(oi, ti, off, n, bank, l=l, st=st):
                c, isg = oi // 2, oi % 2
                if not isg:
                    ev_pw1.abank[ti] = bank
                    return
                ab = ev_pw1.abank[ti]
                plan.op("act", lambda e: e.activation(out=sig[ti][:, 0:n], in_=psum[:, bank, 0:n], func=AF.Sigmoid,
                                                       bias=P("bpw1", l * 16 + 8 + c), scale=1.0),
                        reads=[("ps", bank), "prm"], writes=[("sig", ti)])
                plan.op("dve", lambda e: e.scalar_tensor_tensor(
                    out=uT[:, c, 30 + off:30 + off + n], in0=psum[:, ab, 0:n], scalar=P("bpw1", l * 16 + c),
                    in1=sig[ti][:, 0:n], op0=ALU.add, op1=ALU.mult),
                    reads=[("ps", ab), ("sig", ti), "prm"], writes=r2k(c, 30 + off, 30 + off + n))
                if st == 0 and ti == 0:
                    plan.op("dve", lambda e: e.scalar_tensor_tensor(
                        out=usampf[l][:, c, :], in0=psum[:, ab, 0:64], scalar=P("bpw1", l * 16 + c),
                        in1=sig[ti][:, 0:64], op0=ALU.add, op1=ALU.mult),
                        reads=[("ps", ab), ("sig", ti), "prm"], writes=[("usampf", l, c)])
                    plan.op("dve", lambda e: e.tensor_copy(
                        out=usamp[:, c, :, 30:34], in_=usampf[l][:, c, :].rearrange("p (s t) -> p s t", t=4)),
                        reads=[("usampf", l, c)], writes=["usamp"])
                if st == 2 and ti == 2:
                    plan.op("dve", lambda e: e.scalar_tensor_tensor(
                        out=utailf[l][:, c, :], in0=psum[:, ab, n - 30:n], scalar=P("bpw1", l * 16 + c),
                        in1=sig[ti][:, n - 30:n], op0=ALU.add, op1=ALU.mult),
                        reads=[("ps", ab), ("sig", ti), "prm"], writes=[("utailf", l, c)])
            ev_pw1.abank = {}
            runs = []
            for c in range(0, 8, 2):
                runs.append([(0, c * 128, 128), (128, 1024 + c * 128, 128),
                             (256, (c + 1) * 128, 128), (384, 1024 + (c + 1) * 128, 128)])
            proj(D["w_pw1"][l], 8, runs, [4] * 4, lambda k, off, n: hT[:, k, off:off + n],
                 lambda ti: allk("R1", ti), [TL], set3, ev_pw1, wbuf)
            if st == 1:
                plan.op("dve", lambda e: e.tensor_scalar(out=uT[:, :, 30 + 738:30 + 768], in0=uT[:, :, 30 + 738:30 + 768],
                                                          scalar1=P("valid"), scalar2=None, op0=ALU.mult),
                        reads=r2all(768, 798) + ["prm"], writes=r2all(768, 798))
            if K_STOP <= 2:
                break
            for c in range(8):
                ds = c % 2
                plan.op("dve", lambda e, ds=ds, c=c, l=l: e.tensor_tensor(
                    out=dg[ds], in0=identb.unsqueeze(1).broadcast_to([128, 31, 128]),
                    in1=P("wdw", (l * 8 + c) * 31, 31).unsqueeze(2).broadcast_to([128, 31, 128]), op=ALU.mult),
                    reads=["identb", "prm"], writes=[("dg", ds)])
                for (t, off, n) in TL:
                    b = bank6()

                    def conv_mm(e, ds=ds, c=c, off=off, n=n, b=b, st=st, t=t):
                        ins = None
                        for j in range(31):
                            ins = e.matmul(psum[:, b, 0:n], lhsT=dg[ds][:, j, :], rhs=uT[:, c, off + j:off + j + n],
                                           start=(j == 0), stop=(j == 30))
                        if st == 0 and t == 0:
                            for j in range(31):
                                ins = e.matmul(psum[:, b, 0:64].rearrange("p (s t) -> p s t", t=4), lhsT=dg[ds][:, j, :],
                                               rhs=usamp[:, c, :, j:j + 4], start=(j == 0), stop=(j == 30))
                        return ins
                    rk = [("dg", ds)] + r2k(c, off, off + n + 30)
                    if st == 0 and t == 0:
                        rk.append("usamp")
                    plan.op("pe", conv_mm, reads=rk, writes=[("ps", b)])
                    plan.op("act", lambda e, c=c, off=off, n=n, b=b, l=l: e.activation(
                        out=cT[:, c, off:off + n], in_=psum[:, b, 0:n], func=AF.Identity, bias=P("bdw", l * 8 + c),
                        scale=1.0), reads=[("ps", b), "prm"], writes=[("R1", c, t)])
            plan.op("dve", lambda e, l=l: e.tensor_copy(out=tailb[l], in_=uT[:, :, NST:NST + 30]),
                    reads=r2all(NST, NST + 30), writes=[("tailb", l)])
            if K_STOP <= 3:
                break
            for (t, off, n) in TL:
                plan.op("act", lambda e, off=off, n=n: e.activation(out=sqb[:, :, 0:n], in_=cT[:, :, off:off + n],
                                                                     func=AF.Square),
                        reads=allk("R1", t), writes=["sq"])
                b1, b2 = bankS(), bankS()
                plan.op("pe", lambda e, b1=b1, off=off, n=n: mm_group(
                    e, [(b1, off, n)], 8, lambda k: onesb16, lambda k, o_, n_: cT[:, k, o_:o_ + n_]),
                    reads=allk("R1", t) + ["onesb16"], writes=[("ps", b1)])
                plan.op("pe", lambda e, b2=b2, n=n: mm_group(
                    e, [(b2, 0, n)], 8, lambda k: onesb16, lambda k, o_, n_: sqb[:, k, 0:n_]),
                    reads=["sq", "onesb16"], writes=[("ps", b2)])
                plan.op("act", lambda e, t=t, n=n, b1=b1: e.activation(out=stC[t][:, 0:n], in_=psum[:, b1, 0:n],
                                                                        func=AF.Copy),
                        reads=[("ps", b1)], writes=[("stC", t % 2)])
                plan.op("dve", lambda e, t=t, n=n: e.tensor_tensor(out=stD[t][:, 0:n], in0=stC[t][:, 0:n],
                                                                    in1=stC[t][:, 0:n], op=ALU.mult),
                        reads=[("stC", t % 2)], writes=[("stD", t % 2)])
                plan.op("dve", lambda e, t=t, n=n, b2=b2: e.tensor_tensor(out=stD[t][:, 0:n], in0=psum[:, b2, 0:n],
                                                                           in1=stD[t][:, 0:n], op=ALU.subtract),
                        reads=[("ps", b2), ("stD", t % 2)], writes=[("stD", t % 2)])
                plan.op("act", lambda e, t=t, n=n: e.activation(out=stB[t][:, 0:n], in_=stD[t][:, 0:n], func=AF.Ln,
                                                                 bias=P("eps"), scale=1.0),
                        reads=[("stD", t % 2), "prm"], writes=[("stB", t % 2)])
                plan.op("act", lambda e, t=t, n=n: e.activation(out=stA[t][:, 0:n], in_=stB[t][:, 0:n], func=AF.Exp,
                                                                 scale=-0.5),
                        reads=[("stB", t % 2)], writes=[("stA", t % 2)])
                plan.op("dve", lambda e, t=t, n=n: e.scalar_tensor_tensor(
                    out=stC[t][:, 0:n], in0=stC[t][:, 0:n], scalar=-1.0, in1=stA[t][:, 0:n], op0=ALU.mult,
                    op1=ALU.mult), reads=[("stC", t % 2), ("stA", t % 2)], writes=[("stC", t % 2)])
                plan.op("dve", lambda e, t=t, off=off, n=n: e.tensor_tensor(
                    out=sq[:, :, 0:n], in0=cT[:, :, off:off + n],
                    in1=stA[t][:, 0:n].unsqueeze(1).broadcast_to([128, 8, n]), op=ALU.mult),
                    reads=allk("R1", t) + [("stA", t % 2)], writes=["sq"])
                plan.op("dve", lambda e, t=t, n=n: e.tensor_tensor(
                    out=sq[:, :, 0:n], in0=sq[:, :, 0:n],
                    in1=stC[t][:, 0:n].unsqueeze(1).broadcast_to([128, 8, n]), op=ALU.add),
                    reads=["sq", ("stC", t % 2)], writes=["sq"])
                for k in range(8):
                    plan.op("act", lambda e, k=k, off=off, n=n, l=l: e.activation(
                        out=caT[:, k, off:off + n], in_=sq[:, k, 0:n], func=AF.Silu, bias=P("lnb", l * 8 + k),
                        scale=P("lng", l * 8 + k)), reads=["sq", "prm"], writes=[("R2", k, t)])
            if K_STOP <= 4:
                break
            def ev_pw2(oi, ti, off, n, bank, l=l):
                plan.op("dve", lambda e: e.scalar_tensor_tensor(
                    out=xT[:, oi, off:off + n], in0=psum[:, bank, 0:n], scalar=P("bpw2", l * 8 + oi),
                    in1=xT[:, oi, off:off + n], op0=ALU.add, op1=ALU.add),
                    reads=[("ps", bank), ("xT", oi, ti), "prm"], writes=[("xT", oi, ti)])
            proj(D["w_pw2"][l], 8, [[(0, 0, 512)], [(0, 512, 512)]], [4, 4],
                 lambda k, off, n: caT[:, k, off:off + n], lambda ti: allk("R2", ti), [TL], set3, ev_pw2, wbuf)
            if K_STOP <= 5:
                break
            norm_stage("nml", l, hT, "R1")
            for hg in range(4):
                def ev_up(oi, ti, off, n, bank):
                    plan.op("act", lambda e: e.activation(out=zT[:, oi, off:off + n], in_=psum[:, bank, 0:n],
                                                           func=AF.Relu),
                            reads=[("ps", bank)], writes=[("R2", oi, ti)])
                    plan.op("dve", lambda e: e.tensor_tensor(out=zT[:, oi, off:off + n], in0=zT[:, oi, off:off + n],
                                                              in1=zT[:, oi, off:off + n], op=ALU.mult),
                            reads=[("R2", oi, ti)], writes=[("R2", oi, ti)])
                proj(D["w_up"][l][:, hg * 1024:(hg + 1) * 1024], 8, [[(0, 0, 512)], [(0, 512, 512)]], [4, 4],
                     lambda k, off, n: hT[:, k, off:off + n], lambda ti: allk("R1", ti), [TL], set3, ev_up, wbuf)

                def ev_dn(oi, ti, off, n, bank):
                    plan.op("dve", lambda e: e.tensor_tensor(out=xT[:, oi, off:off + n], in0=psum[:, bank, 0:n],
                                                              in1=xT[:, oi, off:off + n], op=ALU.add),
                            reads=[("ps", bank), ("xT", oi, ti)], writes=[("xT", oi, ti)])
                proj(D["w_dn"][l][hg * 1024:(hg + 1) * 1024, :], 8, [[(0, 0, 512)], [(0, 512, 512)]], [4, 4],
                     lambda k, off, n: zT[:, k, off:off + n], lambda ti: allk("R2", ti), [TL], set3, ev_dn, wbuf)
        if K_STOP <= 6:
            continue
        norm_stage("kvg", 0, hT, "R1")

        def ev_k(oi, ti, off, n, bank, c0=c0):
            sgb = sig[ti].bitcast(BF16)
            plan.op("act", lambda e: e.activation(out=sgb[:, 0:n], in_=psum[:, bank, 0:n], func=AF.Square),
                    reads=[("ps", bank)], writes=[("sig", ti)])
            b2 = bankS()
            plan.op("pe", lambda e: e.matmul(psum[:, b2, 0:n], lhsT=oneshb, rhs=sgb[:, 0:n], start=True, stop=True),
                    reads=[("sig", ti), "oneshb"], writes=[("ps", b2)])
            rsqrt_chain(ti, n, b2, "k")
            plan.op("dve", lambda e: e.scalar_tensor_tensor(
                out=kf[ti][:, 0:n], in0=psum[:, bank, 0:n], scalar=P("kng"), in1=stA[ti][:, 0:n], op0=ALU.mult,
                op1=ALU.mult), reads=[("ps", bank), ("stA", ti % 2), "prm"], writes=[("kf", ti % 2)])
            plan.op("act", lambda e: e.activation(out=kb[ti][:, 0:n], in_=kf[ti][:, 0:n], func=AF.Copy),
                    reads=[("kf", ti % 2)], writes=[("kb", ti % 2)])
            plan.dma("sp", D["kT"][:, oi, c0 + off:c0 + off + n], kf[ti][:, 0:n], reads=[("kf", ti % 2)])
            plan.dma("sp", D["KTs"][oi][:, c0 + off:c0 + off + n], kb[ti][:, 0:n], reads=[("kb", ti % 2)], writes=["KTs"])
        proj(D["w_kv"][:, 0:512], 8, [[(0, 0, 512)]], [4], lambda k, off, n: hT[:, k, off:off + n],
             lambda ti: allk("R1", ti), [TL], set3, ev_k, wbuf)
        if K_STOP <= 7:
            continue
        s = wslot()
        plan.dma("pool", wbuf[s], D["w_kv"].rearrange("(k p) o -> p k o", p=128)[:, :, 512:1024], writes=[("wb", s)])
        for tb in range(NST // 128):
            b = bank6()
            ti = tb // 4
            plan.op("pe", lambda e, b=b, tb=tb, s=s: mm_group(
                e, [(b, 0, 512)], 8, lambda k: hT[:, k, tb * 128:(tb + 1) * 128], lambda k, o_, n_: wbuf[s][:, k, :]),
                reads=allk("R1", ti) + [("wb", s)], writes=[("ps", b)])
            i = tb % 2
            plan.op("act", lambda e, b=b, i=i: e.activation(out=vf[i], in_=psum[:, b, :], func=AF.Copy),
                    reads=[("ps", b)], writes=[("kf", i)])
            plan.op("dve", lambda e, i=i: e.tensor_copy(out=vb[i], in_=vf[i]),
                    reads=[("kf", i)], writes=[("kb", i)])
            r0 = c0 + tb * 128
            plan.dma("sp", D["vN"][r0:r0 + 128, :], vf[i], reads=[("kf", i)])
            plan.dma("sp", D["Vs"][r0:r0 + 128, :], vb[i], reads=[("kb", i)], writes=["Vs"])
        if K_STOP <= 8:
            continue
        if st == 0:
            plan.dma("sp", D["xs"][:, :, 0:64], xT[:, :, 0:64], reads=allk("xT", 0), writes=["xs"])
        elif st == 1:
            plan.dma("sp", D["xs"][:, :, 64:704], xT[:, :, 768:1408], reads=allk("xT", 1) + allk("xT", 2), writes=["xs"])
        else:
            plan.dma("sp", D["xs"][:, :, 704:2112], xT[:, :, 0:1408],
                     reads=allk("xT", 0) + allk("xT", 1) + allk("xT", 2), writes=["xs"])
        if K_STOP <= 9:
            continue
        if st in (0, 2):
            for l in range(2):
                src = usampf[l] if st == 0 else utailf[l]
                w = 64 if st == 0 else 30
                bA, bB = bankS(), bankS()

                def tr(e, src=src, w=w, bA=bA, bB=bB):
                    ins = None
                    for k in range(8):
                        bk = bA if k < 4 else bB
                        ins = e.transpose(psum[0:w, bk, (k % 4) * 128:(k % 4 + 1) * 128], src[:, k, :], identf)
                    return ins
                rk = [("usampf", l, c) for c in range(8)] if st == 0 else [("utailf", l, c) for c in range(8)]
                plan.op("pe", tr, reads=rk + ["identf"], writes=[("ps", bA), ("ps", bB)])
                plan.op("act", lambda e, w=w, bA=bA: e.activation(out=tstage[0:w, 0:512], in_=psum[0:w, bA, :],
                                                                   func=AF.Copy),
                        reads=[("ps", bA)], writes=["sq"])
                plan.op("act", lambda e, w=w, bB=bB: e.activation(out=tstage[0:w, 512:1024], in_=psum[0:w, bB, :],
                                                                   func=AF.Copy),
                        reads=[("ps", bB)], writes=["sq"])
                if st == 0:
                    for s_ in range(16):
                        plan.dma("sp", D["cs"][l, s_, 26:30, :], tstage[4 * s_:4 * s_ + 4, :],
                                 reads=["sq"])
                    plan.dma("sp", D["cs"][l, :, 0:26, :], D["scN"][l, :, 4:30, :])
                else:
                    plan.dma("sp", D["cp"][l], tstage[0:30, :], reads=["sq"])

    plan.barrier()
    if K_STOP > 10:
        bb = Bump(PERSIST_END)
        wbufB = [bb([8, 512], BF16) for _ in range(3)]
        sA = [bb([512], F32) for _ in range(2)]
        sB = [bb([512], F32) for _ in range(2)]
        sC = [bb([512], F32) for _ in range(4)]
        hB = bb([8, TB], BF16)
        X0 = bb.off
        xB = bb([8, TB], F32)
        Z0 = bb.off
        zB = bb([8, TB], BF16)
        S0 = bb.off
        sqB = bb([8, 512], F32)
        sqBb = sqB.rearrange("p a b -> p (a b)").bitcast(BF16)[:, 0:4096].rearrange("p (a b) -> p a b", a=8)
        ab = Bump(X0)
        QT = ab([3, TB], BF16)
        KT = [ab([TA], BF16) for _ in range(2)]
        Vg = [ab([69, 128], BF16) for _ in range(2)]
        ex2 = [ab([512], BF16) for _ in range(2)]
        assert ab.off <= Z0
        ab = Bump(Z0)
        oT = ab([4, TB], BF16)
        Em = ab([3, 512], BF16)
        ex = [ab([512], BF16) for _ in range(2)]
        ex += ex2
        PT = [ab([512], BF16) for _ in range(2)]
        QTs = ab([3, 4, 64], BF16)
        KTsamp = ab([4, 64], BF16)
        PT += [ab([512], BF16) for _ in range(2)]
        assert ab.off <= S0
        ab = Bump(X0)
        CB = [ab([9, 1024], BF16) for _ in range(2)]
        Qtm = ab([3, 512], BF16)
        Vnew = [ab([512], BF16) for _ in range(2)]
        sraw = ab([3, 4, 8], F32)
        pexp = ab([3, 4, 8], F32)
        pn1 = ab([2, 48], F32)
        pm = ab([6, 4, 8], BF16)
        pz = ab([2, 4, 4, 8], BF16)
        pD = ab([32], BF16)
        pns = ab([32], BF16)
        prod = [ab([3, 512], F32) for _ in range(2)]
        rDs = ab([1], F32)
        osel = ab([512], BF16)
        assert ab.off <= Z0
        ab = Bump(S0)
        accN_D = ab([2, 2048], F32)
        accN = accN_D[:, 0, :]
        accD = accN_D[:, 1, :]

        ex += [sC[0].bitcast(BF16)[:, 512:1024], sC[1].bitcast(BF16)[:, 512:1024]]
        PT += [sC[2].bitcast(BF16)[:, 512:1024], sC[3].bitcast(BF16)[:, 512:1024]]
        TLB = [(0, 0, 64), (1, 64, 512), (2, 576, 512), (3, 1088, 512), (4, 1600, 512)]
        TS = [TLB[0:3], TLB[3:5]]
        setB = RR([0, 1, 2, 3, 4, 5])
        setAtt = RR([0, 1, 2, 3, 4])
        statQ = RR([5, 6, 7])
        sBq = sA + sB
        bank4 = RR([0, 1, 2, 3])
        bank6s = RR([0, 1, 2, 3, 4, 5])
        bankND = RR([6, 7])
        bankN = RR([4, 5])
        bankD = RR([6, 7])
        statB = RR([6, 7])

        def xk(ti):
            return allk("xB", ti)

        def normB(gname, gidx):
            for (t, off, n) in TLB:
                plan.op("act", lambda e, off=off, n=n: e.activation(out=sqBb[:, :, 0:n], in_=xB[:, :, off:off + n],
                                                                     func=AF.Square), reads=xk(t), writes=["sqB"])
                b = statB()
                plan.op("pe", lambda e, b=b, n=n: mm_group(e, [(b, 0, n)], 8, lambda k: onesb16,
                                                             lambda k, o_, n_: sqBb[:, k, 0:n_]),
                        reads=["sqB", "onesb16"], writes=[("ps", b)])
                i = t % 2
                plan.op("act", lambda e, i=i, n=n, b=b: e.activation(out=sB[i][:, 0:n], in_=psum[:, b, 0:n],
                                                                      func=AF.Ln, bias=P("eps"), scale=1.0),
                        reads=[("ps", b), "prm"], writes=[("sB", i)])
                plan.op("act", lambda e, i=i, n=n: e.activation(out=sA[i][:, 0:n], in_=sB[i][:, 0:n], func=AF.Exp,
                                                                 scale=-0.5),
                        reads=[("sB", i)], writes=[("sA", i)])
                for k in range(8):
                    plan.op("dve", lambda e, k=k, i=i, off=off, n=n: e.scalar_tensor_tensor(
                        out=hB[:, k, off:off + n], in0=xB[:, k, off:off + n], scalar=P(gname, gidx * 8 + k),
                        in1=sA[i][:, 0:n], op0=ALU.mult, op1=ALU.mult),
                        reads=[("xB", k, t), ("sA", i), "prm"], writes=[("hB", k, t)])

        EsM = smc[:, 0:192].rearrange("p (j t h) -> p j t h", j=6, t=4)
        sel32 = smc[0:32, 192:704]
        for (t, off, n) in TLB:
            plan.dma("sp", xB[:, :, off:off + n], D["xs"][:, :, off:off + n], reads=["xs"], writes=xk(t))
        for lb in range(2):
            L = 2 + lb
            normB("nmg", L)
            if lb == 1:
                for (t, off, n) in TLB:
                    plan.dma("sp", D["xs"][:, :, off:off + n], xB[:, :, off:off + n], reads=xk(t), writes=["xs"])
            if K_DBG and lb == 0:
                plan.dma("sp", D["dbg_h"], hB, reads=[x for t in range(5) for x in allk("hB", t)])
            plan.barrier()
            for hp_ in range(4):
                plan.dma("sp", KTsamp[:, hp_, :], D["KTs"][hp_][:, 0:64], reads=["KTs"], writes=["KTsamp"])
            for hp in range(4):
                kv = hp % 2
                plan.dma("sp", KT[kv], D["KTs"][hp], reads=["KTs"], writes=[("KT", kv)])
                Vs = D["Vs"]
                c_lo = hp * 128
                plan.dma("sp", Vg[kv][:, 0:17, :],
                         Vs[OWN0 - 128:OWN0 + 2048, c_lo:c_lo + 128].rearrange("(b i) c -> i b c", i=128),
                         reads=["Vs"], writes=[("Vg", kv)])
                for r in range(4):
                    plan.dma("sp", Vg[kv][:, 17 + 5 * r:22 + 5 * r, :],
                             Vs[OWN0 - 512 + r:OWN0 + 2048:4, c_lo:c_lo + 128].rearrange("(b i) c -> i b c", i=128),
                             reads=["Vs"], writes=[("Vg", kv)])
                for r in range(16):
                    plan.dma("sp", Vg[kv][:, 37 + 2 * r:39 + 2 * r, :],
                             Vs[OWN0 - 2048 + r:OWN0 + 2048:16, c_lo:c_lo + 128].rearrange("(b i) c -> i b c", i=128),
                             reads=["Vs"], writes=[("Vg", kv)])
                plan.dma("pool", Em, D["emask"][:, hp:12:4, :], writes=["Em"])

                def ev_q(oi, ti, off, n, bank, lb=lb):
                    i = ti % 4
                    scb = sC[i].bitcast(BF16)
                    plan.op("act", lambda e: e.activation(out=scb[:, 0:n], in_=psum[:, bank, 0:n], func=AF.Square),
                            reads=[("ps", bank)], writes=[("sC", i)])
                    def late():
                        b2 = statQ()
                        plan.op("pe", lambda e: e.matmul(psum[:, b2, 0:n], lhsT=oneshb, rhs=scb[:, 0:n], start=True,
                                                          stop=True), reads=[("sC", i), "oneshb"], writes=[("ps", b2)])
                        j = ti % 4
                        plan.op("act", lambda e: e.activation(out=sBq[j][:, 0:n], in_=psum[:, b2, 0:n], func=AF.Ln,
                                                               bias=P("eps"), scale=1.0),
                                reads=[("ps", b2), "prm"], writes=[("sBq", j)])
                        plan.op("act", lambda e: e.activation(out=sBq[j][:, 0:n], in_=sBq[j][:, 0:n], func=AF.Exp,
                                                               scale=-0.5),
                                reads=[("sBq", j)], writes=[("sBq", j)])
                        plan.op("dve", lambda e: e.scalar_tensor_tensor(
                            out=QT[:, oi, off:off + n], in0=psum[:, bank, 0:n], scalar=qg8[:, lb:lb + 1],
                            in1=sBq[j][:, 0:n], op0=ALU.mult, op1=ALU.mult),
                            reads=[("ps", bank), ("sBq", j), "qg8"], writes=["QT"])
                    return late
                proj(D["w_q"][lb], 8, [[(g * 128, (g * 4 + hp) * 128, 128) for g in range(3)]], [3],
                     lambda k, off, n: hB[:, k, off:off + n], lambda ti: allk("hB", ti), TS, setAtt, ev_q, wbufB)
                if K_DBG and lb == 0:
                    plan.dma("sp", D["dbg_q"][hp], QT, reads=["QT"])
                plan.op("act", lambda e, hp=hp: e.activation(out=QTs[:, :, hp, :], in_=QT[:, :, 0:64], func=AF.Copy),
                        reads=["QT"], writes=["QTs"])
                units = []
                for g, (w_, d) in enumerate(WINDOWS):
                    qbs = [(r, c) for r in range(d) for c in range(16 // d)]
                    for q4 in range(4):
                        for pr in range(2):
                            units.append((g, d, q4, pr, qbs[q4 * 4 + pr * 2:q4 * 4 + pr * 2 + 2]))
                nd_banks = {}

                def front(ui, kv=kv, units=units, nd_banks=nd_banks):
                    g, d, q4, pr, pair = units[ui]
                    bX, bY = bank6s(), bank6s()
                    par = ui % 3

                    def st_mm(e):
                        ins = None
                        for qi, (r, c) in enumerate(pair):
                            q0 = 64 + r + d * 128 * c
                            for pc in range(2):
                                k0 = OWN0 + r + d * 128 * (c - 1 + pc)
                                for h, bk in ((0, bX), (1, bY)):
                                    ins = e.matmul(psum[:, bk, (qi * 2 + pc) * 128:(qi * 2 + pc + 1) * 128],
                                                   lhsT=KT[kv][h * 64:(h + 1) * 64, k0:k0 + 127 * d + 1:d],
                                                   rhs=QT[h * 64:(h + 1) * 64, g, q0:q0 + 127 * d + 1:d],
                                                   start=True, stop=True, tile_position=(h * 64, 0))
                        return ins
                    plan.op("pe", st_mm, reads=[("KT", kv), "QT"], writes=[("ps", bX), ("ps", bY)])
                    for h, bk in ((0, bX), (1, bY)):
                        hx = par * 2 + h
                        plan.op("act", lambda e, hx=hx, bk=bk: e.activation(out=ex[hx], in_=psum[:, bk, :], func=AF.Exp),
                                reads=[("ps", bk)], writes=[("ex", hx)])
                        for qi, (r, c) in enumerate(pair):
                            if c == 0:
                                plan.op("act", lambda e, hx=hx, bk=bk, qi=qi: e.activation(
                                    out=ex[hx][:, qi * 256:qi * 256 + 128], in_=psum[:, bk, qi * 256:qi * 256 + 128],
                                    func=AF.Exp, bias=P("hbias"), scale=1.0),
                                    reads=[("ps", bk), "prm"], writes=[("ex", hx)])
                        plan.op("dve", lambda e, h=h, g=g, hx=hx: e.tensor_tensor(
                            out=PT[hx].rearrange("p (a b) -> p a b", a=2),
                            in0=ex[hx].rearrange("p (a b) -> p a b", a=2),
                            in1=Em[:, g, h * 256:(h + 1) * 256].unsqueeze(1).broadcast_to([128, 2, 256]),
                            op=ALU.mult), reads=[("ex", hx), "Em"], writes=[("PT", hx)])

                def back(ui, kv=kv, units=units, nd_banks=nd_banks):
                    g, d, q4, pr, pair = units[ui]
                    bND = bankND()
                    par = ui % 3
                    vbase = [0, 17, 37][g]
                    nper = 16 // d + 1

                    def pv_mm(e):
                        ins = None
                        for qi, (r, c) in enumerate(pair):
                            sl = qi * 128
                            for h in range(2):
                                rhs = [PT[par * 2 + h][:, (qi * 2 + pc) * 128:(qi * 2 + pc + 1) * 128] for pc in range(2)]
                                for pc in range(2):
                                    e.matmul(psum[h * 64:(h + 1) * 64, bND, sl:sl + 128],
                                             lhsT=Vg[kv][:, vbase + r * nper + c + pc, h * 64:(h + 1) * 64],
                                             rhs=rhs[pc], start=(pc == 0), stop=(pc == 1))
                                for pc in range(2):
                                    ins = e.matmul(psum[h * 64:(h + 1) * 64, bND, 256 + sl:256 + sl + 128],
                                                   lhsT=ones64, rhs=rhs[pc], start=(pc == 0), stop=(pc == 1))
                        return ins
                    plan.op("pe", pv_mm, reads=[("PT", par * 2), ("PT", par * 2 + 1), ("Vg", kv), "ones64"],
                            writes=[("ps", bND)])
                    acc2 = accN_D
                    if g == 0:
                        c0_ = (q4 * 4 + pr * 2) * 128
                        dst = acc2[:, :, c0_:c0_ + 256]
                        src = psum[:, bND, :].rearrange("p (a b) -> p a b", a=2)
                    elif g == 1:
                        c0_ = q4 + 1024 * pr
                        dst = acc2[:, :, c0_:c0_ + 1021:4]
                        src = psum[:, bND, :].rearrange("p (a b) -> p a b", a=2)
                    else:
                        r0 = q4 * 4 + pr * 2
                        dst = acc2.rearrange("p a (i r) -> p a r i", r=16)[:, :, r0:r0 + 2, :]
                        src = psum[:, bND, :].rearrange("p (a q i) -> p a q i", a=2, q=2)
                    if g == 0:
                        plan.op("act", lambda e: e.activation(out=dst, in_=src, func=AF.Copy),
                                reads=[("ps", bND)], writes=["accN", "accD"])
                    else:
                        plan.op("dve", lambda e: e.tensor_tensor(out=dst, in0=src, in1=dst, op=ALU.add),
                                reads=[("ps", bND), "accN", "accD"], writes=["accN", "accD"])
                front(0)
                front(1)
                for ui in range(len(units)):
                    if ui + 2 < len(units):
                        front(ui + 2)
                    back(ui)
                if K_DBG and lb == 0:
                    plan.dma("sp", D["dbg_n"][hp], accN, reads=["accN"])
                    plan.dma("sp", D["dbg_d"][hp], accD, reads=["accD"])
                plan.op("act", lambda e: e.activation(out=accD, in_=accD, func=AF.Ln), reads=["accD"], writes=["accD"])
                plan.op("act", lambda e: e.activation(out=accD, in_=accD, func=AF.Exp, scale=-1.0),
                        reads=["accD"], writes=["accD"])
                plan.op("dve", lambda e, hp=hp: e.tensor_tensor(out=oT[:, hp, 64:TB], in0=accN, in1=accD, op=ALU.mult),
                        reads=["accN", "accD"], writes=["oTp"])
            plan.barrier()
            plan.op("dve", lambda e: e.memset(pm, 0.0), writes=["pm"])
            plan.op("dve", lambda e: e.memset(pz, 0.0), writes=["pz"])
            for i_ in range(2):
                plan.op("dve", lambda e, i_=i_: e.memset(Vnew[i_], 0.0), writes=[("Vnew", i_)])
            for g in range(3):
                bq = bank4()
                plan.op("pe", lambda e, g=g, bq=bq: [e.transpose(
                    psum[0:64, bq, hp_ * 64:(hp_ + 1) * 64].bitcast(BF16), QTs[:, g, hp_, :], identb)
                    for hp_ in range(4)][-1], reads=["QTs", "identb"], writes=[("ps", bq)])
                plan.op("act", lambda e, g=g, bq=bq: e.activation(out=Qtm[0:64, g, :],
                                                                   in_=psum[0:64, bq, 0:256].bitcast(BF16), func=AF.Copy),
                        reads=[("ps", bq)], writes=["Qtm"])
            qsets = RR([(0, 1, 2), (3, 4, 5)])
            for sq_ in range(16):
                cb = sq_ % 2
                C = D["cache"][sq_]
                plan.dma("pool", CB[cb][:, 0, :], C[1920:2048, :], writes=[("CB", cb)])
                plan.dma("pool", CB[cb][:, 1:5, :], C[1536:2048, :].rearrange("(m t) f -> m t f", t=4),
                         writes=[("CB", cb)])
                plan.dma("pool", CB[cb][:, 5:9, :], C.rearrange("(m t) f -> m t f", t=16)[:, 0:4, :],
                         writes=[("CB", cb)])
                plan.dma("sp", Vnew[cb][0:4, :], D["Vs"][4 * sq_:4 * sq_ + 4, :], reads=["Vs"], writes=[("Vnew", cb)])
                c4 = slice(4 * sq_, 4 * sq_ + 4)
                def snew(e, c4=c4):
                    ins = None
                    for g in range(3):
                        for hp_ in range(4):
                            for par in range(2):
                                ins = e.matmul(psum[0:4, 6 + par, (g * 4 + hp_) * 4:(g * 4 + hp_) * 4 + 4],
                                               lhsT=KTsamp[par * 64:(par + 1) * 64, hp_, c4],
                                               rhs=QTs[par * 64:(par + 1) * 64, g, hp_, c4],
                                               start=True, stop=True, tile_position=(par * 64, 0))
                    return ins
                plan.op("pe", snew, reads=["KTsamp", "QTs"], writes=[("ps", 6), ("ps", 7)])
                for par in range(2):
                    plan.op("act", lambda e, par=par: e.activation(out=pn1[0:4, par, :], in_=psum[0:4, 6 + par, 0:48],
                                                                    func=AF.Exp), reads=[("ps", 6 + par)], writes=[("pn1", par)])
                    plan.op("dve", lambda e, par=par: e.tensor_tensor(
                        out=pm[0:4, 3:6, :, par:8:2],
                        in0=pn1[0:4, par, :].rearrange("p (g h t) -> p g t h", g=3, h=4),
                        in1=EsM[0:4, 3:6, :, par:8:2], op=ALU.mult),
                        reads=[("pn1", par), "smc"], writes=["pm"])
                for t in range(4):
                    qb_ = qsets()
                    plan.op("pe", lambda e, t=t, qb_=qb_, sq_=sq_: [e.matmul(
                        psum[:, qb_[g], :], lhsT=identb[0:64, 4 * sq_ + t:4 * sq_ + t + 1].broadcast_to([64, 128]),
                        rhs=Qtm[0:64, g, :], start=True, stop=True) for g in range(3)][-1],
                        reads=["Qtm", "identb"], writes=[("ps", b_) for b_ in qb_])
                    pi = t % 2
                    for g in range(3):
                        blk = 0 if g == 0 else (1 + t if g == 1 else 5 + t)
                        plan.op("dve", lambda e, g=g, blk=blk, pi=pi, qb_=qb_, cb=cb: e.tensor_tensor(
                            out=prod[pi][:, g, :], in0=CB[cb][:, blk, 0:512], in1=psum[:, qb_[g], :], op=ALU.mult),
                            reads=[("CB", cb), ("ps", qb_[g])], writes=[("prod", pi, g)])
                    plan.op("dve", lambda e, t=t, pi=pi: e.tensor_reduce(
                        out=sraw[:, :, t, :], in_=prod[pi].rearrange("p g (h d) -> p g h d", d=64), axis=AX.X,
                        op=ALU.add), reads=[("prod", pi, g_) for g_ in range(3)], writes=["sraw"])
                plan.op("act", lambda e: e.activation(out=pexp, in_=sraw, func=AF.Exp), reads=["sraw"], writes=["pexp"])
                plan.op("dve", lambda e: e.tensor_tensor(out=pm[:, 0:3, :, :], in0=pexp, in1=EsM[:, 0:3, :, :],
                                                          op=ALU.mult), reads=["pexp", "smc"], writes=["pm"])
                plan.op("dve", lambda e: e.tensor_copy(
                    out=pz.rearrange("p g a b h -> p g (a b) h")[:, :, 0:16:5, :], in_=pm[:, 1:3, :, :]),
                    reads=["pm"], writes=["pz"])
                plan.op("dve", lambda e: e.tensor_reduce(
                    out=pD, in_=pm.rearrange("p j t h -> p (t h) j"), axis=AX.X, op=ALU.add),
                    reads=["pm"], writes=["pD"])
                plan.op("dve", lambda e: e.tensor_reduce(
                    out=pns, in_=pm[:, 3:6, :, :].rearrange("p j t h -> p (t h) j"), axis=AX.X, op=ALU.add),
                    reads=["pm"], writes=["pns"])

                def pvs(e, cb=cb):
                    e.matmul(psum[0:32, 6, :], lhsT=pm[:, 0, :, :].rearrange("p t h -> p (t h)"),
                             rhs=CB[cb][:, 0, 512:1024], start=True, stop=False)
                    for g in (1, 2):
                        for t in range(4):
                            blk = 1 + t if g == 1 else 5 + t
                            e.matmul(psum[0:32, 6, :], lhsT=pz[:, g - 1, t, :, :].rearrange("p a h -> p (a h)"),
                                     rhs=CB[cb][:, blk, 512:1024], start=False, stop=False)
                    e.matmul(psum[0:32, 6, :], lhsT=pns, rhs=Vnew[cb], start=False, stop=True)
                    return e.matmul(psum[0:32, 7, 0:1], lhsT=pD, rhs=ones64[:, 0:1], start=True, stop=True)
                plan.op("pe", pvs, reads=["pm", "pz", "pns", "pD", ("CB", cb), ("Vnew", cb), "ones64"],
                        writes=[("ps", 6), ("ps", 7)])
                plan.op("dve", lambda e: e.reciprocal(out=rDs[0:32, :], in_=psum[0:32, 7, 0:1]),
                        reads=[("ps", 7)], writes=["rDs"])
                plan.op("dve", lambda e: e.scalar_tensor_tensor(out=osel[0:32, :], in0=psum[0:32, 6, :],
                                                                 scalar=rDs[0:32, :], in1=sel32, op0=ALU.mult,
                                                                 op1=ALU.mult),
                        reads=[("ps", 6), "rDs", "smc"], writes=["osel"])
                plan.op("pe", lambda e: [e.matmul(psum[:, 7, 8 + 4 * hp_:12 + 4 * hp_],
                                                  lhsT=osel[0:32, hp_ * 128:(hp_ + 1) * 128], rhs=tselb[0:32, :],
                                                  start=True, stop=True) for hp_ in range(4)][-1],
                        reads=["osel", "tselb"], writes=[("ps", 7)])
                plan.op("act", lambda e, c4=c4: e.activation(
                    out=oT[:, :, c4], in_=psum[:, 7, 8:24].rearrange("p (a b) -> p a b", a=4), func=AF.Copy),
                    reads=[("ps", 7)], writes=["oTs"])
            if K_DBG and lb == 0:
                plan.dma("sp", D["dbg_o"], oT, reads=["oTs", "oTp"])
                plan.dma("sp", D["dbg_sraw"], sraw.rearrange("p a b c -> p (a b c)"), reads=["sraw"])
                plan.dma("sp", D["dbg_pm"], pm.rearrange("p a b c -> p (a b c)"), reads=["pm"])
                plan.dma("sp", D["dbg_osel"], osel, reads=["osel"])
                plan.dma("sp", D["dbg_qtm"], Qtm.rearrange("p a b -> p (a b)"), reads=["Qtm"])
            plan.barrier()
            if K_STOP <= 11:
                break
            for (t, off, n) in TLB:
                plan.dma("sp", xB[:, :, off:off + n], D["xs"][:, :, off:off + n], reads=["xs"], writes=xk(t))

            def ev_add(oi, ti, off, n, bank):
                plan.op("dve", lambda e: e.tensor_tensor(out=xB[:, oi, off:off + n], in0=psum[:, bank, 0:n],
                                                          in1=xB[:, oi, off:off + n], op=ALU.add),
                        reads=[("ps", bank), ("xB", oi, ti)], writes=[("xB", oi, ti)])
            proj(D["w_o"][lb], 4, [[(0, 0, 512)], [(0, 512, 512)]], [4, 4],
                 lambda k, off, n: oT[:, k, off:off + n], lambda ti: ["oTs", "oTp"], TS, setB, ev_add, wbufB)
            normB("nml", L)
            for hg in range(4):
                def ev_upB(oi, ti, off, n, bank):
                    plan.op("act", lambda e: e.activation(out=zB[:, oi, off:off + n], in_=psum[:, bank, 0:n],
                                                           func=AF.Relu), reads=[("ps", bank)], writes=[("zB", oi, ti)])
                    plan.op("dve", lambda e: e.tensor_tensor(out=zB[:, oi, off:off + n], in0=zB[:, oi, off:off + n],
                                                              in1=zB[:, oi, off:off + n], op=ALU.mult),
                            reads=[("zB", oi, ti)], writes=[("zB", oi, ti)])
                proj(D["w_up"][L][:, hg * 1024:(hg + 1) * 1024], 8, [[(0, 0, 512)], [(0, 512, 512)]], [4, 4],
                     lambda k, off, n: hB[:, k, off:off + n], lambda ti: allk("hB", ti), TS, setB, ev_upB, wbufB)
                proj(D["w_dn"][L][hg * 1024:(hg + 1) * 1024, :], 8, [[(0, 0, 512)], [(0, 512, 512)]], [4, 4],
                     lambda k, off, n: zB[:, k, off:off + n], lambda ti: allk("zB", ti), TS, setB, ev_add, wbufB)
        for (t, off, n) in TLB:
            plan.dma("sp", D["yT"][:, :, off:off + n], xB[:, :, off:off + n], reads=xk(t))

    plan.barrier()
    replay(nc, plan)
    return nc


_NC_CACHE = {}


def _feat_major(v):
    v = np.asarray(v, np.float32)
    lead = v.shape[:-1]
    return np.ascontiguousarray(np.moveaxis(v.reshape(lead + (8, 128)), -1, 0))


def _emask():
    n = 24
    slopes = (2.0 ** (-8.0 * np.arange(1, n + 1, dtype=np.float64) / n)).reshape(3, 8)
    k = np.arange(128)[:, None]
    q = np.arange(128)[None, :]
    out = np.zeros((128, 3, 4, 2, 2, 128), np.float64)
    for g, (w, d) in enumerate(WINDOWS):
        for h in range(8):
            sl = slopes[g, h] * d
            prev = np.where(q <= k, np.exp(-sl * (q - k + 128)), 0.0)
            cur = np.where(q >= k, np.exp(-sl * (q - k)), 0.0)
            out[:, g, h // 2, h % 2, 0, :] = prev
            out[:, g, h // 2, h % 2, 1, :] = cur
    return out.reshape(128, 12, 512).astype(np.float32)


def _smc():
    slopes = (2.0 ** (-8.0 * np.arange(1, 25, dtype=np.float64) / 24)).reshape(3, 8)
    m = np.arange(128, dtype=np.float64)[:, None, None]
    t = np.arange(4, dtype=np.float64)[None, :, None]
    E = np.zeros((128, 6, 4, 8))
    d0 = 128 + t - m
    E[:, 0] = np.where(m >= t, np.exp(-slopes[0][None, None, :] * d0), 0.0)
    E[:, 1] = np.exp(-slopes[1][None, None, :] * 4 * (128 - m)) * np.ones((1, 4, 1))
    E[:, 2] = np.exp(-slopes[2][None, None, :] * 16 * (128 - m)) * np.ones((1, 4, 1))
    E[:, 3] = np.where((m <= t) & (m < 4), np.exp(-slopes[0][None, None, :] * (t - m)), 0.0)
    E[:, 4] = np.where(m == t, 1.0, 0.0) * np.ones((1, 1, 8))
    E[:, 5] = E[:, 4]
    out = np.zeros((128, 708), np.float32)
    out[:, 0:192] = E.reshape(128, 192)
    r = np.arange(32)
    f = np.arange(512)
    out[0:32, 192:704] = ((f[None, :] // 64) == (r[:, None] % 8)).astype(np.float32)
    out[0:32, 704:708] = ((r[:, None] // 8) == np.arange(4)[None, :]).astype(np.float32)
    return out


def kernel(x_prompt, x_sample, state_conv, cache_kv, norm_mix_g, norm_mlp_g, conv_w_pw1, conv_b_pw1,
           conv_w_dw, conv_b_dw, conv_ln_g, conv_ln_b, conv_w_pw2, conv_b_pw2, kv_norm_g, w_kv, k_norm_g,
           attn_w_q, q_norm_g, attn_w_o, mlp_w_up, mlp_w_down):
    f = lambda v: np.ascontiguousarray(np.asarray(v, np.float32))
    if "nc" not in _NC_CACHE:
        t0 = time.time()
        _NC_CACHE["nc"] = build_program()
        print("build time", time.time() - t0)
    nc = _NC_CACHE["nc"]
    x_prompt, x_sample, state_conv, cache_kv = f(x_prompt), f(x_sample), f(state_conv), f(cache_kv)
    prm = np.zeros((128, NPRM), np.float32)

    def put(name, arr):
        arr = np.asarray(arr, np.float32).reshape(128, -1)
        prm[:, PRM[name]:PRM[name] + arr.shape[1]] = arr
    put("nmg", _feat_major(norm_mix_g))
    put("nml", _feat_major(norm_mlp_g))
    put("bpw1", np.moveaxis(np.asarray(conv_b_pw1, np.float32).reshape(2, 16, 128), -1, 0))
    put("wdw", np.moveaxis(np.asarray(conv_w_dw, np.float32).reshape(2, 31, 8, 128), (3, 0, 2, 1), (0, 1, 2, 3)))
    put("bdw", _feat_major(conv_b_dw))
    put("lng", _feat_major(conv_ln_g))
    put("lnb", _feat_major(conv_ln_b))
    put("bpw2", _feat_major(conv_b_pw2))
    put("kvg", _feat_major(kv_norm_g))
    put("kng", np.tile(np.asarray(k_norm_g, np.float32), 2)[:, None])
    put("qng", np.tile(np.asarray(q_norm_g, np.float32), (1, 2)).T)
    prm[:, PRM["eps"]] = EPS
    ident = np.eye(128, dtype=np.float32)
    onesb = np.kron(np.eye(2, dtype=np.float32), np.full((64, 64), 1.0 / 64, np.float32))
    emask = _emask()
    smc = _smc()
    xpT = [np.ascontiguousarray(x_prompt[b].T).reshape(8, 128, 8192).transpose(1, 0, 2) for b in range(2)]
    shared = dict(ident=ident, onesb=onesb, emask=emask, smc=smc, w_pw1=f(conv_w_pw1), w_pw2=f(conv_w_pw2), w_kv=f(w_kv),
                  w_q=f(attn_w_q), w_o=f(attn_w_o), w_up=f(mlp_w_up), w_dn=f(mlp_w_down))
    in_maps = []
    for c in range(NCORES):
        b, j = c // 4, c % 4
        S = 2048 * j
        xin = np.zeros((128, 8, TA), np.float32)
        xs_ = x_sample[16 * c:16 * c + 16].reshape(64, 1024)
        xin[:, :, 0:64] = xs_.T.reshape(8, 128, 64).transpose(1, 0, 2)
        lo = S - 2112
        src_lo = max(lo, 0)
        xin[:, :, 64 + (src_lo - lo):TA] = xpT[b][:, :, src_lo:S + 2048]
        p = prm.copy()
        p[:, PRM["valid"]] = 1.0 if j > 0 else 0.0
        p[:, PRM["hbias"]] = 0.0 if j > 0 else -30000.0
        sc = state_conv[:, 16 * c:16 * c + 16]
        scT = np.ascontiguousarray(sc.reshape(2, 16, 30, 8, 128).transpose(4, 0, 3, 1, 2))
        m = dict(shared)
        m.update(xin=xin, prm=p, scT=scT, scN=np.ascontiguousarray(sc),
                 cache=np.ascontiguousarray(cache_kv[16 * c:16 * c + 16].reshape(16, 2048, 1024)))
        in_maps.append(m)
    if K_CORE is not None:
        t0 = time.time()
        res = run_bass_kernel_spmd(nc, [in_maps[int(K_CORE)]], core_ids=[0])
        print("run time", time.time() - t0)
        R = [res.results[0]] * NCORES
    else:
        res = run_bass_kernel_spmd(nc, in_maps, core_ids=list(range(NCORES)))
        R = res.results
    y_prompt = np.zeros((2, 8192, 1024), np.float32)
    y_sample = np.zeros((128, 4, 1024), np.float32)
    conv_prompt = np.zeros((2, 2, 30, 1024), np.float32)
    conv_sample = np.zeros((2, 128, 30, 1024), np.float32)
    kv_prompt = np.zeros((2, 2048, 2, 8, 64), np.float32)
    kv_sample = np.zeros((128, 4, 2, 8, 64), np.float32)
    for c in range(NCORES):
        b, j = c // 4, c % 4
        r = R[c]
        yT = np.asarray(r["yT"]).transpose(1, 0, 2).reshape(1024, TB)
        y_sample[16 * c:16 * c + 16] = yT[:, 0:64].T.reshape(16, 4, 1024)
        y_prompt[b, 2048 * j:2048 * j + 2048] = yT[:, 64:].T
        conv_sample[:, 16 * c:16 * c + 16] = np.asarray(r["cs"])
        kT = np.asarray(r["kT"]).transpose(1, 0, 2).reshape(512, TA)
        vN = np.asarray(r["vN"])
        kv_sample[16 * c:16 * c + 16, :, 0] = kT[:, 0:64].T.reshape(16, 4, 8, 64)
        kv_sample[16 * c:16 * c + 16, :, 1] = vN[0:64].reshape(16, 4, 8, 64)
        if j == 3:
            conv_prompt[:, b] = np.asarray(r["cp"])
            kv_prompt[b, :, 0] = kT[:, OWN0:].T.reshape(2048, 8, 64)
            kv_prompt[b, :, 1] = vN[OWN0:].reshape(2048, 8, 64)
    return (y_prompt, y_sample, conv_prompt, conv_sample, kv_prompt, kv_sample)
```

```python
import os
import time
import numpy as np
import concourse.bass as bass
import concourse.mybir as mybir
from concourse.bass_utils import run_bass_kernel_spmd
from contextlib import ExitStack

F32, BF16 = mybir.dt.float32, mybir.dt.bfloat16
AF = mybir.ActivationFunctionType
ALU = mybir.AluOpType
AX = mybir.AxisListType

NCORES = 8
TA = 4224
NST = 1408
TB = 2112
OWN0 = 2176
EPS = 1e-6
ARENA = 51800
WINDOWS = ((128, 1), (512, 4), (2048, 16))
K_STOP = int(os.environ.get('K_STOP', '99'))
K_NST = int(os.environ.get('K_NST', '3'))
K_CORE = os.environ.get('K_CORE')
K_DBG = int(os.environ.get('K_DBG', '0'))

PRM = {}
_o = 0
for _n, _w in (("nmg", 32), ("nml", 32), ("bpw1", 32), ("wdw", 2 * 8 * 31), ("bdw", 16), ("lng", 16),
               ("lnb", 16), ("bpw2", 16), ("kvg", 8), ("kng", 1), ("qng", 2), ("valid", 1), ("hbias", 1),
               ("eps", 1), ("zero", 1)):
    PRM[_n] = _o
    _o += _w
NPRM = _o


class Plan:
    ENG = ("pe", "act", "dve", "pool", "sp")
    NDS = 8

    def __init__(self):
        self.items = {e: [] for e in self.ENG}
        self.cnt = {e: 0 for e in self.ENG}
        self.known = {e: {} for e in self.ENG}
        self.bw = {}
        self.br = {}
        self.dma_n = {}
        self.dma_rr = {e: 0 for e in self.ENG}
        self.sems = set("s_" + e for e in self.ENG)

    def _need(self, eng, tok):
        if tok is None:
            return
        sem, val = tok
        if sem == "s_pe" and eng == "pe":
            return
        if self.known[eng].get(sem, 0) >= val:
            return
        self.known[eng][sem] = val
        self.items[eng].append(("w", sem, val))

    def _deps(self, eng, reads, writes):
        for b in reads:
            self._need(eng, self.bw.get(b))
            if isinstance(b, tuple) and b[0] == "ps":
                for s, v in self.br.get(b, {}).items():
                    if s != "s_" + eng:
                        self._need(eng, (s, v))
        for b in writes:
            self._need(eng, self.bw.get(b))
            for s, v in self.br.get(b, {}).items():
                self._need(eng, (s, v))

    def _mark(self, tok, reads, writes):
        for b in reads:
            d = self.br.setdefault(b, {})
            if d.get(tok[0], 0) < tok[1]:
                d[tok[0]] = tok[1]
        for b in writes:
            self.bw[b] = tok
            self.br[b] = {}

    def op(self, eng, fn, reads=(), writes=()):
        self._deps(eng, reads, writes)
        self.cnt[eng] += 1
        tok = ("s_" + eng, self.cnt[eng])
        self.items[eng].append(("x", fn))
        self._mark(tok, reads, writes)
        return tok

    def dma(self, q, out, in_, reads=(), writes=()):
        i = self.dma_rr[q] % self.NDS
        self.dma_rr[q] += 1
        sem = "d_%s%d" % (q, i)
        self.sems.add(sem)
        n = self.dma_n.get(sem, 0)
        if n:
            self._need(q, (sem, 16 * n))
        self._deps(q, reads, writes)
        self.dma_n[sem] = n + 1
        tok = (sem, 16 * (n + 1))
        self.items[q].append(("d", out, in_, sem))
        self._mark(tok, reads, writes)
        return tok

    def barrier(self):
        for e in self.ENG:
            for o in self.ENG:
                if self.cnt[o]:
                    self._need(e, ("s_" + o, self.cnt[o]))
            for sem, n in self.dma_n.items():
                self._need(e, (sem, 16 * n))


def replay(nc, plan):
    with ExitStack() as es:
        es.enter_context(nc.allow_low_precision("bf16 matmul operands, fp32 accumulation"))
        sems = {s: es.enter_context(nc.semaphore(s)) for s in sorted(plan.sems)}
        block = es.enter_context(nc.Block())

        def run(e, name):
            own = sems["s_" + name]
            for it in plan.items[name]:
                if it[0] == "w":
                    e.wait_ge(sems[it[1]], it[2])
                elif it[0] == "x":
                    it[1](e).then_inc(own, 1)
                else:
                    e.dma_start(out=it[1], in_=it[2]).then_inc(sems[it[3]], 16)

        @block.tensor
        def _(e):
            run(e, "pe")

        @block.scalar
        def _(e):
            run(e, "act")

        @block.vector
        def _(e):
            run(e, "dve")

        @block.gpsimd
        def _(e):
            run(e, "pool")

        @block.sync
        def _(e):
            run(e, "sp")


def build_program():
    nc = bass.Bass("TRN2", target_bir_lowering=False)
    plan = Plan()
    D = {}

    def din(name, shape, dt=F32):
        D[name] = nc.dram_tensor(name, list(shape), dt, kind="ExternalInput").ap()

    def dout(name, shape, dt=F32):
        D[name] = nc.dram_tensor(name, list(shape), dt, kind="ExternalOutput").ap()

    def dscr(name, shape, dt=F32):
        D[name] = nc.dram_tensor(name, list(shape), dt, kind="Internal").ap()

    din("xin", [128, 8, TA])
    din("prm", [128, NPRM])
    din("ident", [128, 128])
    din("onesb", [128, 128])
    din("w_pw1", [2, 1024, 2048])
    din("w_pw2", [2, 1024, 1024])
    din("w_kv", [1024, 1024])
    din("w_q", [2, 1024, 1536])
    din("w_o", [2, 512, 1024])
    din("w_up", [4, 1024, 4096])
    din("w_dn", [4, 4096, 1024])
    din("scT", [128, 2, 8, 16, 30])
    din("scN", [2, 16, 30, 1024])
    din("cache", [16, 2048, 1024])
    din("emask", [128, 12, 512])
    din("smc", [128, 708])
    dout("yT", [128, 8, TB])
    dout("kT", [128, 4, TA])
    dout("vN", [TA, 512])
    dout("cs", [2, 16, 30, 1024])
    dout("cp", [2, 30, 1024])
    if K_DBG:
        dout("dbg_h", [128, 8, TB], BF16)
        dout("dbg_q", [4, 128, 3, TB], BF16)
        dout("dbg_o", [128, 4, TB], BF16)
        dout("dbg_n", [4, 128, 2048])
        dout("dbg_d", [4, 128, 2048])
        dout("dbg_sraw", [128, 96])
        dout("dbg_pm", [128, 192], BF16)
        dout("dbg_osel", [128, 512], BF16)
        dout("dbg_qtm", [128, 1536], BF16)
    dscr("xs", [128, 8, TB])
    dscr("KTs", [4, 128, TA], BF16)
    dscr("Vs", [TA, 512], BF16)

    arena = nc.alloc_sbuf_tensor("arena", [128, ARENA], F32)
    psum = nc.alloc_psum_tensor("psum", [128, 8, 512], F32)

    class Bump:
        def __init__(self, start=0):
            self.off = start

        def __call__(self, shape, dt):
            n = int(np.prod(shape))
            nb = n * (4 if dt == F32 else 2)
            nb = (nb + 31) // 32 * 32
            assert self.off + nb <= ARENA * 4, ("SBUF overflow", self.off + nb)
            a = arena[:, self.off // 4:(self.off + nb) // 4]
            if dt != F32:
                a = a.bitcast(dt)
            a = a[:, 0:n]
            if len(shape) == 2:
                a = a.rearrange("p (a b) -> p a b", a=shape[0])
            elif len(shape) == 3:
                a = a.rearrange("p (a b c) -> p a b c", a=shape[0], b=shape[1])
            elif len(shape) == 4:
                a = a.rearrange("p (a b c d) -> p a b c d", a=shape[0], b=shape[1], c=shape[2])
            self.off += nb
            return a

    pb = Bump(0)
    prm = pb([NPRM], F32)
    identf = pb([128], F32)
    identb = pb([128], BF16)
    onesf = pb([128], F32)
    onesb16 = pb([128], BF16)
    oneshf = pb([128], F32)
    oneshb = pb([128], BF16)
    smc = pb([708], F32)
    tselb = pb([4], BF16)
    qg8 = pb([2], F32)
    ones64 = pb([64], BF16)
    PERSIST_END = pb.off

    def P(name, i=0, w=1):
        o = PRM[name] + i
        return prm[:, o:o + w]

    plan.dma("sp", prm, D["prm"], writes=["prm"])
    plan.dma("sp", identf, D["ident"], writes=["identf"])
    plan.dma("sp", oneshf, D["onesb"], writes=["oneshf"])
    plan.op("dve", lambda e: e.tensor_copy(out=identb, in_=identf), reads=["identf"], writes=["identb"])
    plan.op("dve", lambda e: e.tensor_copy(out=oneshb, in_=oneshf), reads=["oneshf"], writes=["oneshb"])
    plan.op("dve", lambda e: e.memset(onesf, 1.0 / 1024), writes=["onesf"])
    plan.op("dve", lambda e: e.memset(onesb16, 1.0 / 1024), writes=["onesb16"])
    plan.op("dve", lambda e: e.memset(ones64, 1.0), writes=["ones64"])
    plan.dma("sp", smc, D["smc"], writes=["smc"])
    plan.op("dve", lambda e: e.tensor_copy(out=tselb, in_=smc[:, 704:708]), reads=["smc"], writes=["tselb"])
    plan.op("dve", lambda e: e.tensor_scalar(out=qg8, in0=P("qng", 0, 2), scalar1=0.125, scalar2=None, op0=ALU.mult),
            reads=["prm"], writes=["qg8"])

    class RR:
        def __init__(self, items):
            self.items = items
            self.i = 0

        def __call__(self):
            v = self.items[self.i % len(self.items)]
            self.i += 1
            return v

    def mm_group(e, bank_tiles, kc, lhs_fn, rhs_fn):
        ins = None
        for k in range(kc):
            for (b, off, n) in bank_tiles:
                ins = e.matmul(psum[:, b, 0:n], lhsT=lhs_fn(k), rhs=rhs_fn(k, off, n),
                               start=(k == 0), stop=(k == kc - 1))
        return ins

    wslot = RR([0, 1, 2])

    def proj(Wap, kc, runs_per_group, nchunks_per_group, rhs_fn, in_keys_fn, tilesets, bankrr, evac_fn, wbuf):
        Wr = Wap.rearrange("(k p) o -> p k o", p=128)
        tiles = [t_ for ts in tilesets for t_ in ts]
        oi0 = 0
        pending = []
        for g, runs in enumerate(runs_per_group):
            s = wslot()
            for (dc, sc, w) in runs:
                plan.dma("pool", wbuf[s][:, 0:kc, dc:dc + w], Wr[:, :, sc:sc + w], writes=[("wb", s)])
            for (ti, off, n) in tiles:
                for lo in range(nchunks_per_group[g]):
                    b = bankrr()
                    if isinstance(b, tuple):
                        b = b[0]
                    plan.op("pe", (lambda e, b=b, off=off, n=n, s=s, lo=lo: mm_group(
                        e, [(b, off, n)], kc, lambda k: wbuf[s][:, k, lo * 128:(lo + 1) * 128], rhs_fn)),
                        reads=[("wb", s)] + in_keys_fn(ti), writes=[("ps", b)])
                    for fn in pending:
                        fn()
                    pending = []
                    r_ = evac_fn(oi0 + lo, ti, off, n, b)
                    if r_ is not None:
                        pending.append(r_)
            oi0 += nchunks_per_group[g]
        for fn in pending:
            fn()

    def allk(name, ti, nk=8):
        return [(name, k, ti) for k in range(nk)]

    R2B = [0, 512, 1024, 1408, 1440]

    def r2k(k, lo, hi):
        return [("R2", k, i) for i in range(4) if lo < R2B[i + 1] and hi > R2B[i]]

    def r2all(lo, hi):
        return [x for k in range(8) for x in r2k(k, lo, hi)]

    a = Bump(PERSIST_END)
    xT = a([8, NST], F32)
    R1 = a([8, NST], BF16)
    R2 = a([8, NST + 32], BF16)
    sq = a([8, 512], F32)
    sqb = sq.rearrange("p a b -> p (a b)").bitcast(BF16)[:, 0:4096].rearrange("p (a b) -> p a b", a=8)
    sig = [a([512], F32) for _ in range(3)]
    stA = [a([512], F32) for _ in range(2)]
    stA.append(stA[0])
    stB = [a([512], F32) for _ in range(2)]
    stB.append(stB[0])
    stC = [a([512], F32) for _ in range(2)]
    stC.append(stC[0])
    stD = [a([512], F32) for _ in range(2)]
    stD.append(stD[0])
    wbuf = [a([8, 512], BF16) for _ in range(3)]
    dg = [a([31, 128], BF16) for _ in range(2)]
    usamp = a([8, 16, 34], BF16)
    usampf = [a([8, 64], F32) for _ in range(2)]
    utailf = [a([8, 30], F32) for _ in range(2)]
    tailb = [a([8, 30], BF16) for _ in range(2)]
    kf = [a([512], F32) for _ in range(2)]
    kf.append(kf[0])
    kb = [a([512], BF16) for _ in range(2)]
    kb.append(kb[0])
    vf, vb = kf, kb
    tstage = sq.rearrange("p a b -> p (a b)")[:, 0:1024]
    hT = R1
    cT = R1
    uT = R2
    caT = R2[:, :, 0:NST]
    zT = R2[:, :, 0:NST]

    TL = [(0, 0, 512), (1, 512, 512), (2, 1024, 384)]
    set3 = RR([0, 1, 2, 3, 4, 5])
    bank6 = RR([0, 1, 2, 3, 4, 5])
    bankS = RR([6, 7])

    def rsqrt_chain(t, n, bank, tag):
        plan.op("act", lambda e: e.activation(out=stB[t][:, 0:n], in_=psum[:, bank, 0:n], func=AF.Ln,
                                               bias=P("eps"), scale=1.0),
                reads=[("ps", bank), "prm"], writes=[("stB", t % 2)])
        plan.op("act", lambda e: e.activation(out=stA[t][:, 0:n], in_=stB[t][:, 0:n], func=AF.Exp, scale=-0.5),
                reads=[("stB", t % 2)], writes=[("stA", t % 2)])

    def norm_stage(gname, gidx, dst, dst_name):
        for (t, off, n) in TL:
            plan.op("act", lambda e, off=off, n=n: e.activation(out=sqb[:, :, 0:n], in_=xT[:, :, off:off + n],
                                                                 func=AF.Square),
                    reads=allk("xT", t), writes=["sq"])
            b = bankS()
            plan.op("pe", lambda e, b=b, n=n: mm_group(e, [(b, 0, n)], 8, lambda k: onesb16,
                                                         lambda k, o_, n_: sqb[:, k, 0:n_]),
                    reads=["sq", "onesb16"], writes=[("ps", b)])
            rsqrt_chain(t, n, b, "n")
            for k in range(8):
                plan.op("dve", lambda e, k=k, t=t, off=off, n=n: e.scalar_tensor_tensor(
                    out=dst[:, k, off:off + n], in0=xT[:, k, off:off + n], scalar=P(gname, gidx * 8 + k),
                    in1=stA[t][:, 0:n], op0=ALU.mult, op1=ALU.mult),
                    reads=[("xT", k, t), ("stA", t % 2), "prm"], writes=[(dst_name, k, t)])

    for st in range(K_NST):
        c0 = st * NST
        for (t, off, n) in TL:
            plan.dma("sp", xT[:, :, off:off + n], D["xin"][:, :, c0 + off:c0 + off + n], writes=allk("xT", t))
        for l in range(2):
            norm_stage("nmg", l, hT, "R1")
            if K_STOP <= 1:
                break
            if st == 0:
                plan.op("dve", lambda e: e.memset(uT[:, :, 0:30], 0.0), writes=r2all(0, 30))
                for hh in range(2):
                    plan.dma("pool", usamp[:, 4 * hh:4 * hh + 4, :, 0:30], D["scT"][:, l, 4 * hh:4 * hh + 4],
                             writes=["usamp"])
            else:
                plan.op("dve", lambda e, l=l: e.tensor_copy(out=uT[:, :, 0:30], in_=tailb[l]),
                        reads=[("tailb", l)], writes=r2all(0, 30))
            def ev_pw1(oi, ti, off, n, bank, l=l, st=st):
                c, isg = oi // 2, oi % 2
                if not isg:
                    ev_pw1.abank[ti] = bank
                    return
                ab = ev_pw1.abank[ti]
                plan.op("act", lambda e: e.activation(out=sig[ti][:, 0:n], in_=psum[:, bank, 0:n], func=AF.Sigmoid,
                                                       bias=P("bpw1", l * 16 + 8 + c), scale=1.0),
                        reads=[("ps", bank), "prm"], writes=[("sig", ti)])
                plan.op("dve", lambda e: e.scalar_tensor_tensor(
                    out=uT[:, c, 30 + off:30 + off + n], in0=psum[:, ab, 0:n], scalar=P("bpw1", l * 16 + c),
                    in1=sig[ti][:, 0:n], op0=ALU.add, op1=ALU.mult),
                    reads=[("ps", ab), ("sig", ti), "prm"], writes=r2k(c, 30 + off, 30 + off + n))
                if st == 0 and ti == 0:
                    plan.op("dve", lambda e: e.scalar_tensor_tensor(
                        out=usampf[l][:, c, :], in0=psum[:, ab, 0:64], scalar=P("bpw1", l * 16 + c),
                        in1=sig[ti][:, 0:64], op0=ALU.add, op1=ALU.mult),
                        reads=[("ps", ab), ("sig", ti), "prm"], writes=[("usampf", l, c)])
                    plan.op("dve", lambda e: e.tensor_copy(
                        out=usamp[:, c, :, 30:34], in_=usampf[l][:, c, :].rearrange("p (s t) -> p s t", t=4)),
                        reads=[("usampf", l, c)], writes=["usamp"])
                if st == 2 and ti == 2:
                    plan.op("dve", lambda e: e.scalar_tensor_tensor(
                        out=utailf[l][:, c, :], in0=psum[:, ab, n - 30:n], scalar=P("bpw1", l * 16 + c),
                        in1=sig[ti][:, n - 30:n], op0=ALU.add, op1=ALU.mult),
                        reads=[("ps", ab), ("sig", ti), "prm"], writes=[("utailf", l, c)])
            ev_pw1.abank = {}
            runs = []
            for c in range(0, 8, 2):
                runs.append([(0, c * 128, 128), (128, 1024 + c * 128, 128),
                             (256, (c + 1) * 128, 128), (384, 1024 + (c + 1) * 128, 128)])
            proj(D["w_pw1"][l], 8, runs, [4] * 4, lambda k, off, n: hT[:, k, off:off + n],
                 lambda ti: allk("R1", ti), [TL], set3, ev_pw1, wbuf)
            if st == 1:
                plan.op("dve", lambda e: e.tensor_scalar(out=uT[:, :, 30 + 738:30 + 768], in0=uT[:, :, 30 + 738:30 + 768],
                                                          scalar1=P("valid"), scalar2=None, op0=ALU.mult),
                        reads=r2all(768, 798) + ["prm"], writes=r2all(768, 798))
            if K_STOP <= 2:
                break
            for c in range(8):
                ds = c % 2
                plan.op("dve", lambda e, ds=ds, c=c, l=l: e.tensor_tensor(
                    out=dg[ds], in0=identb.unsqueeze(1).broadcast_to([128, 31, 128]),
                    in1=P("wdw", (l * 8 + c) * 31, 31).unsqueeze(2).broadcast_to([128, 31, 128]), op=ALU.mult),
                    reads=["identb", "prm"], writes=[("dg", ds)])
                for (t, off, n) in TL:
                    b = bank6()

                    def conv_mm(e, ds=ds, c=c, off=off, n=n, b=b, st=st, t=t):
                        ins = None
                        for j in range(31):
                            ins = e.matmul(psum[:, b, 0:n], lhsT=dg[ds][:, j, :], rhs=uT[:, c, off + j:off + j + n],
                                           start=(j == 0), stop=(j == 30))
                        if st == 0 and t == 0:
                            for j in range(31):
                                ins = e.matmul(psum[:, b, 0:64].rearrange("p (s t) -> p s t", t=4), lhsT=dg[ds][:, j, :],
                                               rhs=usamp[:, c, :, j:j + 4], start=(j == 0), stop=(j == 30))
                        return ins
                    rk = [("dg", ds)] + r2k(c, off, off + n + 30)
                    if st == 0 and t == 0:
                        rk.append("usamp")
                    plan.op("pe", conv_mm, reads=rk, writes=[("ps", b)])
                    plan.op("act", lambda e, c=c, off=off, n=n, b=b, l=l: e.activation(
                        out=cT[:, c, off:off + n], in_=psum[:, b, 0:n], func=AF.Identity, bias=P("bdw", l * 8 + c),
                        scale=1.0), reads=[("ps", b), "prm"], writes=[("R1", c, t)])
            plan.op("dve", lambda e, l=l: e.tensor_copy(out=tailb[l], in_=uT[:, :, NST:NST + 30]),
                    reads=r2all(NST, NST + 30), writes=[("tailb", l)])
            if K_STOP <= 3:
                break
            for (t, off, n) in TL:
                plan.op("act", lambda e, off=off, n=n: e.activation(out=sqb[:, :, 0:n], in_=cT[:, :, off:off + n],
                                                                     func=AF.Square),
                        reads=allk("R1", t), writes=["sq"])
                b1, b2 = bankS(), bankS()
                plan.op("pe", lambda e, b1=b1, off=off, n=n: mm_group(
                    e, [(b1, off, n)], 8, lambda k: onesb16, lambda k, o_, n_: cT[:, k, o_:o_ + n_]),
                    reads=allk("R1", t) + ["onesb16"], writes=[("ps", b1)])
                plan.op("pe", lambda e, b2=b2, n=n: mm_group(
                    e, [(b2, 0, n)], 8, lambda k: onesb16, lambda k, o_, n_: sqb[:, k, 0:n_]),
                    reads=["sq", "onesb16"], writes=[("ps", b2)])
                plan.op("act", lambda e, t=t, n=n, b1=b1: e.activation(out=stC[t][:, 0:n], in_=psum[:, b1, 0:n],
                                                                        func=AF.Copy),
                        reads=[("ps", b1)], writes=[("stC", t % 2)])
                plan.op("dve", lambda e, t=t, n=n: e.tensor_tensor(out=stD[t][:, 0:n], in0=stC[t][:, 0:n],
                                                                    in1=stC[t][:, 0:n], op=ALU.mult),
                        reads=[("stC", t % 2)], writes=[("stD", t % 2)])
                plan.op("dve", lambda e, t=t, n=n, b2=b2: e.tensor_tensor(out=stD[t][:, 0:n], in0=psum[:, b2, 0:n],
                                                                           in1=stD[t][:, 0:n], op=ALU.subtract),
                        reads=[("ps", b2), ("stD", t % 2)], writes=[("stD", t % 2)])
                plan.op("act", lambda e, t=t, n=n: e.activation(out=stB[t][:, 0:n], in_=stD[t][:, 0:n], func=AF.Ln,
                                                                 bias=P("eps"), scale=1.0),
                        reads=[("stD", t % 2), "prm"], writes=[("stB", t % 2)])
                plan.op("act", lambda e, t=t, n=n: e.activation(out=stA[t][:, 0:n], in_=stB[t][:, 0:n], func=AF.Exp,
                                                                 scale=-0.5),
                        reads=[("stB", t % 2)], writes=[("stA", t % 2)])
                plan.op("dve", lambda e, t=t, n=n: e.scalar_tensor_tensor(
                    out=stC[t][:, 0:n], in0=stC[t][:, 0:n], scalar=-1.0, in1=stA[t][:, 0:n], op0=ALU.mult,
                    op1=ALU.mult), reads=[("stC", t % 2), ("stA", t % 2)], writes=[("stC", t % 2)])
                for hf in range(2):
                    ks = slice(4 * hf, 4 * hf + 4)
                    plan.op("dve", lambda e, t=t, off=off, n=n, ks=ks: e.tensor_tensor(
                        out=sq[:, ks, 0:n], in0=cT[:, ks, off:off + n],
                        in1=stA[t][:, 0:n].unsqueeze(1).broadcast_to([128, 4, n]), op=ALU.mult),
                        reads=allk("R1", t) + [("stA", t % 2)], writes=([("sqh", 0), "sq"] if hf == 0 else [("sqh", 1)]))
                    plan.op("dve", lambda e, t=t, n=n, ks=ks: e.tensor_tensor(
                        out=sq[:, ks, 0:n], in0=sq[:, ks, 0:n],
                        in1=stC[t][:, 0:n].unsqueeze(1).broadcast_to([128, 4, n]), op=ALU.add),
                        reads=[("sqh", hf), ("stC", t % 2)], writes=[("sqh", hf)])
                for k in range(8):
                    plan.op("act", lambda e, k=k, off=off, n=n, l=l: e.activation(
                        out=caT[:, k, off:off + n], in_=sq[:, k, 0:n], func=AF.Silu, bias=P("lnb", l * 8 + k),
                        scale=P("lng", l * 8 + k)), reads=[("sqh", k // 4), "prm"], writes=[("R2", k, t)])
                plan.op("act", lambda e, t=t: e.activation(out=stB[t][:, 0:1], in_=stB[t][:, 0:1], func=AF.Copy),
                        reads=[("sqh", 0), ("sqh", 1), ("stB", t % 2)], writes=["sq", ("stB", t % 2)])
            if K_STOP <= 4:
                break
            def ev_pw2(oi, ti, off, n, bank, l=l):
                plan.op("dve", lambda e: e.scalar_tensor_tensor(
                    out=xT[:, oi, off:off + n], in0=psum[:, bank, 0:n], scalar=P("bpw2", l * 8 + oi),
                    in1=xT[:, oi, off:off + n], op0=ALU.add, op1=ALU.add),
                    reads=[("ps", bank), ("xT", oi, ti), "prm"], writes=[("xT", oi, ti)])
            proj(D["w_pw2"][l], 8, [[(0, 0, 512)], [(0, 512, 512)]], [4, 4],
                 lambda k, off, n: caT[:, k, off:off + n], lambda ti: allk("R2", ti), [TL], set3, ev_pw2, wbuf)
            if K_STOP <= 5:
                break
            norm_stage("nml", l, hT, "R1")
            for hg in range(4):
                def ev_up(oi, ti, off, n, bank):
                    plan.op("act", lambda e: e.activation(out=zT[:, oi, off:off + n], in_=psum[:, bank, 0:n],
                                                           func=AF.Relu),
                            reads=[("ps", bank)], writes=[("R2", oi, ti)])
                    plan.op("dve", lambda e: e.tensor_tensor(out=zT[:, oi, off:off + n], in0=zT[:, oi, off:off + n],
                                                              in1=zT[:, oi, off:off + n], op=ALU.mult),
                            reads=[("R2", oi, ti)], writes=[("R2", oi, ti)])
                proj(D["w_up"][l][:, hg * 1024:(hg + 1) * 1024], 8, [[(0, 0, 512)], [(0, 512, 512)]], [4, 4],
                     lambda k, off, n: hT[:, k, off:off + n], lambda ti: allk("R1", ti), [TL], set3, ev_up, wbuf)

                def ev_dn(oi, ti, off, n, bank):
                    plan.op("dve", lambda e: e.tensor_tensor(out=xT[:, oi, off:off + n], in0=psum[:, bank, 0:n],
                                                              in1=xT[:, oi, off:off + n], op=ALU.add),
                            reads=[("ps", bank), ("xT", oi, ti)], writes=[("xT", oi, ti)])
                proj(D["w_dn"][l][hg * 1024:(hg + 1) * 1024, :], 8, [[(0, 0, 512)], [(0, 512, 512)]], [4, 4],
                     lambda k, off, n: zT[:, k, off:off + n], lambda ti: allk("R2", ti), [TL], set3, ev_dn, wbuf)
        if K_STOP <= 6:
            continue
        norm_stage("kvg", 0, hT, "R1")

        def ev_k(oi, ti, off, n, bank, c0=c0):
            sgb = sig[ti].bitcast(BF16)
            plan.op("act", lambda e: e.activation(out=sgb[:, 0:n], in_=psum[:, bank, 0:n], func=AF.Square),
                    reads=[("ps", bank)], writes=[("sig", ti)])
            b2 = bankS()
            plan.op("pe", lambda e: e.matmul(psum[:, b2, 0:n], lhsT=oneshb, rhs=sgb[:, 0:n], start=True, stop=True),
                    reads=[("sig", ti), "oneshb"], writes=[("ps", b2)])
            rsqrt_chain(ti, n, b2, "k")
            plan.op("dve", lambda e: e.scalar_tensor_tensor(
                out=kf[ti][:, 0:n], in0=psum[:, bank, 0:n], scalar=P("kng"), in1=stA[ti][:, 0:n], op0=ALU.mult,
                op1=ALU.mult), reads=[("ps", bank), ("stA", ti % 2), "prm"], writes=[("kf", ti % 2)])
            plan.op("act", lambda e: e.activation(out=kb[ti][:, 0:n], in_=kf[ti][:, 0:n], func=AF.Copy),
                    reads=[("kf", ti % 2)], writes=[("kb", ti % 2)])
            plan.dma("sp", D["kT"][:, oi, c0 + off:c0 + off + n], kf[ti][:, 0:n], reads=[("kf", ti % 2)])
            plan.dma("sp", D["KTs"][oi][:, c0 + off:c0 + off + n], kb[ti][:, 0:n], reads=[("kb", ti % 2)], writes=["KTs"])
        proj(D["w_kv"][:, 0:512], 8, [[(0, 0, 512)]], [4], lambda k, off, n: hT[:, k, off:off + n],
             lambda ti: allk("R1", ti), [TL], set3, ev_k, wbuf)
        if K_STOP <= 7:
            continue
        s = wslot()
        plan.dma("pool", wbuf[s], D["w_kv"].rearrange("(k p) o -> p k o", p=128)[:, :, 512:1024], writes=[("wb", s)])
        for tb in range(NST // 128):
            b = bank6()
            ti = tb // 4
            plan.op("pe", lambda e, b=b, tb=tb, s=s: mm_group(
                e, [(b, 0, 512)], 8, lambda k: hT[:, k, tb * 128:(tb + 1) * 128], lambda k, o_, n_: wbuf[s][:, k, :]),
                reads=allk("R1", ti) + [("wb", s)], writes=[("ps", b)])
            i = tb % 2
            plan.op("act", lambda e, b=b, i=i: e.activation(out=vf[i], in_=psum[:, b, :], func=AF.Copy),
                    reads=[("ps", b)], writes=[("kf", i)])
            plan.op("dve", lambda e, i=i: e.tensor_copy(out=vb[i], in_=vf[i]),
                    reads=[("kf", i)], writes=[("kb", i)])
            r0 = c0 + tb * 128
            plan.dma("sp", D["vN"][r0:r0 + 128, :], vf[i], reads=[("kf", i)])
            plan.dma("sp", D["Vs"][r0:r0 + 128, :], vb[i], reads=[("kb", i)], writes=["Vs"])
        if K_STOP <= 8:
            continue
        if st == 0:
            plan.dma("sp", D["xs"][:, :, 0:64], xT[:, :, 0:64], reads=allk("xT", 0), writes=["xs"])
        elif st == 1:
            plan.dma("sp", D["xs"][:, :, 64:704], xT[:, :, 768:1408], reads=allk("xT", 1) + allk("xT", 2), writes=["xs"])
        else:
            plan.dma("sp", D["xs"][:, :, 704:2112], xT[:, :, 0:1408],
                     reads=allk("xT", 0) + allk("xT", 1) + allk("xT", 2), writes=["xs"])
        if K_STOP <= 9:
            continue
        if st in (0, 2):
            for l in range(2):
                src = usampf[l] if st == 0 else utailf[l]
                w = 64 if st == 0 else 30
                bA, bB = bankS(), bankS()

                def tr(e, src=src, w=w, bA=bA, bB=bB):
                    ins = None
                    for k in range(8):
                        bk = bA if k < 4 else bB
                        ins = e.transpose(psum[0:w, bk, (k % 4) * 128:(k % 4 + 1) * 128], src[:, k, :], identf)
                    return ins
                rk = [("usampf", l, c) for c in range(8)] if st == 0 else [("utailf", l, c) for c in range(8)]
                plan.op("pe", tr, reads=rk + ["identf"], writes=[("ps", bA), ("ps", bB)])
                plan.op("act", lambda e, w=w, bA=bA: e.activation(out=tstage[0:w, 0:512], in_=psum[0:w, bA, :],
                                                                   func=AF.Copy),
                        reads=[("ps", bA)], writes=["sq"])
                plan.op("act", lambda e, w=w, bB=bB: e.activation(out=tstage[0:w, 512:1024], in_=psum[0:w, bB, :],
                                                                   func=AF.Copy),
                        reads=[("ps", bB)], writes=["sq"])
                if st == 0:
                    for s_ in range(16):
                        plan.dma("sp", D["cs"][l, s_, 26:30, :], tstage[4 * s_:4 * s_ + 4, :],
                                 reads=["sq"])
                    plan.dma("sp", D["cs"][l, :, 0:26, :], D["scN"][l, :, 4:30, :])
                else:
                    plan.dma("sp", D["cp"][l], tstage[0:30, :], reads=["sq"])

    plan.barrier()
    if K_STOP > 10:
        bb = Bump(PERSIST_END)
        wbufB = [bb([8, 512], BF16) for _ in range(3)]
        sA = [bb([512], F32) for _ in range(2)]
        sB = [bb([512], F32) for _ in range(2)]
        sC = [bb([512], F32) for _ in range(4)]
        hB = bb([8, TB], BF16)
        X0 = bb.off
        xB = bb([8, TB], F32)
        Z0 = bb.off
        zB = bb([8, TB], BF16)
        S0 = bb.off
        sqB = bb([8, 512], F32)
        sqBb = sqB.rearrange("p a b -> p (a b)").bitcast(BF16)[:, 0:4096].rearrange("p (a b) -> p a b", a=8)
        ab = Bump(X0)
        QT = ab([3, TB], BF16)
        KT = [ab([TA], BF16) for _ in range(2)]
        Vg = [ab([69, 128], BF16) for _ in range(2)]
        ex2 = [ab([512], BF16) for _ in range(2)]
        assert ab.off <= Z0
        ab = Bump(Z0)
        oT = ab([4, TB], BF16)
        Em = ab([3, 512], BF16)
        ex = [ab([512], BF16) for _ in range(2)]
        ex += ex2
        PT = [ab([512], BF16) for _ in range(2)]
        QTs = ab([3, 4, 64], BF16)
        KTsamp = ab([4, 64], BF16)
        PT += [ab([512], BF16) for _ in range(2)]
        assert ab.off <= S0
        ab = Bump(X0)
        CB = [ab([9, 1024], BF16) for _ in range(2)]
        Qtm = ab([3, 512], BF16)
        Vnew = [ab([512], BF16) for _ in range(2)]
        sraw = ab([3, 4, 8], F32)
        pexp = ab([3, 4, 8], F32)
        pn1 = ab([2, 48], F32)
        pm = ab([6, 4, 8], BF16)
        pz = ab([2, 4, 4, 8], BF16)
        pD = ab([32], BF16)
        pns = ab([32], BF16)
        prod = [ab([3, 512], F32) for _ in range(2)]
        rDs = ab([1], F32)
        osel = ab([512], BF16)
        assert ab.off <= Z0
        ab = Bump(S0)
        accN = ab([2048], F32)
        accD = ab([2048], F32)

        TLB = [(0, 0, 64), (1, 64, 512), (2, 576, 512), (3, 1088, 512), (4, 1600, 512)]
        TS = [TLB[0:3], TLB[3:5]]
        setB = RR([0, 1, 2, 3, 4, 5])
        setAtt = RR([0, 1, 2, 3, 4])
        statQ = RR([5, 6, 7])
        sBq = sA + sB
        bank4 = RR([0, 1, 2, 3])
        bankN = RR([4, 5])
        bankD = RR([6, 7])
        statB = RR([6, 7])

        def xk(ti):
            return allk("xB", ti)

        def normB(gname, gidx):
            for (t, off, n) in TLB:
                plan.op("act", lambda e, off=off, n=n: e.activation(out=sqBb[:, :, 0:n], in_=xB[:, :, off:off + n],
                                                                     func=AF.Square), reads=xk(t), writes=["sqB"])
                b = statB()
                plan.op("pe", lambda e, b=b, n=n: mm_group(e, [(b, 0, n)], 8, lambda k: onesb16,
                                                             lambda k, o_, n_: sqBb[:, k, 0:n_]),
                        reads=["sqB", "onesb16"], writes=[("ps", b)])
                i = t % 2
                plan.op("act", lambda e, i=i, n=n, b=b: e.activation(out=sB[i][:, 0:n], in_=psum[:, b, 0:n],
                                                                      func=AF.Ln, bias=P("eps"), scale=1.0),
                        reads=[("ps", b), "prm"], writes=[("sB", i)])
                plan.op("act", lambda e, i=i, n=n: e.activation(out=sA[i][:, 0:n], in_=sB[i][:, 0:n], func=AF.Exp,
                                                                 scale=-0.5),
                        reads=[("sB", i)], writes=[("sA", i)])
                for k in range(8):
                    plan.op("dve", lambda e, k=k, i=i, off=off, n=n: e.scalar_tensor_tensor(
                        out=hB[:, k, off:off + n], in0=xB[:, k, off:off + n], scalar=P(gname, gidx * 8 + k),
                        in1=sA[i][:, 0:n], op0=ALU.mult, op1=ALU.mult),
                        reads=[("xB", k, t), ("sA", i), "prm"], writes=[("hB", k, t)])

        EsM = smc[:, 0:192].rearrange("p (j t h) -> p j t h", j=6, t=4)
        sel32 = smc[0:32, 192:704]
        for (t, off, n) in TLB:
            plan.dma("sp", xB[:, :, off:off + n], D["xs"][:, :, off:off + n], reads=["xs"], writes=xk(t))
        for lb in range(2):
            L = 2 + lb
            if lb == 1:
                for (t, off, n) in TLB:
                    plan.dma("sp", D["xs"][:, :, off:off + n], xB[:, :, off:off + n], reads=xk(t), writes=["xs"])
            normB("nmg", L)
            if K_DBG and lb == 0:
                plan.dma("sp", D["dbg_h"], hB, reads=[x for t in range(5) for x in allk("hB", t)])
            plan.barrier()
            for hp_ in range(4):
                plan.dma("sp", KTsamp[:, hp_, :], D["KTs"][hp_][:, 0:64], reads=["KTs"], writes=["KTsamp"])
            for hp in range(4):
                kv = hp % 2
                plan.dma("sp", KT[kv], D["KTs"][hp], reads=["KTs"], writes=[("KT", kv)])
                Vs = D["Vs"]
                c_lo = hp * 128
                plan.dma("sp", Vg[kv][:, 0:17, :],
                         Vs[OWN0 - 128:OWN0 + 2048, c_lo:c_lo + 128].rearrange("(b i) c -> i b c", i=128),
                         reads=["Vs"], writes=[("Vg", kv)])
                for r in range(4):
                    plan.dma("sp", Vg[kv][:, 17 + 5 * r:22 + 5 * r, :],
                             Vs[OWN0 - 512 + r:OWN0 + 2048:4, c_lo:c_lo + 128].rearrange("(b i) c -> i b c", i=128),
                             reads=["Vs"], writes=[("Vg", kv)])
                for r in range(16):
                    plan.dma("sp", Vg[kv][:, 37 + 2 * r:39 + 2 * r, :],
                             Vs[OWN0 - 2048 + r:OWN0 + 2048:16, c_lo:c_lo + 128].rearrange("(b i) c -> i b c", i=128),
                             reads=["Vs"], writes=[("Vg", kv)])
                plan.dma("pool", Em, D["emask"][:, hp:12:4, :], writes=["Em"])

                def ev_q(oi, ti, off, n, bank, lb=lb):
                    i = ti % 4
                    scb = sC[i].bitcast(BF16)
                    plan.op("act", lambda e: e.activation(out=scb[:, 0:n], in_=psum[:, bank, 0:n], func=AF.Square),
                            reads=[("ps", bank)], writes=[("sC", i)])
                    def late():
                        b2 = statQ()
                        plan.op("pe", lambda e: e.matmul(psum[:, b2, 0:n], lhsT=oneshb, rhs=scb[:, 0:n], start=True,
                                                          stop=True), reads=[("sC", i), "oneshb"], writes=[("ps", b2)])
                        j = ti % 4
                        plan.op("act", lambda e: e.activation(out=sBq[j][:, 0:n], in_=psum[:, b2, 0:n], func=AF.Ln,
                                                               bias=P("eps"), scale=1.0),
                                reads=[("ps", b2), "prm"], writes=[("sBq", j)])
                        plan.op("act", lambda e: e.activation(out=sBq[j][:, 0:n], in_=sBq[j][:, 0:n], func=AF.Exp,
                                                               scale=-0.5),
                                reads=[("sBq", j)], writes=[("sBq", j)])
                        plan.op("dve", lambda e: e.scalar_tensor_tensor(
                            out=QT[:, oi, off:off + n], in0=psum[:, bank, 0:n], scalar=qg8[:, lb:lb + 1],
                            in1=sBq[j][:, 0:n], op0=ALU.mult, op1=ALU.mult),
                            reads=[("ps", bank), ("sBq", j), "qg8"], writes=["QT"])
                    return late
                proj(D["w_q"][lb], 8, [[(g * 128, (g * 4 + hp) * 128, 128) for g in range(3)]], [3],
                     lambda k, off, n: hB[:, k, off:off + n], lambda ti: allk("hB", ti), TS, setAtt, ev_q, wbufB)
                if K_DBG and lb == 0:
                    plan.dma("sp", D["dbg_q"][hp], QT, reads=["QT"])
                plan.op("act", lambda e, hp=hp: e.activation(out=QTs[:, :, hp, :], in_=QT[:, :, 0:64], func=AF.Copy),
                        reads=["QT"], writes=["QTs"])
                units = []
                for g, (w_, d) in enumerate(WINDOWS):
                    qbs = [(r, c) for r in range(d) for c in range(16 // d)]
                    for q4 in range(4):
                        for pr in range(2):
                            units.append((g, d, q4, pr, qbs[q4 * 4 + pr * 2:q4 * 4 + pr * 2 + 2]))
                nd_banks = {}

                def front(ui, kv=kv, units=units, nd_banks=nd_banks):
                    g, d, q4, pr, pair = units[ui]
                    if pr == 0:
                        nd_banks[(g, q4)] = (bankN(), bankD())
                    bX, bY = bank4(), bank4()
                    par = ui % 2

                    def st_mm(e):
                        ins = None
                        for qi, (r, c) in enumerate(pair):
                            q0 = 64 + r + d * 128 * c
                            for pc in range(2):
                                k0 = OWN0 + r + d * 128 * (c - 1 + pc)
                                for h, bk in ((0, bX), (1, bY)):
                                    ins = e.matmul(psum[:, bk, (qi * 2 + pc) * 128:(qi * 2 + pc + 1) * 128],
                                                   lhsT=KT[kv][h * 64:(h + 1) * 64, k0:k0 + 127 * d + 1:d],
                                                   rhs=QT[h * 64:(h + 1) * 64, g, q0:q0 + 127 * d + 1:d],
                                                   start=True, stop=True, tile_position=(h * 64, 0))
                        return ins
                    plan.op("pe", st_mm, reads=[("KT", kv), "QT"], writes=[("ps", bX), ("ps", bY)])
                    for h, bk in ((0, bX), (1, bY)):
                        hx = par * 2 + h
                        plan.op("act", lambda e, hx=hx, bk=bk: e.activation(out=ex[hx], in_=psum[:, bk, :], func=AF.Exp),
                                reads=[("ps", bk)], writes=[("ex", hx)])
                        for qi, (r, c) in enumerate(pair):
                            if c == 0:
                                plan.op("act", lambda e, hx=hx, bk=bk, qi=qi: e.activation(
                                    out=ex[hx][:, qi * 256:qi * 256 + 128], in_=psum[:, bk, qi * 256:qi * 256 + 128],
                                    func=AF.Exp, bias=P("hbias"), scale=1.0),
                                    reads=[("ps", bk), "prm"], writes=[("ex", hx)])
                        plan.op("dve", lambda e, h=h, g=g, hx=hx: e.tensor_tensor(
                            out=PT[hx].rearrange("p (a b) -> p a b", a=2),
                            in0=ex[hx].rearrange("p (a b) -> p a b", a=2),
                            in1=Em[:, g, h * 256:(h + 1) * 256].unsqueeze(1).broadcast_to([128, 2, 256]),
                            op=ALU.mult), reads=[("ex", hx), "Em"], writes=[("PT", hx)])

                def back(ui, kv=kv, units=units, nd_banks=nd_banks):
                    g, d, q4, pr, pair = units[ui]
                    bN, bD = nd_banks[(g, q4)]
                    par = ui % 2
                    vbase = [0, 17, 37][g]
                    nper = 16 // d + 1

                    def pv_mm(e):
                        ins = None
                        for qi, (r, c) in enumerate(pair):
                            sl = (pr * 2 + qi) * 128
                            for h in range(2):
                                for pc in range(2):
                                    rhs = PT[par * 2 + h][:, (qi * 2 + pc) * 128:(qi * 2 + pc + 1) * 128]
                                    e.matmul(psum[h * 64:(h + 1) * 64, bN, sl:sl + 128],
                                             lhsT=Vg[kv][:, vbase + r * nper + c + pc, h * 64:(h + 1) * 64],
                                             rhs=rhs, start=(pc == 0), stop=(pc == 1))
                                    ins = e.matmul(psum[h * 64:(h + 1) * 64, bD, sl:sl + 128],
                                                   lhsT=ones64, rhs=rhs, start=(pc == 0), stop=(pc == 1))
                        return ins
                    plan.op("pe", pv_mm, reads=[("PT", par * 2), ("PT", par * 2 + 1), ("Vg", kv), "ones64"],
                            writes=[("ps", bN), ("ps", bD)])
                    if pr == 0:
                        return
                    if g == 0:
                        cs_ = slice(q4 * 512, q4 * 512 + 512)
                        dstN, dstD = accN[:, cs_], accD[:, cs_]
                        srcN, srcD = psum[:, bN, :], psum[:, bD, :]
                    elif g == 1:
                        dstN, dstD = accN[:, q4:2048:4], accD[:, q4:2048:4]
                        srcN, srcD = psum[:, bN, :], psum[:, bD, :]
                    else:
                        dstN = accN.rearrange("p (i r) -> p r i", r=16)[:, q4 * 4:q4 * 4 + 4, :]
                        dstD = accD.rearrange("p (i r) -> p r i", r=16)[:, q4 * 4:q4 * 4 + 4, :]
                        srcN = psum[:, bN, :].rearrange("p (a b) -> p a b", a=4)
                        srcD = psum[:, bD, :].rearrange("p (a b) -> p a b", a=4)
                    if g == 0:
                        plan.op("act", lambda e: e.activation(out=dstN, in_=srcN, func=AF.Copy),
                                reads=[("ps", bN)], writes=["accN"])
                        plan.op("dve", lambda e: e.tensor_copy(out=dstD, in_=srcD), reads=[("ps", bD)], writes=["accD"])
                    else:
                        plan.op("dve", lambda e: e.tensor_tensor(out=dstN, in0=srcN, in1=dstN, op=ALU.add),
                                reads=[("ps", bN), "accN"], writes=["accN"])
                        plan.op("dve", lambda e: e.tensor_tensor(out=dstD, in0=srcD, in1=dstD, op=ALU.add),
                                reads=[("ps", bD), "accD"], writes=["accD"])
                front(0)
                for ui in range(len(units)):
                    if ui + 1 < len(units):
                        front(ui + 1)
                    back(ui)
                if K_DBG and lb == 0:
                    plan.dma("sp", D["dbg_n"][hp], accN, reads=["accN"])
                    plan.dma("sp", D["dbg_d"][hp], accD, reads=["accD"])
                plan.op("act", lambda e: e.activation(out=accD, in_=accD, func=AF.Ln), reads=["accD"], writes=["accD"])
                plan.op("act", lambda e: e.activation(out=accD, in_=accD, func=AF.Exp, scale=-1.0),
                        reads=["accD"], writes=["accD"])
                plan.op("dve", lambda e, hp=hp: e.tensor_tensor(out=oT[:, hp, 64:TB], in0=accN, in1=accD, op=ALU.mult),
                        reads=["accN", "accD"], writes=["oTp"])
            plan.barrier()
            plan.op("dve", lambda e: e.memset(pm, 0.0), writes=["pm"])
            plan.op("dve", lambda e: e.memset(pz, 0.0), writes=["pz"])
            for i_ in range(2):
                plan.op("dve", lambda e, i_=i_: e.memset(Vnew[i_], 0.0), writes=[("Vnew", i_)])
            for g in range(3):
                bq = bank4()
                plan.op("pe", lambda e, g=g, bq=bq: [e.transpose(
                    psum[0:64, bq, hp_ * 64:(hp_ + 1) * 64].bitcast(BF16), QTs[:, g, hp_, :], identb)
                    for hp_ in range(4)][-1], reads=["QTs", "identb"], writes=[("ps", bq)])
                plan.op("act", lambda e, g=g, bq=bq: e.activation(out=Qtm[0:64, g, :],
                                                                   in_=psum[0:64, bq, 0:256].bitcast(BF16), func=AF.Copy),
                        reads=[("ps", bq)], writes=["Qtm"])
            qsets = RR([(0, 1, 2), (3, 4, 5)])
            for sq_ in range(16):
                cb = sq_ % 2
                C = D["cache"][sq_]
                plan.dma("pool", CB[cb][:, 0, :], C[1920:2048, :], writes=[("CB", cb)])
                plan.dma("pool", CB[cb][:, 1:5, :], C[1536:2048, :].rearrange("(m t) f -> m t f", t=4),
                         writes=[("CB", cb)])
                plan.dma("pool", CB[cb][:, 5:9, :], C.rearrange("(m t) f -> m t f", t=16)[:, 0:4, :],
                         writes=[("CB", cb)])
                plan.dma("sp", Vnew[cb][0:4, :], D["Vs"][4 * sq_:4 * sq_ + 4, :], reads=["Vs"], writes=[("Vnew", cb)])
                c4 = slice(4 * sq_, 4 * sq_ + 4)
                for t in range(4):
                    qb_ = qsets()
                    plan.op("pe", lambda e, t=t, qb_=qb_, sq_=sq_: [e.matmul(
                        psum[:, qb_[g], :], lhsT=identb[0:64, 4 * sq_ + t:4 * sq_ + t + 1].broadcast_to([64, 128]),
                        rhs=Qtm[0:64, g, :], start=True, stop=True) for g in range(3)][-1],
                        reads=["Qtm", "identb"], writes=[("ps", b_) for b_ in qb_])
                    pi = t % 2
                    for g in range(3):
                        blk = 0 if g == 0 else (1 + t if g == 1 else 5 + t)
                        plan.op("dve", lambda e, g=g, blk=blk, pi=pi, qb_=qb_, cb=cb: e.tensor_tensor(
                            out=prod[pi][:, g, :], in0=CB[cb][:, blk, 0:512], in1=psum[:, qb_[g], :], op=ALU.mult),
                            reads=[("CB", cb), ("ps", qb_[g])], writes=[("prod", pi, g)])
                    plan.op("dve", lambda e, t=t, pi=pi: e.tensor_reduce(
                        out=sraw[:, :, t, :], in_=prod[pi].rearrange("p g (h d) -> p g h d", d=64), axis=AX.X,
                        op=ALU.add), reads=[("prod", pi, g_) for g_ in range(3)], writes=["sraw"])
                def snew(e, c4=c4):
                    ins = None
                    for g in range(3):
                        for hp_ in range(4):
                            for par in range(2):
                                ins = e.matmul(psum[0:4, 6 + par, (g * 4 + hp_) * 4:(g * 4 + hp_) * 4 + 4],
                                               lhsT=KTsamp[par * 64:(par + 1) * 64, hp_, c4],
                                               rhs=QTs[par * 64:(par + 1) * 64, g, hp_, c4],
                                               start=True, stop=True, tile_position=(par * 64, 0))
                    return ins
                plan.op("pe", snew, reads=["KTsamp", "QTs"], writes=[("ps", 6), ("ps", 7)])
                for par in range(2):
                    plan.op("act", lambda e, par=par: e.activation(out=pn1[0:4, par, :], in_=psum[0:4, 6 + par, 0:48],
                                                                    func=AF.Exp), reads=[("ps", 6 + par)], writes=[("pn1", par)])
                    plan.op("dve", lambda e, par=par: e.tensor_tensor(
                        out=pm[0:4, 3:6, :, par:8:2],
                        in0=pn1[0:4, par, :].rearrange("p (g h t) -> p g t h", g=3, h=4),
                        in1=EsM[0:4, 3:6, :, par:8:2], op=ALU.mult),
                        reads=[("pn1", par), "smc"], writes=["pm"])
                plan.op("act", lambda e: e.activation(out=pexp, in_=sraw, func=AF.Exp), reads=["sraw"], writes=["pexp"])
                plan.op("dve", lambda e: e.tensor_tensor(out=pm[:, 0:3, :, :], in0=pexp, in1=EsM[:, 0:3, :, :],
                                                          op=ALU.mult), reads=["pexp", "smc"], writes=["pm"])
                plan.op("dve", lambda e: e.tensor_copy(
                    out=pz.rearrange("p g a b h -> p g (a b) h")[:, :, 0:16:5, :], in_=pm[:, 1:3, :, :]),
                    reads=["pm"], writes=["pz"])
                plan.op("dve", lambda e: e.tensor_reduce(
                    out=pD, in_=pm.rearrange("p j t h -> p (t h) j"), axis=AX.X, op=ALU.add),
                    reads=["pm"], writes=["pD"])
                plan.op("dve", lambda e: e.tensor_reduce(
                    out=pns, in_=pm[:, 3:6, :, :].rearrange("p j t h -> p (t h) j"), axis=AX.X, op=ALU.add),
                    reads=["pm"], writes=["pns"])

                def pvs(e, cb=cb):
                    e.matmul(psum[0:32, 6, :], lhsT=pm[:, 0, :, :].rearrange("p t h -> p (t h)"),
                             rhs=CB[cb][:, 0, 512:1024], start=True, stop=False)
                    for g in (1, 2):
                        for t in range(4):
                            blk = 1 + t if g == 1 else 5 + t
                            e.matmul(psum[0:32, 6, :], lhsT=pz[:, g - 1, t, :, :].rearrange("p a h -> p (a h)"),
                                     rhs=CB[cb][:, blk, 512:1024], start=False, stop=False)
                    e.matmul(psum[0:32, 6, :], lhsT=pns, rhs=Vnew[cb], start=False, stop=True)
                    return e.matmul(psum[0:32, 7, 0:1], lhsT=pD, rhs=ones64[:, 0:1], start=True, stop=True)
                plan.op("pe", pvs, reads=["pm", "pz", "pns", "pD", ("CB", cb), ("Vnew", cb), "ones64"],
                        writes=[("ps", 6), ("ps", 7)])
                plan.op("dve", lambda e: e.reciprocal(out=rDs[0:32, :], in_=psum[0:32, 7, 0:1]),
                        reads=[("ps", 7)], writes=["rDs"])
                plan.op("dve", lambda e: e.scalar_tensor_tensor(out=osel[0:32, :], in0=psum[0:32, 6, :],
                                                                 scalar=rDs[0:32, :], in1=sel32, op0=ALU.mult,
                                                                 op1=ALU.mult),
                        reads=[("ps", 6), "rDs", "smc"], writes=["osel"])
                plan.op("pe", lambda e: [e.matmul(psum[:, 7, 8 + 4 * hp_:12 + 4 * hp_],
                                                  lhsT=osel[0:32, hp_ * 128:(hp_ + 1) * 128], rhs=tselb[0:32, :],
                                                  start=True, stop=True) for hp_ in range(4)][-1],
                        reads=["osel", "tselb"], writes=[("ps", 7)])
                plan.op("act", lambda e, c4=c4: e.activation(
                    out=oT[:, :, c4], in_=psum[:, 7, 8:24].rearrange("p (a b) -> p a b", a=4), func=AF.Copy),
                    reads=[("ps", 7)], writes=["oTs"])
            if K_DBG and lb == 0:
                plan.dma("sp", D["dbg_o"], oT, reads=["oTs", "oTp"])
                plan.dma("sp", D["dbg_sraw"], sraw.rearrange("p a b c -> p (a b c)"), reads=["sraw"])
                plan.dma("sp", D["dbg_pm"], pm.rearrange("p a b c -> p (a b c)"), reads=["pm"])
                plan.dma("sp", D["dbg_osel"], osel, reads=["osel"])
                plan.dma("sp", D["dbg_qtm"], Qtm.rearrange("p a b -> p (a b)"), reads=["Qtm"])
            plan.barrier()
            if K_STOP <= 11:
                break
            for (t, off, n) in TLB:
                plan.dma("sp", xB[:, :, off:off + n], D["xs"][:, :, off:off + n], reads=["xs"], writes=xk(t))

            def ev_add(oi, ti, off, n, bank):
                plan.op("dve", lambda e: e.tensor_tensor(out=xB[:, oi, off:off + n], in0=psum[:, bank, 0:n],
                                                          in1=xB[:, oi, off:off + n], op=ALU.add),
                        reads=[("ps", bank), ("xB", oi, ti)], writes=[("xB", oi, ti)])
            proj(D["w_o"][lb], 4, [[(0, 0, 512)], [(0, 512, 512)]], [4, 4],
                 lambda k, off, n: oT[:, k, off:off + n], lambda ti: ["oTs", "oTp"], TS, setB, ev_add, wbufB)
            normB("nml", L)
            for hg in range(4):
                def ev_upB(oi, ti, off, n, bank):
                    plan.op("act", lambda e: e.activation(out=zB[:, oi, off:off + n], in_=psum[:, bank, 0:n],
                                                           func=AF.Relu), reads=[("ps", bank)], writes=[("zB", oi, ti)])
                    plan.op("dve", lambda e: e.tensor_tensor(out=zB[:, oi, off:off + n], in0=zB[:, oi, off:off + n],
                                                              in1=zB[:, oi, off:off + n], op=ALU.mult),
                            reads=[("zB", oi, ti)], writes=[("zB", oi, ti)])
                proj(D["w_up"][L][:, hg * 1024:(hg + 1) * 1024], 8, [[(0, 0, 512)], [(0, 512, 512)]], [4, 4],
                     lambda k, off, n: hB[:, k, off:off + n], lambda ti: allk("hB", ti), TS, setB, ev_upB, wbufB)
                proj(D["w_dn"][L][hg * 1024:(hg + 1) * 1024, :], 8, [[(0, 0, 512)], [(0, 512, 512)]], [4, 4],
                     lambda k, off, n: zB[:, k, off:off + n], lambda ti: allk("zB", ti), TS, setB, ev_add, wbufB)
        for (t, off, n) in TLB:
            plan.dma("sp", D["yT"][:, :, off:off + n], xB[:, :, off:off + n], reads=xk(t))

    plan.barrier()
    replay(nc, plan)
    return nc


_NC_CACHE = {}


def _feat_major(v):
    v = np.asarray(v, np.float32)
    lead = v.shape[:-1]
    return np.ascontiguousarray(np.moveaxis(v.reshape(lead + (8, 128)), -1, 0))


def _emask():
    n = 24
    slopes = (2.0 ** (-8.0 * np.arange(1, n + 1, dtype=np.float64) / n)).reshape(3, 8)
    k = np.arange(128)[:, None]
    q = np.arange(128)[None, :]
    out = np.zeros((128, 3, 4, 2, 2, 128), np.float64)
    for g, (w, d) in enumerate(WINDOWS):
        for h in range(8):
            sl = slopes[g, h] * d
            prev = np.where(q <= k, np.exp(-sl * (q - k + 128)), 0.0)
            cur = np.where(q >= k, np.exp(-sl * (q - k)), 0.0)
            out[:, g, h // 2, h % 2, 0, :] = prev
            out[:, g, h // 2, h % 2, 1, :] = cur
    return out.reshape(128, 12, 512).astype(np.float32)


def _smc():
    slopes = (2.0 ** (-8.0 * np.arange(1, 25, dtype=np.float64) / 24)).reshape(3, 8)
    m = np.arange(128, dtype=np.float64)[:, None, None]
    t = np.arange(4, dtype=np.float64)[None, :, None]
    E = np.zeros((128, 6, 4, 8))
    d0 = 128 + t - m
    E[:, 0] = np.where(m >= t, np.exp(-slopes[0][None, None, :] * d0), 0.0)
    E[:, 1] = np.exp(-slopes[1][None, None, :] * 4 * (128 - m)) * np.ones((1, 4, 1))
    E[:, 2] = np.exp(-slopes[2][None, None, :] * 16 * (128 - m)) * np.ones((1, 4, 1))
    E[:, 3] = np.where((m <= t) & (m < 4), np.exp(-slopes[0][None, None, :] * (t - m)), 0.0)
    E[:, 4] = np.where(m == t, 1.0, 0.0) * np.ones((1, 1, 8))
    E[:, 5] = E[:, 4]
    out = np.zeros((128, 708), np.float32)
    out[:, 0:192] = E.reshape(128, 192)
    r = np.arange(32)
    f = np.arange(512)
    out[0:32, 192:704] = ((f[None, :] // 64) == (r[:, None] % 8)).astype(np.float32)
    out[0:32, 704:708] = ((r[:, None] // 8) == np.arange(4)[None, :]).astype(np.float32)
    return out


def kernel(x_prompt, x_sample, state_conv, cache_kv, norm_mix_g, norm_mlp_g, conv_w_pw1, conv_b_pw1,
           conv_w_dw, conv_b_dw, conv_ln_g, conv_ln_b, conv_w_pw2, conv_b_pw2, kv_norm_g, w_kv, k_norm_g,
           attn_w_q, q_norm_g, attn_w_o, mlp_w_up, mlp_w_down):
    f = lambda v: np.ascontiguousarray(np.asarray(v, np.float32))
    if "nc" not in _NC_CACHE:
        t0 = time.time()
        _NC_CACHE["nc"] = build_program()
        print("build time", time.time() - t0)
    nc = _NC_CACHE["nc"]
    x_prompt, x_sample, state_conv, cache_kv = f(x_prompt), f(x_sample), f(state_conv), f(cache_kv)
    prm = np.zeros((128, NPRM), np.float32)

    def put(name, arr):
        arr = np.asarray(arr, np.float32).reshape(128, -1)
        prm[:, PRM[name]:PRM[name] + arr.shape[1]] = arr
    put("nmg", _feat_major(norm_mix_g))
    put("nml", _feat_major(norm_mlp_g))
    put("bpw1", np.moveaxis(np.asarray(conv_b_pw1, np.float32).reshape(2, 16, 128), -1, 0))
    put("wdw", np.moveaxis(np.asarray(conv_w_dw, np.float32).reshape(2, 31, 8, 128), (3, 0, 2, 1), (0, 1, 2, 3)))
    put("bdw", _feat_major(conv_b_dw))
    put("lng", _feat_major(conv_ln_g))
    put("lnb", _feat_major(conv_ln_b))
    put("bpw2", _feat_major(conv_b_pw2))
    put("kvg", _feat_major(kv_norm_g))
    put("kng", np.tile(np.asarray(k_norm_g, np.float32), 2)[:, None])
    put("qng", np.tile(np.asarray(q_norm_g, np.float32), (1, 2)).T)
    prm[:, PRM["eps"]] = EPS
    ident = np.eye(128, dtype=np.float32)
    onesb = np.kron(np.eye(2, dtype=np.float32), np.full((64, 64), 1.0 / 64, np.float32))
    emask = _emask()
    smc = _smc()
    xpT = [np.ascontiguousarray(x_prompt[b].T).reshape(8, 128, 8192).transpose(1, 0, 2) for b in range(2)]
    shared = dict(ident=ident, onesb=onesb, emask=emask, smc=smc, w_pw1=f(conv_w_pw1), w_pw2=f(conv_w_pw2), w_kv=f(w_kv),
                  w_q=f(attn_w_q), w_o=f(attn_w_o), w_up=f(mlp_w_up), w_dn=f(mlp_w_down))
    in_maps = []
    for c in range(NCORES):
        b, j = c // 4, c % 4
        S = 2048 * j
        xin = np.zeros((128, 8, TA), np.float32)
        xs_ = x_sample[16 * c:16 * c + 16].reshape(64, 1024)
        xin[:, :, 0:64] = xs_.T.reshape(8, 128, 64).transpose(1, 0, 2)
        lo = S - 2112
        src_lo = max(lo, 0)
        xin[:, :, 64 + (src_lo - lo):TA] = xpT[b][:, :, src_lo:S + 2048]
        p = prm.copy()
        p[:, PRM["valid"]] = 1.0 if j > 0 else 0.0
        p[:, PRM["hbias"]] = 0.0 if j > 0 else -30000.0
        sc = state_conv[:, 16 * c:16 * c + 16]
        scT = np.ascontiguousarray(sc.reshape(2, 16, 30, 8, 128).transpose(4, 0, 3, 1, 2))
        m = dict(shared)
        m.update(xin=xin, prm=p, scT=scT, scN=np.ascontiguousarray(sc),
                 cache=np.ascontiguousarray(cache_kv[16 * c:16 * c + 16].reshape(16, 2048, 1024)))
        in_maps.append(m)
    if K_CORE is not None:
        t0 = time.time()
        res = run_bass_kernel_spmd(nc, [in_maps[int(K_CORE)]], core_ids=[0])
        print("run time", time.time() - t0)
        R = [res.results[0]] * NCORES
    else:
        res = run_bass_kernel_spmd(nc, in_maps, core_ids=list(range(NCORES)))
        R = res.results
    y_prompt = np.zeros((2, 8192, 1024), np.float32)
    y_sample = np.zeros((128, 4, 1024), np.float32)
    conv_prompt = np.zeros((2, 2, 30, 1024), np.float32)
    conv_sample = np.zeros((2, 128, 30, 1024), np.float32)
    kv_prompt = np.zeros((2, 2048, 2, 8, 64), np.float32)
    kv_sample = np.zeros((128, 4, 2, 8, 64), np.float32)
    for c in range(NCORES):
        b, j = c // 4, c % 4
        r = R[c]
        yT = np.asarray(r["yT"]).transpose(1, 0, 2).reshape(1024, TB)
        y_sample[16 * c:16 * c + 16] = yT[:, 0:64].T.reshape(16, 4, 1024)
        y_prompt[b, 2048 * j:2048 * j + 2048] = yT[:, 64:].T
        conv_sample[:, 16 * c:16 * c + 16] = np.asarray(r["cs"])
        kT = np.asarray(r["kT"]).transpose(1, 0, 2).reshape(512, TA)
        vN = np.asarray(r["vN"])
        kv_sample[16 * c:16 * c + 16, :, 0] = kT[:, 0:64].T.reshape(16, 4, 8, 64)
        kv_sample[16 * c:16 * c + 16, :, 1] = vN[0:64].reshape(16, 4, 8, 64)
        if j == 3:
            conv_prompt[:, b] = np.asarray(r["cp"])
            kv_prompt[b, :, 0] = kT[:, OWN0:].T.reshape(2048, 8, 64)
            kv_prompt[b, :, 1] = vN[OWN0:].reshape(2048, 8, 64)
    return (y_prompt, y_sample, conv_prompt, conv_sample, kv_prompt, kv_sample)
```
